# Optimizing a Trainium2 kernel written in Bass

```python
import jax, jax.numpy as jnp
from jax import lax
import numpy as np

D_MODEL = 1024
BATCH = 8
SEQ = 4096
DEPTH = 1

CHUNK = 64
N_MEM = 256
HEAD_DIM = 64
RWKV_HEADS = 8
RWKV_DIM = RWKV_HEADS * HEAD_DIM
DECAY_LORA = 64
ICLR_LORA = 64
GATE_LORA = 128
POOL_WINDOWS = (2, 4, 8, 16)
POOL_GROUPS = len(POOL_WINDOWS)
POOL_GROUP_DIM = 64
POOL_DIM = POOL_GROUPS * POOL_GROUP_DIM
MEM_HEADS = 4
MEM_DIM = MEM_HEADS * HEAD_DIM
N_BRANCH = 3
RWKV_IN = 3 * RWKV_DIM + DECAY_LORA + ICLR_LORA + GATE_LORA
MIX_IN = RWKV_IN + POOL_DIM + MEM_DIM
D_FF = 2816
CONV_W = 3
NORM_EPS = 1e-6
GN_EPS = 64e-5

kernel_name = "hybrid_rwkv7_pool_memxattn_convffn"


def rms_norm(x, g):
    xf = x.astype(jnp.float32)
    y = xf * lax.rsqrt(jnp.mean(xf * xf, axis=-1, keepdims=True) + NORM_EPS)
    return (y * g.astype(jnp.float32)).astype(x.dtype)


def shift_prev(p):
    return jnp.pad(p[:, :-1], ((0, 0), (1, 0), (0, 0)))


def rwkv7_scan(r, w, k, v, kk, a):
    B, T, H, N = r.shape
    xs = tuple(jnp.moveaxis(t, 1, 0) for t in (r, w, k, v, kk, a))

    def step(S, inp):
        r_t, w_t, k_t, v_t, kk_t, a_t = inp
        s_kk = jnp.einsum('bhvk,bhk->bhv', S, kk_t)
        S = (S * w_t[:, :, None, :]
             - jnp.einsum('bhv,bhk->bhvk', s_kk, kk_t * a_t)
             + jnp.einsum('bhv,bhk->bhvk', v_t, k_t))
        y_t = jnp.einsum('bhvk,bhk->bhv', S, r_t)
        return S, y_t

    S0 = jnp.zeros((B, H, N, N), jnp.float32)
    _, ys = lax.scan(step, S0, xs)
    return jnp.moveaxis(ys, 0, 1)


def rwkv7_branch(p, mu, w0, w_lora_b, a0, a_lora_b, g_lora_b, k_k, k_a, r_k, ln_w, ln_b):
    B, T, _ = p.shape
    f32 = jnp.float32
    p = p + (shift_prev(p) - p) * mu
    splits = [RWKV_DIM, 2 * RWKV_DIM, 3 * RWKV_DIM,
              3 * RWKV_DIM + DECAY_LORA, 3 * RWKV_DIM + DECAY_LORA + ICLR_LORA]
    r, k, v, wd, ad, gd = jnp.split(p, splits, axis=-1)
    w_log = -jax.nn.softplus(-(w0 + jnp.tanh(wd) @ w_lora_b).astype(f32)) - 0.5
    decay = jnp.exp(-jnp.exp(w_log))
    a = jax.nn.sigmoid((a0 + ad @ a_lora_b).astype(f32))
    g = jax.nn.sigmoid(gd) @ g_lora_b

    def heads(t):
        return t.astype(f32).reshape(B, T, RWKV_HEADS, HEAD_DIM)

    kk = heads(k * k_k)
    kk = kk * lax.rsqrt(jnp.sum(kk * kk, axis=-1, keepdims=True) + 1e-12)
    kf = k.astype(f32) * (1.0 + (a - 1.0) * k_a.astype(f32))
    rh, kh, vh, wh, ah = heads(r), heads(kf), heads(v), heads(decay), heads(a)
    y = rwkv7_scan(rh, wh, kh, vh, kk, ah)
    mean = jnp.mean(y, axis=-1, keepdims=True)
    var = jnp.mean(jnp.square(y - mean), axis=-1, keepdims=True)
    yn = ((y - mean) * lax.rsqrt(var + GN_EPS)).reshape(B, T, RWKV_DIM)
    yn = yn * ln_w.astype(f32) + ln_b.astype(f32)
    bonus = jnp.sum(rh * kh * r_k.astype(f32), axis=-1, keepdims=True) * vh
    out = (yn + bonus.reshape(B, T, RWKV_DIM)) * g.astype(f32)
    return out.astype(p.dtype)


def pool_branch(p, pool_w, pool_scale):
    B, T, _ = p.shape
    pf = p.astype(jnp.float32)
    pos = jnp.arange(1, T + 1)
    outs = []
    for gi, win in enumerate(POOL_WINDOWS):
        xg = pf[..., gi * POOL_GROUP_DIM:(gi + 1) * POOL_GROUP_DIM]
        cs = jnp.cumsum(xg, axis=1)
        cs_lag = jnp.pad(cs[:, :T - win], ((0, 0), (win, 0), (0, 0)))
        cnt = jnp.minimum(pos, win).astype(jnp.float32)[None, :, None]
        outs.append((cs - cs_lag) / cnt - xg)
    d = jnp.stack(outs, axis=2)
    y = jnp.einsum('btgc,gcd->btgd', d, pool_w.astype(jnp.float32)).reshape(B, T, POOL_DIM)
    return (y * pool_scale.astype(jnp.float32)).astype(p.dtype)


def memory_branch(q, mem_n, w_mem_kv):
    B, T, _ = q.shape
    M = mem_n.shape[1]
    k, v = jnp.split(mem_n @ w_mem_kv, 2, axis=-1)
    qh = q.reshape(B, T, MEM_HEADS, HEAD_DIM)
    kh = k.reshape(B, M, MEM_HEADS, HEAD_DIM)
    vh = v.reshape(B, M, MEM_HEADS, HEAD_DIM)
    s = jnp.einsum('bthd,bmhd->bhtm', qh, kh).astype(jnp.float32) * (HEAD_DIM ** -0.5)
    prob = jax.nn.softmax(s, axis=-1).astype(vh.dtype)
    o = jnp.einsum('bhtm,bmhd->bthd', prob, vh)
    return o.reshape(B, T, MEM_DIM)


def conv_ffn(h, w_in, conv_w, conv_b, w_out):
    u, gv = jnp.split(h @ w_in, 2, axis=-1)
    T = u.shape[1]
    up = jnp.pad(u, ((0, 0), (CONV_W - 1, 0), (0, 0)))
    uc = conv_w[0] * up[:, 0:T]
    for j in range(1, CONV_W):
        uc = uc + conv_w[j] * up[:, j:j + T]
    uc = uc + conv_b
    return (jax.nn.gelu(uc, approximate=False) * gv) @ w_out


def setup_inputs(seed: int = 0) -> dict:
    key = jax.random.key(seed)
    ks = jax.random.split(key, 32)
    L, D, F = DEPTH, D_MODEL, D_FF

    def nrm(k, shape, scale):
        return jax.random.normal(k, shape, jnp.float32) * scale

    def gain(k, shape, s=0.02):
        return 1.0 + nrm(k, shape, s)

    return {
        "x": nrm(ks[0], (BATCH, SEQ, D), 1.0),
        "mem": nrm(ks[1], (BATCH, N_MEM, D), 1.0),
        "norm_mix_g": gain(ks[2], (L, D)),
        "w_in_mix": nrm(ks[3], (L, D, MIX_IN), D ** -0.5),
        "mu_shift": jax.random.uniform(ks[4], (L, RWKV_IN), jnp.float32),
        "w0": jax.random.uniform(ks[5], (L, RWKV_DIM), jnp.float32, -4.0, 1.0),
        "w_lora_b": nrm(ks[6], (L, DECAY_LORA, RWKV_DIM), 0.5 * DECAY_LORA ** -0.5),
        "a0": nrm(ks[7], (L, RWKV_DIM), 0.1),
        "a_lora_b": nrm(ks[8], (L, ICLR_LORA, RWKV_DIM), ICLR_LORA ** -0.5),
        "g_lora_b": nrm(ks[9], (L, GATE_LORA, RWKV_DIM), GATE_LORA ** -0.5),
        "k_k": gain(ks[10], (L, RWKV_DIM), 0.1),
        "k_a": gain(ks[11], (L, RWKV_DIM), 0.1),
        "r_k": nrm(ks[12], (L, RWKV_HEADS, HEAD_DIM), 0.1),
        "ln_x_w": gain(ks[13], (L, RWKV_DIM)),
        "ln_x_b": nrm(ks[14], (L, RWKV_DIM), 0.02),
        "pool_w": nrm(ks[15], (L, POOL_GROUPS, POOL_GROUP_DIM, POOL_GROUP_DIM), POOL_GROUP_DIM ** -0.5),
        "pool_scale": gain(ks[16], (L, POOL_DIM), 0.1),
        "norm_mem_g": gain(ks[17], (L, D)),
        "w_mem_kv": nrm(ks[18], (L, D, 2 * MEM_DIM), D ** -0.5),
        "w_up_rwkv": nrm(ks[19], (L, RWKV_DIM, D), RWKV_DIM ** -0.5),
        "w_up_pool": nrm(ks[20], (L, POOL_DIM, D), POOL_DIM ** -0.5),
        "w_up_mem": nrm(ks[21], (L, MEM_DIM, D), MEM_DIM ** -0.5),
        "w_gate": nrm(ks[22], (L, D, N_BRANCH * D), D ** -0.5),
        "b_gate": nrm(ks[23], (L, N_BRANCH * D), 0.02),
        "w_o": nrm(ks[24], (L, D, D), D ** -0.5),
        "norm_ffn_g": gain(ks[25], (L, D)),
        "w_ffn_in": nrm(ks[26], (L, D, 2 * F), D ** -0.5),
        "ffn_conv_w": nrm(ks[27], (L, CONV_W, F), CONV_W ** -0.5),
        "ffn_conv_b": nrm(ks[28], (L, F), 0.02),
        "w_ffn_out": nrm(ks[29], (L, F, D), F ** -0.5),
        "norm_final_g": gain(ks[30], (D,)),
    }


def reference(x, mem, norm_mix_g, w_in_mix, mu_shift, w0, w_lora_b, a0, a_lora_b, g_lora_b,
              k_k, k_a, r_k, ln_x_w, ln_x_b, pool_w, pool_scale, norm_mem_g, w_mem_kv,
              w_up_rwkv, w_up_pool, w_up_mem, w_gate, b_gate, w_o, norm_ffn_g, w_ffn_in,
              ffn_conv_w, ffn_conv_b, w_ffn_out, norm_final_g):
    B, T, D = x.shape
    for l in range(DEPTH):
        h = rms_norm(x, norm_mix_g[l])
        p = h @ w_in_mix[l]
        p_rwkv = p[..., :RWKV_IN]
        p_pool = p[..., RWKV_IN:RWKV_IN + POOL_DIM]
        q_mem = p[..., RWKV_IN + POOL_DIM:]
        y_a = rwkv7_branch(p_rwkv, mu_shift[l], w0[l], w_lora_b[l], a0[l], a_lora_b[l],
                           g_lora_b[l], k_k[l], k_a[l], r_k[l], ln_x_w[l], ln_x_b[l])
        y_b = pool_branch(p_pool, pool_w[l], pool_scale[l])
        y_c = memory_branch(q_mem, rms_norm(mem, norm_mem_g[l]), w_mem_kv[l])
        gates = jax.nn.sigmoid((h @ w_gate[l] + b_gate[l]).astype(jnp.float32))
        gates = gates.astype(x.dtype).reshape(B, T, N_BRANCH, D)
        merged = (gates[:, :, 0] * (y_a @ w_up_rwkv[l])
                  + gates[:, :, 1] * (y_b @ w_up_pool[l])
                  + gates[:, :, 2] * (y_c @ w_up_mem[l]))
        x = x + merged @ w_o[l]
        x = x + conv_ffn(rms_norm(x, norm_ffn_g[l]), w_ffn_in[l], ffn_conv_w[l],
                         ffn_conv_b[l], w_ffn_out[l])
    return rms_norm(x, norm_final_g)
```

```python
import numpy as np
import contextlib
import concourse.bass as bass
import concourse.mybir as mybir
from concourse.bass_utils import run_bass_kernel_spmd

F32 = mybir.dt.float32
BF16 = mybir.dt.bfloat16
AF = mybir.ActivationFunctionType
ALU = mybir.AluOpType
AX = mybir.AxisListType

D = 1024
T = 4096
TT = 512
NST = T // TT
NCH = TT // 64
DFF = 2816
NFC = DFF // 128
MIX_IN = 2304
NEG_EH = -float(np.exp(-0.5))

C_G1, C_MU, C_W0, C_A0, C_KK, C_KA, C_RK, C_LNW, C_LNB, C_PS, C_BG, C_G2, C_CW, C_CB, C_GM = (
    0, 8, 22, 26, 30, 34, 38, 42, 46, 50, 52, 76, 84, 150, 172)
NCV = 180
M_M2, M_ML, M_I64, M_SCAN, M_INVC0, M_INVC = 0, 128, 192, 256, 768, 1792
NCM = 2816


class Buf:
    __slots__ = ("w", "r")

    def __init__(self):
        self.w = None
        self.r = {}


class Rec:
    def __getattr__(self, name):
        def f(*a, **k):
            self.call = (name, a, k)
            return self
        return f


class Eng:
    def __init__(self, name, sem, is_pe=False):
        self.name = name
        self.sem = sem
        self.count = 0
        self.seen = {}
        self.prog = []
        self.is_pe = is_pe
        self.pend = 0
        self.last = None


class Sched:
    def __init__(self, nc, es):
        self.nc = nc
        self.E = {}
        for n in ("pe", "act", "dve", "pool", "sp"):
            self.E[n] = Eng(n, es.enter_context(nc.semaphore("s_" + n)), is_pe=(n == "pe"))
        self.es = es
        self.ndsem = 0
        self.dsems = []

    def dsem(self):
        self.ndsem += 1
        ds = [self.es.enter_context(self.nc.semaphore("d%d" % self.ndsem)), 0]
        self.dsems.append(ds)
        return ds

    def barrier(self):
        self._flush_pe()
        for E in self.E.values():
            for X in self.E.values():
                if X is not E and X.count > 0 and E.seen.get(id(X.sem), 0) < X.count:
                    E.seen[id(X.sem)] = X.count
                    E.prog.append(("w", X.sem, X.count))
            for ds in self.dsems:
                if ds[1] > 0 and E.seen.get(id(ds[0]), 0) < ds[1]:
                    E.seen[id(ds[0])] = ds[1]
                    E.prog.append(("w", ds[0], ds[1]))

    def _flush_pe(self):
        P = self.E["pe"]
        if P.pend:
            P.last[3] = 1
            P.count += 1
            P.pend = 0

    def _deps(self, E, reads, writes):
        need = {}

        def add(tok, raw):
            sem, val, eng = tok
            if eng is E and E.is_pe:
                return
            k = id(sem)
            if k not in need or need[k][1] < val:
                need[k] = (sem, val)

        for b in reads:
            if b.w is not None:
                add(b.w, True)
        for b in writes:
            if b.w is not None:
                add(b.w, False)
            for t in b.r.values():
                add(t, False)
        P = self.E["pe"]
        for k, (sem, val) in need.items():
            if E.seen.get(k, 0) >= val:
                continue
            if sem is P.sem and val > P.count:
                self._flush_pe()
            E.seen[k] = val
            E.prog.append(("w", sem, val))

    def _commit(self, tok, reads, writes):
        for b in writes:
            b.w = tok
            b.r = {}
        k = id(tok[0])
        for b in reads:
            b.r[k] = tok

    def op(self, en, fn, r=(), w=()):
        E = self.E[en]
        self._deps(E, r, w)
        rec = Rec()
        fn(rec)
        if E.is_pe:
            ent = ["i", rec.call, E.sem, 0]
            E.prog.append(ent)
            E.last = ent
            E.pend += 1
            self._commit((E.sem, E.count + 1, E), r, w)
            return
        E.count += 1
        E.prog.append(["i", rec.call, E.sem, 1])
        self._commit((E.sem, E.count, E), r, w)

    def dma(self, en, ds, out, in_, r=(), w=()):
        E = self.E[en]
        self._deps(E, r, w)
        ds[1] += 16
        E.prog.append(["i", ("dma_start", (), dict(out=out, in_=in_)), ds[0], 16])
        self._commit((ds[0], ds[1], None), r, w)

    def seal(self, ds, bufs):
        for b in bufs:
            b.w = (ds[0], ds[1], None)

    def wait_all(self, en, toks):
        E = self.E[en]
        for sem, val, _ in toks:
            E.prog.append(("w", sem, val))

    def emit(self):
        nc = self.nc
        self._flush_pe()
        progs = {n: e.prog for n, e in self.E.items()}
        for e in self.E.values():
            e.prog = []

        def run(eng, prog):
            for it in prog:
                if it[0] == "w":
                    eng.wait_ge(it[1], it[2])
                else:
                    name, a, k = it[1]
                    ins = getattr(eng, name)(*a, **k)
                    if it[3]:
                        ins.then_inc(it[2], it[3])

        with nc.Block() as block:
            @block.tensor
            def _(e):
                run(e, progs["pe"])

            @block.scalar
            def _(e):
                run(e, progs["act"])

            @block.vector
            def _(e):
                run(e, progs["dve"])

            @block.gpsimd
            def _(e):
                run(e, progs["pool"])

            @block.sync
            def _(e):
                run(e, progs["sp"])


class Tl:
    def __init__(self, t):
        self.t = t
        self.bufs = {}

    def b(self, key=0):
        if key not in self.bufs:
            self.bufs[key] = Buf()
        return self.bufs[key]


def build(debug=False, stop=99):
    nc = bass.Bass("TRN2", target_bir_lowering=False)
    din = lambda n, s, dt=F32: nc.dram_tensor(n, s, dt, kind="ExternalInput").ap()
    x_d = din("x", [T, D])
    mem_d = din("mem", [256, D])
    win_d = din("w_in_mix", [D, MIX_IN])
    wl_d = din("w_lora_b", [64, 512])
    al_d = din("a_lora_b", [64, 512])
    gl_d = din("g_lora_b", [128, 512])
    pw_d = din("pool_w", [4, 64, 64])
    wkv_d = din("w_mem_kv", [D, 512])
    wupr_d = din("w_up_rwkv", [512, D])
    wupp_d = din("w_up_pool", [256, D])
    wupm_d = din("w_up_mem", [256, D])
    wg_d = din("w_gate", [D, 3 * D])
    wo_d = din("w_o", [D, D])
    wfi_d = din("w_ffn_in", [D, 2 * DFF])
    wfo_d = din("w_ffn_out", [DFF, D])
    cvec_d = din("cvec", [128, NCV])
    cm32_d = din("cm32", [128, NCM])
    cmb_d = din("cmb", [128, 320])
    gfin_d = din("gfin", [128, D])
    skind = "ExternalOutput" if debug else "Internal"
    hT_d = nc.dram_tensor("hT_d", [128, 8, T], BF16, kind=skind).ap()
    yT_d = nc.dram_tensor("yT_d", [128, 8, T], BF16, kind=skind).ap()
    x1_d = nc.dram_tensor("x1_d", [T, D], F32, kind=skind).ap()
    out_d = nc.dram_tensor("out", [T, D], F32, kind="ExternalOutput").ap()
    hT_db = [Buf() for _ in range(NST)]
    yT_db = [Buf() for _ in range(NST)]
    x1_db = [Buf() for _ in range(NST * 4)]

    with contextlib.ExitStack() as top:
        S = Sched(nc, top)
        sb = lambda es, n, s, dt=F32: Tl(es.enter_context(nc.sbuf_tensor("sb_" + n, s, dt)))
        PB = [Tl(top.enter_context(nc.psum_tensor("pb%d" % i, [128, 512], F32))) for i in range(7)]
        PT = Tl(top.enter_context(nc.psum_tensor("pt", [128, 1024], BF16)))
        cvec = sb(top, "cvec", [128, NCV])
        cder = sb(top, "cder", [128, 18])
        ident = sb(top, "ident", [128, 320], BF16)
        junk = sb(top, "junk", [128, D], BF16)
        hb = sb(top, "hb", [128, D], BF16)
        st4 = sb(top, "st4", [128, 4])
        dconst = S.dsem()
        S.dma("sp", dconst, cvec.t[:], cvec_d[:, :], w=[cvec.b()])
        dconst2 = S.dsem()
        S.dma("pool", dconst2, ident.t[:], cmb_d[:, :], w=[ident.b()])
        S.op("dve", lambda e: e.tensor_scalar(out=cder.t[:, 0:14], in0=cvec.t[:, C_MU:C_MU + 14], scalar1=-1.0, scalar2=1.0,
                                              op0=ALU.mult, op1=ALU.add), r=[cvec.b()], w=[cder.b()])
        S.op("dve", lambda e: e.tensor_scalar(out=cder.t[:, 14:18], in0=cvec.t[:, C_KA:C_KA + 4], scalar1=-1.0, scalar2=1.0,
                                              op0=ALU.mult, op1=ALU.add), r=[cvec.b()], w=[cder.b()])
        idn = ident.t[:, 0:128]
        bdo = ident.t[:, 128:256]
        ones64 = ident.t[:, 256:320]
        cb = [cvec.b(), cder.b(), ident.b()]

        def norm_T(xs_ap, bx, gcol, hT, bh, col0, npart=128):
            ss, ms = st4.t[0:npart, 0:1], st4.t[0:npart, 1:2]
            S.op("act", lambda e: e.activation(out=junk.t[0:npart, :], in_=xs_ap, func=AF.Square, accum_out=ss),
                 r=[bx, st4.b()], w=[junk.b(), st4.b()])
            S.op("dve", lambda e: e.tensor_scalar(out=ms, in0=ss, scalar1=1.0 / D, scalar2=1e-6, op0=ALU.mult, op1=ALU.add),
                 r=[st4.b()], w=[st4.b()])
            S.op("act", lambda e: e.activation(out=ms, in_=ms, func=AF.Sqrt), r=[st4.b()], w=[st4.b()])
            S.op("dve", lambda e: e.reciprocal(out=ms, in_=ms), r=[st4.b()], w=[st4.b()])
            S.op("dve", lambda e: e.tensor_scalar(out=hb.t[0:npart, :], in0=xs_ap, scalar1=ms, scalar2=None, op0=ALU.mult),
                 r=[bx, st4.b()], w=[hb.b()])
            for c in range(8):
                S.op("pe", lambda e, c=c: e.transpose(PT.t[:, c * 128:c * 128 + npart], hb.t[0:npart, c * 128:(c + 1) * 128],
                                                      idn[0:npart, 0:npart]), r=[hb.b(), ident.b()], w=[PT.b("A"), PT.b("B")])
            pv = PT.t[:, :].rearrange("p (c t) -> p c t", c=8)[:, :, 0:npart]
            gv = cvec.t[:, gcol:gcol + 8].unsqueeze(2).broadcast_to([128, 8, npart])
            S.op("dve", lambda e: e.tensor_tensor(out=hT[:, :, col0:col0 + npart], in0=pv, in1=gv, op=ALU.mult),
                 r=[PT.b("A"), PT.b("B"), cvec.b()], w=[bh])

        def load_w(es, name, wd, kchunks, ncols, ds, row0=0, tile=None, kc0=0):
            if tile is None:
                tile = sb(es, name, [128, kchunks, ncols], BF16)
            step = 1024
            for kc in range(kchunks):
                for n0 in range(0, ncols, step):
                    n1 = min(ncols, n0 + step)
                    S.dma("pool", ds, tile.t[:, kc0 + kc, n0:n1], wd[row0 + kc * 128:row0 + (kc + 1) * 128, n0:n1], w=[tile.b()])
            return tile

        with contextlib.ExitStack() as p1:
            dw1 = S.dsem()
            cm = sb(p1, "cm", [128, NCM])
            dcm = S.dsem()
            S.dma("sp", dcm, cm.t[:], cm32_d[:, :], w=[cm.b()])
            cb = cb + [cm.b()]
            win = load_w(p1, "win", win_d, 8, MIX_IN, dw1)
            lora = sb(p1, "lora", [128, 512], BF16)
            S.dma("pool", dw1, lora.t[0:64, :], wl_d[:, :], w=[lora.b()])
            S.dma("pool", dw1, lora.t[64:128, :], al_d[:, :], w=[lora.b()])
            gl = sb(p1, "gl", [128, 512], BF16)
            S.dma("pool", dw1, gl.t[:], gl_d[:, :], w=[gl.b()])
            pw = sb(p1, "pw", [128, 2, 64], BF16)
            for g in range(4):
                S.dma("pool", dw1, pw.t[64 * (g % 2):64 * (g % 2) + 64, g // 2, :], pw_d[g, :, :], w=[pw.b()])
            kT = sb(p1, "kT", [128, 2, 256], BF16)
            vtok = sb(p1, "vtok", [128, 2, 256], BF16)
            with contextlib.ExitStack() as p0:
                wkv = load_w(p0, "wkv", wkv_d, 8, 512, dw1)
                S.seal(dw1, [win.b(), lora.b(), gl.b(), pw.b(), wkv.b()])
                mems = sb(p0, "mems", [128, 2, D])
                memT = sb(p0, "memT", [128, 8, 256], BF16)
                dm = S.dsem()
                for mh in range(2):
                    S.dma("sp", dm, mems.t[:, mh, :], mem_d[mh * 128:(mh + 1) * 128, :], w=[mems.b(mh)])
                S.seal(dm, [mems.b(0), mems.b(1)])
                for mh in range(2):
                    norm_T(mems.t[:, mh, :], mems.b(mh), C_GM, memT.t, memT.b(), mh * 128)
                for fc in range(2):
                    for kc in range(8):
                        S.op("pe", lambda e, fc=fc, kc=kc: e.matmul(PB[0].t[:, 0:256], wkv.t[:, kc, fc * 128:(fc + 1) * 128],
                                                                     memT.t[:, kc, :], start=(kc == 0), stop=(kc == 7)),
                             r=[wkv.b(), memT.b()], w=[PB[0].b()])
                    S.op("act", lambda e, fc=fc: e.activation(out=kT.t[:, fc, :], in_=PB[0].t[:, 0:256], func=AF.Copy),
                         r=[PB[0].b()], w=[kT.b()])
                for mh in range(2):
                    for kc in range(8):
                        S.op("pe", lambda e, mh=mh, kc=kc: e.matmul(PB[1].t[:, 0:256], memT.t[:, kc, mh * 128:(mh + 1) * 128],
                                                                     wkv.t[:, kc, 256:512], start=(kc == 0), stop=(kc == 7)),
                             r=[wkv.b(), memT.b()], w=[PB[1].b()])
                    S.op("act", lambda e, mh=mh: e.activation(out=vtok.t[:, mh, :], in_=PB[1].t[:, 0:256], func=AF.Copy),
                         r=[PB[1].b()], w=[vtok.b()])
                S.emit()
            S.barrier()
            if stop == 0:
                S.emit()
                return nc

            xs = [sb(p1, "xs%d" % i, [128, D]) for i in range(2)]
            dxs = [S.dsem() for _ in range(2)]
            hT = sb(p1, "hT", [128, 8, TT], BF16)
            dh = S.dsem()
            pm = sb(p1, "pm", [128, 14, TT], BF16)
            tmp = sb(p1, "tmp", [128, TT])
            cy = sb(p1, "cy", [128, 14])
            pp = sb(p1, "pp", [128, 2, 16 + TT])
            ppa = sb(p1, "ppa", [128, 16 + TT])
            ppb = sb(p1, "ppb", [128, 16 + TT])
            dT = sb(p1, "dT", [128, 2, TT], BF16)
            qT = sb(p1, "qT", [128, 2, TT], BF16)
            tw = sb(p1, "tw", [128, TT], BF16)
            sg = sb(p1, "sg", [128, TT], BF16)
            fn = ["ld", "cum", "cx", "E1", "E2", "E3", "a", "kk", "rs", "kkn", "x1", "kf"]
            f = {n: sb(p1, "f_" + n, [128, TT]) for n in fn}
            kk2 = sb(p1, "kk2", [128, TT], BF16)
            KR = sb(p1, "KR", [128, 4, NCH, 2, 64], BF16)
            BK = sb(p1, "BK", [128, 4, NCH, 2, 64], BF16)
            WC = sb(p1, "WC", [128, 4, NCH])
            bonT = sb(p1, "bonT", [128, 4, TT], BF16)
            gT = sb(p1, "gT", [128, 4, TT], BF16)
            Asb = [sb(p1, "Asb%d" % i, [128, 4, 2, 128], BF16) for i in range(2)]
            GF = [sb(p1, "GF%d" % i, [128, 4, 64], BF16) for i in range(2)]
            Lsb = sb(p1, "Lsb", [128, 4, 64], BF16)
            GT = [sb(p1, "GT%d" % i, [128, 4, 64], BF16) for i in range(2)]
            P2 = [sb(p1, "P2%d" % i, [128, 4, 64], BF16) for i in range(2)]
            P2T = [sb(p1, "P2T%d" % i, [128, 4, 64], BF16) for i in range(2)]
            Zsb = sb(p1, "Zsb", [128, 4, 64], BF16)
            Un = sb(p1, "Un", [128, 4, 64], BF16)
            VBK = [sb(p1, "VBK%d" % i, [128, 3, 4, 64], BF16) for i in range(2)]
            Ysb = [sb(p1, "Ysb%d" % i, [128, 4, 64]) for i in range(2)]
            Ysq = [sb(p1, "Ysq%d" % i, [128, 4, 64]) for i in range(2)]
            yh = [sb(p1, "yh%d" % i, [128, 4, 64], BF16) for i in range(2)]
            yst = [sb(p1, "yst%d" % i, [128, 4, 4]) for i in range(2)]
            eps_gn = sb(p1, "eps_gn", [128, 1])
            Sf = sb(p1, "Sf", [128, 4, 64])
            Sbf = sb(p1, "Sbf", [128, 4, 64], BF16)
            yhT = sb(p1, "yhT", [128, 4, TT], BF16)
            yT = sb(p1, "yT", [128, 8, TT], BF16)
            dy = S.dsem()
            eT = sb(p1, "eT", [128, 2, TT], BF16)
            rden = sb(p1, "rden", [128, TT])

            S.op("dve", lambda e: e.memset(cy.t[:], 0.0), w=[cy.b()])
            S.op("dve", lambda e: e.memset(pp.t[:], 0.0), w=[pp.b()])
            S.op("dve", lambda e: e.memset(ppa.t[:], 0.0), w=[ppa.b(0), ppa.b(64)])
            S.op("dve", lambda e: e.memset(ppb.t[:], 0.0), w=[ppb.b(0), ppb.b(64)])
            S.op("dve", lambda e: e.memset(Sf.t[:], 0.0), w=[Sf.b()])
            S.op("dve", lambda e: e.memset(Sbf.t[:], 0.0), w=[Sbf.b()])
            S.op("dve", lambda e: e.memset(eps_gn.t[:], 64e-5), w=[eps_gn.b()])
            M2v = cm.t[:, M_M2:M_M2 + 128].unsqueeze(1).broadcast_to([128, 4, 128])
            MLv = cm.t[:, M_ML:M_ML + 64].unsqueeze(1).broadcast_to([128, 4, 64])
            I64v = cm.t[:, M_I64:M_I64 + 64].unsqueeze(1).broadcast_to([128, 4, 64])
            scanm = cm.t[:, M_SCAN:M_SCAN + 512]
            mmb = 0

            for st in range(NST if stop >= 2 else 1):
                t0 = st * TT
                for sub in range(4):
                    i = (st * 4 + sub) % 2
                    S.dma("sp", dxs[i], xs[i].t[:], x_d[t0 + sub * 128:t0 + (sub + 1) * 128, :], w=[xs[i].b()])
                    norm_T(xs[i].t[:], xs[i].b(), C_G1, hT.t, hT.b(), sub * 128)
                S.dma("sp", dh, hT_d[:, :, t0:t0 + TT], hT.t[:], r=[hT.b()], w=[hT_db[st]])
                if stop == 1.1:
                    break
                for oc in range(18):
                    pbk = PB[mmb % 2]
                    mmb += 1
                    for kc in range(8):
                        S.op("pe", lambda e, oc=oc, kc=kc, pbk=pbk: e.matmul(pbk.t[:], win.t[:, kc, oc * 128:(oc + 1) * 128], hT.t[:, kc, :],
                                                                            start=(kc == 0), stop=(kc == 7)),
                             r=[win.b(), hT.b()], w=[pbk.b()])
                    ps = pbk.t
                    if oc < 14:
                        mu = cvec.t[:, C_MU + oc:C_MU + oc + 1]
                        om = cder.t[:, oc:oc + 1]
                        S.op("act", lambda e, ps=ps, mu=mu: e.activation(out=tmp.t[:, 1:TT], in_=ps[:, 0:TT - 1], func=AF.Copy, scale=mu),
                             r=[pbk.b()] + cb, w=[tmp.b()])
                        S.op("dve", lambda e, oc=oc, mu=mu: e.tensor_scalar(out=tmp.t[:, 0:1], in0=cy.t[:, oc:oc + 1], scalar1=mu, scalar2=None,
                                                                           op0=ALU.mult), r=[cy.b()] + cb, w=[tmp.b()])
                        S.op("dve", lambda e, oc=oc, ps=ps: e.tensor_copy(out=cy.t[:, oc:oc + 1], in_=ps[:, TT - 1:TT]), r=[pbk.b()], w=[cy.b()])
                        S.op("dve", lambda e, oc=oc, ps=ps, om=om: e.scalar_tensor_tensor(out=pm.t[:, oc, :], in0=ps[:, :], scalar=om, in1=tmp.t[:, :],
                                                                                          op0=ALU.mult, op1=ALU.add),
                             r=[pbk.b(), tmp.b()] + cb, w=[pm.b(oc)])
                    elif oc < 16:
                        S.op("act", lambda e, oc=oc, ps=ps: e.activation(out=pp.t[:, oc - 14, 16:16 + TT], in_=ps[:, :], func=AF.Copy),
                             r=[pbk.b()], w=[pp.b()])
                    else:
                        S.op("act", lambda e, oc=oc, ps=ps: e.activation(out=qT.t[:, oc - 16, :], in_=ps[:, :], func=AF.Copy),
                             r=[pbk.b()], w=[qT.b()])
                if stop == 1.2:
                    break
                invc = cm.t[:, (M_INVC0 if st == 0 else M_INVC):(M_INVC0 if st == 0 else M_INVC) + 1024].rearrange("p (c t) -> p c t", c=2)
                for g in range(4):
                    ci, pb = g // 2, 64 * (g % 2)
                    src, bsrc = pp.t[pb:pb + 64, ci, :], pp.b()
                    for lv in range(g + 1):
                        sh = 1 << lv
                        dst = ppa if lv % 2 == 0 else ppb
                        S.op("dve", lambda e, src=src, dst=dst, sh=sh, pb=pb: e.tensor_tensor(out=dst.t[pb:pb + 64, sh:16 + TT], in0=src[:, sh:16 + TT],
                                                                                            in1=src[:, 0:16 + TT - sh], op=ALU.add),
                             r=[bsrc], w=[dst.b(pb)])
                        if sh > 1:
                            pass
                        src, bsrc = dst.t[pb:pb + 64, :], dst.b(pb)
                    S.op("dve", lambda e, src=src, pb=pb, ci=ci: e.tensor_tensor(out=ppa.t[pb:pb + 64, 16:16 + TT] if False else tmp.t[pb:pb + 64, :],
                                                                                 in0=src[:, 16:16 + TT], in1=invc[pb:pb + 64, ci, :], op=ALU.mult),
                         r=[bsrc] + cb, w=[tmp.b()])
                    S.op("dve", lambda e, pb=pb, ci=ci: e.tensor_tensor(out=dT.t[pb:pb + 64, ci, :], in0=tmp.t[pb:pb + 64, :],
                                                                        in1=pp.t[pb:pb + 64, ci, 16:16 + TT], op=ALU.subtract),
                         r=[tmp.b(), pp.b()], w=[dT.b()])
                for ci in range(2):
                    pbk = PB[mmb % 2]
                    mmb += 1
                    for g2 in range(2):
                        pb = 64 * g2
                        S.op("pe", lambda e, ci=ci, pb=pb, pbk=pbk: e.matmul(pbk.t[pb:pb + 64, :], pw.t[pb:pb + 64, ci, :], dT.t[pb:pb + 64, ci, :],
                                                                            start=True, stop=True), r=[pw.b(), dT.b()], w=[pbk.b()])
                    S.op("act", lambda e, ci=ci, pbk=pbk: e.activation(out=yT.t[:, 4 + ci, :], in_=pbk.t[:, :], func=AF.Copy,
                                                                      scale=cvec.t[:, C_PS + ci:C_PS + ci + 1]), r=[pbk.b()] + cb, w=[yT.b(4 + ci)])
                S.op("dve", lambda e: e.tensor_copy(out=pp.t[:, :, 0:16], in_=pp.t[:, :, TT:TT + 16]), r=[pp.b()], w=[pp.b()])
                if stop == 1.3:
                    break
                for jm in range(2):
                    for hh in range(2):
                        pb = 64 * hh
                        hm = 2 * jm + hh
                        for mh in range(2):
                            S.op("pe", lambda e, jm=jm, pb=pb, mh=mh: e.matmul(PB[2 + mh].t[:, :], kT.t[pb:pb + 64, jm, mh * 128:(mh + 1) * 128],
                                                                               qT.t[pb:pb + 64, jm, :], start=True, stop=True),
                                 r=[kT.b(), qT.b()], w=[PB[2 + mh].b()])
                            S.op("act", lambda e, mh=mh: e.activation(out=eT.t[:, mh, :], in_=PB[2 + mh].t[:, :], func=AF.Exp, scale=0.125),
                                 r=[PB[2 + mh].b()], w=[eT.b(mh)])
                        for mh in range(2):
                            S.op("pe", lambda e, hm=hm, pb=pb, mh=mh: e.matmul(PB[4].t[pb:pb + 64, :], vtok.t[:, mh, hm * 64:(hm + 1) * 64], eT.t[:, mh, :],
                                                                               start=(mh == 0), stop=(mh == 1)),
                                 r=[vtok.b(), eT.b(mh)], w=[PB[4].b()])
                        for mh in range(2):
                            S.op("pe", lambda e, pb=pb, mh=mh: e.matmul(PB[5].t[pb:pb + 64, :], ones64, eT.t[:, mh, :],
                                                                        start=(mh == 0), stop=(mh == 1)),
                                 r=[ident.b(), eT.b(mh)], w=[PB[5].b()])
                    S.op("dve", lambda e: e.reciprocal(out=rden.t[:], in_=PB[5].t[:, :]), r=[PB[5].b()], w=[rden.b()])
                    S.op("dve", lambda e, jm=jm: e.tensor_tensor(out=yT.t[:, 6 + jm, :], in0=PB[4].t[:, :], in1=rden.t[:], op=ALU.mult),
                         r=[PB[4].b(), rden.b()], w=[yT.b(6 + jm)])
                if stop == 1.4:
                    break
                S.op("act", lambda e: e.activation(out=tw.t[0:64, :], in_=pm.t[0:64, 12, :], func=AF.Tanh), r=[pm.b(12)], w=[tw.b()])
                S.op("act", lambda e: e.activation(out=sg.t[:], in_=pm.t[:, 13, :], func=AF.Sigmoid), r=[pm.b(13)], w=[sg.b()])
                for j in range(4):
                    cs = slice(j * 128, (j + 1) * 128)
                    r_, k_, v_ = pm.t[:, j, :], pm.t[:, 4 + j, :], pm.t[:, 8 + j, :]
                    cv = lambda c0: cvec.t[:, c0 + j:c0 + j + 1]
                    pbk = PB[mmb % 2]
                    mmb += 1
                    S.op("pe", lambda e, pbk=pbk, cs=cs: e.matmul(pbk.t[:], lora.t[0:64, cs], tw.t[0:64, :], start=True, stop=True),
                         r=[lora.b(), tw.b()], w=[pbk.b()])
                    S.op("act", lambda e, pbk=pbk, cv=cv: e.activation(out=f["ld"].t[:], in_=pbk.t[:], func=AF.Sigmoid, bias=cv(C_W0)),
                         r=[pbk.b()] + cb, w=[f["ld"].b()])
                    S.op("dve", lambda e: e.tensor_scalar(out=f["ld"].t[:], in0=f["ld"].t[:], scalar1=NEG_EH, scalar2=None, op0=ALU.mult),
                         r=[f["ld"].b()], w=[f["ld"].b()])
                    S.op("dve", lambda e: e.tensor_tensor_scan(out=f["cum"].t[:], data0=scanm, data1=f["ld"].t[:], initial=0.0,
                                                               op0=ALU.mult, op1=ALU.add), r=[f["ld"].b()] + cb, w=[f["cum"].b()])
                    S.op("dve", lambda e: e.tensor_tensor(out=f["cx"].t[:], in0=f["cum"].t[:], in1=f["ld"].t[:], op=ALU.subtract),
                         r=[f["cum"].b(), f["ld"].b()], w=[f["cx"].b()])
                    S.op("act", lambda e: e.activation(out=f["E1"].t[:], in_=f["cum"].t[:], func=AF.Exp), r=[f["cum"].b()], w=[f["E1"].b()])
                    S.op("act", lambda e: e.activation(out=f["E2"].t[:], in_=f["cum"].t[:], func=AF.Exp, scale=-1.0), r=[f["cum"].b()], w=[f["E2"].b()])
                    S.op("act", lambda e: e.activation(out=f["E3"].t[:], in_=f["cx"].t[:], func=AF.Exp), r=[f["cx"].b()], w=[f["E3"].b()])
                    S.op("dve", lambda e, j=j: e.tensor_copy(out=WC.t[:, j, :], in_=f["E1"].t[:, :].rearrange("p (c t) -> p c t", t=64)[:, :, 63]),
                         r=[f["E1"].b()], w=[WC.b()])
                    pbk = PB[mmb % 2]
                    mmb += 1
                    S.op("pe", lambda e, pbk=pbk, cs=cs: e.matmul(pbk.t[:], lora.t[64:128, cs], pm.t[64:128, 12, :], start=True, stop=True),
                         r=[lora.b(), pm.b(12)], w=[pbk.b()])
                    S.op("act", lambda e, pbk=pbk, cv=cv: e.activation(out=f["a"].t[:], in_=pbk.t[:], func=AF.Sigmoid, bias=cv(C_A0)),
                         r=[pbk.b()] + cb, w=[f["a"].b()])
                    S.op("dve", lambda e, k_=k_, cv=cv: e.tensor_scalar(out=f["kk"].t[:], in0=k_, scalar1=cv(C_KK), scalar2=None, op0=ALU.mult),
                         r=[pm.b(4 + j)] + cb, w=[f["kk"].b()])
                    S.op("dve", lambda e: e.tensor_tensor(out=kk2.t[:], in0=f["kk"].t[:], in1=f["kk"].t[:], op=ALU.mult),
                         r=[f["kk"].b()], w=[kk2.b()])
                    pbk = PB[mmb % 2]
                    mmb += 1
                    S.op("pe", lambda e, pbk=pbk: e.matmul(pbk.t[:], bdo, kk2.t[:], start=True, stop=True), r=[ident.b(), kk2.b()], w=[pbk.b()])
                    S.op("act", lambda e, pbk=pbk: e.activation(out=f["rs"].t[:], in_=pbk.t[:], func=AF.Sqrt, bias=1e-12), r=[pbk.b()], w=[f["rs"].b()])
                    S.op("dve", lambda e: e.reciprocal(out=f["rs"].t[:], in_=f["rs"].t[:]), r=[f["rs"].b()], w=[f["rs"].b()])
                    S.op("dve", lambda e: e.tensor_tensor(out=f["kkn"].t[:], in0=f["kk"].t[:], in1=f["rs"].t[:], op=ALU.mult),
                         r=[f["kk"].b(), f["rs"].b()], w=[f["kkn"].b()])
                    c3 = lambda tl: tl.t[:, :].rearrange("p (c t) -> p c t", t=64)
                    S.op("dve", lambda e, j=j: e.tensor_tensor(out=KR.t[:, j, :, 0, :], in0=c3(f["kkn"]), in1=c3(f["E3"]), op=ALU.mult),
                         r=[f["kkn"].b(), f["E3"].b()], w=[KR.b(j)])
                    S.op("dve", lambda e, j=j, r_=r_: e.tensor_tensor(out=KR.t[:, j, :, 1, :], in0=r_.rearrange("p (c t) -> p c t", t=64), in1=c3(f["E1"]),
                                                                     op=ALU.mult), r=[pm.b(j), f["E1"].b()], w=[KR.b(j)])
                    S.op("dve", lambda e: e.tensor_tensor(out=f["x1"].t[:], in0=f["kkn"].t[:], in1=f["a"].t[:], op=ALU.mult),
                         r=[f["kkn"].b(), f["a"].b()], w=[f["x1"].b()])
                    S.op("dve", lambda e, j=j: e.tensor_tensor(out=BK.t[:, j, :, 0, :], in0=c3(f["x1"]), in1=c3(f["E2"]), op=ALU.mult),
                         r=[f["x1"].b(), f["E2"].b()], w=[BK.b(j)])
                    S.op("dve", lambda e, j=j, cv=cv: e.tensor_scalar(out=f["x1"].t[:], in0=f["a"].t[:], scalar1=cv(C_KA), scalar2=cder.t[:, 14 + j:15 + j],
                                                                     op0=ALU.mult, op1=ALU.add), r=[f["a"].b()] + cb, w=[f["x1"].b()])
                    S.op("dve", lambda e, k_=k_: e.tensor_tensor(out=f["kf"].t[:], in0=f["x1"].t[:], in1=k_, op=ALU.mult),
                         r=[f["x1"].b(), pm.b(4 + j)], w=[f["kf"].b()])
                    S.op("dve", lambda e, j=j: e.tensor_tensor(out=BK.t[:, j, :, 1, :], in0=c3(f["kf"]), in1=c3(f["E2"]), op=ALU.mult),
                         r=[f["kf"].b(), f["E2"].b()], w=[BK.b(j)])
                    S.op("dve", lambda e, r_=r_: e.tensor_tensor(out=f["x1"].t[:], in0=f["kf"].t[:], in1=r_, op=ALU.mult),
                         r=[f["kf"].b(), pm.b(j)], w=[f["x1"].b()])
                    S.op("dve", lambda e, cv=cv: e.tensor_scalar(out=kk2.t[:], in0=f["x1"].t[:], scalar1=cv(C_RK), scalar2=None, op0=ALU.mult),
                         r=[f["x1"].b()] + cb, w=[kk2.b()])
                    pbk = PB[mmb % 2]
                    mmb += 1
                    S.op("pe", lambda e, pbk=pbk: e.matmul(pbk.t[:], bdo, kk2.t[:], start=True, stop=True), r=[ident.b(), kk2.b()], w=[pbk.b()])
                    S.op("dve", lambda e, pbk=pbk, j=j, v_=v_: e.tensor_tensor(out=bonT.t[:, j, :], in0=pbk.t[:], in1=v_, op=ALU.mult),
                         r=[pbk.b(), pm.b(8 + j)], w=[bonT.b(j)])
                    pbk = PB[mmb % 2]
                    mmb += 1
                    S.op("pe", lambda e, pbk=pbk, cs=cs: e.matmul(pbk.t[:], gl.t[:, cs], sg.t[:], start=True, stop=True), r=[gl.b(), sg.b()], w=[pbk.b()])
                    S.op("act", lambda e, pbk=pbk, j=j: e.activation(out=gT.t[:, j, :], in_=pbk.t[:], func=AF.Copy), r=[pbk.b()], w=[gT.b(j)])
                if stop == 1.5:
                    break
                krb = [KR.b(j) for j in range(4)]
                bkb = [BK.b(j) for j in range(4)]
                HP = [(h // 2, 64 * (h % 2)) for h in range(8)]
                v3 = lambda ap_, w=64: ap_.rearrange("p (j c) -> p j c", j=4)
                lo = lambda bank, w=64: v3(bank.t[:, 0:4 * w], w)
                hi = lambda bank: v3(bank.t[:, 256:512])

                def stageA(c):
                    par = c % 2
                    cc = slice(c * 64, (c + 1) * 64)
                    vbk, asb, gf = VBK[par], Asb[par], GF[par]
                    for j, pb in HP:
                        ps_ = slice(pb, pb + 64)
                        S.op("pe", lambda e: e.transpose(PT.t[ps_, j * 64:(j + 1) * 64], pm.t[ps_, 8 + j, cc], idn[ps_, ps_]),
                             r=[pm.b(8 + j), ident.b()], w=[PT.b("A")])
                        S.op("pe", lambda e: e.transpose(PT.t[ps_, 256 + j * 64:256 + (j + 1) * 64], BK.t[ps_, j, c, 0, :], idn[ps_, ps_]),
                             r=[bkb[j], ident.b()], w=[PT.b("A")])
                        S.op("pe", lambda e: e.transpose(PT.t[ps_, 512 + j * 64:512 + (j + 1) * 64], BK.t[ps_, j, c, 1, :], idn[ps_, ps_]),
                             r=[bkb[j], ident.b()], w=[PT.b("A")])
                    for j, pb in HP:
                        ps_ = slice(pb, pb + 64)
                        for kind in range(2):
                            S.op("pe", lambda e: e.matmul(PB[2 + kind].t[ps_, j * 128:(j + 1) * 128], BK.t[ps_, j, c, kind, :],
                                                          KR.t[ps_, j, c, :, :].rearrange("p a b -> p (a b)"), start=True, stop=True),
                                 r=[bkb[j], krb[j]], w=[PB[2 + kind].b()])
                        S.op("pe", lambda e: e.matmul(PB[4].t[ps_, j * 64:(j + 1) * 64], KR.t[ps_, j, c, 0, :], BK.t[ps_, j, c, 0, :],
                                                      start=True, stop=True), r=[bkb[j], krb[j]], w=[PB[4].b()])
                    yield
                    S.op("act", lambda e: e.activation(out=vbk.t[:], in_=PT.t[:, 0:768].rearrange("p (a j c) -> p a j c", a=3, j=4), func=AF.Copy),
                         r=[PT.b("A")], w=[vbk.b()])
                    S.op("dve", lambda e: e.tensor_tensor(out=asb.t[:, :, 0, :], in0=lo(PB[2], 128), in1=M2v, op=ALU.mult),
                         r=[PB[2].b()] + cb, w=[asb.b()])
                    S.op("dve", lambda e: e.tensor_tensor(out=Lsb.t[:], in0=lo(PB[4]), in1=MLv, op=ALU.mult), r=[PB[4].b()] + cb, w=[Lsb.b()])
                    S.op("dve", lambda e: e.scalar_tensor_tensor(out=GT[0].t[:], in0=asb.t[:, :, 0, 0:64], scalar=-1.0, in1=I64v,
                                                                 op0=ALU.mult, op1=ALU.add), r=[asb.b()] + cb, w=[GT[0].b()])
                    S.op("dve", lambda e: e.tensor_tensor(out=asb.t[:, :, 1, :], in0=lo(PB[3], 128), in1=M2v, op=ALU.mult),
                         r=[PB[3].b()] + cb, w=[asb.b()])
                    yield
                    Pc, PTc, bP, bPT = Lsb.t, asb.t[:, :, 0, 0:64], Lsb.b(), asb.b()
                    gi = 0
                    pend = None
                    for lv in range(6):
                        if lv < 5:
                            for j, pb in HP:
                                ps_ = slice(pb, pb + 64)
                                S.op("pe", lambda e: e.matmul(PB[4].t[ps_, j * 64:(j + 1) * 64], PTc[ps_, j, :], Pc[ps_, j, :], start=True, stop=True),
                                     r=[bP, bPT], w=[PB[4].b()])
                            if lv < 4:
                                for j, pb in HP:
                                    ps_ = slice(pb, pb + 64)
                                    S.op("pe", lambda e: e.matmul(PB[5].t[ps_, j * 64:(j + 1) * 64], Pc[ps_, j, :], PTc[ps_, j, :], start=True, stop=True),
                                         r=[bP, bPT], w=[PB[5].b()])
                        if pend is not None:
                            pn2, plv = pend
                            gsrc = GT[gi]
                            gdst = gf if plv == 4 else GT[1 - gi]
                            for j, pb in HP:
                                ps_ = slice(pb, pb + 64)
                                S.op("pe", lambda e: e.matmul(PB[6].t[ps_, j * 64:(j + 1) * 64], pn2.t[ps_, j, :], gsrc.t[ps_, j, :], start=True, stop=True),
                                     r=[pn2.b(), gsrc.b()], w=[PB[6].b()])
                        yield
                        if lv < 5:
                            n2 = P2[lv % 2]
                            S.op("act", lambda e: e.activation(out=n2.t[:], in_=lo(PB[4]), func=AF.Copy), r=[PB[4].b()], w=[n2.b()])
                            if lv < 4:
                                n2t = P2T[lv % 2]
                                S.op("act", lambda e: e.activation(out=n2t.t[:], in_=lo(PB[5]), func=AF.Copy), r=[PB[5].b()], w=[n2t.b()])
                        if pend is not None:
                            S.op("dve", lambda e: e.tensor_tensor(out=gdst.t[:], in0=lo(PB[6]), in1=gsrc.t[:], op=ALU.add),
                                 r=[PB[6].b(), gsrc.b()], w=[gdst.b()])
                            gi = 1 - gi
                            pend = None
                        if lv < 5:
                            pend = (n2, lv)
                            if lv < 4:
                                Pc, PTc, bP, bPT = n2.t, n2t.t, n2.b(), n2t.b()
                            yield

                def stageB1(c):
                    par = c % 2
                    vbk, asb, G = VBK[par], Asb[par], GF[par]
                    ysb, ysq = Ysb[par], Ysq[par]
                    Vt, Bt, Kt = vbk.t[:, 0], vbk.t[:, 1], vbk.t[:, 2]
                    b0, b1 = PB[0].b(), PB[1].b()
                    for j, pb in HP:
                        ps_ = slice(pb, pb + 64)
                        S.op("pe", lambda e: e.matmul(PB[0].t[ps_, j * 64:(j + 1) * 64], KR.t[ps_, j, c, 0, :], Sbf.t[ps_, j, :], start=True, stop=False),
                             r=[krb[j], Sbf.b()], w=[b0])
                        S.op("pe", lambda e: e.matmul(PB[0].t[ps_, j * 64:(j + 1) * 64], asb.t[ps_, j, 1, 0:64], Vt[ps_, j, :], start=False, stop=True),
                             r=[asb.b(), vbk.b()], w=[b0])
                    yield
                    S.op("act", lambda e: e.activation(out=Zsb.t[:], in_=lo(PB[0]), func=AF.Copy), r=[b0], w=[Zsb.b()])
                    yield
                    for j, pb in HP:
                        ps_ = slice(pb, pb + 64)
                        S.op("pe", lambda e: e.matmul(PB[0].t[ps_, j * 64:(j + 1) * 64], G.t[ps_, j, :], Zsb.t[ps_, j, :], start=True, stop=True),
                             r=[G.b(), Zsb.b()], w=[b0])
                    yield
                    S.op("act", lambda e: e.activation(out=Un.t[:], in_=lo(PB[0]), func=AF.Copy, scale=-1.0), r=[b0], w=[Un.b()])
                    yield
                    for j, pb in HP:
                        ps_ = slice(pb, pb + 64)
                        S.op("pe", lambda e: e.matmul(PB[0].t[ps_, j * 64:(j + 1) * 64], Kt[ps_, j, :], Vt[ps_, j, :], start=True, stop=False),
                             r=[vbk.b()], w=[b0])
                        S.op("pe", lambda e: e.matmul(PB[0].t[ps_, j * 64:(j + 1) * 64], Bt[ps_, j, :], Un.t[ps_, j, :], start=False, stop=True),
                             r=[vbk.b(), Un.b()], w=[b0])
                    for j, pb in HP:
                        ps_ = slice(pb, pb + 64)
                        S.op("pe", lambda e: e.matmul(PB[1].t[ps_, j * 64:(j + 1) * 64], KR.t[ps_, j, c, 1, :], Sbf.t[ps_, j, :], start=True, stop=False),
                             r=[krb[j], Sbf.b()], w=[b1])
                        S.op("pe", lambda e: e.matmul(PB[1].t[ps_, j * 64:(j + 1) * 64], asb.t[ps_, j, 1, 64:128], Vt[ps_, j, :], start=False, stop=False),
                             r=[asb.b(), vbk.b()], w=[b1])
                        S.op("pe", lambda e: e.matmul(PB[1].t[ps_, j * 64:(j + 1) * 64], asb.t[ps_, j, 0, 64:128], Un.t[ps_, j, :], start=False, stop=True),
                             r=[asb.b(), Un.b()], w=[b1])
                    yield
                    S.op("dve", lambda e: e.tensor_tensor(out=Sf.t[:], in0=lo(PB[0]), in1=Sf.t[:], op=ALU.add), r=[b0, Sf.b()], w=[Sf.b()])
                    S.op("act", lambda e: e.activation(out=ysb.t[:], in_=lo(PB[1]), func=AF.Copy), r=[b1], w=[ysb.b()])
                    S.op("act", lambda e: e.activation(out=ysq.t[:], in_=lo(PB[1]), func=AF.Square), r=[b1], w=[ysq.b()])
                    yield
                    S.op("dve", lambda e: e.tensor_tensor(out=Sf.t[:], in0=Sf.t[:], in1=WC.t[:, :, c].unsqueeze(2).broadcast_to([128, 4, 64]), op=ALU.mult),
                         r=[Sf.b(), WC.b()], w=[Sf.b()])
                    yield
                    S.op("act", lambda e: e.activation(out=Sbf.t[:], in_=Sf.t[:], func=AF.Copy), r=[Sf.b()], w=[Sbf.b()])

                def stageB2(c):
                    par = c % 2
                    cc = slice(c * 64, (c + 1) * 64)
                    ysb, ysq, ys, yh_ = Ysb[par], Ysq[par], yst[par], yh[par]
                    S.op("dve", lambda e: e.tensor_reduce(out=ys.t[:, 0, :], in_=ysb.t[:], axis=AX.X, op=ALU.add), r=[ysb.b()], w=[ys.b()])
                    S.op("dve", lambda e: e.tensor_reduce(out=ys.t[:, 1, :], in_=ysq.t[:], axis=AX.X, op=ALU.add), r=[ysq.b()], w=[ys.b()])
                    yield
                    S.op("dve", lambda e: e.tensor_scalar(out=ys.t[:, 0, :], in0=ys.t[:, 0, :], scalar1=1.0 / 64, scalar2=None, op0=ALU.mult),
                         r=[ys.b()], w=[ys.b()])
                    yield
                    S.op("dve", lambda e: e.tensor_tensor(out=ys.t[:, 2, :], in0=ys.t[:, 0, :], in1=ys.t[:, 0, :], op=ALU.mult), r=[ys.b()], w=[ys.b()])
                    yield
                    S.op("dve", lambda e: e.scalar_tensor_tensor(out=ys.t[:, 3, :], in0=ys.t[:, 1, :], scalar=1.0 / 64, in1=ys.t[:, 2, :],
                                                                 op0=ALU.mult, op1=ALU.subtract), r=[ys.b()], w=[ys.b()])
                    yield
                    S.op("act", lambda e: e.activation(out=ys.t[:, 3, :], in_=ys.t[:, 3, :], func=AF.Sqrt, bias=eps_gn.t[:, 0:1]), r=[ys.b(), eps_gn.b()], w=[ys.b()])
                    yield
                    S.op("dve", lambda e: e.reciprocal(out=ys.t[:, 3, :], in_=ys.t[:, 3, :]), r=[ys.b()], w=[ys.b()])
                    S.op("dve", lambda e: e.tensor_tensor(out=ysb.t[:], in0=ysb.t[:], in1=ys.t[:, 0, :].unsqueeze(2).broadcast_to([128, 4, 64]), op=ALU.subtract),
                         r=[ysb.b(), ys.b()], w=[ysb.b()])
                    yield
                    S.op("dve", lambda e: e.tensor_tensor(out=yh_.t[:], in0=ysb.t[:], in1=ys.t[:, 3, :].unsqueeze(2).broadcast_to([128, 4, 64]), op=ALU.mult),
                         r=[ysb.b(), ys.b()], w=[yh_.b()])
                    yield
                    for j, pb in HP:
                        ps_ = slice(pb, pb + 64)
                        S.op("pe", lambda e: e.transpose(PT.t[ps_, 768 + j * 64:768 + (j + 1) * 64], yh_.t[ps_, j, :], idn[ps_, ps_]),
                             r=[yh_.b(), ident.b()], w=[PT.b("A")])
                    yield
                    S.op("act", lambda e: e.activation(out=yhT.t[:, :, cc], in_=PT.t[:, 768:1024].rearrange("p (j t) -> p j t", j=4), func=AF.Copy),
                         r=[PT.b("A")], w=[yhT.b()])

                for c in range(NCH + 2):
                    gens = []
                    if 1 <= c <= NCH:
                        gens.append(stageB1(c - 1))
                    if c < NCH:
                        gens.append(stageA(c))
                    if 2 <= c:
                        gens.append(stageB2(c - 2))
                    while gens:
                        for g_ in list(gens):
                            try:
                                next(g_)
                            except StopIteration:
                                gens.remove(g_)
                if stop == 1.6:
                    break
                for j in range(4):
                    S.op("dve", lambda e, j=j: e.tensor_scalar(out=f["x1"].t[:], in0=yhT.t[:, j, :], scalar1=cvec.t[:, C_LNW + j:C_LNW + j + 1],
                                                               scalar2=cvec.t[:, C_LNB + j:C_LNB + j + 1], op0=ALU.mult, op1=ALU.add),
                         r=[yhT.b()] + cb, w=[f["x1"].b()])
                    S.op("dve", lambda e, j=j: e.tensor_tensor(out=f["x1"].t[:], in0=f["x1"].t[:], in1=bonT.t[:, j, :], op=ALU.add),
                         r=[f["x1"].b(), bonT.b(j)], w=[f["x1"].b()])
                    S.op("dve", lambda e, j=j: e.tensor_tensor(out=yT.t[:, j, :], in0=f["x1"].t[:], in1=gT.t[:, j, :], op=ALU.mult),
                         r=[f["x1"].b(), gT.b(j)], w=[yT.b(j)])
                S.dma("sp", dy, yT_d[:, :, t0:t0 + TT], yT.t[:], r=[yT.b(jj) for jj in range(8)], w=[yT_db[st]])
            S.emit()
            if stop <= 2:
                S.barrier()
                S.emit()
                return nc

        rot = [0]

        def nb():
            rot[0] += 1
            return PB[rot[0] % 7]

        with contextlib.ExitStack() as p2:
            S.barrier()
            dw2 = S.dsem()
            wg = load_w(p2, "wg", wg_d, 8, 3 * D, dw2)
            wup = sb(p2, "wup", [128, 8, D], BF16)
            load_w(p2, "", wupr_d, 4, D, dw2, tile=wup, kc0=0)
            load_w(p2, "", wupp_d, 2, D, dw2, tile=wup, kc0=4)
            load_w(p2, "", wupm_d, 2, D, dw2, tile=wup, kc0=6)
            wo = load_w(p2, "wo", wo_d, 8, D, dw2)
            S.seal(dw2, [wg.b(), wup.b(), wo.b()])
            hT2 = [sb(p2, "hT2%d" % i, [128, 8, TT], BF16) for i in range(2)]
            yT2 = [sb(p2, "yT2%d" % i, [128, 8, TT], BF16) for i in range(2)]
            dl2 = [S.dsem() for _ in range(2)]
            gs = [sb(p2, "gs%d" % i, [128, TT]) for i in range(3)]
            mm_ = [sb(p2, "mm%d" % i, [128, TT]) for i in range(3)]
            mg = sb(p2, "mg", [128, 8, TT], BF16)
            xs2 = [sb(p2, "xs2%d" % i, [128, D]) for i in range(2)]
            dx2 = [S.dsem() for _ in range(2)]
            x1s = [sb(p2, "x1s%d" % i, [128, D]) for i in range(2)]
            ds2 = [S.dsem() for _ in range(2)]
            kr = [(0, 4), (4, 6), (6, 8)]
            for st in range(NST):
                t0 = st * TT
                i2 = st % 2
                S.dma("sp", dl2[i2], hT2[i2].t[:], hT_d[:, :, t0:t0 + TT], r=[hT_db[st]], w=[hT2[i2].b()])
                S.dma("sp", dl2[i2], yT2[i2].t[:], yT_d[:, :, t0:t0 + TT], r=[yT_db[st]], w=[yT2[i2].b()])
                S.seal(dl2[i2], [hT2[i2].b(), yT2[i2].b()])
                for fo in range(8):
                    fs = slice(fo * 128, (fo + 1) * 128)
                    for b in range(3):
                        pg, pu = nb(), nb()
                        for kc in range(8):
                            S.op("pe", lambda e: e.matmul(pg.t[:], wg.t[:, kc, b * D + fo * 128:b * D + (fo + 1) * 128], hT2[i2].t[:, kc, :],
                                                          start=(kc == 0), stop=(kc == 7)), r=[wg.b(), hT2[i2].b()], w=[pg.b()])
                        k0, k1 = kr[b]
                        for kc in range(k0, k1):
                            S.op("pe", lambda e: e.matmul(pu.t[:], wup.t[:, kc, fs], yT2[i2].t[:, kc, :], start=(kc == k0), stop=(kc == k1 - 1)),
                                 r=[wup.b(), yT2[i2].b()], w=[pu.b()])
                        S.op("act", lambda e: e.activation(out=gs[b].t[:], in_=pg.t[:], func=AF.Sigmoid,
                                                           bias=cvec.t[:, C_BG + b * 8 + fo:C_BG + b * 8 + fo + 1]), r=[pg.b()] + cb, w=[gs[b].b()])
                        S.op("dve", lambda e: e.tensor_tensor(out=mm_[b].t[:], in0=pu.t[:], in1=gs[b].t[:], op=ALU.mult),
                             r=[pu.b(), gs[b].b()], w=[mm_[b].b()])
                    S.op("pool", lambda e: e.tensor_tensor(out=mm_[0].t[:], in0=mm_[0].t[:], in1=mm_[1].t[:], op=ALU.add),
                         r=[mm_[0].b(), mm_[1].b()], w=[mm_[0].b()])
                    S.op("pool", lambda e: e.tensor_tensor(out=mg.t[:, fo, :], in0=mm_[0].t[:], in1=mm_[2].t[:], op=ALU.add),
                         r=[mm_[0].b(), mm_[2].b()], w=[mg.b()])
                for sub in range(4):
                    i = (st * 4 + sub) % 2
                    r0 = t0 + sub * 128
                    S.dma("sp", dx2[i], xs2[i].t[:], x_d[r0:r0 + 128, :], w=[xs2[i].b()])
                    for half in range(2):
                        pbk = nb()
                        for kc in range(8):
                            S.op("pe", lambda e: e.matmul(pbk.t[:], mg.t[:, kc, sub * 128:(sub + 1) * 128], wo.t[:, kc, half * 512:(half + 1) * 512],
                                                          start=(kc == 0), stop=(kc == 7)), r=[mg.b(), wo.b()], w=[pbk.b()])
                        S.op("dve", lambda e: e.tensor_tensor(out=x1s[i].t[:, half * 512:(half + 1) * 512], in0=pbk.t[:],
                                                              in1=xs2[i].t[:, half * 512:(half + 1) * 512], op=ALU.add),
                             r=[pbk.b(), xs2[i].b()], w=[x1s[i].b()])
                    S.dma("sp", ds2[i], x1_d[r0:r0 + 128, :], x1s[i].t[:], r=[x1s[i].b()], w=[x1_db[st * 4 + sub]])
            S.emit()
            if stop == 3:
                S.barrier()
                S.emit()
                return nc

        with contextlib.ExitStack() as p3:
            S.barrier()
            dw3 = S.dsem()
            wfi = load_w(p3, "wfi", wfi_d, 8, 2 * DFF, dw3)
            wfo = load_w(p3, "wfo", wfo_d, NFC, D, dw3)
            gfin = sb(p3, "gfin", [128, D])
            dgf = S.dsem()
            S.dma("sp", dgf, gfin.t[:], gfin_d[:, :], w=[gfin.b()])
            S.seal(dw3, [wfi.b(), wfo.b()])
            x1k = sb(p3, "x1k", [128, 4, D])
            dk = [S.dsem() for _ in range(4)]
            h2T = sb(p3, "h2T", [128, 8, TT], BF16)
            ub = [sb(p3, "ub%d" % i, [128, 2 + TT]) for i in range(2)]
            ucar = sb(p3, "ucar", [128, NFC, 2])
            c1 = [sb(p3, "c1%d" % i, [128, TT]) for i in range(2)]
            actT = sb(p3, "actT", [128, NFC, TT], BF16)
            x2s = [sb(p3, "x2s%d" % i, [128, D]) for i in range(2)]
            do = [S.dsem() for _ in range(2)]
            fst = sb(p3, "fst", [128, 2])
            S.op("dve", lambda e: e.memset(ucar.t[:], 0.0), w=[ucar.b()])
            for st in range(NST):
                t0 = st * TT
                for sub in range(4):
                    r0 = t0 + sub * 128
                    S.dma("sp", dk[sub], x1k.t[:, sub, :], x1_d[r0:r0 + 128, :], r=[x1_db[st * 4 + sub]], w=[x1k.b(sub)])
                    norm_T(x1k.t[:, sub, :], x1k.b(sub), C_G2, h2T.t, h2T.b(), sub * 128)
                for fc in range(NFC):
                    pu, pgv = nb(), nb()
                    u_, c_ = ub[fc % 2], c1[fc % 2]
                    for half, pbk in ((0, pu), (1, pgv)):
                        for kc in range(8):
                            S.op("pe", lambda e: e.matmul(pbk.t[:], wfi.t[:, kc, half * DFF + fc * 128:half * DFF + (fc + 1) * 128], h2T.t[:, kc, :],
                                                          start=(kc == 0), stop=(kc == 7)), r=[wfi.b(), h2T.b()], w=[pbk.b()])
                    cw = lambda jx: cvec.t[:, C_CW + jx * NFC + fc:C_CW + jx * NFC + fc + 1]
                    S.op("act", lambda e: e.activation(out=u_.t[:, 2:2 + TT], in_=pu.t[:], func=AF.Copy), r=[pu.b()], w=[u_.b()])
                    S.op("act", lambda e: e.activation(out=c_.t[:, 2:TT], in_=pu.t[:, 0:TT - 2], func=AF.Copy, scale=cw(0)), r=[pu.b()] + cb, w=[c_.b()])
                    S.op("pool", lambda e: e.tensor_copy(out=u_.t[:, 0:2], in_=ucar.t[:, fc, :]), r=[ucar.b()], w=[u_.b()])
                    S.op("pool", lambda e: e.tensor_scalar(out=c_.t[:, 0:2], in0=ucar.t[:, fc, :], scalar1=cw(0), scalar2=None, op0=ALU.mult),
                         r=[ucar.b()] + cb, w=[c_.b()])
                    S.op("pool", lambda e: e.tensor_copy(out=ucar.t[:, fc, :], in_=u_.t[:, TT:TT + 2]), r=[u_.b()], w=[ucar.b()])
                    S.op("dve", lambda e: e.scalar_tensor_tensor(out=c_.t[:], in0=u_.t[:, 1:1 + TT], scalar=cw(1), in1=c_.t[:], op0=ALU.mult, op1=ALU.add),
                         r=[u_.b(), c_.b()] + cb, w=[c_.b()])
                    S.op("dve", lambda e: e.scalar_tensor_tensor(out=c_.t[:], in0=u_.t[:, 2:2 + TT], scalar=cw(2), in1=c_.t[:], op0=ALU.mult, op1=ALU.add),
                         r=[u_.b(), c_.b()] + cb, w=[c_.b()])
                    S.op("act", lambda e: e.activation(out=c_.t[:], in_=c_.t[:], func=AF.Gelu, bias=cvec.t[:, C_CB + fc:C_CB + fc + 1]),
                         r=[c_.b()] + cb, w=[c_.b()])
                    S.op("dve", lambda e: e.tensor_tensor(out=actT.t[:, fc, :], in0=pgv.t[:], in1=c_.t[:], op=ALU.mult),
                         r=[pgv.b(), c_.b()], w=[actT.b()])
                for sub in range(4):
                    i = (st * 4 + sub) % 2
                    r0 = t0 + sub * 128
                    for half in range(2):
                        pbk = nb()
                        for fc in range(NFC):
                            S.op("pe", lambda e: e.matmul(pbk.t[:], actT.t[:, fc, sub * 128:(sub + 1) * 128], wfo.t[:, fc, half * 512:(half + 1) * 512],
                                                          start=(fc == 0), stop=(fc == NFC - 1)), r=[actT.b(), wfo.b()], w=[pbk.b()])
                        S.op("dve", lambda e: e.tensor_tensor(out=x2s[i].t[:, half * 512:(half + 1) * 512], in0=pbk.t[:],
                                                              in1=x1k.t[:, sub, half * 512:(half + 1) * 512], op=ALU.add),
                             r=[pbk.b(), x1k.b(sub)], w=[x2s[i].b()])
                    ss, ms = fst.t[:, 0:1], fst.t[:, 1:2]
                    S.op("act", lambda e: e.activation(out=junk.t[:], in_=x2s[i].t[:], func=AF.Square, accum_out=ss),
                         r=[x2s[i].b(), fst.b()], w=[junk.b(), fst.b()])
                    S.op("dve", lambda e: e.tensor_scalar(out=ms, in0=ss, scalar1=1.0 / D, scalar2=1e-6, op0=ALU.mult, op1=ALU.add), r=[fst.b()], w=[fst.b()])
                    S.op("act", lambda e: e.activation(out=ms, in_=ms, func=AF.Sqrt), r=[fst.b()], w=[fst.b()])
                    S.op("dve", lambda e: e.reciprocal(out=ms, in_=ms), r=[fst.b()], w=[fst.b()])
                    S.op("act", lambda e: e.activation(out=x2s[i].t[:], in_=x2s[i].t[:], func=AF.Copy, scale=ms), r=[x2s[i].b(), fst.b()], w=[x2s[i].b()])
                    S.op("dve", lambda e: e.tensor_tensor(out=x2s[i].t[:], in0=x2s[i].t[:], in1=gfin.t[:], op=ALU.mult),
                         r=[x2s[i].b(), gfin.b()], w=[x2s[i].b()])
                    S.dma("sp", do[i], out_d[r0:r0 + 128, :], x2s[i].t[:], r=[x2s[i].b()])
            S.wait_all("sp", [(d[0], d[1], None) for d in do])
            S.emit()
    return nc


def _cols(v, n):
    return np.ascontiguousarray(np.asarray(v, np.float32).reshape(n, 128).T)


def _host_consts():
    cm = np.zeros((128, NCM), np.float32)
    s = np.arange(128)[:, None] % 64
    t = np.arange(64)[None, :]
    cm[:, M_M2:M_M2 + 64] = (s < t)
    cm[:, M_M2 + 64:M_M2 + 128] = (s <= t)
    tt = np.arange(128)[:, None] % 64
    ss = np.arange(64)[None, :]
    cm[:, M_ML:M_ML + 64] = (tt > ss)
    cm[:, M_I64:M_I64 + 64] = (tt == ss)
    sc = np.ones(512, np.float32)
    sc[::64] = 0.0
    cm[:, M_SCAN:M_SCAN + 512] = sc[None, :]
    wins = [2, 4, 8, 16]
    for g in range(4):
        ci, pb = g // 2, 64 * (g % 2)
        pos = np.arange(1, 513)
        cm[pb:pb + 64, M_INVC0 + ci * 512:M_INVC0 + (ci + 1) * 512] = (1.0 / np.minimum(pos, wins[g]))[None, :]
        cm[pb:pb + 64, M_INVC + ci * 512:M_INVC + (ci + 1) * 512] = 1.0 / wins[g]
    cmb = np.zeros((128, 320), np.float32)
    cmb[:, 0:128] = np.eye(128)
    cmb[0:64, 128:192] = 1.0
    cmb[64:128, 192:256] = 1.0
    cmb[:, 256:320] = 1.0
    return cm, cmb


_NC_CACHE = {}


def _prep(inputs):
    g = lambda k: np.asarray(inputs[k], np.float32)
    cv = np.zeros((128, NCV), np.float32)
    cv[:, C_G1:C_G1 + 8] = _cols(g("norm_mix_g")[0], 8)
    cv[:, C_MU:C_MU + 14] = _cols(g("mu_shift")[0], 14)
    cv[:, C_W0:C_W0 + 4] = _cols(g("w0")[0], 4)
    cv[:, C_A0:C_A0 + 4] = _cols(g("a0")[0], 4)
    cv[:, C_KK:C_KK + 4] = _cols(g("k_k")[0], 4)
    cv[:, C_KA:C_KA + 4] = _cols(g("k_a")[0], 4)
    cv[:, C_RK:C_RK + 4] = _cols(g("r_k")[0].reshape(-1), 4)
    cv[:, C_LNW:C_LNW + 4] = _cols(g("ln_x_w")[0], 4)
    cv[:, C_LNB:C_LNB + 4] = _cols(g("ln_x_b")[0], 4)
    cv[:, C_PS:C_PS + 2] = _cols(g("pool_scale")[0], 2)
    cv[:, C_BG:C_BG + 24] = _cols(g("b_gate")[0], 24)
    cv[:, C_G2:C_G2 + 8] = _cols(g("norm_ffn_g")[0], 8)
    for j in range(3):
        cv[:, C_CW + j * NFC:C_CW + (j + 1) * NFC] = _cols(g("ffn_conv_w")[0, j], NFC)
    cv[:, C_CB:C_CB + NFC] = _cols(g("ffn_conv_b")[0], NFC)
    cv[:, C_GM:C_GM + 8] = _cols(g("norm_mem_g")[0], 8)
    cm, cmb = _host_consts()
    shared = {
        "w_in_mix": g("w_in_mix")[0], "w_lora_b": g("w_lora_b")[0], "a_lora_b": g("a_lora_b")[0], "g_lora_b": g("g_lora_b")[0],
        "pool_w": g("pool_w")[0], "w_mem_kv": g("w_mem_kv")[0], "w_up_rwkv": g("w_up_rwkv")[0], "w_up_pool": g("w_up_pool")[0],
        "w_up_mem": g("w_up_mem")[0], "w_gate": g("w_gate")[0], "w_o": g("w_o")[0], "w_ffn_in": g("w_ffn_in")[0],
        "w_ffn_out": g("w_ffn_out")[0], "cvec": cv, "cm32": cm, "cmb": cmb,
        "gfin": np.ascontiguousarray(np.broadcast_to(g("norm_final_g")[None, :], (128, D))),
    }
    shared = {k: np.ascontiguousarray(v, dtype=np.float32) for k, v in shared.items()}
    x = g("x")
    mem = g("mem")
    return [dict(shared, x=np.ascontiguousarray(x[b]), mem=np.ascontiguousarray(mem[b])) for b in range(8)]


def kernel(**inputs):
    in_maps = _prep(inputs)
    if "nc" not in _NC_CACHE:
        _NC_CACHE["nc"] = build(False)
    res = run_bass_kernel_spmd(_NC_CACHE["nc"], in_maps, core_ids=list(range(8)))
    return np.stack([np.asarray(r["out"], np.float32) for r in res.results], axis=0)
```

```python
import numpy as np
import contextlib
import concourse.bass as bass
import concourse.mybir as mybir
from concourse.bass_utils import run_bass_kernel_spmd

F32 = mybir.dt.float32
BF16 = mybir.dt.bfloat16
AF = mybir.ActivationFunctionType
ALU = mybir.AluOpType
AX = mybir.AxisListType

D = 1024
T = 4096
TT = 512
NST = T // TT
NCH = TT // 64
DFF = 2816
NFC = DFF // 128
MIX_IN = 2304
NEG_EH = -float(np.exp(-0.5))

C_G1, C_MU, C_W0, C_A0, C_KK, C_KA, C_RK, C_LNW, C_LNB, C_PS, C_BG, C_G2, C_CW, C_CB, C_GM = (
    0, 8, 22, 26, 30, 34, 38, 42, 46, 50, 52, 76, 84, 150, 172)
NCV = 180
M_M2, M_ML, M_I64, M_SCAN, M_INVC0, M_INVC = 0, 128, 192, 256, 768, 1792
NCM = 2816


class Buf:
    __slots__ = ("w", "r")

    def __init__(self):
        self.w = None
        self.r = {}


class Rec:
    def __getattr__(self, name):
        def f(*a, **k):
            self.call = (name, a, k)
            return self
        return f


class Eng:
    def __init__(self, name, sem, is_pe=False):
        self.name = name
        self.sem = sem
        self.count = 0
        self.seen = {}
        self.prog = []
        self.is_pe = is_pe
        self.pend = 0
        self.last = None


class Sched:
    def __init__(self, nc, es):
        self.nc = nc
        self.E = {}
        for n in ("pe", "act", "dve", "pool", "sp"):
            self.E[n] = Eng(n, es.enter_context(nc.semaphore("s_" + n)), is_pe=(n == "pe"))
        self.es = es
        self.ndsem = 0
        self.dsems = []

    def dsem(self):
        self.ndsem += 1
        ds = [self.es.enter_context(self.nc.semaphore("d%d" % self.ndsem)), 0]
        self.dsems.append(ds)
        return ds

    def barrier(self):
        self._flush_pe()
        for E in self.E.values():
            for X in self.E.values():
                if X is not E and X.count > 0 and E.seen.get(id(X.sem), 0) < X.count:
                    E.seen[id(X.sem)] = X.count
                    E.prog.append(("w", X.sem, X.count))
            for ds in self.dsems:
                if ds[1] > 0 and E.seen.get(id(ds[0]), 0) < ds[1]:
                    E.seen[id(ds[0])] = ds[1]
                    E.prog.append(("w", ds[0], ds[1]))

    def _flush_pe(self):
        P = self.E["pe"]
        if P.pend:
            P.last[3] = 1
            P.count += 1
            P.pend = 0

    def _deps(self, E, reads, writes):
        need = {}

        def add(tok, raw):
            sem, val, eng = tok
            if eng is E and E.is_pe:
                return
            k = id(sem)
            if k not in need or need[k][1] < val:
                need[k] = (sem, val)

        for b in reads:
            if b.w is not None:
                add(b.w, True)
        for b in writes:
            if b.w is not None:
                add(b.w, False)
            for t in b.r.values():
                add(t, False)
        P = self.E["pe"]
        for k, (sem, val) in need.items():
            if E.seen.get(k, 0) >= val:
                continue
            if sem is P.sem and val > P.count:
                self._flush_pe()
            E.seen[k] = val
            E.prog.append(("w", sem, val))

    def _commit(self, tok, reads, writes):
        for b in writes:
            b.w = tok
            b.r = {}
        k = id(tok[0])
        for b in reads:
            b.r[k] = tok

    def op(self, en, fn, r=(), w=()):
        E = self.E[en]
        self._deps(E, r, w)
        rec = Rec()
        fn(rec)
        if E.is_pe:
            ent = ["i", rec.call, E.sem, 0]
            E.prog.append(ent)
            E.last = ent
            E.pend += 1
            self._commit((E.sem, E.count + 1, E), r, w)
            return
        E.count += 1
        E.prog.append(["i", rec.call, E.sem, 1])
        self._commit((E.sem, E.count, E), r, w)

    def dma(self, en, ds, out, in_, r=(), w=()):
        E = self.E[en]
        self._deps(E, r, w)
        ds[1] += 16
        E.prog.append(["i", ("dma_start", (), dict(out=out, in_=in_)), ds[0], 16])
        self._commit((ds[0], ds[1], None), r, w)

    def seal(self, ds, bufs):
        for b in bufs:
            b.w = (ds[0], ds[1], None)

    def wait_all(self, en, toks):
        E = self.E[en]
        for sem, val, _ in toks:
            E.prog.append(("w", sem, val))

    def emit(self):
        nc = self.nc
        self._flush_pe()
        progs = {n: e.prog for n, e in self.E.items()}
        for e in self.E.values():
            e.prog = []

        def run(eng, prog):
            for it in prog:
                if it[0] == "w":
                    eng.wait_ge(it[1], it[2])
                else:
                    name, a, k = it[1]
                    ins = getattr(eng, name)(*a, **k)
                    if it[3]:
                        ins.then_inc(it[2], it[3])

        with nc.Block() as block:
            @block.tensor
            def _(e):
                run(e, progs["pe"])

            @block.scalar
            def _(e):
                run(e, progs["act"])

            @block.vector
            def _(e):
                run(e, progs["dve"])

            @block.gpsimd
            def _(e):
                run(e, progs["pool"])

            @block.sync
            def _(e):
                run(e, progs["sp"])


class Tl:
    def __init__(self, t):
        self.t = t
        self.bufs = {}

    def b(self, key=0):
        if key not in self.bufs:
            self.bufs[key] = Buf()
        return self.bufs[key]


def build(debug=False, stop=99):
    nc = bass.Bass("TRN2", target_bir_lowering=False)
    din = lambda n, s, dt=F32: nc.dram_tensor(n, s, dt, kind="ExternalInput").ap()
    x_d = din("x", [T, D])
    mem_d = din("mem", [256, D])
    win_d = din("w_in_mix", [D, MIX_IN])
    wl_d = din("w_lora_b", [64, 512])
    al_d = din("a_lora_b", [64, 512])
    gl_d = din("g_lora_b", [128, 512])
    pw_d = din("pool_w", [4, 64, 64])
    wkv_d = din("w_mem_kv", [D, 512])
    wupr_d = din("w_up_rwkv", [512, D])
    wupp_d = din("w_up_pool", [256, D])
    wupm_d = din("w_up_mem", [256, D])
    wg_d = din("w_gate", [D, 3 * D])
    wo_d = din("w_o", [D, D])
    wfi_d = din("w_ffn_in", [D, 2 * DFF])
    wfo_d = din("w_ffn_out", [DFF, D])
    cvec_d = din("cvec", [128, NCV])
    cm32_d = din("cm32", [128, NCM])
    cmb_d = din("cmb", [128, 320])
    gfin_d = din("gfin", [128, D])
    skind = "ExternalOutput" if debug else "Internal"
    hT_d = nc.dram_tensor("hT_d", [128, 8, T], BF16, kind=skind).ap()
    yT_d = nc.dram_tensor("yT_d", [128, 8, T], BF16, kind=skind).ap()
    x1_d = nc.dram_tensor("x1_d", [T, D], F32, kind=skind).ap()
    out_d = nc.dram_tensor("out", [T, D], F32, kind="ExternalOutput").ap()
    hT_db = [Buf() for _ in range(NST)]
    yT_db = [Buf() for _ in range(NST)]
    x1_db = [Buf() for _ in range(NST * 4)]

    with contextlib.ExitStack() as top:
        S = Sched(nc, top)
        sb = lambda es, n, s, dt=F32: Tl(es.enter_context(nc.sbuf_tensor("sb_" + n, s, dt)))
        PB = [Tl(top.enter_context(nc.psum_tensor("pb%d" % i, [128, 512], F32))) for i in range(7)]
        PT = Tl(top.enter_context(nc.psum_tensor("pt", [128, 1024], BF16)))
        cvec = sb(top, "cvec", [128, NCV])
        cder = sb(top, "cder", [128, 18])
        ident = sb(top, "ident", [128, 320], BF16)
        junk = sb(top, "junk", [128, D], BF16)
        hb = sb(top, "hb", [128, D], BF16)
        st4 = sb(top, "st4", [128, 4])
        dconst = S.dsem()
        S.dma("sp", dconst, cvec.t[:], cvec_d[:, :], w=[cvec.b()])
        dconst2 = S.dsem()
        S.dma("pool", dconst2, ident.t[:], cmb_d[:, :], w=[ident.b()])
        S.op("dve", lambda e: e.tensor_scalar(out=cder.t[:, 0:14], in0=cvec.t[:, C_MU:C_MU + 14], scalar1=-1.0, scalar2=1.0,
                                              op0=ALU.mult, op1=ALU.add), r=[cvec.b()], w=[cder.b()])
        S.op("dve", lambda e: e.tensor_scalar(out=cder.t[:, 14:18], in0=cvec.t[:, C_KA:C_KA + 4], scalar1=-1.0, scalar2=1.0,
                                              op0=ALU.mult, op1=ALU.add), r=[cvec.b()], w=[cder.b()])
        idn = ident.t[:, 0:128]
        bdo = ident.t[:, 128:256]
        ones64 = ident.t[:, 256:320]
        cb = [cvec.b(), cder.b(), ident.b()]

        def norm_T(xs_ap, bx, gcol, hT, bh, col0, npart=128):
            ss, ms = st4.t[0:npart, 0:1], st4.t[0:npart, 1:2]
            S.op("act", lambda e: e.activation(out=junk.t[0:npart, :], in_=xs_ap, func=AF.Square, accum_out=ss),
                 r=[bx, st4.b()], w=[junk.b(), st4.b()])
            S.op("dve", lambda e: e.tensor_scalar(out=ms, in0=ss, scalar1=1.0 / D, scalar2=1e-6, op0=ALU.mult, op1=ALU.add),
                 r=[st4.b()], w=[st4.b()])
            S.op("act", lambda e: e.activation(out=ms, in_=ms, func=AF.Sqrt), r=[st4.b()], w=[st4.b()])
            S.op("dve", lambda e: e.reciprocal(out=ms, in_=ms), r=[st4.b()], w=[st4.b()])
            S.op("dve", lambda e: e.tensor_scalar(out=hb.t[0:npart, :], in0=xs_ap, scalar1=ms, scalar2=None, op0=ALU.mult),
                 r=[bx, st4.b()], w=[hb.b()])
            for c in range(8):
                S.op("pe", lambda e, c=c: e.transpose(PT.t[:, c * 128:c * 128 + npart], hb.t[0:npart, c * 128:(c + 1) * 128],
                                                      idn[0:npart, 0:npart]), r=[hb.b(), ident.b()], w=[PT.b("A"), PT.b("B")])
            pv = PT.t[:, :].rearrange("p (c t) -> p c t", c=8)[:, :, 0:npart]
            gv = cvec.t[:, gcol:gcol + 8].unsqueeze(2).broadcast_to([128, 8, npart])
            S.op("dve", lambda e: e.tensor_tensor(out=hT[:, :, col0:col0 + npart], in0=pv, in1=gv, op=ALU.mult),
                 r=[PT.b("A"), PT.b("B"), cvec.b()], w=[bh])

        def load_w(es, name, wd, kchunks, ncols, ds, row0=0, tile=None, kc0=0):
            if tile is None:
                tile = sb(es, name, [128, kchunks, ncols], BF16)
            step = 1024
            for kc in range(kchunks):
                for n0 in range(0, ncols, step):
                    n1 = min(ncols, n0 + step)
                    S.dma("pool", ds, tile.t[:, kc0 + kc, n0:n1], wd[row0 + kc * 128:row0 + (kc + 1) * 128, n0:n1], w=[tile.b()])
            return tile

        with contextlib.ExitStack() as p1:
            dw1 = S.dsem()
            cm = sb(p1, "cm", [128, NCM])
            dcm = S.dsem()
            S.dma("sp", dcm, cm.t[:], cm32_d[:, :], w=[cm.b()])
            cb = cb + [cm.b()]
            win = load_w(p1, "win", win_d, 8, MIX_IN, dw1)
            lora = sb(p1, "lora", [128, 512], BF16)
            S.dma("pool", dw1, lora.t[0:64, :], wl_d[:, :], w=[lora.b()])
            S.dma("pool", dw1, lora.t[64:128, :], al_d[:, :], w=[lora.b()])
            gl = sb(p1, "gl", [128, 512], BF16)
            S.dma("pool", dw1, gl.t[:], gl_d[:, :], w=[gl.b()])
            pw = sb(p1, "pw", [128, 2, 64], BF16)
            for g in range(4):
                S.dma("pool", dw1, pw.t[64 * (g % 2):64 * (g % 2) + 64, g // 2, :], pw_d[g, :, :], w=[pw.b()])
            kT = sb(p1, "kT", [128, 2, 256], BF16)
            vtok = sb(p1, "vtok", [128, 2, 256], BF16)
            with contextlib.ExitStack() as p0:
                wkv = load_w(p0, "wkv", wkv_d, 8, 512, dw1)
                S.seal(dw1, [win.b(), lora.b(), gl.b(), pw.b(), wkv.b()])
                mems = sb(p0, "mems", [128, 2, D])
                memT = sb(p0, "memT", [128, 8, 256], BF16)
                dm = S.dsem()
                for mh in range(2):
                    S.dma("sp", dm, mems.t[:, mh, :], mem_d[mh * 128:(mh + 1) * 128, :], w=[mems.b(mh)])
                S.seal(dm, [mems.b(0), mems.b(1)])
                for mh in range(2):
                    norm_T(mems.t[:, mh, :], mems.b(mh), C_GM, memT.t, memT.b(), mh * 128)
                for fc in range(2):
                    for kc in range(8):
                        S.op("pe", lambda e, fc=fc, kc=kc: e.matmul(PB[0].t[:, 0:256], wkv.t[:, kc, fc * 128:(fc + 1) * 128],
                                                                     memT.t[:, kc, :], start=(kc == 0), stop=(kc == 7)),
                             r=[wkv.b(), memT.b()], w=[PB[0].b()])
                    S.op("act", lambda e, fc=fc: e.activation(out=kT.t[:, fc, :], in_=PB[0].t[:, 0:256], func=AF.Copy),
                         r=[PB[0].b()], w=[kT.b()])
                for mh in range(2):
                    for kc in range(8):
                        S.op("pe", lambda e, mh=mh, kc=kc: e.matmul(PB[1].t[:, 0:256], memT.t[:, kc, mh * 128:(mh + 1) * 128],
                                                                     wkv.t[:, kc, 256:512], start=(kc == 0), stop=(kc == 7)),
                             r=[wkv.b(), memT.b()], w=[PB[1].b()])
                    S.op("act", lambda e, mh=mh: e.activation(out=vtok.t[:, mh, :], in_=PB[1].t[:, 0:256], func=AF.Copy),
                         r=[PB[1].b()], w=[vtok.b()])
                S.emit()
            S.barrier()
            if stop == 0:
                S.emit()
                return nc

            xs = [sb(p1, "xs%d" % i, [128, D]) for i in range(2)]
            dxs = [S.dsem() for _ in range(2)]
            hT = sb(p1, "hT", [128, 8, TT], BF16)
            dh = S.dsem()
            pm = sb(p1, "pm", [128, 14, TT], BF16)
            tmp = sb(p1, "tmp", [128, TT])
            cy = sb(p1, "cy", [128, 14])
            pp = sb(p1, "pp", [128, 2, 16 + TT])
            ppa = sb(p1, "ppa", [128, 16 + TT])
            ppb = sb(p1, "ppb", [128, 16 + TT])
            dT = sb(p1, "dT", [128, 2, TT], BF16)
            qT = sb(p1, "qT", [128, 2, TT], BF16)
            tw = sb(p1, "tw", [128, TT], BF16)
            sg = sb(p1, "sg", [128, TT], BF16)
            fn = ["ld", "cum", "cx", "E1", "E2", "E3", "a", "kk", "rs", "kkn", "x1", "kf"]
            f = {n: sb(p1, "f_" + n, [128, TT]) for n in fn}
            kk2 = sb(p1, "kk2", [128, TT], BF16)
            KR = sb(p1, "KR", [128, 4, NCH, 2, 64], BF16)
            BK = sb(p1, "BK", [128, 4, NCH, 2, 64], BF16)
            WC = sb(p1, "WC", [128, 4, NCH])
            bonT = sb(p1, "bonT", [128, 4, TT], BF16)
            gT = sb(p1, "gT", [128, 4, TT], BF16)
            Asb = [sb(p1, "Asb%d" % i, [128, 4, 2, 128], BF16) for i in range(2)]
            GF = [sb(p1, "GF%d" % i, [128, 4, 64], BF16) for i in range(2)]
            Lsb = sb(p1, "Lsb", [128, 4, 64], BF16)
            GT = [sb(p1, "GT%d" % i, [128, 4, 64], BF16) for i in range(2)]
            P2 = [sb(p1, "P2%d" % i, [128, 4, 64], BF16) for i in range(2)]
            P2T = [sb(p1, "P2T%d" % i, [128, 4, 64], BF16) for i in range(2)]
            Zsb = sb(p1, "Zsb", [128, 4, 64], BF16)
            Un = sb(p1, "Un", [128, 4, 64], BF16)
            VBK = [sb(p1, "VBK%d" % i, [128, 3, 4, 64], BF16) for i in range(2)]
            Ysb = [sb(p1, "Ysb%d" % i, [128, 4, 64]) for i in range(2)]
            Ysq = [sb(p1, "Ysq%d" % i, [128, 4, 64]) for i in range(2)]
            yh = [sb(p1, "yh%d" % i, [128, 4, 64], BF16) for i in range(2)]
            yst = [sb(p1, "yst%d" % i, [128, 4, 4]) for i in range(2)]
            eps_gn = sb(p1, "eps_gn", [128, 1])
            Sf = sb(p1, "Sf", [128, 4, 64])
            Sbf = sb(p1, "Sbf", [128, 4, 64], BF16)
            yhT = sb(p1, "yhT", [128, 4, TT], BF16)
            yT = sb(p1, "yT", [128, 8, TT], BF16)
            dy = S.dsem()
            eT = sb(p1, "eT", [128, 2, TT], BF16)
            rden = sb(p1, "rden", [128, TT])

            S.op("dve", lambda e: e.memset(cy.t[:], 0.0), w=[cy.b()])
            S.op("dve", lambda e: e.memset(pp.t[:], 0.0), w=[pp.b()])
            S.op("dve", lambda e: e.memset(ppa.t[:], 0.0), w=[ppa.b(0), ppa.b(64)])
            S.op("dve", lambda e: e.memset(ppb.t[:], 0.0), w=[ppb.b(0), ppb.b(64)])
            S.op("dve", lambda e: e.memset(Sf.t[:], 0.0), w=[Sf.b()])
            S.op("dve", lambda e: e.memset(Sbf.t[:], 0.0), w=[Sbf.b()])
            S.op("dve", lambda e: e.memset(eps_gn.t[:], 64e-5), w=[eps_gn.b()])
            M2v = cm.t[:, M_M2:M_M2 + 128].unsqueeze(1).broadcast_to([128, 4, 128])
            MLv = cm.t[:, M_ML:M_ML + 64].unsqueeze(1).broadcast_to([128, 4, 64])
            I64v = cm.t[:, M_I64:M_I64 + 64].unsqueeze(1).broadcast_to([128, 4, 64])
            scanm = cm.t[:, M_SCAN:M_SCAN + 512]
            mmb = 0

            for st in range(NST if stop >= 2 else 1):
                t0 = st * TT
                for sub in range(4):
                    i = (st * 4 + sub) % 2
                    S.dma("sp", dxs[i], xs[i].t[:], x_d[t0 + sub * 128:t0 + (sub + 1) * 128, :], w=[xs[i].b()])
                    norm_T(xs[i].t[:], xs[i].b(), C_G1, hT.t, hT.b(), sub * 128)
                S.dma("sp", dh, hT_d[:, :, t0:t0 + TT], hT.t[:], r=[hT.b()], w=[hT_db[st]])
                if stop == 1.1:
                    break
                for oc in range(18):
                    pbk = PB[mmb % 2]
                    mmb += 1
                    for kc in range(8):
                        S.op("pe", lambda e, oc=oc, kc=kc, pbk=pbk: e.matmul(pbk.t[:], win.t[:, kc, oc * 128:(oc + 1) * 128], hT.t[:, kc, :],
                                                                            start=(kc == 0), stop=(kc == 7)),
                             r=[win.b(), hT.b()], w=[pbk.b()])
                    ps = pbk.t
                    if oc < 14:
                        mu = cvec.t[:, C_MU + oc:C_MU + oc + 1]
                        om = cder.t[:, oc:oc + 1]
                        S.op("act", lambda e, ps=ps, mu=mu: e.activation(out=tmp.t[:, 1:TT], in_=ps[:, 0:TT - 1], func=AF.Copy, scale=mu),
                             r=[pbk.b()] + cb, w=[tmp.b()])
                        S.op("dve", lambda e, oc=oc, mu=mu: e.tensor_scalar(out=tmp.t[:, 0:1], in0=cy.t[:, oc:oc + 1], scalar1=mu, scalar2=None,
                                                                           op0=ALU.mult), r=[cy.b()] + cb, w=[tmp.b()])
                        S.op("dve", lambda e, oc=oc, ps=ps: e.tensor_copy(out=cy.t[:, oc:oc + 1], in_=ps[:, TT - 1:TT]), r=[pbk.b()], w=[cy.b()])
                        S.op("dve", lambda e, oc=oc, ps=ps, om=om: e.scalar_tensor_tensor(out=pm.t[:, oc, :], in0=ps[:, :], scalar=om, in1=tmp.t[:, :],
                                                                                          op0=ALU.mult, op1=ALU.add),
                             r=[pbk.b(), tmp.b()] + cb, w=[pm.b(oc)])
                    elif oc < 16:
                        S.op("act", lambda e, oc=oc, ps=ps: e.activation(out=pp.t[:, oc - 14, 16:16 + TT], in_=ps[:, :], func=AF.Copy),
                             r=[pbk.b()], w=[pp.b()])
                    else:
                        S.op("act", lambda e, oc=oc, ps=ps: e.activation(out=qT.t[:, oc - 16, :], in_=ps[:, :], func=AF.Copy),
                             r=[pbk.b()], w=[qT.b()])
                if stop == 1.2:
                    break
                invc = cm.t[:, (M_INVC0 if st == 0 else M_INVC):(M_INVC0 if st == 0 else M_INVC) + 1024].rearrange("p (c t) -> p c t", c=2)
                for g in range(4):
                    ci, pb = g // 2, 64 * (g % 2)
                    src, bsrc = pp.t[pb:pb + 64, ci, :], pp.b()
                    for lv in range(g + 1):
                        sh = 1 << lv
                        dst = ppa if lv % 2 == 0 else ppb
                        S.op("dve", lambda e, src=src, dst=dst, sh=sh, pb=pb: e.tensor_tensor(out=dst.t[pb:pb + 64, sh:16 + TT], in0=src[:, sh:16 + TT],
                                                                                            in1=src[:, 0:16 + TT - sh], op=ALU.add),
                             r=[bsrc], w=[dst.b(pb)])
                        if sh > 1:
                            pass
                        src, bsrc = dst.t[pb:pb + 64, :], dst.b(pb)
                    S.op("dve", lambda e, src=src, pb=pb, ci=ci: e.tensor_tensor(out=ppa.t[pb:pb + 64, 16:16 + TT] if False else tmp.t[pb:pb + 64, :],
                                                                                 in0=src[:, 16:16 + TT], in1=invc[pb:pb + 64, ci, :], op=ALU.mult),
                         r=[bsrc] + cb, w=[tmp.b()])
                    S.op("dve", lambda e, pb=pb, ci=ci: e.tensor_tensor(out=dT.t[pb:pb + 64, ci, :], in0=tmp.t[pb:pb + 64, :],
                                                                        in1=pp.t[pb:pb + 64, ci, 16:16 + TT], op=ALU.subtract),
                         r=[tmp.b(), pp.b()], w=[dT.b()])
                for ci in range(2):
                    pbk = PB[mmb % 2]
                    mmb += 1
                    for g2 in range(2):
                        pb = 64 * g2
                        S.op("pe", lambda e, ci=ci, pb=pb, pbk=pbk: e.matmul(pbk.t[pb:pb + 64, :], pw.t[pb:pb + 64, ci, :], dT.t[pb:pb + 64, ci, :],
                                                                            start=True, stop=True), r=[pw.b(), dT.b()], w=[pbk.b()])
                    S.op("act", lambda e, ci=ci, pbk=pbk: e.activation(out=yT.t[:, 4 + ci, :], in_=pbk.t[:, :], func=AF.Copy,
                                                                      scale=cvec.t[:, C_PS + ci:C_PS + ci + 1]), r=[pbk.b()] + cb, w=[yT.b(4 + ci)])
                S.op("dve", lambda e: e.tensor_copy(out=pp.t[:, :, 0:16], in_=pp.t[:, :, TT:TT + 16]), r=[pp.b()], w=[pp.b()])
                if stop == 1.3:
                    break
                for jm in range(2):
                    for hh in range(2):
                        pb = 64 * hh
                        hm = 2 * jm + hh
                        for mh in range(2):
                            S.op("pe", lambda e, jm=jm, pb=pb, mh=mh: e.matmul(PB[2 + mh].t[:, :], kT.t[pb:pb + 64, jm, mh * 128:(mh + 1) * 128],
                                                                               qT.t[pb:pb + 64, jm, :], start=True, stop=True),
                                 r=[kT.b(), qT.b()], w=[PB[2 + mh].b()])
                            S.op("act", lambda e, mh=mh: e.activation(out=eT.t[:, mh, :], in_=PB[2 + mh].t[:, :], func=AF.Exp, scale=0.125),
                                 r=[PB[2 + mh].b()], w=[eT.b(mh)])
                        for mh in range(2):
                            S.op("pe", lambda e, hm=hm, pb=pb, mh=mh: e.matmul(PB[4].t[pb:pb + 64, :], vtok.t[:, mh, hm * 64:(hm + 1) * 64], eT.t[:, mh, :],
                                                                               start=(mh == 0), stop=(mh == 1)),
                                 r=[vtok.b(), eT.b(mh)], w=[PB[4].b()])
                        for mh in range(2):
                            S.op("pe", lambda e, pb=pb, mh=mh: e.matmul(PB[5].t[pb:pb + 64, :], ones64, eT.t[:, mh, :],
                                                                        start=(mh == 0), stop=(mh == 1)),
                                 r=[ident.b(), eT.b(mh)], w=[PB[5].b()])
                    S.op("dve", lambda e: e.reciprocal(out=rden.t[:], in_=PB[5].t[:, :]), r=[PB[5].b()], w=[rden.b()])
                    S.op("dve", lambda e, jm=jm: e.tensor_tensor(out=yT.t[:, 6 + jm, :], in0=PB[4].t[:, :], in1=rden.t[:], op=ALU.mult),
                         r=[PB[4].b(), rden.b()], w=[yT.b(6 + jm)])
                if stop == 1.4:
                    break
                S.op("act", lambda e: e.activation(out=tw.t[0:64, :], in_=pm.t[0:64, 12, :], func=AF.Tanh), r=[pm.b(12)], w=[tw.b()])
                S.op("act", lambda e: e.activation(out=sg.t[:], in_=pm.t[:, 13, :], func=AF.Sigmoid), r=[pm.b(13)], w=[sg.b()])
                for j in range(4):
                    cs = slice(j * 128, (j + 1) * 128)
                    r_, k_, v_ = pm.t[:, j, :], pm.t[:, 4 + j, :], pm.t[:, 8 + j, :]
                    cv = lambda c0: cvec.t[:, c0 + j:c0 + j + 1]
                    pbk = PB[mmb % 2]
                    mmb += 1
                    S.op("pe", lambda e, pbk=pbk, cs=cs: e.matmul(pbk.t[:], lora.t[0:64, cs], tw.t[0:64, :], start=True, stop=True),
                         r=[lora.b(), tw.b()], w=[pbk.b()])
                    S.op("act", lambda e, pbk=pbk, cv=cv: e.activation(out=f["ld"].t[:], in_=pbk.t[:], func=AF.Sigmoid, bias=cv(C_W0)),
                         r=[pbk.b()] + cb, w=[f["ld"].b()])
                    S.op("dve", lambda e: e.tensor_scalar(out=f["ld"].t[:], in0=f["ld"].t[:], scalar1=NEG_EH, scalar2=None, op0=ALU.mult),
                         r=[f["ld"].b()], w=[f["ld"].b()])
                    S.op("dve", lambda e: e.tensor_tensor_scan(out=f["cum"].t[:], data0=scanm, data1=f["ld"].t[:], initial=0.0,
                                                               op0=ALU.mult, op1=ALU.add), r=[f["ld"].b()] + cb, w=[f["cum"].b()])
                    S.op("dve", lambda e: e.tensor_tensor(out=f["cx"].t[:], in0=f["cum"].t[:], in1=f["ld"].t[:], op=ALU.subtract),
                         r=[f["cum"].b(), f["ld"].b()], w=[f["cx"].b()])
                    S.op("act", lambda e: e.activation(out=f["E1"].t[:], in_=f["cum"].t[:], func=AF.Exp), r=[f["cum"].b()], w=[f["E1"].b()])
                    S.op("act", lambda e: e.activation(out=f["E2"].t[:], in_=f["cum"].t[:], func=AF.Exp, scale=-1.0), r=[f["cum"].b()], w=[f["E2"].b()])
                    S.op("act", lambda e: e.activation(out=f["E3"].t[:], in_=f["cx"].t[:], func=AF.Exp), r=[f["cx"].b()], w=[f["E3"].b()])
                    S.op("dve", lambda e, j=j: e.tensor_copy(out=WC.t[:, j, :], in_=f["E1"].t[:, :].rearrange("p (c t) -> p c t", t=64)[:, :, 63]),
                         r=[f["E1"].b()], w=[WC.b()])
                    pbk = PB[mmb % 2]
                    mmb += 1
                    S.op("pe", lambda e, pbk=pbk, cs=cs: e.matmul(pbk.t[:], lora.t[64:128, cs], pm.t[64:128, 12, :], start=True, stop=True),
                         r=[lora.b(), pm.b(12)], w=[pbk.b()])
                    S.op("act", lambda e, pbk=pbk, cv=cv: e.activation(out=f["a"].t[:], in_=pbk.t[:], func=AF.Sigmoid, bias=cv(C_A0)),
                         r=[pbk.b()] + cb, w=[f["a"].b()])
                    S.op("dve", lambda e, k_=k_, cv=cv: e.tensor_scalar(out=f["kk"].t[:], in0=k_, scalar1=cv(C_KK), scalar2=None, op0=ALU.mult),
                         r=[pm.b(4 + j)] + cb, w=[f["kk"].b()])
                    S.op("dve", lambda e: e.tensor_tensor(out=kk2.t[:], in0=f["kk"].t[:], in1=f["kk"].t[:], op=ALU.mult),
                         r=[f["kk"].b()], w=[kk2.b()])
                    pbk = PB[mmb % 2]
                    mmb += 1
                    S.op("pe", lambda e, pbk=pbk: e.matmul(pbk.t[:], bdo, kk2.t[:], start=True, stop=True), r=[ident.b(), kk2.b()], w=[pbk.b()])
                    S.op("act", lambda e, pbk=pbk: e.activation(out=f["rs"].t[:], in_=pbk.t[:], func=AF.Sqrt, bias=1e-12), r=[pbk.b()], w=[f["rs"].b()])
                    S.op("dve", lambda e: e.reciprocal(out=f["rs"].t[:], in_=f["rs"].t[:]), r=[f["rs"].b()], w=[f["rs"].b()])
                    S.op("dve", lambda e: e.tensor_tensor(out=f["kkn"].t[:], in0=f["kk"].t[:], in1=f["rs"].t[:], op=ALU.mult),
                         r=[f["kk"].b(), f["rs"].b()], w=[f["kkn"].b()])
                    c3 = lambda tl: tl.t[:, :].rearrange("p (c t) -> p c t", t=64)
                    S.op("dve", lambda e, j=j: e.tensor_tensor(out=KR.t[:, j, :, 0, :], in0=c3(f["kkn"]), in1=c3(f["E3"]), op=ALU.mult),
                         r=[f["kkn"].b(), f["E3"].b()], w=[KR.b(j)])
                    S.op("dve", lambda e, j=j, r_=r_: e.tensor_tensor(out=KR.t[:, j, :, 1, :], in0=r_.rearrange("p (c t) -> p c t", t=64), in1=c3(f["E1"]),
                                                                     op=ALU.mult), r=[pm.b(j), f["E1"].b()], w=[KR.b(j)])
                    S.op("dve", lambda e: e.tensor_tensor(out=f["x1"].t[:], in0=f["kkn"].t[:], in1=f["a"].t[:], op=ALU.mult),
                         r=[f["kkn"].b(), f["a"].b()], w=[f["x1"].b()])
                    S.op("dve", lambda e, j=j: e.tensor_tensor(out=BK.t[:, j, :, 0, :], in0=c3(f["x1"]), in1=c3(f["E2"]), op=ALU.mult),
                         r=[f["x1"].b(), f["E2"].b()], w=[BK.b(j)])
                    S.op("dve", lambda e, j=j, cv=cv: e.tensor_scalar(out=f["x1"].t[:], in0=f["a"].t[:], scalar1=cv(C_KA), scalar2=cder.t[:, 14 + j:15 + j],
                                                                     op0=ALU.mult, op1=ALU.add), r=[f["a"].b()] + cb, w=[f["x1"].b()])
                    S.op("dve", lambda e, k_=k_: e.tensor_tensor(out=f["kf"].t[:], in0=f["x1"].t[:], in1=k_, op=ALU.mult),
                         r=[f["x1"].b(), pm.b(4 + j)], w=[f["kf"].b()])
                    S.op("dve", lambda e, j=j: e.tensor_tensor(out=BK.t[:, j, :, 1, :], in0=c3(f["kf"]), in1=c3(f["E2"]), op=ALU.mult),
                         r=[f["kf"].b(), f["E2"].b()], w=[BK.b(j)])
                    S.op("dve", lambda e, r_=r_: e.tensor_tensor(out=f["x1"].t[:], in0=f["kf"].t[:], in1=r_, op=ALU.mult),
                         r=[f["kf"].b(), pm.b(j)], w=[f["x1"].b()])
                    S.op("dve", lambda e, cv=cv: e.tensor_scalar(out=kk2.t[:], in0=f["x1"].t[:], scalar1=cv(C_RK), scalar2=None, op0=ALU.mult),
                         r=[f["x1"].b()] + cb, w=[kk2.b()])
                    pbk = PB[mmb % 2]
                    mmb += 1
                    S.op("pe", lambda e, pbk=pbk: e.matmul(pbk.t[:], bdo, kk2.t[:], start=True, stop=True), r=[ident.b(), kk2.b()], w=[pbk.b()])
                    S.op("dve", lambda e, pbk=pbk, j=j, v_=v_: e.tensor_tensor(out=bonT.t[:, j, :], in0=pbk.t[:], in1=v_, op=ALU.mult),
                         r=[pbk.b(), pm.b(8 + j)], w=[bonT.b(j)])
                    pbk = PB[mmb % 2]
                    mmb += 1
                    S.op("pe", lambda e, pbk=pbk, cs=cs: e.matmul(pbk.t[:], gl.t[:, cs], sg.t[:], start=True, stop=True), r=[gl.b(), sg.b()], w=[pbk.b()])
                    S.op("act", lambda e, pbk=pbk, j=j: e.activation(out=gT.t[:, j, :], in_=pbk.t[:], func=AF.Copy), r=[pbk.b()], w=[gT.b(j)])
                if stop == 1.5:
                    break
                krb = [KR.b(j) for j in range(4)]
                bkb = [BK.b(j) for j in range(4)]
                HP = [(h // 2, 64 * (h % 2)) for h in range(8)]
                v3 = lambda ap_, w=64: ap_.rearrange("p (j c) -> p j c", j=4)
                lo = lambda bank, w=64: v3(bank.t[:, 0:4 * w], w)
                hi = lambda bank: v3(bank.t[:, 256:512])

                def stageA(c):
                    par = c % 2
                    cc = slice(c * 64, (c + 1) * 64)
                    vbk, asb, gf = VBK[par], Asb[par], GF[par]
                    for j, pb in HP:
                        ps_ = slice(pb, pb + 64)
                        S.op("pe", lambda e: e.transpose(PT.t[ps_, j * 64:(j + 1) * 64], pm.t[ps_, 8 + j, cc], idn[ps_, ps_]),
                             r=[pm.b(8 + j), ident.b()], w=[PT.b("A")])
                        S.op("pe", lambda e: e.transpose(PT.t[ps_, 256 + j * 64:256 + (j + 1) * 64], BK.t[ps_, j, c, 0, :], idn[ps_, ps_]),
                             r=[bkb[j], ident.b()], w=[PT.b("A")])
                        S.op("pe", lambda e: e.transpose(PT.t[ps_, 512 + j * 64:512 + (j + 1) * 64], BK.t[ps_, j, c, 1, :], idn[ps_, ps_]),
                             r=[bkb[j], ident.b()], w=[PT.b("A")])
                    for j, pb in HP:
                        ps_ = slice(pb, pb + 64)
                        for kind in range(2):
                            S.op("pe", lambda e: e.matmul(PB[2 + kind].t[ps_, j * 128:(j + 1) * 128], BK.t[ps_, j, c, kind, :],
                                                          KR.t[ps_, j, c, :, :].rearrange("p a b -> p (a b)"), start=True, stop=True),
                                 r=[bkb[j], krb[j]], w=[PB[2 + kind].b()])
                        S.op("pe", lambda e: e.matmul(PB[4].t[ps_, j * 64:(j + 1) * 64], KR.t[ps_, j, c, 0, :], BK.t[ps_, j, c, 0, :],
                                                      start=True, stop=True), r=[bkb[j], krb[j]], w=[PB[4].b()])
                    yield
                    S.op("act", lambda e: e.activation(out=vbk.t[:], in_=PT.t[:, 0:768].rearrange("p (a j c) -> p a j c", a=3, j=4), func=AF.Copy),
                         r=[PT.b("A")], w=[vbk.b()])
                    S.op("dve", lambda e: e.tensor_tensor(out=asb.t[:, :, 0, :], in0=lo(PB[2], 128), in1=M2v, op=ALU.mult),
                         r=[PB[2].b()] + cb, w=[asb.b()])
                    S.op("dve", lambda e: e.tensor_tensor(out=Lsb.t[:], in0=lo(PB[4]), in1=MLv, op=ALU.mult), r=[PB[4].b()] + cb, w=[Lsb.b()])
                    S.op("dve", lambda e: e.scalar_tensor_tensor(out=GT[0].t[:], in0=asb.t[:, :, 0, 0:64], scalar=-1.0, in1=I64v,
                                                                 op0=ALU.mult, op1=ALU.add), r=[asb.b()] + cb, w=[GT[0].b()])
                    S.op("dve", lambda e: e.tensor_tensor(out=asb.t[:, :, 1, :], in0=lo(PB[3], 128), in1=M2v, op=ALU.mult),
                         r=[PB[3].b()] + cb, w=[asb.b()])
                    yield
                    Pc, PTc, bP, bPT = Lsb.t, asb.t[:, :, 0, 0:64], Lsb.b(), asb.b()
                    gi = 0
                    pend = None
                    for lv in range(6):
                        if lv < 5:
                            for j, pb in HP:
                                ps_ = slice(pb, pb + 64)
                                S.op("pe", lambda e: e.matmul(PB[4].t[ps_, j * 64:(j + 1) * 64], PTc[ps_, j, :], Pc[ps_, j, :], start=True, stop=True),
                                     r=[bP, bPT], w=[PB[4].b()])
                            if lv < 4:
                                for j, pb in HP:
                                    ps_ = slice(pb, pb + 64)
                                    S.op("pe", lambda e: e.matmul(PB[5].t[ps_, j * 64:(j + 1) * 64], Pc[ps_, j, :], PTc[ps_, j, :], start=True, stop=True),
                                         r=[bP, bPT], w=[PB[5].b()])
                        if pend is not None:
                            pn2, plv = pend
                            gsrc = GT[gi]
                            gdst = gf if plv == 4 else GT[1 - gi]
                            for j, pb in HP:
                                ps_ = slice(pb, pb + 64)
                                S.op("pe", lambda e: e.matmul(PB[6].t[ps_, j * 64:(j + 1) * 64], pn2.t[ps_, j, :], gsrc.t[ps_, j, :], start=True, stop=True),
                                     r=[pn2.b(), gsrc.b()], w=[PB[6].b()])
                        yield
                        if lv < 5:
                            n2 = P2[lv % 2]
                            S.op("act", lambda e: e.activation(out=n2.t[:], in_=lo(PB[4]), func=AF.Copy), r=[PB[4].b()], w=[n2.b()])
                            if lv < 4:
                                n2t = P2T[lv % 2]
                                S.op("act", lambda e: e.activation(out=n2t.t[:], in_=lo(PB[5]), func=AF.Copy), r=[PB[5].b()], w=[n2t.b()])
                        if pend is not None:
                            S.op("dve", lambda e: e.tensor_tensor(out=gdst.t[:], in0=lo(PB[6]), in1=gsrc.t[:], op=ALU.add),
                                 r=[PB[6].b(), gsrc.b()], w=[gdst.b()])
                            gi = 1 - gi
                            pend = None
                        if lv < 5:
                            pend = (n2, lv)
                            if lv < 4:
                                Pc, PTc, bP, bPT = n2.t, n2t.t, n2.b(), n2t.b()
                            yield

                def stageB1(c):
                    par = c % 2
                    vbk, asb, G = VBK[par], Asb[par], GF[par]
                    ysb, ysq = Ysb[par], Ysq[par]
                    Vt, Bt, Kt = vbk.t[:, 0], vbk.t[:, 1], vbk.t[:, 2]
                    b0, b1 = PB[0].b(), PB[1].b()
                    for j, pb in HP:
                        ps_ = slice(pb, pb + 64)
                        S.op("pe", lambda e: e.matmul(PB[0].t[ps_, j * 64:(j + 1) * 64], KR.t[ps_, j, c, 0, :], Sbf.t[ps_, j, :], start=True, stop=False),
                             r=[krb[j], Sbf.b()], w=[b0])
                        S.op("pe", lambda e: e.matmul(PB[0].t[ps_, j * 64:(j + 1) * 64], asb.t[ps_, j, 1, 0:64], Vt[ps_, j, :], start=False, stop=True),
                             r=[asb.b(), vbk.b()], w=[b0])
                    yield
                    S.op("act", lambda e: e.activation(out=Zsb.t[:], in_=lo(PB[0]), func=AF.Copy), r=[b0], w=[Zsb.b()])
                    yield
                    for j, pb in HP:
                        ps_ = slice(pb, pb + 64)
                        S.op("pe", lambda e: e.matmul(PB[0].t[ps_, j * 64:(j + 1) * 64], G.t[ps_, j, :], Zsb.t[ps_, j, :], start=True, stop=True),
                             r=[G.b(), Zsb.b()], w=[b0])
                    yield
                    S.op("act", lambda e: e.activation(out=Un.t[:], in_=lo(PB[0]), func=AF.Copy, scale=-1.0), r=[b0], w=[Un.b()])
                    yield
                    for j, pb in HP:
                        ps_ = slice(pb, pb + 64)
                        S.op("pe", lambda e: e.matmul(PB[0].t[ps_, j * 64:(j + 1) * 64], Kt[ps_, j, :], Vt[ps_, j, :], start=True, stop=False),
                             r=[vbk.b()], w=[b0])
                        S.op("pe", lambda e: e.matmul(PB[0].t[ps_, j * 64:(j + 1) * 64], Bt[ps_, j, :], Un.t[ps_, j, :], start=False, stop=True),
                             r=[vbk.b(), Un.b()], w=[b0])
                    for j, pb in HP:
                        ps_ = slice(pb, pb + 64)
                        S.op("pe", lambda e: e.matmul(PB[1].t[ps_, j * 64:(j + 1) * 64], KR.t[ps_, j, c, 1, :], Sbf.t[ps_, j, :], start=True, stop=False),
                             r=[krb[j], Sbf.b()], w=[b1])
                        S.op("pe", lambda e: e.matmul(PB[1].t[ps_, j * 64:(j + 1) * 64], asb.t[ps_, j, 1, 64:128], Vt[ps_, j, :], start=False, stop=False),
                             r=[asb.b(), vbk.b()], w=[b1])
                        S.op("pe", lambda e: e.matmul(PB[1].t[ps_, j * 64:(j + 1) * 64], asb.t[ps_, j, 0, 64:128], Un.t[ps_, j, :], start=False, stop=True),
                             r=[asb.b(), Un.b()], w=[b1])
                    yield
                    S.op("dve", lambda e: e.tensor_tensor(out=Sf.t[:], in0=lo(PB[0]), in1=Sf.t[:], op=ALU.add), r=[b0, Sf.b()], w=[Sf.b()])
                    S.op("act", lambda e: e.activation(out=ysb.t[:], in_=lo(PB[1]), func=AF.Copy), r=[b1], w=[ysb.b()])
                    S.op("act", lambda e: e.activation(out=ysq.t[:], in_=lo(PB[1]), func=AF.Square), r=[b1], w=[ysq.b()])
                    yield
                    S.op("dve", lambda e: e.tensor_tensor(out=Sf.t[:], in0=Sf.t[:], in1=WC.t[:, :, c].unsqueeze(2).broadcast_to([128, 4, 64]), op=ALU.mult),
                         r=[Sf.b(), WC.b()], w=[Sf.b()])
                    yield
                    S.op("act", lambda e: e.activation(out=Sbf.t[:], in_=Sf.t[:], func=AF.Copy), r=[Sf.b()], w=[Sbf.b()])

                def stageB2(c):
                    par = c % 2
                    cc = slice(c * 64, (c + 1) * 64)
                    ysb, ysq, ys, yh_ = Ysb[par], Ysq[par], yst[par], yh[par]
                    S.op("dve", lambda e: e.tensor_reduce(out=ys.t[:, 0, :], in_=ysb.t[:], axis=AX.X, op=ALU.add), r=[ysb.b()], w=[ys.b()])
                    S.op("dve", lambda e: e.tensor_reduce(out=ys.t[:, 1, :], in_=ysq.t[:], axis=AX.X, op=ALU.add), r=[ysq.b()], w=[ys.b()])
                    yield
                    S.op("dve", lambda e: e.tensor_scalar(out=ys.t[:, 0, :], in0=ys.t[:, 0, :], scalar1=1.0 / 64, scalar2=None, op0=ALU.mult),
                         r=[ys.b()], w=[ys.b()])
                    yield
                    S.op("dve", lambda e: e.tensor_tensor(out=ys.t[:, 2, :], in0=ys.t[:, 0, :], in1=ys.t[:, 0, :], op=ALU.mult), r=[ys.b()], w=[ys.b()])
                    yield
                    S.op("dve", lambda e: e.scalar_tensor_tensor(out=ys.t[:, 3, :], in0=ys.t[:, 1, :], scalar=1.0 / 64, in1=ys.t[:, 2, :],
                                                                 op0=ALU.mult, op1=ALU.subtract), r=[ys.b()], w=[ys.b()])
                    yield
                    S.op("act", lambda e: e.activation(out=ys.t[:, 3, :], in_=ys.t[:, 3, :], func=AF.Sqrt, bias=eps_gn.t[:, 0:1]), r=[ys.b(), eps_gn.b()], w=[ys.b()])
                    yield
                    S.op("dve", lambda e: e.reciprocal(out=ys.t[:, 3, :], in_=ys.t[:, 3, :]), r=[ys.b()], w=[ys.b()])
                    S.op("dve", lambda e: e.tensor_tensor(out=ysb.t[:], in0=ysb.t[:], in1=ys.t[:, 0, :].unsqueeze(2).broadcast_to([128, 4, 64]), op=ALU.subtract),
                         r=[ysb.b(), ys.b()], w=[ysb.b()])
                    yield
                    S.op("dve", lambda e: e.tensor_tensor(out=yh_.t[:], in0=ysb.t[:], in1=ys.t[:, 3, :].unsqueeze(2).broadcast_to([128, 4, 64]), op=ALU.mult),
                         r=[ysb.b(), ys.b()], w=[yh_.b()])
                    yield
                    for j, pb in HP:
                        ps_ = slice(pb, pb + 64)
                        S.op("pe", lambda e: e.transpose(PT.t[ps_, 768 + j * 64:768 + (j + 1) * 64], yh_.t[ps_, j, :], idn[ps_, ps_]),
                             r=[yh_.b(), ident.b()], w=[PT.b("A")])
                    yield
                    S.op("act", lambda e: e.activation(out=yhT.t[:, :, cc], in_=PT.t[:, 768:1024].rearrange("p (j t) -> p j t", j=4), func=AF.Copy),
                         r=[PT.b("A")], w=[yhT.b()])

                for c in range(NCH + 2):
                    gens = []
                    if 1 <= c <= NCH:
                        gens.append(stageB1(c - 1))
                    if c < NCH:
                        gens.append(stageA(c))
                    if 2 <= c:
                        gens.append(stageB2(c - 2))
                    while gens:
                        for g_ in list(gens):
                            try:
                                next(g_)
                            except StopIteration:
                                gens.remove(g_)
                            S._flush_pe()
                if stop == 1.6:
                    break
                for j in range(4):
                    S.op("dve", lambda e, j=j: e.tensor_scalar(out=f["x1"].t[:], in0=yhT.t[:, j, :], scalar1=cvec.t[:, C_LNW + j:C_LNW + j + 1],
                                                               scalar2=cvec.t[:, C_LNB + j:C_LNB + j + 1], op0=ALU.mult, op1=ALU.add),
                         r=[yhT.b()] + cb, w=[f["x1"].b()])
                    S.op("dve", lambda e, j=j: e.tensor_tensor(out=f["x1"].t[:], in0=f["x1"].t[:], in1=bonT.t[:, j, :], op=ALU.add),
                         r=[f["x1"].b(), bonT.b(j)], w=[f["x1"].b()])
                    S.op("dve", lambda e, j=j: e.tensor_tensor(out=yT.t[:, j, :], in0=f["x1"].t[:], in1=gT.t[:, j, :], op=ALU.mult),
                         r=[f["x1"].b(), gT.b(j)], w=[yT.b(j)])
                S.dma("sp", dy, yT_d[:, :, t0:t0 + TT], yT.t[:], r=[yT.b(jj) for jj in range(8)], w=[yT_db[st]])
            S.emit()
            if stop <= 2:
                S.barrier()
                S.emit()
                return nc

        rot = [0]

        def nb():
            rot[0] += 1
            return PB[rot[0] % 7]

        with contextlib.ExitStack() as p2:
            S.barrier()
            dw2 = S.dsem()
            wg = load_w(p2, "wg", wg_d, 8, 3 * D, dw2)
            wup = sb(p2, "wup", [128, 8, D], BF16)
            load_w(p2, "", wupr_d, 4, D, dw2, tile=wup, kc0=0)
            load_w(p2, "", wupp_d, 2, D, dw2, tile=wup, kc0=4)
            load_w(p2, "", wupm_d, 2, D, dw2, tile=wup, kc0=6)
            wo = load_w(p2, "wo", wo_d, 8, D, dw2)
            S.seal(dw2, [wg.b(), wup.b(), wo.b()])
            hT2 = [sb(p2, "hT2%d" % i, [128, 8, TT], BF16) for i in range(2)]
            yT2 = [sb(p2, "yT2%d" % i, [128, 8, TT], BF16) for i in range(2)]
            dl2 = [S.dsem() for _ in range(2)]
            gs = [sb(p2, "gs%d" % i, [128, TT]) for i in range(3)]
            mm_ = [sb(p2, "mm%d" % i, [128, TT]) for i in range(3)]
            mg = sb(p2, "mg", [128, 8, TT], BF16)
            xs2 = [sb(p2, "xs2%d" % i, [128, D]) for i in range(2)]
            dx2 = [S.dsem() for _ in range(2)]
            x1s = [sb(p2, "x1s%d" % i, [128, D]) for i in range(2)]
            ds2 = [S.dsem() for _ in range(2)]
            kr = [(0, 4), (4, 6), (6, 8)]
            for st in range(NST):
                t0 = st * TT
                i2 = st % 2
                S.dma("sp", dl2[i2], hT2[i2].t[:], hT_d[:, :, t0:t0 + TT], r=[hT_db[st]], w=[hT2[i2].b()])
                S.dma("sp", dl2[i2], yT2[i2].t[:], yT_d[:, :, t0:t0 + TT], r=[yT_db[st]], w=[yT2[i2].b()])
                S.seal(dl2[i2], [hT2[i2].b(), yT2[i2].b()])
                for fo in range(8):
                    fs = slice(fo * 128, (fo + 1) * 128)
                    for b in range(3):
                        pg, pu = nb(), nb()
                        for kc in range(8):
                            S.op("pe", lambda e: e.matmul(pg.t[:], wg.t[:, kc, b * D + fo * 128:b * D + (fo + 1) * 128], hT2[i2].t[:, kc, :],
                                                          start=(kc == 0), stop=(kc == 7)), r=[wg.b(), hT2[i2].b()], w=[pg.b()])
                        k0, k1 = kr[b]
                        for kc in range(k0, k1):
                            S.op("pe", lambda e: e.matmul(pu.t[:], wup.t[:, kc, fs], yT2[i2].t[:, kc, :], start=(kc == k0), stop=(kc == k1 - 1)),
                                 r=[wup.b(), yT2[i2].b()], w=[pu.b()])
                        S.op("act", lambda e: e.activation(out=gs[b].t[:], in_=pg.t[:], func=AF.Sigmoid,
                                                           bias=cvec.t[:, C_BG + b * 8 + fo:C_BG + b * 8 + fo + 1]), r=[pg.b()] + cb, w=[gs[b].b()])
                        S.op("dve", lambda e: e.tensor_tensor(out=mm_[b].t[:], in0=pu.t[:], in1=gs[b].t[:], op=ALU.mult),
                             r=[pu.b(), gs[b].b()], w=[mm_[b].b()])
                    S.op("pool", lambda e: e.tensor_tensor(out=mm_[0].t[:], in0=mm_[0].t[:], in1=mm_[1].t[:], op=ALU.add),
                         r=[mm_[0].b(), mm_[1].b()], w=[mm_[0].b()])
                    S.op("pool", lambda e: e.tensor_tensor(out=mg.t[:, fo, :], in0=mm_[0].t[:], in1=mm_[2].t[:], op=ALU.add),
                         r=[mm_[0].b(), mm_[2].b()], w=[mg.b()])
                for sub in range(4):
                    i = (st * 4 + sub) % 2
                    r0 = t0 + sub * 128
                    S.dma("sp", dx2[i], xs2[i].t[:], x_d[r0:r0 + 128, :], w=[xs2[i].b()])
                    for half in range(2):
                        pbk = nb()
                        for kc in range(8):
                            S.op("pe", lambda e: e.matmul(pbk.t[:], mg.t[:, kc, sub * 128:(sub + 1) * 128], wo.t[:, kc, half * 512:(half + 1) * 512],
                                                          start=(kc == 0), stop=(kc == 7)), r=[mg.b(), wo.b()], w=[pbk.b()])
                        S.op("dve", lambda e: e.tensor_tensor(out=x1s[i].t[:, half * 512:(half + 1) * 512], in0=pbk.t[:],
                                                              in1=xs2[i].t[:, half * 512:(half + 1) * 512], op=ALU.add),
                             r=[pbk.b(), xs2[i].b()], w=[x1s[i].b()])
                    S.dma("sp", ds2[i], x1_d[r0:r0 + 128, :], x1s[i].t[:], r=[x1s[i].b()], w=[x1_db[st * 4 + sub]])
            S.emit()
            if stop == 3:
                S.barrier()
                S.emit()
                return nc

        with contextlib.ExitStack() as p3:
            S.barrier()
            dw3 = S.dsem()
            wfi = load_w(p3, "wfi", wfi_d, 8, 2 * DFF, dw3)
            wfo = load_w(p3, "wfo", wfo_d, NFC, D, dw3)
            gfin = sb(p3, "gfin", [128, D])
            dgf = S.dsem()
            S.dma("sp", dgf, gfin.t[:], gfin_d[:, :], w=[gfin.b()])
            S.seal(dw3, [wfi.b(), wfo.b()])
            x1k = sb(p3, "x1k", [128, 4, D])
            dk = [S.dsem() for _ in range(4)]
            h2T = sb(p3, "h2T", [128, 8, TT], BF16)
            ub = [sb(p3, "ub%d" % i, [128, 2 + TT]) for i in range(2)]
            ucar = sb(p3, "ucar", [128, NFC, 2])
            c1 = [sb(p3, "c1%d" % i, [128, TT]) for i in range(2)]
            actT = sb(p3, "actT", [128, NFC, TT], BF16)
            x2s = [sb(p3, "x2s%d" % i, [128, D]) for i in range(2)]
            do = [S.dsem() for _ in range(2)]
            fst = sb(p3, "fst", [128, 2])
            S.op("dve", lambda e: e.memset(ucar.t[:], 0.0), w=[ucar.b()])
            for st in range(NST):
                t0 = st * TT
                for sub in range(4):
                    r0 = t0 + sub * 128
                    S.dma("sp", dk[sub], x1k.t[:, sub, :], x1_d[r0:r0 + 128, :], r=[x1_db[st * 4 + sub]], w=[x1k.b(sub)])
                    norm_T(x1k.t[:, sub, :], x1k.b(sub), C_G2, h2T.t, h2T.b(), sub * 128)
                for fc in range(NFC):
                    pu, pgv = nb(), nb()
                    u_, c_ = ub[fc % 2], c1[fc % 2]
                    for half, pbk in ((0, pu), (1, pgv)):
                        for kc in range(8):
                            S.op("pe", lambda e: e.matmul(pbk.t[:], wfi.t[:, kc, half * DFF + fc * 128:half * DFF + (fc + 1) * 128], h2T.t[:, kc, :],
                                                          start=(kc == 0), stop=(kc == 7)), r=[wfi.b(), h2T.b()], w=[pbk.b()])
                    cw = lambda jx: cvec.t[:, C_CW + jx * NFC + fc:C_CW + jx * NFC + fc + 1]
                    S.op("act", lambda e: e.activation(out=u_.t[:, 2:2 + TT], in_=pu.t[:], func=AF.Copy), r=[pu.b()], w=[u_.b()])
                    S.op("act", lambda e: e.activation(out=c_.t[:, 2:TT], in_=pu.t[:, 0:TT - 2], func=AF.Copy, scale=cw(0)), r=[pu.b()] + cb, w=[c_.b()])
                    S.op("pool", lambda e: e.tensor_copy(out=u_.t[:, 0:2], in_=ucar.t[:, fc, :]), r=[ucar.b()], w=[u_.b()])
                    S.op("pool", lambda e: e.tensor_scalar(out=c_.t[:, 0:2], in0=ucar.t[:, fc, :], scalar1=cw(0), scalar2=None, op0=ALU.mult),
                         r=[ucar.b()] + cb, w=[c_.b()])
                    S.op("pool", lambda e: e.tensor_copy(out=ucar.t[:, fc, :], in_=u_.t[:, TT:TT + 2]), r=[u_.b()], w=[ucar.b()])
                    S.op("dve", lambda e: e.scalar_tensor_tensor(out=c_.t[:], in0=u_.t[:, 1:1 + TT], scalar=cw(1), in1=c_.t[:], op0=ALU.mult, op1=ALU.add),
                         r=[u_.b(), c_.b()] + cb, w=[c_.b()])
                    S.op("dve", lambda e: e.scalar_tensor_tensor(out=c_.t[:], in0=u_.t[:, 2:2 + TT], scalar=cw(2), in1=c_.t[:], op0=ALU.mult, op1=ALU.add),
                         r=[u_.b(), c_.b()] + cb, w=[c_.b()])
                    S.op("act", lambda e: e.activation(out=c_.t[:], in_=c_.t[:], func=AF.Gelu, bias=cvec.t[:, C_CB + fc:C_CB + fc + 1]),
                         r=[c_.b()] + cb, w=[c_.b()])
                    S.op("dve", lambda e: e.tensor_tensor(out=actT.t[:, fc, :], in0=pgv.t[:], in1=c_.t[:], op=ALU.mult),
                         r=[pgv.b(), c_.b()], w=[actT.b()])
                for sub in range(4):
                    i = (st * 4 + sub) % 2
                    r0 = t0 + sub * 128
                    for half in range(2):
                        pbk = nb()
                        for fc in range(NFC):
                            S.op("pe", lambda e: e.matmul(pbk.t[:], actT.t[:, fc, sub * 128:(sub + 1) * 128], wfo.t[:, fc, half * 512:(half + 1) * 512],
                                                          start=(fc == 0), stop=(fc == NFC - 1)), r=[actT.b(), wfo.b()], w=[pbk.b()])
                        S.op("dve", lambda e: e.tensor_tensor(out=x2s[i].t[:, half * 512:(half + 1) * 512], in0=pbk.t[:],
                                                              in1=x1k.t[:, sub, half * 512:(half + 1) * 512], op=ALU.add),
                             r=[pbk.b(), x1k.b(sub)], w=[x2s[i].b()])
                    ss, ms = fst.t[:, 0:1], fst.t[:, 1:2]
                    S.op("act", lambda e: e.activation(out=junk.t[:], in_=x2s[i].t[:], func=AF.Square, accum_out=ss),
                         r=[x2s[i].b(), fst.b()], w=[junk.b(), fst.b()])
                    S.op("dve", lambda e: e.tensor_scalar(out=ms, in0=ss, scalar1=1.0 / D, scalar2=1e-6, op0=ALU.mult, op1=ALU.add), r=[fst.b()], w=[fst.b()])
                    S.op("act", lambda e: e.activation(out=ms, in_=ms, func=AF.Sqrt), r=[fst.b()], w=[fst.b()])
                    S.op("dve", lambda e: e.reciprocal(out=ms, in_=ms), r=[fst.b()], w=[fst.b()])
                    S.op("act", lambda e: e.activation(out=x2s[i].t[:], in_=x2s[i].t[:], func=AF.Copy, scale=ms), r=[x2s[i].b(), fst.b()], w=[x2s[i].b()])
                    S.op("dve", lambda e: e.tensor_tensor(out=x2s[i].t[:], in0=x2s[i].t[:], in1=gfin.t[:], op=ALU.mult),
                         r=[x2s[i].b(), gfin.b()], w=[x2s[i].b()])
                    S.dma("sp", do[i], out_d[r0:r0 + 128, :], x2s[i].t[:], r=[x2s[i].b()])
            S.wait_all("sp", [(d[0], d[1], None) for d in do])
            S.emit()
    return nc


def _cols(v, n):
    return np.ascontiguousarray(np.asarray(v, np.float32).reshape(n, 128).T)


def _host_consts():
    cm = np.zeros((128, NCM), np.float32)
    s = np.arange(128)[:, None] % 64
    t = np.arange(64)[None, :]
    cm[:, M_M2:M_M2 + 64] = (s < t)
    cm[:, M_M2 + 64:M_M2 + 128] = (s <= t)
    tt = np.arange(128)[:, None] % 64
    ss = np.arange(64)[None, :]
    cm[:, M_ML:M_ML + 64] = (tt > ss)
    cm[:, M_I64:M_I64 + 64] = (tt == ss)
    sc = np.ones(512, np.float32)
    sc[::64] = 0.0
    cm[:, M_SCAN:M_SCAN + 512] = sc[None, :]
    wins = [2, 4, 8, 16]
    for g in range(4):
        ci, pb = g // 2, 64 * (g % 2)
        pos = np.arange(1, 513)
        cm[pb:pb + 64, M_INVC0 + ci * 512:M_INVC0 + (ci + 1) * 512] = (1.0 / np.minimum(pos, wins[g]))[None, :]
        cm[pb:pb + 64, M_INVC + ci * 512:M_INVC + (ci + 1) * 512] = 1.0 / wins[g]
    cmb = np.zeros((128, 320), np.float32)
    cmb[:, 0:128] = np.eye(128)
    cmb[0:64, 128:192] = 1.0
    cmb[64:128, 192:256] = 1.0
    cmb[:, 256:320] = 1.0
    return cm, cmb


_NC_CACHE = {}


def _prep(inputs):
    g = lambda k: np.asarray(inputs[k], np.float32)
    cv = np.zeros((128, NCV), np.float32)
    cv[:, C_G1:C_G1 + 8] = _cols(g("norm_mix_g")[0], 8)
    cv[:, C_MU:C_MU + 14] = _cols(g("mu_shift")[0], 14)
    cv[:, C_W0:C_W0 + 4] = _cols(g("w0")[0], 4)
    cv[:, C_A0:C_A0 + 4] = _cols(g("a0")[0], 4)
    cv[:, C_KK:C_KK + 4] = _cols(g("k_k")[0], 4)
    cv[:, C_KA:C_KA + 4] = _cols(g("k_a")[0], 4)
    cv[:, C_RK:C_RK + 4] = _cols(g("r_k")[0].reshape(-1), 4)
    cv[:, C_LNW:C_LNW + 4] = _cols(g("ln_x_w")[0], 4)
    cv[:, C_LNB:C_LNB + 4] = _cols(g("ln_x_b")[0], 4)
    cv[:, C_PS:C_PS + 2] = _cols(g("pool_scale")[0], 2)
    cv[:, C_BG:C_BG + 24] = _cols(g("b_gate")[0], 24)
    cv[:, C_G2:C_G2 + 8] = _cols(g("norm_ffn_g")[0], 8)
    for j in range(3):
        cv[:, C_CW + j * NFC:C_CW + (j + 1) * NFC] = _cols(g("ffn_conv_w")[0, j], NFC)
    cv[:, C_CB:C_CB + NFC] = _cols(g("ffn_conv_b")[0], NFC)
    cv[:, C_GM:C_GM + 8] = _cols(g("norm_mem_g")[0], 8)
    cm, cmb = _host_consts()
    shared = {
        "w_in_mix": g("w_in_mix")[0], "w_lora_b": g("w_lora_b")[0], "a_lora_b": g("a_lora_b")[0], "g_lora_b": g("g_lora_b")[0],
        "pool_w": g("pool_w")[0], "w_mem_kv": g("w_mem_kv")[0], "w_up_rwkv": g("w_up_rwkv")[0], "w_up_pool": g("w_up_pool")[0],
        "w_up_mem": g("w_up_mem")[0], "w_gate": g("w_gate")[0], "w_o": g("w_o")[0], "w_ffn_in": g("w_ffn_in")[0],
        "w_ffn_out": g("w_ffn_out")[0], "cvec": cv, "cm32": cm, "cmb": cmb,
        "gfin": np.ascontiguousarray(np.broadcast_to(g("norm_final_g")[None, :], (128, D))),
    }
    shared = {k: np.ascontiguousarray(v, dtype=np.float32) for k, v in shared.items()}
    x = g("x")
    mem = g("mem")
    return [dict(shared, x=np.ascontiguousarray(x[b]), mem=np.ascontiguousarray(mem[b])) for b in range(8)]


def kernel(**inputs):
    in_maps = _prep(inputs)
    if "nc" not in _NC_CACHE:
        _NC_CACHE["nc"] = build(False)
    res = run_bass_kernel_spmd(_NC_CACHE["nc"], in_maps, core_ids=list(range(8)))
    return np.stack([np.asarray(r["out"], np.float32) for r in res.results], axis=0)
```

```python
import numpy as np
import contextlib
import concourse.bass as bass
import concourse.mybir as mybir
from concourse.bass_utils import run_bass_kernel_spmd

F32 = mybir.dt.float32
BF16 = mybir.dt.bfloat16
AF = mybir.ActivationFunctionType
ALU = mybir.AluOpType
AX = mybir.AxisListType

D = 1024
T = 4096
TT = 512
NST = T // TT
NCH = TT // 64
DFF = 2816
NFC = DFF // 128
MIX_IN = 2304
NEG_EH = -float(np.exp(-0.5))

C_G1, C_MU, C_W0, C_A0, C_KK, C_KA, C_RK, C_LNW, C_LNB, C_PS, C_BG, C_G2, C_CW, C_CB, C_GM = (
    0, 8, 22, 26, 30, 34, 38, 42, 46, 50, 52, 76, 84, 150, 172)
NCV = 180
M_M2, M_ML, M_I64, M_SCAN, M_INVC0, M_INVC = 0, 128, 192, 256, 768, 1792
NCM = 2816


class Buf:
    __slots__ = ("w", "r")

    def __init__(self):
        self.w = None
        self.r = {}


class Rec:
    def __getattr__(self, name):
        def f(*a, **k):
            self.call = (name, a, k)
            return self
        return f


class Eng:
    def __init__(self, name, sem, is_pe=False):
        self.name = name
        self.sem = sem
        self.count = 0
        self.seen = {}
        self.prog = []
        self.is_pe = is_pe


class Sched:
    def __init__(self, nc, es):
        self.nc = nc
        self.E = {}
        for n in ("pe", "act", "dve", "pool", "sp"):
            self.E[n] = Eng(n, es.enter_context(nc.semaphore("s_" + n)), is_pe=(n == "pe"))
        self.es = es
        self.ndsem = 0
        self.dsems = []

    def dsem(self):
        self.ndsem += 1
        ds = [self.es.enter_context(self.nc.semaphore("d%d" % self.ndsem)), 0]
        self.dsems.append(ds)
        return ds

    def barrier(self):
        for E in self.E.values():
            for X in self.E.values():
                if X is not E and X.count > 0 and E.seen.get(id(X.sem), 0) < X.count:
                    E.seen[id(X.sem)] = X.count
                    E.prog.append(("w", X.sem, X.count))
            for ds in self.dsems:
                if ds[1] > 0 and E.seen.get(id(ds[0]), 0) < ds[1]:
                    E.seen[id(ds[0])] = ds[1]
                    E.prog.append(("w", ds[0], ds[1]))

    def _deps(self, E, reads, writes):
        need = {}

        def add(tok, raw):
            sem, val, eng = tok
            if eng is E and E.is_pe:
                return
            k = id(sem)
            if k not in need or need[k][1] < val:
                need[k] = (sem, val)

        for b in reads:
            if b.w is not None:
                add(b.w, True)
        for b in writes:
            if b.w is not None:
                add(b.w, False)
            for t in b.r.values():
                add(t, False)
        for k, (sem, val) in need.items():
            if E.seen.get(k, 0) >= val:
                continue
            E.seen[k] = val
            E.prog.append(("w", sem, val))

    def _commit(self, tok, reads, writes):
        for b in writes:
            b.w = tok
            b.r = {}
        k = id(tok[0])
        for b in reads:
            b.r[k] = tok

    def op(self, en, fn, r=(), w=()):
        E = self.E[en]
        self._deps(E, r, w)
        E.count += 1
        rec = Rec()
        fn(rec)
        E.prog.append(("i", rec.call, E.sem, 1))
        self._commit((E.sem, E.count, E), r, w)

    def dma(self, en, ds, out, in_, r=(), w=()):
        E = self.E[en]
        self._deps(E, r, w)
        ds[1] += 16
        E.prog.append(("i", ("dma_start", (), dict(out=out, in_=in_)), ds[0], 16))
        self._commit((ds[0], ds[1], None), r, w)

    def seal(self, ds, bufs):
        for b in bufs:
            b.w = (ds[0], ds[1], None)

    def wait_all(self, en, toks):
        E = self.E[en]
        for sem, val, _ in toks:
            E.prog.append(("w", sem, val))

    def emit(self):
        nc = self.nc
        progs = {n: e.prog for n, e in self.E.items()}
        for e in self.E.values():
            e.prog = []

        def run(eng, prog):
            for it in prog:
                if it[0] == "w":
                    eng.wait_ge(it[1], it[2])
                else:
                    name, a, k = it[1]
                    getattr(eng, name)(*a, **k).then_inc(it[2], it[3])

        with nc.Block() as block:
            @block.tensor
            def _(e):
                run(e, progs["pe"])

            @block.scalar
            def _(e):
                run(e, progs["act"])

            @block.vector
            def _(e):
                run(e, progs["dve"])

            @block.gpsimd
            def _(e):
                run(e, progs["pool"])

            @block.sync
            def _(e):
                run(e, progs["sp"])


class Tl:
    def __init__(self, t):
        self.t = t
        self.bufs = {}

    def b(self, key=0):
        if key not in self.bufs:
            self.bufs[key] = Buf()
        return self.bufs[key]


def build(debug=False, stop=99):
    nc = bass.Bass("TRN2", target_bir_lowering=False)
    din = lambda n, s, dt=F32: nc.dram_tensor(n, s, dt, kind="ExternalInput").ap()
    x_d = din("x", [T, D])
    mem_d = din("mem", [256, D])
    win_d = din("w_in_mix", [D, MIX_IN])
    wl_d = din("w_lora_b", [64, 512])
    al_d = din("a_lora_b", [64, 512])
    gl_d = din("g_lora_b", [128, 512])
    pw_d = din("pool_w", [4, 64, 64])
    wkv_d = din("w_mem_kv", [D, 512])
    wupr_d = din("w_up_rwkv", [512, D])
    wupp_d = din("w_up_pool", [256, D])
    wupm_d = din("w_up_mem", [256, D])
    wg_d = din("w_gate", [D, 3 * D])
    wo_d = din("w_o", [D, D])
    wfi_d = din("w_ffn_in", [D, 2 * DFF])
    wfo_d = din("w_ffn_out", [DFF, D])
    cvec_d = din("cvec", [128, NCV])
    cm32_d = din("cm32", [128, NCM])
    cmb_d = din("cmb", [128, 320])
    gfin_d = din("gfin", [128, D])
    skind = "ExternalOutput" if debug else "Internal"
    hT_d = nc.dram_tensor("hT_d", [128, 8, T], BF16, kind=skind).ap()
    yT_d = nc.dram_tensor("yT_d", [128, 8, T], BF16, kind=skind).ap()
    x1_d = nc.dram_tensor("x1_d", [T, D], F32, kind=skind).ap()
    out_d = nc.dram_tensor("out", [T, D], F32, kind="ExternalOutput").ap()
    wsc = {n: nc.dram_tensor(n + "_bf", shp, BF16, kind="Internal").ap() for n, shp in
           (("wg", [D, 3 * D]), ("wup", [D, D]), ("wo", [D, D]), ("wfi", [D, 2 * DFF]), ("wfo", [DFF, D]))}
    wsc_b = {n: Buf() for n in wsc}
    hT_db = [Buf() for _ in range(NST)]
    yT_db = [Buf() for _ in range(NST)]
    x1_db = [Buf() for _ in range(NST * 4)]

    with contextlib.ExitStack() as top:
        S = Sched(nc, top)
        sb = lambda es, n, s, dt=F32: Tl(es.enter_context(nc.sbuf_tensor("sb_" + n, s, dt)))
        PB = [Tl(top.enter_context(nc.psum_tensor("pb%d" % i, [128, 512], F32))) for i in range(7)]
        PT = Tl(top.enter_context(nc.psum_tensor("pt", [128, 1024], BF16)))
        cvec = sb(top, "cvec", [128, NCV])
        cder = sb(top, "cder", [128, 18])
        ident = sb(top, "ident", [128, 320], BF16)
        junk = sb(top, "junk", [128, D], BF16)
        hb = sb(top, "hb", [128, D], BF16)
        st4 = sb(top, "st4", [128, 4])
        dconst = S.dsem()
        S.dma("sp", dconst, cvec.t[:], cvec_d[:, :], w=[cvec.b()])
        dconst2 = S.dsem()
        S.dma("pool", dconst2, ident.t[:], cmb_d[:, :], w=[ident.b()])
        S.op("dve", lambda e: e.tensor_scalar(out=cder.t[:, 0:14], in0=cvec.t[:, C_MU:C_MU + 14], scalar1=-1.0, scalar2=1.0,
                                              op0=ALU.mult, op1=ALU.add), r=[cvec.b()], w=[cder.b()])
        S.op("dve", lambda e: e.tensor_scalar(out=cder.t[:, 14:18], in0=cvec.t[:, C_KA:C_KA + 4], scalar1=-1.0, scalar2=1.0,
                                              op0=ALU.mult, op1=ALU.add), r=[cvec.b()], w=[cder.b()])
        idn = ident.t[:, 0:128]
        bdo = ident.t[:, 128:256]
        ones64 = ident.t[:, 256:320]
        cb = [cvec.b(), cder.b(), ident.b()]

        def norm_T(xs_ap, bx, gcol, hT, bh, col0, npart=128):
            ss, ms = st4.t[0:npart, 0:1], st4.t[0:npart, 1:2]
            S.op("act", lambda e: e.activation(out=junk.t[0:npart, :], in_=xs_ap, func=AF.Square, accum_out=ss),
                 r=[bx, st4.b()], w=[junk.b(), st4.b()])
            S.op("dve", lambda e: e.tensor_scalar(out=ms, in0=ss, scalar1=1.0 / D, scalar2=1e-6, op0=ALU.mult, op1=ALU.add),
                 r=[st4.b()], w=[st4.b()])
            S.op("act", lambda e: e.activation(out=ms, in_=ms, func=AF.Sqrt), r=[st4.b()], w=[st4.b()])
            S.op("dve", lambda e: e.reciprocal(out=ms, in_=ms), r=[st4.b()], w=[st4.b()])
            S.op("dve", lambda e: e.tensor_scalar(out=hb.t[0:npart, :], in0=xs_ap, scalar1=ms, scalar2=None, op0=ALU.mult),
                 r=[bx, st4.b()], w=[hb.b()])
            for c in range(8):
                S.op("pe", lambda e, c=c: e.transpose(PT.t[:, c * 128:c * 128 + npart], hb.t[0:npart, c * 128:(c + 1) * 128],
                                                      idn[0:npart, 0:npart]), r=[hb.b(), ident.b()], w=[PT.b("A"), PT.b("B")])
            pv = PT.t[:, :].rearrange("p (c t) -> p c t", c=8)[:, :, 0:npart]
            gv = cvec.t[:, gcol:gcol + 8].unsqueeze(2).broadcast_to([128, 8, npart])
            S.op("dve", lambda e: e.tensor_tensor(out=hT[:, :, col0:col0 + npart], in0=pv, in1=gv, op=ALU.mult),
                 r=[PT.b("A"), PT.b("B"), cvec.b()], w=[bh])

        def load_w(es, name, wd, kchunks, ncols, ds, row0=0, tile=None, kc0=0):
            if tile is None:
                tile = sb(es, name, [128, kchunks, ncols], BF16)
            step = 1024
            for kc in range(kchunks):
                for n0 in range(0, ncols, step):
                    n1 = min(ncols, n0 + step)
                    S.dma("pool", ds, tile.t[:, kc0 + kc, n0:n1], wd[row0 + kc * 128:row0 + (kc + 1) * 128, n0:n1], w=[tile.b()])
            return tile

        def stage_w(ds, name, wd, nrows, ncols, row0=0):
            for r0 in range(0, nrows, 128):
                for n0 in range(0, ncols, 2048):
                    n1 = min(ncols, n0 + 2048)
                    S.dma("pool", ds, wsc[name][row0 + r0:row0 + r0 + 128, n0:n1], wd[r0:r0 + 128, n0:n1], w=[wsc_b[name]])

        def load_bf(es, name, kchunks, ncols, ds):
            tile = sb(es, name, [128, kchunks, ncols], BF16)
            for kc in range(kchunks):
                S.dma("sp", ds, tile.t[:, kc, :], wsc[name][kc * 128:(kc + 1) * 128, :], r=[wsc_b[name]], w=[tile.b()])
            return tile

        with contextlib.ExitStack() as p1:
            dw1 = S.dsem()
            cm = sb(p1, "cm", [128, NCM])
            dcm = S.dsem()
            S.dma("sp", dcm, cm.t[:], cm32_d[:, :], w=[cm.b()])
            cb = cb + [cm.b()]
            win = load_w(p1, "win", win_d, 8, MIX_IN, dw1)
            lora = sb(p1, "lora", [128, 512], BF16)
            S.dma("pool", dw1, lora.t[0:64, :], wl_d[:, :], w=[lora.b()])
            S.dma("pool", dw1, lora.t[64:128, :], al_d[:, :], w=[lora.b()])
            gl = sb(p1, "gl", [128, 512], BF16)
            S.dma("pool", dw1, gl.t[:], gl_d[:, :], w=[gl.b()])
            pw = sb(p1, "pw", [128, 2, 64], BF16)
            for g in range(4):
                S.dma("pool", dw1, pw.t[64 * (g % 2):64 * (g % 2) + 64, g // 2, :], pw_d[g, :, :], w=[pw.b()])
            kT = sb(p1, "kT", [128, 2, 256], BF16)
            vtok = sb(p1, "vtok", [128, 2, 256], BF16)
            with contextlib.ExitStack() as p0:
                wkv = load_w(p0, "wkv", wkv_d, 8, 512, dw1)
                S.seal(dw1, [win.b(), lora.b(), gl.b(), pw.b(), wkv.b()])
                mems = sb(p0, "mems", [128, 2, D])
                memT = sb(p0, "memT", [128, 8, 256], BF16)
                dm = S.dsem()
                for mh in range(2):
                    S.dma("sp", dm, mems.t[:, mh, :], mem_d[mh * 128:(mh + 1) * 128, :], w=[mems.b(mh)])
                S.seal(dm, [mems.b(0), mems.b(1)])
                for mh in range(2):
                    norm_T(mems.t[:, mh, :], mems.b(mh), C_GM, memT.t, memT.b(), mh * 128)
                for fc in range(2):
                    for kc in range(8):
                        S.op("pe", lambda e, fc=fc, kc=kc: e.matmul(PB[0].t[:, 0:256], wkv.t[:, kc, fc * 128:(fc + 1) * 128],
                                                                     memT.t[:, kc, :], start=(kc == 0), stop=(kc == 7)),
                             r=[wkv.b(), memT.b()], w=[PB[0].b()])
                    S.op("act", lambda e, fc=fc: e.activation(out=kT.t[:, fc, :], in_=PB[0].t[:, 0:256], func=AF.Copy),
                         r=[PB[0].b()], w=[kT.b()])
                for mh in range(2):
                    for kc in range(8):
                        S.op("pe", lambda e, mh=mh, kc=kc: e.matmul(PB[1].t[:, 0:256], memT.t[:, kc, mh * 128:(mh + 1) * 128],
                                                                     wkv.t[:, kc, 256:512], start=(kc == 0), stop=(kc == 7)),
                             r=[wkv.b(), memT.b()], w=[PB[1].b()])
                    S.op("act", lambda e, mh=mh: e.activation(out=vtok.t[:, mh, :], in_=PB[1].t[:, 0:256], func=AF.Copy),
                         r=[PB[1].b()], w=[vtok.b()])
                S.emit()
            S.barrier()
            if stop == 0:
                S.emit()
                return nc
            dstg = S.dsem()
            stage_w(dstg, "wg", wg_d, D, 3 * D)
            stage_w(dstg, "wup", wupr_d, 512, D, row0=0)
            stage_w(dstg, "wup", wupp_d, 256, D, row0=512)
            stage_w(dstg, "wup", wupm_d, 256, D, row0=768)
            stage_w(dstg, "wo", wo_d, D, D)
            stage_w(dstg, "wfi", wfi_d, D, 2 * DFF)
            stage_w(dstg, "wfo", wfo_d, DFF, D)
            S.seal(dstg, list(wsc_b.values()))

            xs = [sb(p1, "xs%d" % i, [128, D]) for i in range(2)]
            dxs = [S.dsem() for _ in range(2)]
            hT = sb(p1, "hT", [128, 8, TT], BF16)
            dh = S.dsem()
            pm = sb(p1, "pm", [128, 14, TT], BF16)
            tmp = sb(p1, "tmp", [128, TT])
            cy = sb(p1, "cy", [128, 14])
            pp = sb(p1, "pp", [128, 2, 16 + TT])
            ppa = sb(p1, "ppa", [128, 16 + TT])
            ppb = sb(p1, "ppb", [128, 16 + TT])
            dT = sb(p1, "dT", [128, 2, TT], BF16)
            qT = sb(p1, "qT", [128, 2, TT], BF16)
            tw = sb(p1, "tw", [128, TT], BF16)
            sg = sb(p1, "sg", [128, TT], BF16)
            fn = ["ld", "cum", "cx", "E1", "E2", "E3", "a", "kk", "rs", "kkn", "x1", "kf"]
            f = {n: sb(p1, "f_" + n, [128, TT]) for n in fn}
            kk2 = sb(p1, "kk2", [128, TT], BF16)
            KR = sb(p1, "KR", [128, 4, NCH, 2, 64], BF16)
            BK = sb(p1, "BK", [128, 4, NCH, 2, 64], BF16)
            WC = sb(p1, "WC", [128, 4, NCH])
            bonT = sb(p1, "bonT", [128, 4, TT], BF16)
            gT = sb(p1, "gT", [128, 4, TT], BF16)
            Asb = [sb(p1, "Asb%d" % i, [128, 4, 2, 128], BF16) for i in range(2)]
            GF = [sb(p1, "GF%d" % i, [128, 4, 64], BF16) for i in range(2)]
            Lsb = sb(p1, "Lsb", [128, 4, 64], BF16)
            GT = [sb(p1, "GT%d" % i, [128, 4, 64], BF16) for i in range(2)]
            P2 = [sb(p1, "P2%d" % i, [128, 4, 64], BF16) for i in range(2)]
            P2T = [sb(p1, "P2T%d" % i, [128, 4, 64], BF16) for i in range(2)]
            Zsb = sb(p1, "Zsb", [128, 4, 64], BF16)
            Un = sb(p1, "Un", [128, 4, 64], BF16)
            VBK = [sb(p1, "VBK%d" % i, [128, 3, 4, 64], BF16) for i in range(2)]
            Ysb = [sb(p1, "Ysb%d" % i, [128, 4, 64]) for i in range(2)]
            Ysq = [sb(p1, "Ysq%d" % i, [128, 4, 64]) for i in range(2)]
            yh = [sb(p1, "yh%d" % i, [128, 4, 64], BF16) for i in range(2)]
            yst = [sb(p1, "yst%d" % i, [128, 4, 4]) for i in range(2)]
            eps_gn = sb(p1, "eps_gn", [128, 1])
            Sf = sb(p1, "Sf", [128, 4, 64])
            Sbf = sb(p1, "Sbf", [128, 4, 64], BF16)
            yhT = sb(p1, "yhT", [128, 4, TT], BF16)
            yT = sb(p1, "yT", [128, 8, TT], BF16)
            dy = S.dsem()
            eT = sb(p1, "eT", [128, 2, TT], BF16)
            rden = sb(p1, "rden", [128, TT])

            S.op("dve", lambda e: e.memset(cy.t[:], 0.0), w=[cy.b()])
            S.op("dve", lambda e: e.memset(pp.t[:], 0.0), w=[pp.b()])
            S.op("dve", lambda e: e.memset(ppa.t[:], 0.0), w=[ppa.b(0), ppa.b(64)])
            S.op("dve", lambda e: e.memset(ppb.t[:], 0.0), w=[ppb.b(0), ppb.b(64)])
            S.op("dve", lambda e: e.memset(Sf.t[:], 0.0), w=[Sf.b()])
            S.op("dve", lambda e: e.memset(Sbf.t[:], 0.0), w=[Sbf.b()])
            S.op("dve", lambda e: e.memset(eps_gn.t[:], 64e-5), w=[eps_gn.b()])
            M2v = cm.t[:, M_M2:M_M2 + 128].unsqueeze(1).broadcast_to([128, 4, 128])
            MLv = cm.t[:, M_ML:M_ML + 64].unsqueeze(1).broadcast_to([128, 4, 64])
            I64v = cm.t[:, M_I64:M_I64 + 64].unsqueeze(1).broadcast_to([128, 4, 64])
            scanm = cm.t[:, M_SCAN:M_SCAN + 512]
            mmb = 0

            for st in range(NST if stop >= 2 else 1):
                t0 = st * TT
                for sub in range(4):
                    i = (st * 4 + sub) % 2
                    S.dma("sp", dxs[i], xs[i].t[:], x_d[t0 + sub * 128:t0 + (sub + 1) * 128, :], w=[xs[i].b()])
                    norm_T(xs[i].t[:], xs[i].b(), C_G1, hT.t, hT.b(), sub * 128)
                S.dma("sp", dh, hT_d[:, :, t0:t0 + TT], hT.t[:], r=[hT.b()], w=[hT_db[st]])
                if stop == 1.1:
                    break
                for oc in range(18):
                    pbk = PB[mmb % 2]
                    mmb += 1
                    for kc in range(8):
                        S.op("pe", lambda e, oc=oc, kc=kc, pbk=pbk: e.matmul(pbk.t[:], win.t[:, kc, oc * 128:(oc + 1) * 128], hT.t[:, kc, :],
                                                                            start=(kc == 0), stop=(kc == 7)),
                             r=[win.b(), hT.b()], w=[pbk.b()])
                    ps = pbk.t
                    if oc < 14:
                        mu = cvec.t[:, C_MU + oc:C_MU + oc + 1]
                        om = cder.t[:, oc:oc + 1]
                        S.op("act", lambda e, ps=ps, mu=mu: e.activation(out=tmp.t[:, 1:TT], in_=ps[:, 0:TT - 1], func=AF.Copy, scale=mu),
                             r=[pbk.b()] + cb, w=[tmp.b()])
                        S.op("dve", lambda e, oc=oc, mu=mu: e.tensor_scalar(out=tmp.t[:, 0:1], in0=cy.t[:, oc:oc + 1], scalar1=mu, scalar2=None,
                                                                           op0=ALU.mult), r=[cy.b()] + cb, w=[tmp.b()])
                        S.op("dve", lambda e, oc=oc, ps=ps: e.tensor_copy(out=cy.t[:, oc:oc + 1], in_=ps[:, TT - 1:TT]), r=[pbk.b()], w=[cy.b()])
                        S.op("dve", lambda e, oc=oc, ps=ps, om=om: e.scalar_tensor_tensor(out=pm.t[:, oc, :], in0=ps[:, :], scalar=om, in1=tmp.t[:, :],
                                                                                          op0=ALU.mult, op1=ALU.add),
                             r=[pbk.b(), tmp.b()] + cb, w=[pm.b(oc)])
                    elif oc < 16:
                        S.op("act", lambda e, oc=oc, ps=ps: e.activation(out=pp.t[:, oc - 14, 16:16 + TT], in_=ps[:, :], func=AF.Copy),
                             r=[pbk.b()], w=[pp.b()])
                    else:
                        S.op("act", lambda e, oc=oc, ps=ps: e.activation(out=qT.t[:, oc - 16, :], in_=ps[:, :], func=AF.Copy),
                             r=[pbk.b()], w=[qT.b()])
                if stop == 1.2:
                    break
                invc = cm.t[:, (M_INVC0 if st == 0 else M_INVC):(M_INVC0 if st == 0 else M_INVC) + 1024].rearrange("p (c t) -> p c t", c=2)
                for g in range(4):
                    ci, pb = g // 2, 64 * (g % 2)
                    src, bsrc = pp.t[pb:pb + 64, ci, :], pp.b()
                    for lv in range(g + 1):
                        sh = 1 << lv
                        dst = ppa if lv % 2 == 0 else ppb
                        S.op("dve", lambda e, src=src, dst=dst, sh=sh, pb=pb: e.tensor_tensor(out=dst.t[pb:pb + 64, sh:16 + TT], in0=src[:, sh:16 + TT],
                                                                                            in1=src[:, 0:16 + TT - sh], op=ALU.add),
                             r=[bsrc], w=[dst.b(pb)])
                        if sh > 1:
                            pass
                        src, bsrc = dst.t[pb:pb + 64, :], dst.b(pb)
                    S.op("dve", lambda e, src=src, pb=pb, ci=ci: e.tensor_tensor(out=ppa.t[pb:pb + 64, 16:16 + TT] if False else tmp.t[pb:pb + 64, :],
                                                                                 in0=src[:, 16:16 + TT], in1=invc[pb:pb + 64, ci, :], op=ALU.mult),
                         r=[bsrc] + cb, w=[tmp.b()])
                    S.op("dve", lambda e, pb=pb, ci=ci: e.tensor_tensor(out=dT.t[pb:pb + 64, ci, :], in0=tmp.t[pb:pb + 64, :],
                                                                        in1=pp.t[pb:pb + 64, ci, 16:16 + TT], op=ALU.subtract),
                         r=[tmp.b(), pp.b()], w=[dT.b()])
                for ci in range(2):
                    pbk = PB[mmb % 2]
                    mmb += 1
                    for g2 in range(2):
                        pb = 64 * g2
                        S.op("pe", lambda e, ci=ci, pb=pb, pbk=pbk: e.matmul(pbk.t[pb:pb + 64, :], pw.t[pb:pb + 64, ci, :], dT.t[pb:pb + 64, ci, :],
                                                                            start=True, stop=True), r=[pw.b(), dT.b()], w=[pbk.b()])
                    S.op("act", lambda e, ci=ci, pbk=pbk: e.activation(out=yT.t[:, 4 + ci, :], in_=pbk.t[:, :], func=AF.Copy,
                                                                      scale=cvec.t[:, C_PS + ci:C_PS + ci + 1]), r=[pbk.b()] + cb, w=[yT.b(4 + ci)])
                S.op("dve", lambda e: e.tensor_copy(out=pp.t[:, :, 0:16], in_=pp.t[:, :, TT:TT + 16]), r=[pp.b()], w=[pp.b()])
                if stop == 1.3:
                    break
                for jm in range(2):
                    for hh in range(2):
                        pb = 64 * hh
                        hm = 2 * jm + hh
                        for mh in range(2):
                            S.op("pe", lambda e, jm=jm, pb=pb, mh=mh: e.matmul(PB[2 + mh].t[:, :], kT.t[pb:pb + 64, jm, mh * 128:(mh + 1) * 128],
                                                                               qT.t[pb:pb + 64, jm, :], start=True, stop=True),
                                 r=[kT.b(), qT.b()], w=[PB[2 + mh].b()])
                            S.op("act", lambda e, mh=mh: e.activation(out=eT.t[:, mh, :], in_=PB[2 + mh].t[:, :], func=AF.Exp, scale=0.125),
                                 r=[PB[2 + mh].b()], w=[eT.b(mh)])
                        for mh in range(2):
                            S.op("pe", lambda e, hm=hm, pb=pb, mh=mh: e.matmul(PB[4].t[pb:pb + 64, :], vtok.t[:, mh, hm * 64:(hm + 1) * 64], eT.t[:, mh, :],
                                                                               start=(mh == 0), stop=(mh == 1)),
                                 r=[vtok.b(), eT.b(mh)], w=[PB[4].b()])
                        for mh in range(2):
                            S.op("pe", lambda e, pb=pb, mh=mh: e.matmul(PB[5].t[pb:pb + 64, :], ones64, eT.t[:, mh, :],
                                                                        start=(mh == 0), stop=(mh == 1)),
                                 r=[ident.b(), eT.b(mh)], w=[PB[5].b()])
                    S.op("dve", lambda e: e.reciprocal(out=rden.t[:], in_=PB[5].t[:, :]), r=[PB[5].b()], w=[rden.b()])
                    S.op("dve", lambda e, jm=jm: e.tensor_tensor(out=yT.t[:, 6 + jm, :], in0=PB[4].t[:, :], in1=rden.t[:], op=ALU.mult),
                         r=[PB[4].b(), rden.b()], w=[yT.b(6 + jm)])
                if stop == 1.4:
                    break
                S.op("act", lambda e: e.activation(out=tw.t[0:64, :], in_=pm.t[0:64, 12, :], func=AF.Tanh), r=[pm.b(12)], w=[tw.b()])
                S.op("act", lambda e: e.activation(out=sg.t[:], in_=pm.t[:, 13, :], func=AF.Sigmoid), r=[pm.b(13)], w=[sg.b()])
                for j in range(4):
                    cs = slice(j * 128, (j + 1) * 128)
                    r_, k_, v_ = pm.t[:, j, :], pm.t[:, 4 + j, :], pm.t[:, 8 + j, :]
                    cv = lambda c0: cvec.t[:, c0 + j:c0 + j + 1]
                    pbk = PB[mmb % 2]
                    mmb += 1
                    S.op("pe", lambda e, pbk=pbk, cs=cs: e.matmul(pbk.t[:], lora.t[0:64, cs], tw.t[0:64, :], start=True, stop=True),
                         r=[lora.b(), tw.b()], w=[pbk.b()])
                    S.op("act", lambda e, pbk=pbk, cv=cv: e.activation(out=f["ld"].t[:], in_=pbk.t[:], func=AF.Sigmoid, bias=cv(C_W0)),
                         r=[pbk.b()] + cb, w=[f["ld"].b()])
                    S.op("dve", lambda e: e.tensor_scalar(out=f["ld"].t[:], in0=f["ld"].t[:], scalar1=NEG_EH, scalar2=None, op0=ALU.mult),
                         r=[f["ld"].b()], w=[f["ld"].b()])
                    S.op("dve", lambda e: e.tensor_tensor_scan(out=f["cum"].t[:], data0=scanm, data1=f["ld"].t[:], initial=0.0,
                                                               op0=ALU.mult, op1=ALU.add), r=[f["ld"].b()] + cb, w=[f["cum"].b()])
                    S.op("dve", lambda e: e.tensor_tensor(out=f["cx"].t[:], in0=f["cum"].t[:], in1=f["ld"].t[:], op=ALU.subtract),
                         r=[f["cum"].b(), f["ld"].b()], w=[f["cx"].b()])
                    S.op("act", lambda e: e.activation(out=f["E1"].t[:], in_=f["cum"].t[:], func=AF.Exp), r=[f["cum"].b()], w=[f["E1"].b()])
                    S.op("act", lambda e: e.activation(out=f["E2"].t[:], in_=f["cum"].t[:], func=AF.Exp, scale=-1.0), r=[f["cum"].b()], w=[f["E2"].b()])
                    S.op("act", lambda e: e.activation(out=f["E3"].t[:], in_=f["cx"].t[:], func=AF.Exp), r=[f["cx"].b()], w=[f["E3"].b()])
                    S.op("dve", lambda e, j=j: e.tensor_copy(out=WC.t[:, j, :], in_=f["E1"].t[:, :].rearrange("p (c t) -> p c t", t=64)[:, :, 63]),
                         r=[f["E1"].b()], w=[WC.b()])
                    pbk = PB[mmb % 2]
                    mmb += 1
                    S.op("pe", lambda e, pbk=pbk, cs=cs: e.matmul(pbk.t[:], lora.t[64:128, cs], pm.t[64:128, 12, :], start=True, stop=True),
                         r=[lora.b(), pm.b(12)], w=[pbk.b()])
                    S.op("act", lambda e, pbk=pbk, cv=cv: e.activation(out=f["a"].t[:], in_=pbk.t[:], func=AF.Sigmoid, bias=cv(C_A0)),
                         r=[pbk.b()] + cb, w=[f["a"].b()])
                    S.op("dve", lambda e, k_=k_, cv=cv: e.tensor_scalar(out=f["kk"].t[:], in0=k_, scalar1=cv(C_KK), scalar2=None, op0=ALU.mult),
                         r=[pm.b(4 + j)] + cb, w=[f["kk"].b()])
                    S.op("dve", lambda e: e.tensor_tensor(out=kk2.t[:], in0=f["kk"].t[:], in1=f["kk"].t[:], op=ALU.mult),
                         r=[f["kk"].b()], w=[kk2.b()])
                    pbk = PB[mmb % 2]
                    mmb += 1
                    S.op("pe", lambda e, pbk=pbk: e.matmul(pbk.t[:], bdo, kk2.t[:], start=True, stop=True), r=[ident.b(), kk2.b()], w=[pbk.b()])
                    S.op("act", lambda e, pbk=pbk: e.activation(out=f["rs"].t[:], in_=pbk.t[:], func=AF.Sqrt, bias=1e-12), r=[pbk.b()], w=[f["rs"].b()])
                    S.op("dve", lambda e: e.reciprocal(out=f["rs"].t[:], in_=f["rs"].t[:]), r=[f["rs"].b()], w=[f["rs"].b()])
                    S.op("dve", lambda e: e.tensor_tensor(out=f["kkn"].t[:], in0=f["kk"].t[:], in1=f["rs"].t[:], op=ALU.mult),
                         r=[f["kk"].b(), f["rs"].b()], w=[f["kkn"].b()])
                    c3 = lambda tl: tl.t[:, :].rearrange("p (c t) -> p c t", t=64)
                    S.op("dve", lambda e, j=j: e.tensor_tensor(out=KR.t[:, j, :, 0, :], in0=c3(f["kkn"]), in1=c3(f["E3"]), op=ALU.mult),
                         r=[f["kkn"].b(), f["E3"].b()], w=[KR.b(j)])
                    S.op("dve", lambda e, j=j, r_=r_: e.tensor_tensor(out=KR.t[:, j, :, 1, :], in0=r_.rearrange("p (c t) -> p c t", t=64), in1=c3(f["E1"]),
                                                                     op=ALU.mult), r=[pm.b(j), f["E1"].b()], w=[KR.b(j)])
                    S.op("dve", lambda e: e.tensor_tensor(out=f["x1"].t[:], in0=f["kkn"].t[:], in1=f["a"].t[:], op=ALU.mult),
                         r=[f["kkn"].b(), f["a"].b()], w=[f["x1"].b()])
                    S.op("dve", lambda e, j=j: e.tensor_tensor(out=BK.t[:, j, :, 0, :], in0=c3(f["x1"]), in1=c3(f["E2"]), op=ALU.mult),
                         r=[f["x1"].b(), f["E2"].b()], w=[BK.b(j)])
                    S.op("dve", lambda e, j=j, cv=cv: e.tensor_scalar(out=f["x1"].t[:], in0=f["a"].t[:], scalar1=cv(C_KA), scalar2=cder.t[:, 14 + j:15 + j],
                                                                     op0=ALU.mult, op1=ALU.add), r=[f["a"].b()] + cb, w=[f["x1"].b()])
                    S.op("dve", lambda e, k_=k_: e.tensor_tensor(out=f["kf"].t[:], in0=f["x1"].t[:], in1=k_, op=ALU.mult),
                         r=[f["x1"].b(), pm.b(4 + j)], w=[f["kf"].b()])
                    S.op("dve", lambda e, j=j: e.tensor_tensor(out=BK.t[:, j, :, 1, :], in0=c3(f["kf"]), in1=c3(f["E2"]), op=ALU.mult),
                         r=[f["kf"].b(), f["E2"].b()], w=[BK.b(j)])
                    S.op("dve", lambda e, r_=r_: e.tensor_tensor(out=f["x1"].t[:], in0=f["kf"].t[:], in1=r_, op=ALU.mult),
                         r=[f["kf"].b(), pm.b(j)], w=[f["x1"].b()])
                    S.op("dve", lambda e, cv=cv: e.tensor_scalar(out=kk2.t[:], in0=f["x1"].t[:], scalar1=cv(C_RK), scalar2=None, op0=ALU.mult),
                         r=[f["x1"].b()] + cb, w=[kk2.b()])
                    pbk = PB[mmb % 2]
                    mmb += 1
                    S.op("pe", lambda e, pbk=pbk: e.matmul(pbk.t[:], bdo, kk2.t[:], start=True, stop=True), r=[ident.b(), kk2.b()], w=[pbk.b()])
                    S.op("dve", lambda e, pbk=pbk, j=j, v_=v_: e.tensor_tensor(out=bonT.t[:, j, :], in0=pbk.t[:], in1=v_, op=ALU.mult),
                         r=[pbk.b(), pm.b(8 + j)], w=[bonT.b(j)])
                    pbk = PB[mmb % 2]
                    mmb += 1
                    S.op("pe", lambda e, pbk=pbk, cs=cs: e.matmul(pbk.t[:], gl.t[:, cs], sg.t[:], start=True, stop=True), r=[gl.b(), sg.b()], w=[pbk.b()])
                    S.op("act", lambda e, pbk=pbk, j=j: e.activation(out=gT.t[:, j, :], in_=pbk.t[:], func=AF.Copy), r=[pbk.b()], w=[gT.b(j)])
                if stop == 1.5:
                    break
                krb = [KR.b(j) for j in range(4)]
                bkb = [BK.b(j) for j in range(4)]
                HP = [(h // 2, 64 * (h % 2)) for h in range(8)]
                v3 = lambda ap_, w=64: ap_.rearrange("p (j c) -> p j c", j=4)
                lo = lambda bank, w=64: v3(bank.t[:, 0:4 * w], w)
                hi = lambda bank: v3(bank.t[:, 256:512])

                def stageA(c):
                    par = c % 2
                    cc = slice(c * 64, (c + 1) * 64)
                    vbk, asb, gf = VBK[par], Asb[par], GF[par]
                    for j, pb in HP:
                        ps_ = slice(pb, pb + 64)
                        S.op("pe", lambda e: e.transpose(PT.t[ps_, j * 64:(j + 1) * 64], pm.t[ps_, 8 + j, cc], idn[ps_, ps_]),
                             r=[pm.b(8 + j), ident.b()], w=[PT.b("A")])
                        S.op("pe", lambda e: e.transpose(PT.t[ps_, 256 + j * 64:256 + (j + 1) * 64], BK.t[ps_, j, c, 0, :], idn[ps_, ps_]),
                             r=[bkb[j], ident.b()], w=[PT.b("A")])
                        S.op("pe", lambda e: e.transpose(PT.t[ps_, 512 + j * 64:512 + (j + 1) * 64], BK.t[ps_, j, c, 1, :], idn[ps_, ps_]),
                             r=[bkb[j], ident.b()], w=[PT.b("A")])
                    for j, pb in HP:
                        ps_ = slice(pb, pb + 64)
                        for kind in range(2):
                            S.op("pe", lambda e: e.matmul(PB[2 + kind].t[ps_, j * 128:(j + 1) * 128], BK.t[ps_, j, c, kind, :],
                                                          KR.t[ps_, j, c, :, :].rearrange("p a b -> p (a b)"), start=True, stop=True),
                                 r=[bkb[j], krb[j]], w=[PB[2 + kind].b()])
                        S.op("pe", lambda e: e.matmul(PB[4].t[ps_, j * 64:(j + 1) * 64], KR.t[ps_, j, c, 0, :], BK.t[ps_, j, c, 0, :],
                                                      start=True, stop=True), r=[bkb[j], krb[j]], w=[PB[4].b()])
                    yield
                    S.op("act", lambda e: e.activation(out=vbk.t[:], in_=PT.t[:, 0:768].rearrange("p (a j c) -> p a j c", a=3, j=4), func=AF.Copy),
                         r=[PT.b("A")], w=[vbk.b()])
                    S.op("dve", lambda e: e.tensor_tensor(out=asb.t[:, :, 0, :], in0=lo(PB[2], 128), in1=M2v, op=ALU.mult),
                         r=[PB[2].b()] + cb, w=[asb.b()])
                    S.op("dve", lambda e: e.tensor_tensor(out=Lsb.t[:], in0=lo(PB[4]), in1=MLv, op=ALU.mult), r=[PB[4].b()] + cb, w=[Lsb.b()])
                    S.op("dve", lambda e: e.scalar_tensor_tensor(out=GT[0].t[:], in0=asb.t[:, :, 0, 0:64], scalar=-1.0, in1=I64v,
                                                                 op0=ALU.mult, op1=ALU.add), r=[asb.b()] + cb, w=[GT[0].b()])
                    S.op("dve", lambda e: e.tensor_tensor(out=asb.t[:, :, 1, :], in0=lo(PB[3], 128), in1=M2v, op=ALU.mult),
                         r=[PB[3].b()] + cb, w=[asb.b()])
                    yield
                    Pc, PTc, bP, bPT = Lsb.t, asb.t[:, :, 0, 0:64], Lsb.b(), asb.b()
                    gi = 0
                    pend = None
                    for lv in range(6):
                        if lv < 5:
                            for j, pb in HP:
                                ps_ = slice(pb, pb + 64)
                                S.op("pe", lambda e: e.matmul(PB[4].t[ps_, j * 64:(j + 1) * 64], PTc[ps_, j, :], Pc[ps_, j, :], start=True, stop=True),
                                     r=[bP, bPT], w=[PB[4].b()])
                            if lv < 4:
                                for j, pb in HP:
                                    ps_ = slice(pb, pb + 64)
                                    S.op("pe", lambda e: e.matmul(PB[5].t[ps_, j * 64:(j + 1) * 64], Pc[ps_, j, :], PTc[ps_, j, :], start=True, stop=True),
                                         r=[bP, bPT], w=[PB[5].b()])
                        if pend is not None:
                            pn2, plv = pend
                            gsrc = GT[gi]
                            gdst = gf if plv == 4 else GT[1 - gi]
                            for j, pb in HP:
                                ps_ = slice(pb, pb + 64)
                                S.op("pe", lambda e: e.matmul(PB[6].t[ps_, j * 64:(j + 1) * 64], pn2.t[ps_, j, :], gsrc.t[ps_, j, :], start=True, stop=True),
                                     r=[pn2.b(), gsrc.b()], w=[PB[6].b()])
                        yield
                        if lv < 5:
                            n2 = P2[lv % 2]
                            S.op("act", lambda e: e.activation(out=n2.t[:], in_=lo(PB[4]), func=AF.Copy), r=[PB[4].b()], w=[n2.b()])
                            if lv < 4:
                                n2t = P2T[lv % 2]
                                S.op("act", lambda e: e.activation(out=n2t.t[:], in_=lo(PB[5]), func=AF.Copy), r=[PB[5].b()], w=[n2t.b()])
                        if pend is not None:
                            S.op("dve", lambda e: e.tensor_tensor(out=gdst.t[:], in0=lo(PB[6]), in1=gsrc.t[:], op=ALU.add),
                                 r=[PB[6].b(), gsrc.b()], w=[gdst.b()])
                            gi = 1 - gi
                            pend = None
                        if lv < 5:
                            pend = (n2, lv)
                            if lv < 4:
                                Pc, PTc, bP, bPT = n2.t, n2t.t, n2.b(), n2t.b()
                            yield

                def stageB1(c):
                    par = c % 2
                    vbk, asb, G = VBK[par], Asb[par], GF[par]
                    ysb, ysq = Ysb[par], Ysq[par]
                    Vt, Bt, Kt = vbk.t[:, 0], vbk.t[:, 1], vbk.t[:, 2]
                    b0, b1 = PB[0].b(), PB[1].b()
                    for j, pb in HP:
                        ps_ = slice(pb, pb + 64)
                        S.op("pe", lambda e: e.matmul(PB[0].t[ps_, j * 64:(j + 1) * 64], KR.t[ps_, j, c, 0, :], Sbf.t[ps_, j, :], start=True, stop=False),
                             r=[krb[j], Sbf.b()], w=[b0])
                        S.op("pe", lambda e: e.matmul(PB[0].t[ps_, j * 64:(j + 1) * 64], asb.t[ps_, j, 1, 0:64], Vt[ps_, j, :], start=False, stop=True),
                             r=[asb.b(), vbk.b()], w=[b0])
                    yield
                    S.op("act", lambda e: e.activation(out=Zsb.t[:], in_=lo(PB[0]), func=AF.Copy), r=[b0], w=[Zsb.b()])
                    yield
                    for j, pb in HP:
                        ps_ = slice(pb, pb + 64)
                        S.op("pe", lambda e: e.matmul(PB[0].t[ps_, j * 64:(j + 1) * 64], G.t[ps_, j, :], Zsb.t[ps_, j, :], start=True, stop=True),
                             r=[G.b(), Zsb.b()], w=[b0])
                    yield
                    S.op("act", lambda e: e.activation(out=Un.t[:], in_=lo(PB[0]), func=AF.Copy, scale=-1.0), r=[b0], w=[Un.b()])
                    yield
                    for j, pb in HP:
                        ps_ = slice(pb, pb + 64)
                        S.op("pe", lambda e: e.matmul(PB[0].t[ps_, j * 64:(j + 1) * 64], Kt[ps_, j, :], Vt[ps_, j, :], start=True, stop=False),
                             r=[vbk.b()], w=[b0])
                        S.op("pe", lambda e: e.matmul(PB[0].t[ps_, j * 64:(j + 1) * 64], Bt[ps_, j, :], Un.t[ps_, j, :], start=False, stop=True),
                             r=[vbk.b(), Un.b()], w=[b0])
                    for j, pb in HP:
                        ps_ = slice(pb, pb + 64)
                        S.op("pe", lambda e: e.matmul(PB[1].t[ps_, j * 64:(j + 1) * 64], KR.t[ps_, j, c, 1, :], Sbf.t[ps_, j, :], start=True, stop=False),
                             r=[krb[j], Sbf.b()], w=[b1])
                        S.op("pe", lambda e: e.matmul(PB[1].t[ps_, j * 64:(j + 1) * 64], asb.t[ps_, j, 1, 64:128], Vt[ps_, j, :], start=False, stop=False),
                             r=[asb.b(), vbk.b()], w=[b1])
                        S.op("pe", lambda e: e.matmul(PB[1].t[ps_, j * 64:(j + 1) * 64], asb.t[ps_, j, 0, 64:128], Un.t[ps_, j, :], start=False, stop=True),
                             r=[asb.b(), Un.b()], w=[b1])
                    yield
                    S.op("dve", lambda e: e.tensor_tensor(out=Sf.t[:], in0=lo(PB[0]), in1=Sf.t[:], op=ALU.add), r=[b0, Sf.b()], w=[Sf.b()])
                    S.op("act", lambda e: e.activation(out=ysb.t[:], in_=lo(PB[1]), func=AF.Copy), r=[b1], w=[ysb.b()])
                    S.op("act", lambda e: e.activation(out=ysq.t[:], in_=lo(PB[1]), func=AF.Square), r=[b1], w=[ysq.b()])
                    yield
                    S.op("dve", lambda e: e.tensor_tensor(out=Sf.t[:], in0=Sf.t[:], in1=WC.t[:, :, c].unsqueeze(2).broadcast_to([128, 4, 64]), op=ALU.mult),
                         r=[Sf.b(), WC.b()], w=[Sf.b()])
                    yield
                    S.op("act", lambda e: e.activation(out=Sbf.t[:], in_=Sf.t[:], func=AF.Copy), r=[Sf.b()], w=[Sbf.b()])

                def stageB2(c):
                    par = c % 2
                    cc = slice(c * 64, (c + 1) * 64)
                    ysb, ysq, ys, yh_ = Ysb[par], Ysq[par], yst[par], yh[par]
                    S.op("dve", lambda e: e.tensor_reduce(out=ys.t[:, 0, :], in_=ysb.t[:], axis=AX.X, op=ALU.add), r=[ysb.b()], w=[ys.b()])
                    S.op("dve", lambda e: e.tensor_reduce(out=ys.t[:, 1, :], in_=ysq.t[:], axis=AX.X, op=ALU.add), r=[ysq.b()], w=[ys.b()])
                    yield
                    S.op("dve", lambda e: e.tensor_scalar(out=ys.t[:, 0, :], in0=ys.t[:, 0, :], scalar1=1.0 / 64, scalar2=None, op0=ALU.mult),
                         r=[ys.b()], w=[ys.b()])
                    yield
                    S.op("dve", lambda e: e.tensor_tensor(out=ys.t[:, 2, :], in0=ys.t[:, 0, :], in1=ys.t[:, 0, :], op=ALU.mult), r=[ys.b()], w=[ys.b()])
                    yield
                    S.op("dve", lambda e: e.scalar_tensor_tensor(out=ys.t[:, 3, :], in0=ys.t[:, 1, :], scalar=1.0 / 64, in1=ys.t[:, 2, :],
                                                                 op0=ALU.mult, op1=ALU.subtract), r=[ys.b()], w=[ys.b()])
                    yield
                    S.op("act", lambda e: e.activation(out=ys.t[:, 3, :], in_=ys.t[:, 3, :], func=AF.Sqrt, bias=eps_gn.t[:, 0:1]), r=[ys.b(), eps_gn.b()], w=[ys.b()])
                    yield
                    S.op("dve", lambda e: e.reciprocal(out=ys.t[:, 3, :], in_=ys.t[:, 3, :]), r=[ys.b()], w=[ys.b()])
                    S.op("dve", lambda e: e.tensor_tensor(out=ysb.t[:], in0=ysb.t[:], in1=ys.t[:, 0, :].unsqueeze(2).broadcast_to([128, 4, 64]), op=ALU.subtract),
                         r=[ysb.b(), ys.b()], w=[ysb.b()])
                    yield
                    S.op("dve", lambda e: e.tensor_tensor(out=yh_.t[:], in0=ysb.t[:], in1=ys.t[:, 3, :].unsqueeze(2).broadcast_to([128, 4, 64]), op=ALU.mult),
                         r=[ysb.b(), ys.b()], w=[yh_.b()])
                    yield
                    for j, pb in HP:
                        ps_ = slice(pb, pb + 64)
                        S.op("pe", lambda e: e.transpose(PT.t[ps_, 768 + j * 64:768 + (j + 1) * 64], yh_.t[ps_, j, :], idn[ps_, ps_]),
                             r=[yh_.b(), ident.b()], w=[PT.b("A")])
                    yield
                    S.op("act", lambda e: e.activation(out=yhT.t[:, :, cc], in_=PT.t[:, 768:1024].rearrange("p (j t) -> p j t", j=4), func=AF.Copy),
                         r=[PT.b("A")], w=[yhT.b()])

                for c in range(NCH + 2):
                    gens = []
                    if 1 <= c <= NCH:
                        gens.append(stageB1(c - 1))
                    if c < NCH:
                        gens.append(stageA(c))
                    if 2 <= c:
                        gens.append(stageB2(c - 2))
                    while gens:
                        for g_ in list(gens):
                            try:
                                next(g_)
                            except StopIteration:
                                gens.remove(g_)
                if stop == 1.6:
                    break
                for j in range(4):
                    S.op("dve", lambda e, j=j: e.tensor_scalar(out=f["x1"].t[:], in0=yhT.t[:, j, :], scalar1=cvec.t[:, C_LNW + j:C_LNW + j + 1],
                                                               scalar2=cvec.t[:, C_LNB + j:C_LNB + j + 1], op0=ALU.mult, op1=ALU.add),
                         r=[yhT.b()] + cb, w=[f["x1"].b()])
                    S.op("dve", lambda e, j=j: e.tensor_tensor(out=f["x1"].t[:], in0=f["x1"].t[:], in1=bonT.t[:, j, :], op=ALU.add),
                         r=[f["x1"].b(), bonT.b(j)], w=[f["x1"].b()])
                    S.op("dve", lambda e, j=j: e.tensor_tensor(out=yT.t[:, j, :], in0=f["x1"].t[:], in1=gT.t[:, j, :], op=ALU.mult),
                         r=[f["x1"].b(), gT.b(j)], w=[yT.b(j)])
                S.dma("sp", dy, yT_d[:, :, t0:t0 + TT], yT.t[:], r=[yT.b(jj) for jj in range(8)], w=[yT_db[st]])
            S.emit()
            if stop <= 2:
                S.barrier()
                S.emit()
                return nc

        rot = [0]

        def nb():
            rot[0] += 1
            return PB[rot[0] % 7]

        with contextlib.ExitStack() as p2:
            S.barrier()
            dw2 = S.dsem()
            wg = load_bf(p2, "wg", 8, 3 * D, dw2)
            wup = load_bf(p2, "wup", 8, D, dw2)
            wo = load_bf(p2, "wo", 8, D, dw2)
            S.seal(dw2, [wg.b(), wup.b(), wo.b()])
            hT2 = [sb(p2, "hT2%d" % i, [128, 8, TT], BF16) for i in range(2)]
            yT2 = [sb(p2, "yT2%d" % i, [128, 8, TT], BF16) for i in range(2)]
            dl2 = [S.dsem() for _ in range(2)]
            gs = [sb(p2, "gs%d" % i, [128, TT]) for i in range(3)]
            mm_ = [sb(p2, "mm%d" % i, [128, TT]) for i in range(3)]
            mg = sb(p2, "mg", [128, 8, TT], BF16)
            xs2 = [sb(p2, "xs2%d" % i, [128, D]) for i in range(2)]
            dx2 = [S.dsem() for _ in range(2)]
            x1s = [sb(p2, "x1s%d" % i, [128, D]) for i in range(2)]
            ds2 = [S.dsem() for _ in range(2)]
            kr = [(0, 4), (4, 6), (6, 8)]
            for st in range(NST):
                t0 = st * TT
                i2 = st % 2
                S.dma("sp", dl2[i2], hT2[i2].t[:], hT_d[:, :, t0:t0 + TT], r=[hT_db[st]], w=[hT2[i2].b()])
                S.dma("sp", dl2[i2], yT2[i2].t[:], yT_d[:, :, t0:t0 + TT], r=[yT_db[st]], w=[yT2[i2].b()])
                S.seal(dl2[i2], [hT2[i2].b(), yT2[i2].b()])
                for fo in range(8):
                    fs = slice(fo * 128, (fo + 1) * 128)
                    for b in range(3):
                        pg, pu = nb(), nb()
                        for kc in range(8):
                            S.op("pe", lambda e: e.matmul(pg.t[:], wg.t[:, kc, b * D + fo * 128:b * D + (fo + 1) * 128], hT2[i2].t[:, kc, :],
                                                          start=(kc == 0), stop=(kc == 7)), r=[wg.b(), hT2[i2].b()], w=[pg.b()])
                        k0, k1 = kr[b]
                        for kc in range(k0, k1):
                            S.op("pe", lambda e: e.matmul(pu.t[:], wup.t[:, kc, fs], yT2[i2].t[:, kc, :], start=(kc == k0), stop=(kc == k1 - 1)),
                                 r=[wup.b(), yT2[i2].b()], w=[pu.b()])
                        S.op("act", lambda e: e.activation(out=gs[b].t[:], in_=pg.t[:], func=AF.Sigmoid,
                                                           bias=cvec.t[:, C_BG + b * 8 + fo:C_BG + b * 8 + fo + 1]), r=[pg.b()] + cb, w=[gs[b].b()])
                        S.op("dve", lambda e: e.tensor_tensor(out=mm_[b].t[:], in0=pu.t[:], in1=gs[b].t[:], op=ALU.mult),
                             r=[pu.b(), gs[b].b()], w=[mm_[b].b()])
                    S.op("pool", lambda e: e.tensor_tensor(out=mm_[0].t[:], in0=mm_[0].t[:], in1=mm_[1].t[:], op=ALU.add),
                         r=[mm_[0].b(), mm_[1].b()], w=[mm_[0].b()])
                    S.op("pool", lambda e: e.tensor_tensor(out=mg.t[:, fo, :], in0=mm_[0].t[:], in1=mm_[2].t[:], op=ALU.add),
                         r=[mm_[0].b(), mm_[2].b()], w=[mg.b()])
                for sub in range(4):
                    i = (st * 4 + sub) % 2
                    r0 = t0 + sub * 128
                    S.dma("sp", dx2[i], xs2[i].t[:], x_d[r0:r0 + 128, :], w=[xs2[i].b()])
                    for half in range(2):
                        pbk = nb()
                        for kc in range(8):
                            S.op("pe", lambda e: e.matmul(pbk.t[:], mg.t[:, kc, sub * 128:(sub + 1) * 128], wo.t[:, kc, half * 512:(half + 1) * 512],
                                                          start=(kc == 0), stop=(kc == 7)), r=[mg.b(), wo.b()], w=[pbk.b()])
                        S.op("dve", lambda e: e.tensor_tensor(out=x1s[i].t[:, half * 512:(half + 1) * 512], in0=pbk.t[:],
                                                              in1=xs2[i].t[:, half * 512:(half + 1) * 512], op=ALU.add),
                             r=[pbk.b(), xs2[i].b()], w=[x1s[i].b()])
                    S.dma("sp", ds2[i], x1_d[r0:r0 + 128, :], x1s[i].t[:], r=[x1s[i].b()], w=[x1_db[st * 4 + sub]])
            S.emit()
            if stop == 3:
                S.barrier()
                S.emit()
                return nc

        with contextlib.ExitStack() as p3:
            S.barrier()
            dw3 = S.dsem()
            wfi = load_bf(p3, "wfi", 8, 2 * DFF, dw3)
            wfo = load_bf(p3, "wfo", NFC, D, dw3)
            gfin = sb(p3, "gfin", [128, D])
            dgf = S.dsem()
            S.dma("sp", dgf, gfin.t[:], gfin_d[:, :], w=[gfin.b()])
            S.seal(dw3, [wfi.b(), wfo.b()])
            x1k = sb(p3, "x1k", [128, 4, D])
            dk = [S.dsem() for _ in range(4)]
            h2T = sb(p3, "h2T", [128, 8, TT], BF16)
            ub = [sb(p3, "ub%d" % i, [128, 2 + TT]) for i in range(2)]
            ucar = sb(p3, "ucar", [128, NFC, 2])
            c1 = [sb(p3, "c1%d" % i, [128, TT]) for i in range(2)]
            actT = sb(p3, "actT", [128, NFC, TT], BF16)
            x2s = [sb(p3, "x2s%d" % i, [128, D]) for i in range(2)]
            do = [S.dsem() for _ in range(2)]
            fst = sb(p3, "fst", [128, 2])
            S.op("dve", lambda e: e.memset(ucar.t[:], 0.0), w=[ucar.b()])
            for st in range(NST):
                t0 = st * TT
                for sub in range(4):
                    r0 = t0 + sub * 128
                    S.dma("sp", dk[sub], x1k.t[:, sub, :], x1_d[r0:r0 + 128, :], r=[x1_db[st * 4 + sub]], w=[x1k.b(sub)])
                    norm_T(x1k.t[:, sub, :], x1k.b(sub), C_G2, h2T.t, h2T.b(), sub * 128)
                for fc in range(NFC):
                    pu, pgv = nb(), nb()
                    u_, c_ = ub[fc % 2], c1[fc % 2]
                    for half, pbk in ((0, pu), (1, pgv)):
                        for kc in range(8):
                            S.op("pe", lambda e: e.matmul(pbk.t[:], wfi.t[:, kc, half * DFF + fc * 128:half * DFF + (fc + 1) * 128], h2T.t[:, kc, :],
                                                          start=(kc == 0), stop=(kc == 7)), r=[wfi.b(), h2T.b()], w=[pbk.b()])
                    cw = lambda jx: cvec.t[:, C_CW + jx * NFC + fc:C_CW + jx * NFC + fc + 1]
                    S.op("act", lambda e: e.activation(out=u_.t[:, 2:2 + TT], in_=pu.t[:], func=AF.Copy), r=[pu.b()], w=[u_.b()])
                    S.op("act", lambda e: e.activation(out=c_.t[:, 2:TT], in_=pu.t[:, 0:TT - 2], func=AF.Copy, scale=cw(0)), r=[pu.b()] + cb, w=[c_.b()])
                    S.op("pool", lambda e: e.tensor_copy(out=u_.t[:, 0:2], in_=ucar.t[:, fc, :]), r=[ucar.b()], w=[u_.b()])
                    S.op("pool", lambda e: e.tensor_scalar(out=c_.t[:, 0:2], in0=ucar.t[:, fc, :], scalar1=cw(0), scalar2=None, op0=ALU.mult),
                         r=[ucar.b()] + cb, w=[c_.b()])
                    S.op("pool", lambda e: e.tensor_copy(out=ucar.t[:, fc, :], in_=u_.t[:, TT:TT + 2]), r=[u_.b()], w=[ucar.b()])
                    S.op("dve", lambda e: e.scalar_tensor_tensor(out=c_.t[:], in0=u_.t[:, 1:1 + TT], scalar=cw(1), in1=c_.t[:], op0=ALU.mult, op1=ALU.add),
                         r=[u_.b(), c_.b()] + cb, w=[c_.b()])
                    S.op("dve", lambda e: e.scalar_tensor_tensor(out=c_.t[:], in0=u_.t[:, 2:2 + TT], scalar=cw(2), in1=c_.t[:], op0=ALU.mult, op1=ALU.add),
                         r=[u_.b(), c_.b()] + cb, w=[c_.b()])
                    S.op("act", lambda e: e.activation(out=c_.t[:], in_=c_.t[:], func=AF.Gelu, bias=cvec.t[:, C_CB + fc:C_CB + fc + 1]),
                         r=[c_.b()] + cb, w=[c_.b()])
                    S.op("dve", lambda e: e.tensor_tensor(out=actT.t[:, fc, :], in0=pgv.t[:], in1=c_.t[:], op=ALU.mult),
                         r=[pgv.b(), c_.b()], w=[actT.b()])
                for sub in range(4):
                    i = (st * 4 + sub) % 2
                    r0 = t0 + sub * 128
                    for half in range(2):
                        pbk = nb()
                        for fc in range(NFC):
                            S.op("pe", lambda e: e.matmul(pbk.t[:], actT.t[:, fc, sub * 128:(sub + 1) * 128], wfo.t[:, fc, half * 512:(half + 1) * 512],
                                                          start=(fc == 0), stop=(fc == NFC - 1)), r=[actT.b(), wfo.b()], w=[pbk.b()])
                        S.op("dve", lambda e: e.tensor_tensor(out=x2s[i].t[:, half * 512:(half + 1) * 512], in0=pbk.t[:],
                                                              in1=x1k.t[:, sub, half * 512:(half + 1) * 512], op=ALU.add),
                             r=[pbk.b(), x1k.b(sub)], w=[x2s[i].b()])
                    ss, ms = fst.t[:, 0:1], fst.t[:, 1:2]
                    S.op("act", lambda e: e.activation(out=junk.t[:], in_=x2s[i].t[:], func=AF.Square, accum_out=ss),
                         r=[x2s[i].b(), fst.b()], w=[junk.b(), fst.b()])
                    S.op("dve", lambda e: e.tensor_scalar(out=ms, in0=ss, scalar1=1.0 / D, scalar2=1e-6, op0=ALU.mult, op1=ALU.add), r=[fst.b()], w=[fst.b()])
                    S.op("act", lambda e: e.activation(out=ms, in_=ms, func=AF.Sqrt), r=[fst.b()], w=[fst.b()])
                    S.op("dve", lambda e: e.reciprocal(out=ms, in_=ms), r=[fst.b()], w=[fst.b()])
                    S.op("act", lambda e: e.activation(out=x2s[i].t[:], in_=x2s[i].t[:], func=AF.Copy, scale=ms), r=[x2s[i].b(), fst.b()], w=[x2s[i].b()])
                    S.op("dve", lambda e: e.tensor_tensor(out=x2s[i].t[:], in0=x2s[i].t[:], in1=gfin.t[:], op=ALU.mult),
                         r=[x2s[i].b(), gfin.b()], w=[x2s[i].b()])
                    S.dma("sp", do[i], out_d[r0:r0 + 128, :], x2s[i].t[:], r=[x2s[i].b()])
            S.wait_all("sp", [(d[0], d[1], None) for d in do])
            S.emit()
    return nc


def _cols(v, n):
    return np.ascontiguousarray(np.asarray(v, np.float32).reshape(n, 128).T)


def _host_consts():
    cm = np.zeros((128, NCM), np.float32)
    s = np.arange(128)[:, None] % 64
    t = np.arange(64)[None, :]
    cm[:, M_M2:M_M2 + 64] = (s < t)
    cm[:, M_M2 + 64:M_M2 + 128] = (s <= t)
    tt = np.arange(128)[:, None] % 64
    ss = np.arange(64)[None, :]
    cm[:, M_ML:M_ML + 64] = (tt > ss)
    cm[:, M_I64:M_I64 + 64] = (tt == ss)
    sc = np.ones(512, np.float32)
    sc[::64] = 0.0
    cm[:, M_SCAN:M_SCAN + 512] = sc[None, :]
    wins = [2, 4, 8, 16]
    for g in range(4):
        ci, pb = g // 2, 64 * (g % 2)
        pos = np.arange(1, 513)
        cm[pb:pb + 64, M_INVC0 + ci * 512:M_INVC0 + (ci + 1) * 512] = (1.0 / np.minimum(pos, wins[g]))[None, :]
        cm[pb:pb + 64, M_INVC + ci * 512:M_INVC + (ci + 1) * 512] = 1.0 / wins[g]
    cmb = np.zeros((128, 320), np.float32)
    cmb[:, 0:128] = np.eye(128)
    cmb[0:64, 128:192] = 1.0
    cmb[64:128, 192:256] = 1.0
    cmb[:, 256:320] = 1.0
    return cm, cmb


_NC_CACHE = {}


def _prep(inputs):
    g = lambda k: np.asarray(inputs[k], np.float32)
    cv = np.zeros((128, NCV), np.float32)
    cv[:, C_G1:C_G1 + 8] = _cols(g("norm_mix_g")[0], 8)
    cv[:, C_MU:C_MU + 14] = _cols(g("mu_shift")[0], 14)
    cv[:, C_W0:C_W0 + 4] = _cols(g("w0")[0], 4)
    cv[:, C_A0:C_A0 + 4] = _cols(g("a0")[0], 4)
    cv[:, C_KK:C_KK + 4] = _cols(g("k_k")[0], 4)
    cv[:, C_KA:C_KA + 4] = _cols(g("k_a")[0], 4)
    cv[:, C_RK:C_RK + 4] = _cols(g("r_k")[0].reshape(-1), 4)
    cv[:, C_LNW:C_LNW + 4] = _cols(g("ln_x_w")[0], 4)
    cv[:, C_LNB:C_LNB + 4] = _cols(g("ln_x_b")[0], 4)
    cv[:, C_PS:C_PS + 2] = _cols(g("pool_scale")[0], 2)
    cv[:, C_BG:C_BG + 24] = _cols(g("b_gate")[0], 24)
    cv[:, C_G2:C_G2 + 8] = _cols(g("norm_ffn_g")[0], 8)
    for j in range(3):
        cv[:, C_CW + j * NFC:C_CW + (j + 1) * NFC] = _cols(g("ffn_conv_w")[0, j], NFC)
    cv[:, C_CB:C_CB + NFC] = _cols(g("ffn_conv_b")[0], NFC)
    cv[:, C_GM:C_GM + 8] = _cols(g("norm_mem_g")[0], 8)
    cm, cmb = _host_consts()
    shared = {
        "w_in_mix": g("w_in_mix")[0], "w_lora_b": g("w_lora_b")[0], "a_lora_b": g("a_lora_b")[0], "g_lora_b": g("g_lora_b")[0],
        "pool_w": g("pool_w")[0], "w_mem_kv": g("w_mem_kv")[0], "w_up_rwkv": g("w_up_rwkv")[0], "w_up_pool": g("w_up_pool")[0],
        "w_up_mem": g("w_up_mem")[0], "w_gate": g("w_gate")[0], "w_o": g("w_o")[0], "w_ffn_in": g("w_ffn_in")[0],
        "w_ffn_out": g("w_ffn_out")[0], "cvec": cv, "cm32": cm, "cmb": cmb,
        "gfin": np.ascontiguousarray(np.broadcast_to(g("norm_final_g")[None, :], (128, D))),
    }
    shared = {k: np.ascontiguousarray(v, dtype=np.float32) for k, v in shared.items()}
    x = g("x")
    mem = g("mem")
    return [dict(shared, x=np.ascontiguousarray(x[b]), mem=np.ascontiguousarray(mem[b])) for b in range(8)]


def kernel(**inputs):
    in_maps = _prep(inputs)
    if "nc" not in _NC_CACHE:
        _NC_CACHE["nc"] = build(False)
    res = run_bass_kernel_spmd(_NC_CACHE["nc"], in_maps, core_ids=list(range(8)))
    return np.stack([np.asarray(r["out"], np.float32) for r in res.results], axis=0)
```

```python
import numpy as np
import contextlib
import concourse.bass as bass
import concourse.mybir as mybir
from concourse.bass_utils import run_bass_kernel_spmd

F32 = mybir.dt.float32
BF16 = mybir.dt.bfloat16
AF = mybir.ActivationFunctionType
ALU = mybir.AluOpType
AX = mybir.AxisListType

D = 1024
T = 4096
TT = 512
NST = T // TT
NCH = TT // 64
DFF = 2816
NFC = DFF // 128
MIX_IN = 2304
NEG_EH = -float(np.exp(-0.5))

C_G1, C_MU, C_W0, C_A0, C_KK, C_KA, C_RK, C_LNW, C_LNB, C_PS, C_BG, C_G2, C_CW, C_CB, C_GM = (
    0, 8, 22, 26, 30, 34, 38, 42, 46, 50, 52, 76, 84, 150, 172)
NCV = 180
M_M2, M_ML, M_I64, M_SCAN, M_INVC0, M_INVC = 0, 128, 192, 256, 768, 1792
NCM = 2816


class Buf:
    __slots__ = ("w", "r")

    def __init__(self):
        self.w = None
        self.r = {}


class Rec:
    def __getattr__(self, name):
        def f(*a, **k):
            self.call = (name, a, k)
            return self
        return f


class Eng:
    def __init__(self, name, sem, is_pe=False):
        self.name = name
        self.sem = sem
        self.count = 0
        self.seen = {}
        self.prog = []
        self.is_pe = is_pe


class Sched:
    def __init__(self, nc, es):
        self.nc = nc
        self.E = {}
        for n in ("pe", "act", "dve", "pool", "sp"):
            self.E[n] = Eng(n, es.enter_context(nc.semaphore("s_" + n)), is_pe=(n == "pe"))
        self.es = es
        self.ndsem = 0
        self.dsems = []

    def dsem(self):
        self.ndsem += 1
        ds = [self.es.enter_context(self.nc.semaphore("d%d" % self.ndsem)), 0]
        self.dsems.append(ds)
        return ds

    def barrier(self):
        for E in self.E.values():
            for X in self.E.values():
                if X is not E and X.count > 0 and E.seen.get(id(X.sem), 0) < X.count:
                    E.seen[id(X.sem)] = X.count
                    E.prog.append(("w", X.sem, X.count))
            for ds in self.dsems:
                if ds[1] > 0 and E.seen.get(id(ds[0]), 0) < ds[1]:
                    E.seen[id(ds[0])] = ds[1]
                    E.prog.append(("w", ds[0], ds[1]))

    def _deps(self, E, reads, writes):
        need = {}

        def add(tok, raw):
            sem, val, eng = tok
            if eng is E and E.is_pe:
                return
            k = id(sem)
            if k not in need or need[k][1] < val:
                need[k] = (sem, val)

        for b in reads:
            if b.w is not None:
                add(b.w, True)
        for b in writes:
            if b.w is not None:
                add(b.w, False)
            for t in b.r.values():
                add(t, False)
        for k, (sem, val) in need.items():
            if E.seen.get(k, 0) >= val:
                continue
            E.seen[k] = val
            E.prog.append(("w", sem, val))

    def _commit(self, tok, reads, writes):
        for b in writes:
            b.w = tok
            b.r = {}
        k = id(tok[0])
        for b in reads:
            b.r[k] = tok

    def op(self, en, fn, r=(), w=()):
        E = self.E[en]
        self._deps(E, r, w)
        E.count += 1
        rec = Rec()
        fn(rec)
        E.prog.append(("i", rec.call, E.sem, 1))
        self._commit((E.sem, E.count, E), r, w)

    def dma(self, en, ds, out, in_, r=(), w=()):
        E = self.E[en]
        self._deps(E, r, w)
        ds[1] += 16
        E.prog.append(("i", ("dma_start", (), dict(out=out, in_=in_)), ds[0], 16))
        self._commit((ds[0], ds[1], None), r, w)

    def seal(self, ds, bufs):
        for b in bufs:
            b.w = (ds[0], ds[1], None)

    def wait_all(self, en, toks):
        E = self.E[en]
        for sem, val, _ in toks:
            E.prog.append(("w", sem, val))

    def emit(self):
        nc = self.nc
        progs = {n: e.prog for n, e in self.E.items()}
        for e in self.E.values():
            e.prog = []

        def run(eng, prog):
            for it in prog:
                if it[0] == "w":
                    eng.wait_ge(it[1], it[2])
                else:
                    name, a, k = it[1]
                    getattr(eng, name)(*a, **k).then_inc(it[2], it[3])

        with nc.Block() as block:
            @block.tensor
            def _(e):
                run(e, progs["pe"])

            @block.scalar
            def _(e):
                run(e, progs["act"])

            @block.vector
            def _(e):
                run(e, progs["dve"])

            @block.gpsimd
            def _(e):
                run(e, progs["pool"])

            @block.sync
            def _(e):
                run(e, progs["sp"])


class Tl:
    def __init__(self, t):
        self.t = t
        self.bufs = {}

    def b(self, key=0):
        if key not in self.bufs:
            self.bufs[key] = Buf()
        return self.bufs[key]


def build(debug=False, stop=99):
    nc = bass.Bass("TRN2", target_bir_lowering=False)
    din = lambda n, s, dt=F32: nc.dram_tensor(n, s, dt, kind="ExternalInput").ap()
    x_d = din("x", [T, D])
    mem_d = din("mem", [256, D])
    win_d = din("w_in_mix", [D, MIX_IN])
    wl_d = din("w_lora_b", [64, 512])
    al_d = din("a_lora_b", [64, 512])
    gl_d = din("g_lora_b", [128, 512])
    pw_d = din("pool_w", [4, 64, 64])
    wkv_d = din("w_mem_kv", [D, 512])
    wupr_d = din("w_up_rwkv", [512, D])
    wupp_d = din("w_up_pool", [256, D])
    wupm_d = din("w_up_mem", [256, D])
    wg_d = din("w_gate", [D, 3 * D])
    wo_d = din("w_o", [D, D])
    wfi_d = din("w_ffn_in", [D, 2 * DFF])
    wfo_d = din("w_ffn_out", [DFF, D])
    cvec_d = din("cvec", [128, NCV])
    cm32_d = din("cm32", [128, NCM])
    cmb_d = din("cmb", [128, 320])
    gfin_d = din("gfin", [128, D])
    skind = "ExternalOutput" if debug else "Internal"
    hT_d = nc.dram_tensor("hT_d", [128, 8, T], BF16, kind=skind).ap()
    yT_d = nc.dram_tensor("yT_d", [128, 8, T], BF16, kind=skind).ap()
    x1_d = nc.dram_tensor("x1_d", [T, D], F32, kind=skind).ap()
    out_d = nc.dram_tensor("out", [T, D], F32, kind="ExternalOutput").ap()
    wsc = {n: nc.dram_tensor(n + "_bf", shp, BF16, kind="Internal").ap() for n, shp in
           (("wg", [D, 3 * D]), ("wup", [D, D]), ("wo", [D, D]), ("wfi", [D, 2 * DFF]), ("wfo", [DFF, D]))}
    wsc_b = {n: Buf() for n in wsc}
    hT_db = [Buf() for _ in range(NST)]
    yT_db = [Buf() for _ in range(NST)]
    x1_db = [Buf() for _ in range(NST * 4)]

    with contextlib.ExitStack() as top:
        S = Sched(nc, top)
        sb = lambda es, n, s, dt=F32: Tl(es.enter_context(nc.sbuf_tensor("sb_" + n, s, dt)))
        PB = [Tl(top.enter_context(nc.psum_tensor("pb%d" % i, [128, 512], F32))) for i in range(7)]
        PT = Tl(top.enter_context(nc.psum_tensor("pt", [128, 1024], BF16)))
        cvec = sb(top, "cvec", [128, NCV])
        cder = sb(top, "cder", [128, 18])
        ident = sb(top, "ident", [128, 320], BF16)
        junk = sb(top, "junk", [128, D], BF16)
        hb = sb(top, "hb", [128, D], BF16)
        st4 = sb(top, "st4", [128, 4])
        dconst = S.dsem()
        S.dma("sp", dconst, cvec.t[:], cvec_d[:, :], w=[cvec.b()])
        dconst2 = S.dsem()
        S.dma("pool", dconst2, ident.t[:], cmb_d[:, :], w=[ident.b()])
        S.op("dve", lambda e: e.tensor_scalar(out=cder.t[:, 0:14], in0=cvec.t[:, C_MU:C_MU + 14], scalar1=-1.0, scalar2=1.0,
                                              op0=ALU.mult, op1=ALU.add), r=[cvec.b()], w=[cder.b()])
        S.op("dve", lambda e: e.tensor_scalar(out=cder.t[:, 14:18], in0=cvec.t[:, C_KA:C_KA + 4], scalar1=-1.0, scalar2=1.0,
                                              op0=ALU.mult, op1=ALU.add), r=[cvec.b()], w=[cder.b()])
        idn = ident.t[:, 0:128]
        bdo = ident.t[:, 128:256]
        ones64 = ident.t[:, 256:320]
        cb = [cvec.b(), cder.b(), ident.b()]

        def norm_T(xs_ap, bx, gcol, hT, bh, col0, npart=128):
            ss, ms = st4.t[0:npart, 0:1], st4.t[0:npart, 1:2]
            S.op("act", lambda e: e.activation(out=junk.t[0:npart, :], in_=xs_ap, func=AF.Square, accum_out=ss),
                 r=[bx, st4.b()], w=[junk.b(), st4.b()])
            S.op("dve", lambda e: e.tensor_scalar(out=ms, in0=ss, scalar1=1.0 / D, scalar2=1e-6, op0=ALU.mult, op1=ALU.add),
                 r=[st4.b()], w=[st4.b()])
            S.op("act", lambda e: e.activation(out=ms, in_=ms, func=AF.Sqrt), r=[st4.b()], w=[st4.b()])
            S.op("dve", lambda e: e.reciprocal(out=ms, in_=ms), r=[st4.b()], w=[st4.b()])
            S.op("dve", lambda e: e.tensor_scalar(out=hb.t[0:npart, :], in0=xs_ap, scalar1=ms, scalar2=None, op0=ALU.mult),
                 r=[bx, st4.b()], w=[hb.b()])
            for c in range(8):
                S.op("pe", lambda e, c=c: e.transpose(PT.t[:, c * 128:c * 128 + npart], hb.t[0:npart, c * 128:(c + 1) * 128],
                                                      idn[0:npart, 0:npart]), r=[hb.b(), ident.b()], w=[PT.b("A"), PT.b("B")])
            pv = PT.t[:, :].rearrange("p (c t) -> p c t", c=8)[:, :, 0:npart]
            gv = cvec.t[:, gcol:gcol + 8].unsqueeze(2).broadcast_to([128, 8, npart])
            S.op("dve", lambda e: e.tensor_tensor(out=hT[:, :, col0:col0 + npart], in0=pv, in1=gv, op=ALU.mult),
                 r=[PT.b("A"), PT.b("B"), cvec.b()], w=[bh])

        def load_w(es, name, wd, kchunks, ncols, ds, row0=0, tile=None, kc0=0):
            if tile is None:
                tile = sb(es, name, [128, kchunks, ncols], BF16)
            step = 1024
            for kc in range(kchunks):
                for n0 in range(0, ncols, step):
                    n1 = min(ncols, n0 + step)
                    S.dma("pool", ds, tile.t[:, kc0 + kc, n0:n1], wd[row0 + kc * 128:row0 + (kc + 1) * 128, n0:n1], w=[tile.b()])
            return tile

        def stage_w(ds, name, wd, nrows, ncols, row0=0):
            for r0 in range(0, nrows, 128):
                for n0 in range(0, ncols, 2048):
                    n1 = min(ncols, n0 + 2048)
                    S.dma("pool", ds, wsc[name][row0 + r0:row0 + r0 + 128, n0:n1], wd[r0:r0 + 128, n0:n1], w=[wsc_b[name]])

        def load_bf(es, name, kchunks, ncols, ds):
            tile = sb(es, name, [128, kchunks, ncols], BF16)
            for kc in range(kchunks):
                S.dma("sp", ds, tile.t[:, kc, :], wsc[name][kc * 128:(kc + 1) * 128, :], r=[wsc_b[name]], w=[tile.b()])
            return tile

        with contextlib.ExitStack() as p1:
            dw1 = S.dsem()
            cm = sb(p1, "cm", [128, NCM])
            dcm = S.dsem()
            S.dma("sp", dcm, cm.t[:], cm32_d[:, :], w=[cm.b()])
            cb = cb + [cm.b()]
            win = load_w(p1, "win", win_d, 8, MIX_IN, dw1)
            lora = sb(p1, "lora", [128, 512], BF16)
            S.dma("pool", dw1, lora.t[0:64, :], wl_d[:, :], w=[lora.b()])
            S.dma("pool", dw1, lora.t[64:128, :], al_d[:, :], w=[lora.b()])
            gl = sb(p1, "gl", [128, 512], BF16)
            S.dma("pool", dw1, gl.t[:], gl_d[:, :], w=[gl.b()])
            pw = sb(p1, "pw", [128, 2, 64], BF16)
            for g in range(4):
                S.dma("pool", dw1, pw.t[64 * (g % 2):64 * (g % 2) + 64, g // 2, :], pw_d[g, :, :], w=[pw.b()])
            kT = sb(p1, "kT", [128, 2, 256], BF16)
            vtok = sb(p1, "vtok", [128, 2, 256], BF16)
            with contextlib.ExitStack() as p0:
                wkv = load_w(p0, "wkv", wkv_d, 8, 512, dw1)
                S.seal(dw1, [win.b(), lora.b(), gl.b(), pw.b(), wkv.b()])
                mems = sb(p0, "mems", [128, 2, D])
                memT = sb(p0, "memT", [128, 8, 256], BF16)
                dm = S.dsem()
                for mh in range(2):
                    S.dma("sp", dm, mems.t[:, mh, :], mem_d[mh * 128:(mh + 1) * 128, :], w=[mems.b(mh)])
                S.seal(dm, [mems.b(0), mems.b(1)])
                for mh in range(2):
                    norm_T(mems.t[:, mh, :], mems.b(mh), C_GM, memT.t, memT.b(), mh * 128)
                for fc in range(2):
                    for kc in range(8):
                        S.op("pe", lambda e, fc=fc, kc=kc: e.matmul(PB[0].t[:, 0:256], wkv.t[:, kc, fc * 128:(fc + 1) * 128],
                                                                     memT.t[:, kc, :], start=(kc == 0), stop=(kc == 7)),
                             r=[wkv.b(), memT.b()], w=[PB[0].b()])
                    S.op("act", lambda e, fc=fc: e.activation(out=kT.t[:, fc, :], in_=PB[0].t[:, 0:256], func=AF.Copy),
                         r=[PB[0].b()], w=[kT.b()])
                for mh in range(2):
                    for kc in range(8):
                        S.op("pe", lambda e, mh=mh, kc=kc: e.matmul(PB[1].t[:, 0:256], memT.t[:, kc, mh * 128:(mh + 1) * 128],
                                                                     wkv.t[:, kc, 256:512], start=(kc == 0), stop=(kc == 7)),
                             r=[wkv.b(), memT.b()], w=[PB[1].b()])
                    S.op("act", lambda e, mh=mh: e.activation(out=vtok.t[:, mh, :], in_=PB[1].t[:, 0:256], func=AF.Copy),
                         r=[PB[1].b()], w=[vtok.b()])
                S.emit()
            S.barrier()
            if stop == 0:
                S.emit()
                return nc
            dstg = S.dsem()
            stage_w(dstg, "wg", wg_d, D, 3 * D)
            stage_w(dstg, "wup", wupr_d, 512, D, row0=0)
            stage_w(dstg, "wup", wupp_d, 256, D, row0=512)
            stage_w(dstg, "wup", wupm_d, 256, D, row0=768)
            stage_w(dstg, "wo", wo_d, D, D)
            stage_w(dstg, "wfi", wfi_d, D, 2 * DFF)
            stage_w(dstg, "wfo", wfo_d, DFF, D)
            S.seal(dstg, list(wsc_b.values()))

            xs = [sb(p1, "xs%d" % i, [128, D]) for i in range(2)]
            dxs = [S.dsem() for _ in range(2)]
            hT = sb(p1, "hT", [128, 8, TT], BF16)
            dh = S.dsem()
            pm = sb(p1, "pm", [128, 14, TT], BF16)
            tmp = sb(p1, "tmp", [128, TT])
            cy = sb(p1, "cy", [128, 14])
            pp = sb(p1, "pp", [128, 2, 16 + TT])
            ppa = sb(p1, "ppa", [128, 16 + TT])
            ppb = sb(p1, "ppb", [128, 16 + TT])
            dT = sb(p1, "dT", [128, 2, TT], BF16)
            qT = sb(p1, "qT", [128, 2, TT], BF16)
            tw = sb(p1, "tw", [128, TT], BF16)
            sg = sb(p1, "sg", [128, TT], BF16)
            fn = ["ld", "cum", "cx", "E1", "E2", "E3", "a", "kk", "rs", "kkn", "x1", "kf"]
            f = {n: sb(p1, "f_" + n, [128, TT]) for n in fn}
            kk2 = sb(p1, "kk2", [128, TT], BF16)
            KR = sb(p1, "KR", [128, 4, NCH, 2, 64], BF16)
            BK = sb(p1, "BK", [128, 4, NCH, 2, 64], BF16)
            WC = sb(p1, "WC", [128, 4, NCH])
            bonT = sb(p1, "bonT", [128, 4, TT], BF16)
            gT = sb(p1, "gT", [128, 4, TT], BF16)
            Asb = [sb(p1, "Asb%d" % i, [128, 4, 2, 128], BF16) for i in range(2)]
            GF = [sb(p1, "GF%d" % i, [128, 4, 64], BF16) for i in range(2)]
            Lsb = sb(p1, "Lsb", [128, 4, 64], BF16)
            GT = [sb(p1, "GT%d" % i, [128, 4, 64], BF16) for i in range(2)]
            P2 = [sb(p1, "P2%d" % i, [128, 4, 64], BF16) for i in range(2)]
            P2T = [sb(p1, "P2T%d" % i, [128, 4, 64], BF16) for i in range(2)]
            Zsb = sb(p1, "Zsb", [128, 4, 64], BF16)
            Un = sb(p1, "Un", [128, 4, 64], BF16)
            VBK = [sb(p1, "VBK%d" % i, [128, 3, 4, 64], BF16) for i in range(2)]
            Ysb = [sb(p1, "Ysb%d" % i, [128, 4, 64]) for i in range(2)]
            Ysq = [sb(p1, "Ysq%d" % i, [128, 4, 64]) for i in range(2)]
            yh = [sb(p1, "yh%d" % i, [128, 4, 64], BF16) for i in range(2)]
            yst = [sb(p1, "yst%d" % i, [128, 4, 4]) for i in range(2)]
            eps_gn = sb(p1, "eps_gn", [128, 1])
            Sf = sb(p1, "Sf", [128, 4, 64])
            Sbf = sb(p1, "Sbf", [128, 4, 64], BF16)
            yhT = sb(p1, "yhT", [128, 4, TT], BF16)
            yT = sb(p1, "yT", [128, 8, TT], BF16)
            dy = S.dsem()
            eT = sb(p1, "eT", [128, 2, TT], BF16)
            rden = sb(p1, "rden", [128, TT])

            S.op("dve", lambda e: e.memset(cy.t[:], 0.0), w=[cy.b()])
            S.op("dve", lambda e: e.memset(pp.t[:], 0.0), w=[pp.b()])
            S.op("dve", lambda e: e.memset(ppa.t[:], 0.0), w=[ppa.b(0), ppa.b(64)])
            S.op("dve", lambda e: e.memset(ppb.t[:], 0.0), w=[ppb.b(0), ppb.b(64)])
            S.op("dve", lambda e: e.memset(Sf.t[:], 0.0), w=[Sf.b()])
            S.op("dve", lambda e: e.memset(Sbf.t[:], 0.0), w=[Sbf.b()])
            S.op("dve", lambda e: e.memset(eps_gn.t[:], 64e-5), w=[eps_gn.b()])
            M2v = cm.t[:, M_M2:M_M2 + 128].unsqueeze(1).broadcast_to([128, 4, 128])
            MLv = cm.t[:, M_ML:M_ML + 64].unsqueeze(1).broadcast_to([128, 4, 64])
            I64v = cm.t[:, M_I64:M_I64 + 64].unsqueeze(1).broadcast_to([128, 4, 64])
            scanm = cm.t[:, M_SCAN:M_SCAN + 512]
            mmb = 0

            for st in range(NST if stop >= 2 else 1):
                t0 = st * TT
                for sub in range(4):
                    i = (st * 4 + sub) % 2
                    S.dma("sp", dxs[i], xs[i].t[:], x_d[t0 + sub * 128:t0 + (sub + 1) * 128, :], w=[xs[i].b()])
                    norm_T(xs[i].t[:], xs[i].b(), C_G1, hT.t, hT.b(), sub * 128)
                S.dma("sp", dh, hT_d[:, :, t0:t0 + TT], hT.t[:], r=[hT.b()], w=[hT_db[st]])
                if stop == 1.1:
                    break
                for oc in range(18):
                    pbk = PB[mmb % 2]
                    mmb += 1
                    for kc in range(8):
                        S.op("pe", lambda e, oc=oc, kc=kc, pbk=pbk: e.matmul(pbk.t[:], win.t[:, kc, oc * 128:(oc + 1) * 128], hT.t[:, kc, :],
                                                                            start=(kc == 0), stop=(kc == 7)),
                             r=[win.b(), hT.b()], w=[pbk.b()])
                    ps = pbk.t
                    if oc < 14:
                        mu = cvec.t[:, C_MU + oc:C_MU + oc + 1]
                        om = cder.t[:, oc:oc + 1]
                        S.op("act", lambda e, ps=ps, mu=mu: e.activation(out=tmp.t[:, 1:TT], in_=ps[:, 0:TT - 1], func=AF.Copy, scale=mu),
                             r=[pbk.b()] + cb, w=[tmp.b()])
                        S.op("dve", lambda e, oc=oc, mu=mu: e.tensor_scalar(out=tmp.t[:, 0:1], in0=cy.t[:, oc:oc + 1], scalar1=mu, scalar2=None,
                                                                           op0=ALU.mult), r=[cy.b()] + cb, w=[tmp.b()])
                        S.op("dve", lambda e, oc=oc, ps=ps: e.tensor_copy(out=cy.t[:, oc:oc + 1], in_=ps[:, TT - 1:TT]), r=[pbk.b()], w=[cy.b()])
                        S.op("dve", lambda e, oc=oc, ps=ps, om=om: e.scalar_tensor_tensor(out=pm.t[:, oc, :], in0=ps[:, :], scalar=om, in1=tmp.t[:, :],
                                                                                          op0=ALU.mult, op1=ALU.add),
                             r=[pbk.b(), tmp.b()] + cb, w=[pm.b(oc)])
                    elif oc < 16:
                        S.op("act", lambda e, oc=oc, ps=ps: e.activation(out=pp.t[:, oc - 14, 16:16 + TT], in_=ps[:, :], func=AF.Copy),
                             r=[pbk.b()], w=[pp.b()])
                    else:
                        S.op("act", lambda e, oc=oc, ps=ps: e.activation(out=qT.t[:, oc - 16, :], in_=ps[:, :], func=AF.Copy),
                             r=[pbk.b()], w=[qT.b()])
                if stop == 1.2:
                    break
                invc = cm.t[:, (M_INVC0 if st == 0 else M_INVC):(M_INVC0 if st == 0 else M_INVC) + 1024].rearrange("p (c t) -> p c t", c=2)
                for g in range(4):
                    ci, pb = g // 2, 64 * (g % 2)
                    src, bsrc = pp.t[pb:pb + 64, ci, :], pp.b()
                    for lv in range(g + 1):
                        sh = 1 << lv
                        dst = ppa if lv % 2 == 0 else ppb
                        S.op("dve", lambda e, src=src, dst=dst, sh=sh, pb=pb: e.tensor_tensor(out=dst.t[pb:pb + 64, sh:16 + TT], in0=src[:, sh:16 + TT],
                                                                                            in1=src[:, 0:16 + TT - sh], op=ALU.add),
                             r=[bsrc], w=[dst.b(pb)])
                        if sh > 1:
                            pass
                        src, bsrc = dst.t[pb:pb + 64, :], dst.b(pb)
                    S.op("dve", lambda e, src=src, pb=pb, ci=ci: e.tensor_tensor(out=ppa.t[pb:pb + 64, 16:16 + TT] if False else tmp.t[pb:pb + 64, :],
                                                                                 in0=src[:, 16:16 + TT], in1=invc[pb:pb + 64, ci, :], op=ALU.mult),
                         r=[bsrc] + cb, w=[tmp.b()])
                    S.op("dve", lambda e, pb=pb, ci=ci: e.tensor_tensor(out=dT.t[pb:pb + 64, ci, :], in0=tmp.t[pb:pb + 64, :],
                                                                        in1=pp.t[pb:pb + 64, ci, 16:16 + TT], op=ALU.subtract),
                         r=[tmp.b(), pp.b()], w=[dT.b()])
                for ci in range(2):
                    pbk = PB[mmb % 2]
                    mmb += 1
                    for g2 in range(2):
                        pb = 64 * g2
                        S.op("pe", lambda e, ci=ci, pb=pb, pbk=pbk: e.matmul(pbk.t[pb:pb + 64, :], pw.t[pb:pb + 64, ci, :], dT.t[pb:pb + 64, ci, :],
                                                                            start=True, stop=True), r=[pw.b(), dT.b()], w=[pbk.b()])
                    S.op("act", lambda e, ci=ci, pbk=pbk: e.activation(out=yT.t[:, 4 + ci, :], in_=pbk.t[:, :], func=AF.Copy,
                                                                      scale=cvec.t[:, C_PS + ci:C_PS + ci + 1]), r=[pbk.b()] + cb, w=[yT.b(4 + ci)])
                S.op("dve", lambda e: e.tensor_copy(out=pp.t[:, :, 0:16], in_=pp.t[:, :, TT:TT + 16]), r=[pp.b()], w=[pp.b()])
                if stop == 1.3:
                    break
                for jm in range(2):
                    for hh in range(2):
                        pb = 64 * hh
                        hm = 2 * jm + hh
                        for mh in range(2):
                            S.op("pe", lambda e, jm=jm, pb=pb, mh=mh: e.matmul(PB[2 + mh].t[:, :], kT.t[pb:pb + 64, jm, mh * 128:(mh + 1) * 128],
                                                                               qT.t[pb:pb + 64, jm, :], start=True, stop=True),
                                 r=[kT.b(), qT.b()], w=[PB[2 + mh].b()])
                            S.op("act", lambda e, mh=mh: e.activation(out=eT.t[:, mh, :], in_=PB[2 + mh].t[:, :], func=AF.Exp, scale=0.125),
                                 r=[PB[2 + mh].b()], w=[eT.b(mh)])
                        for mh in range(2):
                            S.op("pe", lambda e, hm=hm, pb=pb, mh=mh: e.matmul(PB[4].t[pb:pb + 64, :], vtok.t[:, mh, hm * 64:(hm + 1) * 64], eT.t[:, mh, :],
                                                                               start=(mh == 0), stop=(mh == 1)),
                                 r=[vtok.b(), eT.b(mh)], w=[PB[4].b()])
                        for mh in range(2):
                            S.op("pe", lambda e, pb=pb, mh=mh: e.matmul(PB[5].t[pb:pb + 64, :], ones64, eT.t[:, mh, :],
                                                                        start=(mh == 0), stop=(mh == 1)),
                                 r=[ident.b(), eT.b(mh)], w=[PB[5].b()])
                    S.op("dve", lambda e: e.reciprocal(out=rden.t[:], in_=PB[5].t[:, :]), r=[PB[5].b()], w=[rden.b()])
                    S.op("dve", lambda e, jm=jm: e.tensor_tensor(out=yT.t[:, 6 + jm, :], in0=PB[4].t[:, :], in1=rden.t[:], op=ALU.mult),
                         r=[PB[4].b(), rden.b()], w=[yT.b(6 + jm)])
                if stop == 1.4:
                    break
                S.op("act", lambda e: e.activation(out=tw.t[0:64, :], in_=pm.t[0:64, 12, :], func=AF.Tanh), r=[pm.b(12)], w=[tw.b()])
                S.op("act", lambda e: e.activation(out=sg.t[:], in_=pm.t[:, 13, :], func=AF.Sigmoid), r=[pm.b(13)], w=[sg.b()])
                for j in range(4):
                    cs = slice(j * 128, (j + 1) * 128)
                    r_, k_, v_ = pm.t[:, j, :], pm.t[:, 4 + j, :], pm.t[:, 8 + j, :]
                    cv = lambda c0: cvec.t[:, c0 + j:c0 + j + 1]
                    pbk = PB[mmb % 2]
                    mmb += 1
                    S.op("pe", lambda e, pbk=pbk, cs=cs: e.matmul(pbk.t[:], lora.t[0:64, cs], tw.t[0:64, :], start=True, stop=True),
                         r=[lora.b(), tw.b()], w=[pbk.b()])
                    S.op("act", lambda e, pbk=pbk, cv=cv: e.activation(out=f["ld"].t[:], in_=pbk.t[:], func=AF.Sigmoid, bias=cv(C_W0)),
                         r=[pbk.b()] + cb, w=[f["ld"].b()])
                    S.op("dve", lambda e: e.tensor_scalar(out=f["ld"].t[:], in0=f["ld"].t[:], scalar1=NEG_EH, scalar2=None, op0=ALU.mult),
                         r=[f["ld"].b()], w=[f["ld"].b()])
                    S.op("dve", lambda e: e.tensor_tensor_scan(out=f["cum"].t[:], data0=scanm, data1=f["ld"].t[:], initial=0.0,
                                                               op0=ALU.mult, op1=ALU.add), r=[f["ld"].b()] + cb, w=[f["cum"].b()])
                    S.op("dve", lambda e: e.tensor_tensor(out=f["cx"].t[:], in0=f["cum"].t[:], in1=f["ld"].t[:], op=ALU.subtract),
                         r=[f["cum"].b(), f["ld"].b()], w=[f["cx"].b()])
                    S.op("act", lambda e: e.activation(out=f["E1"].t[:], in_=f["cum"].t[:], func=AF.Exp), r=[f["cum"].b()], w=[f["E1"].b()])
                    S.op("act", lambda e: e.activation(out=f["E2"].t[:], in_=f["cum"].t[:], func=AF.Exp, scale=-1.0), r=[f["cum"].b()], w=[f["E2"].b()])
                    S.op("act", lambda e: e.activation(out=f["E3"].t[:], in_=f["cx"].t[:], func=AF.Exp), r=[f["cx"].b()], w=[f["E3"].b()])
                    S.op("dve", lambda e, j=j: e.tensor_copy(out=WC.t[:, j, :], in_=f["E1"].t[:, :].rearrange("p (c t) -> p c t", t=64)[:, :, 63]),
                         r=[f["E1"].b()], w=[WC.b()])
                    pbk = PB[mmb % 2]
                    mmb += 1
                    S.op("pe", lambda e, pbk=pbk, cs=cs: e.matmul(pbk.t[:], lora.t[64:128, cs], pm.t[64:128, 12, :], start=True, stop=True),
                         r=[lora.b(), pm.b(12)], w=[pbk.b()])
                    S.op("act", lambda e, pbk=pbk, cv=cv: e.activation(out=f["a"].t[:], in_=pbk.t[:], func=AF.Sigmoid, bias=cv(C_A0)),
                         r=[pbk.b()] + cb, w=[f["a"].b()])
                    S.op("dve", lambda e, k_=k_, cv=cv: e.tensor_scalar(out=f["kk"].t[:], in0=k_, scalar1=cv(C_KK), scalar2=None, op0=ALU.mult),
                         r=[pm.b(4 + j)] + cb, w=[f["kk"].b()])
                    S.op("dve", lambda e: e.tensor_tensor(out=kk2.t[:], in0=f["kk"].t[:], in1=f["kk"].t[:], op=ALU.mult),
                         r=[f["kk"].b()], w=[kk2.b()])
                    pbk = PB[mmb % 2]
                    mmb += 1
                    S.op("pe", lambda e, pbk=pbk: e.matmul(pbk.t[:], bdo, kk2.t[:], start=True, stop=True), r=[ident.b(), kk2.b()], w=[pbk.b()])
                    S.op("act", lambda e, pbk=pbk: e.activation(out=f["rs"].t[:], in_=pbk.t[:], func=AF.Sqrt, bias=1e-12), r=[pbk.b()], w=[f["rs"].b()])
                    S.op("dve", lambda e: e.reciprocal(out=f["rs"].t[:], in_=f["rs"].t[:]), r=[f["rs"].b()], w=[f["rs"].b()])
                    S.op("dve", lambda e: e.tensor_tensor(out=f["kkn"].t[:], in0=f["kk"].t[:], in1=f["rs"].t[:], op=ALU.mult),
                         r=[f["kk"].b(), f["rs"].b()], w=[f["kkn"].b()])
                    c3 = lambda tl: tl.t[:, :].rearrange("p (c t) -> p c t", t=64)
                    S.op("dve", lambda e, j=j: e.tensor_tensor(out=KR.t[:, j, :, 0, :], in0=c3(f["kkn"]), in1=c3(f["E3"]), op=ALU.mult),
                         r=[f["kkn"].b(), f["E3"].b()], w=[KR.b(j)])
                    S.op("dve", lambda e, j=j, r_=r_: e.tensor_tensor(out=KR.t[:, j, :, 1, :], in0=r_.rearrange("p (c t) -> p c t", t=64), in1=c3(f["E1"]),
                                                                     op=ALU.mult), r=[pm.b(j), f["E1"].b()], w=[KR.b(j)])
                    S.op("dve", lambda e: e.tensor_tensor(out=f["x1"].t[:], in0=f["kkn"].t[:], in1=f["a"].t[:], op=ALU.mult),
                         r=[f["kkn"].b(), f["a"].b()], w=[f["x1"].b()])
                    S.op("dve", lambda e, j=j: e.tensor_tensor(out=BK.t[:, j, :, 0, :], in0=c3(f["x1"]), in1=c3(f["E2"]), op=ALU.mult),
                         r=[f["x1"].b(), f["E2"].b()], w=[BK.b(j)])
                    S.op("dve", lambda e, j=j, cv=cv: e.tensor_scalar(out=f["x1"].t[:], in0=f["a"].t[:], scalar1=cv(C_KA), scalar2=cder.t[:, 14 + j:15 + j],
                                                                     op0=ALU.mult, op1=ALU.add), r=[f["a"].b()] + cb, w=[f["x1"].b()])
                    S.op("dve", lambda e, k_=k_: e.tensor_tensor(out=f["kf"].t[:], in0=f["x1"].t[:], in1=k_, op=ALU.mult),
                         r=[f["x1"].b(), pm.b(4 + j)], w=[f["kf"].b()])
                    S.op("dve", lambda e, j=j: e.tensor_tensor(out=BK.t[:, j, :, 1, :], in0=c3(f["kf"]), in1=c3(f["E2"]), op=ALU.mult),
                         r=[f["kf"].b(), f["E2"].b()], w=[BK.b(j)])
                    S.op("dve", lambda e, r_=r_: e.tensor_tensor(out=f["x1"].t[:], in0=f["kf"].t[:], in1=r_, op=ALU.mult),
                         r=[f["kf"].b(), pm.b(j)], w=[f["x1"].b()])
                    S.op("dve", lambda e, cv=cv: e.tensor_scalar(out=kk2.t[:], in0=f["x1"].t[:], scalar1=cv(C_RK), scalar2=None, op0=ALU.mult),
                         r=[f["x1"].b()] + cb, w=[kk2.b()])
                    pbk = PB[mmb % 2]
                    mmb += 1
                    S.op("pe", lambda e, pbk=pbk: e.matmul(pbk.t[:], bdo, kk2.t[:], start=True, stop=True), r=[ident.b(), kk2.b()], w=[pbk.b()])
                    S.op("dve", lambda e, pbk=pbk, j=j, v_=v_: e.tensor_tensor(out=bonT.t[:, j, :], in0=pbk.t[:], in1=v_, op=ALU.mult),
                         r=[pbk.b(), pm.b(8 + j)], w=[bonT.b(j)])
                    pbk = PB[mmb % 2]
                    mmb += 1
                    S.op("pe", lambda e, pbk=pbk, cs=cs: e.matmul(pbk.t[:], gl.t[:, cs], sg.t[:], start=True, stop=True), r=[gl.b(), sg.b()], w=[pbk.b()])
                    S.op("act", lambda e, pbk=pbk, j=j: e.activation(out=gT.t[:, j, :], in_=pbk.t[:], func=AF.Copy), r=[pbk.b()], w=[gT.b(j)])
                if stop == 1.5:
                    break
                krb = [KR.b(j) for j in range(4)]
                bkb = [BK.b(j) for j in range(4)]
                HP = [(h // 2, 64 * (h % 2)) for h in range(8)]
                v3 = lambda ap_, w=64: ap_.rearrange("p (j c) -> p j c", j=4)
                lo = lambda bank, w=64: v3(bank.t[:, 0:4 * w], w)
                hi = lambda bank: v3(bank.t[:, 256:512])

                def stageA(c):
                    par = c % 2
                    cc = slice(c * 64, (c + 1) * 64)
                    vbk, asb, gf = VBK[par], Asb[par], GF[par]
                    for j, pb in HP:
                        ps_ = slice(pb, pb + 64)
                        S.op("pe", lambda e: e.transpose(PT.t[ps_, j * 64:(j + 1) * 64], pm.t[ps_, 8 + j, cc], idn[ps_, ps_]),
                             r=[pm.b(8 + j), ident.b()], w=[PT.b("A")])
                        S.op("pe", lambda e: e.transpose(PT.t[ps_, 256 + j * 64:256 + (j + 1) * 64], BK.t[ps_, j, c, 0, :], idn[ps_, ps_]),
                             r=[bkb[j], ident.b()], w=[PT.b("A")])
                        S.op("pe", lambda e: e.transpose(PT.t[ps_, 512 + j * 64:512 + (j + 1) * 64], BK.t[ps_, j, c, 1, :], idn[ps_, ps_]),
                             r=[bkb[j], ident.b()], w=[PT.b("A")])
                    for j, pb in HP:
                        ps_ = slice(pb, pb + 64)
                        for kind in range(2):
                            S.op("pe", lambda e: e.matmul(PB[2 + kind].t[ps_, j * 128:(j + 1) * 128], BK.t[ps_, j, c, kind, :],
                                                          KR.t[ps_, j, c, :, :].rearrange("p a b -> p (a b)"), start=True, stop=True),
                                 r=[bkb[j], krb[j]], w=[PB[2 + kind].b()])
                        S.op("pe", lambda e: e.matmul(PB[4].t[ps_, j * 64:(j + 1) * 64], KR.t[ps_, j, c, 0, :], BK.t[ps_, j, c, 0, :],
                                                      start=True, stop=True), r=[bkb[j], krb[j]], w=[PB[4].b()])
                    yield
                    S.op("act", lambda e: e.activation(out=vbk.t[:], in_=PT.t[:, 0:768].rearrange("p (a j c) -> p a j c", a=3, j=4), func=AF.Copy),
                         r=[PT.b("A")], w=[vbk.b()])
                    S.op("dve", lambda e: e.tensor_tensor(out=asb.t[:, :, 0, :], in0=lo(PB[2], 128), in1=M2v, op=ALU.mult),
                         r=[PB[2].b()] + cb, w=[asb.b()])
                    S.op("dve", lambda e: e.tensor_tensor(out=Lsb.t[:], in0=lo(PB[4]), in1=MLv, op=ALU.mult), r=[PB[4].b()] + cb, w=[Lsb.b()])
                    S.op("dve", lambda e: e.scalar_tensor_tensor(out=GT[0].t[:], in0=asb.t[:, :, 0, 0:64], scalar=-1.0, in1=I64v,
                                                                 op0=ALU.mult, op1=ALU.add), r=[asb.b()] + cb, w=[GT[0].b()])
                    S.op("dve", lambda e: e.tensor_tensor(out=asb.t[:, :, 1, :], in0=lo(PB[3], 128), in1=M2v, op=ALU.mult),
                         r=[PB[3].b()] + cb, w=[asb.b()])
                    yield
                    Pc, PTc, bP, bPT = Lsb.t, asb.t[:, :, 0, 0:64], Lsb.b(), asb.b()
                    gi = 0
                    pend = None
                    for lv in range(6):
                        if lv < 5:
                            for j, pb in HP:
                                ps_ = slice(pb, pb + 64)
                                S.op("pe", lambda e: e.matmul(PB[4].t[ps_, j * 64:(j + 1) * 64], PTc[ps_, j, :], Pc[ps_, j, :], start=True, stop=True),
                                     r=[bP, bPT], w=[PB[4].b()])
                            if lv < 4:
                                for j, pb in HP:
                                    ps_ = slice(pb, pb + 64)
                                    S.op("pe", lambda e: e.matmul(PB[5].t[ps_, j * 64:(j + 1) * 64], Pc[ps_, j, :], PTc[ps_, j, :], start=True, stop=True),
                                         r=[bP, bPT], w=[PB[5].b()])
                        if pend is not None:
                            pn2, plv = pend
                            gsrc = GT[gi]
                            gdst = gf if plv == 4 else GT[1 - gi]
                            for j, pb in HP:
                                ps_ = slice(pb, pb + 64)
                                S.op("pe", lambda e: e.matmul(PB[6].t[ps_, j * 64:(j + 1) * 64], pn2.t[ps_, j, :], gsrc.t[ps_, j, :], start=True, stop=True),
                                     r=[pn2.b(), gsrc.b()], w=[PB[6].b()])
                        yield
                        if lv < 5:
                            n2 = P2[lv % 2]
                            S.op("act", lambda e: e.activation(out=n2.t[:], in_=lo(PB[4]), func=AF.Copy), r=[PB[4].b()], w=[n2.b()])
                            if lv < 4:
                                n2t = P2T[lv % 2]
                                S.op("act", lambda e: e.activation(out=n2t.t[:], in_=lo(PB[5]), func=AF.Copy), r=[PB[5].b()], w=[n2t.b()])
                        if pend is not None:
                            S.op("dve", lambda e: e.tensor_tensor(out=gdst.t[:], in0=lo(PB[6]), in1=gsrc.t[:], op=ALU.add),
                                 r=[PB[6].b(), gsrc.b()], w=[gdst.b()])
                            gi = 1 - gi
                            pend = None
                        if lv < 5:
                            pend = (n2, lv)
                            if lv < 4:
                                Pc, PTc, bP, bPT = n2.t, n2t.t, n2.b(), n2t.b()
                            yield

                def stageB1(c):
                    par = c % 2
                    vbk, asb, G = VBK[par], Asb[par], GF[par]
                    ysb, ysq = Ysb[par], Ysq[par]
                    Vt, Bt, Kt = vbk.t[:, 0], vbk.t[:, 1], vbk.t[:, 2]
                    b0, b1 = PB[0].b(), PB[1].b()
                    for j, pb in HP:
                        ps_ = slice(pb, pb + 64)
                        S.op("pe", lambda e: e.matmul(PB[0].t[ps_, j * 64:(j + 1) * 64], KR.t[ps_, j, c, 0, :], Sbf.t[ps_, j, :], start=True, stop=False),
                             r=[krb[j], Sbf.b()], w=[b0])
                        S.op("pe", lambda e: e.matmul(PB[0].t[ps_, j * 64:(j + 1) * 64], asb.t[ps_, j, 1, 0:64], Vt[ps_, j, :], start=False, stop=True),
                             r=[asb.b(), vbk.b()], w=[b0])
                    yield
                    S.op("act", lambda e: e.activation(out=Zsb.t[:], in_=lo(PB[0]), func=AF.Copy), r=[b0], w=[Zsb.b()])
                    yield
                    for j, pb in HP:
                        ps_ = slice(pb, pb + 64)
                        S.op("pe", lambda e: e.matmul(PB[0].t[ps_, j * 64:(j + 1) * 64], G.t[ps_, j, :], Zsb.t[ps_, j, :], start=True, stop=True),
                             r=[G.b(), Zsb.b()], w=[b0])
                    yield
                    S.op("act", lambda e: e.activation(out=Un.t[:], in_=lo(PB[0]), func=AF.Copy, scale=-1.0), r=[b0], w=[Un.b()])
                    yield
                    for j, pb in HP:
                        ps_ = slice(pb, pb + 64)
                        S.op("pe", lambda e: e.matmul(PB[0].t[ps_, j * 64:(j + 1) * 64], Kt[ps_, j, :], Vt[ps_, j, :], start=True, stop=False),
                             r=[vbk.b()], w=[b0])
                        S.op("pe", lambda e: e.matmul(PB[0].t[ps_, j * 64:(j + 1) * 64], Bt[ps_, j, :], Un.t[ps_, j, :], start=False, stop=True),
                             r=[vbk.b(), Un.b()], w=[b0])
                    for j, pb in HP:
                        ps_ = slice(pb, pb + 64)
                        S.op("pe", lambda e: e.matmul(PB[1].t[ps_, j * 64:(j + 1) * 64], KR.t[ps_, j, c, 1, :], Sbf.t[ps_, j, :], start=True, stop=False),
                             r=[krb[j], Sbf.b()], w=[b1])
                        S.op("pe", lambda e: e.matmul(PB[1].t[ps_, j * 64:(j + 1) * 64], asb.t[ps_, j, 1, 64:128], Vt[ps_, j, :], start=False, stop=False),
                             r=[asb.b(), vbk.b()], w=[b1])
                        S.op("pe", lambda e: e.matmul(PB[1].t[ps_, j * 64:(j + 1) * 64], asb.t[ps_, j, 0, 64:128], Un.t[ps_, j, :], start=False, stop=True),
                             r=[asb.b(), Un.b()], w=[b1])
                    yield
                    S.op("dve", lambda e: e.tensor_tensor(out=Sf.t[:], in0=lo(PB[0]), in1=Sf.t[:], op=ALU.add), r=[b0, Sf.b()], w=[Sf.b()])
                    S.op("act", lambda e: e.activation(out=ysb.t[:], in_=lo(PB[1]), func=AF.Copy), r=[b1], w=[ysb.b()])
                    S.op("act", lambda e: e.activation(out=ysq.t[:], in_=lo(PB[1]), func=AF.Square), r=[b1], w=[ysq.b()])
                    yield
                    S.op("dve", lambda e: e.tensor_tensor(out=Sf.t[:], in0=Sf.t[:], in1=WC.t[:, :, c].unsqueeze(2).broadcast_to([128, 4, 64]), op=ALU.mult),
                         r=[Sf.b(), WC.b()], w=[Sf.b()])
                    yield
                    S.op("act", lambda e: e.activation(out=Sbf.t[:], in_=Sf.t[:], func=AF.Copy), r=[Sf.b()], w=[Sbf.b()])

                def stageB2(c):
                    par = c % 2
                    cc = slice(c * 64, (c + 1) * 64)
                    ysb, ysq, ys, yh_ = Ysb[par], Ysq[par], yst[par], yh[par]
                    S.op("dve", lambda e: e.tensor_reduce(out=ys.t[:, 0, :], in_=ysb.t[:], axis=AX.X, op=ALU.add), r=[ysb.b()], w=[ys.b()])
                    S.op("dve", lambda e: e.tensor_reduce(out=ys.t[:, 1, :], in_=ysq.t[:], axis=AX.X, op=ALU.add), r=[ysq.b()], w=[ys.b()])
                    yield
                    S.op("dve", lambda e: e.tensor_scalar(out=ys.t[:, 0, :], in0=ys.t[:, 0, :], scalar1=1.0 / 64, scalar2=None, op0=ALU.mult),
                         r=[ys.b()], w=[ys.b()])
                    yield
                    S.op("dve", lambda e: e.tensor_tensor(out=ys.t[:, 2, :], in0=ys.t[:, 0, :], in1=ys.t[:, 0, :], op=ALU.mult), r=[ys.b()], w=[ys.b()])
                    yield
                    S.op("dve", lambda e: e.scalar_tensor_tensor(out=ys.t[:, 3, :], in0=ys.t[:, 1, :], scalar=1.0 / 64, in1=ys.t[:, 2, :],
                                                                 op0=ALU.mult, op1=ALU.subtract), r=[ys.b()], w=[ys.b()])
                    yield
                    S.op("act", lambda e: e.activation(out=ys.t[:, 3, :], in_=ys.t[:, 3, :], func=AF.Sqrt, bias=eps_gn.t[:, 0:1]), r=[ys.b(), eps_gn.b()], w=[ys.b()])
                    yield
                    S.op("dve", lambda e: e.reciprocal(out=ys.t[:, 3, :], in_=ys.t[:, 3, :]), r=[ys.b()], w=[ys.b()])
                    S.op("dve", lambda e: e.tensor_tensor(out=ysb.t[:], in0=ysb.t[:], in1=ys.t[:, 0, :].unsqueeze(2).broadcast_to([128, 4, 64]), op=ALU.subtract),
                         r=[ysb.b(), ys.b()], w=[ysb.b()])
                    yield
                    S.op("dve", lambda e: e.tensor_tensor(out=yh_.t[:], in0=ysb.t[:], in1=ys.t[:, 3, :].unsqueeze(2).broadcast_to([128, 4, 64]), op=ALU.mult),
                         r=[ysb.b(), ys.b()], w=[yh_.b()])
                    yield
                    for j, pb in HP:
                        ps_ = slice(pb, pb + 64)
                        S.op("pe", lambda e: e.transpose(PT.t[ps_, 768 + j * 64:768 + (j + 1) * 64], yh_.t[ps_, j, :], idn[ps_, ps_]),
                             r=[yh_.b(), ident.b()], w=[PT.b("A")])
                    yield
                    S.op("act", lambda e: e.activation(out=yhT.t[:, :, cc], in_=PT.t[:, 768:1024].rearrange("p (j t) -> p j t", j=4), func=AF.Copy),
                         r=[PT.b("A")], w=[yhT.b()])

                for c in range(NCH + 2):
                    gens = []
                    if 1 <= c <= NCH:
                        gens.append(stageB1(c - 1))
                    if c < NCH:
                        gens.append(stageA(c))
                    if 2 <= c:
                        gens.append(stageB2(c - 2))
                    while gens:
                        for g_ in list(gens):
                            try:
                                next(g_)
                            except StopIteration:
                                gens.remove(g_)
                if stop == 1.6:
                    break
                for j in range(4):
                    S.op("dve", lambda e, j=j: e.tensor_scalar(out=f["x1"].t[:], in0=yhT.t[:, j, :], scalar1=cvec.t[:, C_LNW + j:C_LNW + j + 1],
                                                               scalar2=cvec.t[:, C_LNB + j:C_LNB + j + 1], op0=ALU.mult, op1=ALU.add),
                         r=[yhT.b()] + cb, w=[f["x1"].b()])
                    S.op("dve", lambda e, j=j: e.tensor_tensor(out=f["x1"].t[:], in0=f["x1"].t[:], in1=bonT.t[:, j, :], op=ALU.add),
                         r=[f["x1"].b(), bonT.b(j)], w=[f["x1"].b()])
                    S.op("dve", lambda e, j=j: e.tensor_tensor(out=yT.t[:, j, :], in0=f["x1"].t[:], in1=gT.t[:, j, :], op=ALU.mult),
                         r=[f["x1"].b(), gT.b(j)], w=[yT.b(j)])
                S.dma("sp", dy, yT_d[:, :, t0:t0 + TT], yT.t[:], r=[yT.b(jj) for jj in range(8)], w=[yT_db[st]])
            S.emit()
            if stop <= 2:
                S.barrier()
                S.emit()
                return nc

        rot = [0]

        def nb():
            rot[0] += 1
            return PB[rot[0] % 7]

        with contextlib.ExitStack() as p2:
            S.barrier()
            dw2 = S.dsem()
            wg = load_bf(p2, "wg", 8, 3 * D, dw2)
            wup = load_bf(p2, "wup", 8, D, dw2)
            wo = load_bf(p2, "wo", 8, D, dw2)
            S.seal(dw2, [wg.b(), wup.b(), wo.b()])
            hT2 = [sb(p2, "hT2%d" % i, [128, 8, TT], BF16) for i in range(2)]
            yT2 = [sb(p2, "yT2%d" % i, [128, 8, TT], BF16) for i in range(2)]
            dl2 = [S.dsem() for _ in range(2)]
            gs = [sb(p2, "gs%d" % i, [128, TT]) for i in range(3)]
            mm_ = [sb(p2, "mm%d" % i, [128, TT]) for i in range(3)]
            mg = sb(p2, "mg", [128, 8, TT], BF16)
            xs2 = [sb(p2, "xs2%d" % i, [128, D]) for i in range(2)]
            dx2 = [S.dsem() for _ in range(2)]
            x1s = [sb(p2, "x1s%d" % i, [128, D]) for i in range(2)]
            ds2 = [S.dsem() for _ in range(2)]
            kr = [(0, 4), (4, 6), (6, 8)]
            def ld2(st_):
                i_, c0 = st_ % 2, st_ * TT
                S.dma("sp", dl2[i_], hT2[i_].t[:], hT_d[:, :, c0:c0 + TT], r=[hT_db[st_]], w=[hT2[i_].b()])
                S.dma("sp", dl2[i_], yT2[i_].t[:], yT_d[:, :, c0:c0 + TT], r=[yT_db[st_]], w=[yT2[i_].b()])
                S.seal(dl2[i_], [hT2[i_].b(), yT2[i_].b()])

            ld2(0)
            for st in range(NST):
                t0 = st * TT
                i2 = st % 2
                if st + 1 < NST:
                    ld2(st + 1)
                for fo in range(8):
                    fs = slice(fo * 128, (fo + 1) * 128)
                    for b in range(3):
                        pg, pu = nb(), nb()
                        for kc in range(8):
                            S.op("pe", lambda e: e.matmul(pg.t[:], wg.t[:, kc, b * D + fo * 128:b * D + (fo + 1) * 128], hT2[i2].t[:, kc, :],
                                                          start=(kc == 0), stop=(kc == 7)), r=[wg.b(), hT2[i2].b()], w=[pg.b()])
                        k0, k1 = kr[b]
                        for kc in range(k0, k1):
                            S.op("pe", lambda e: e.matmul(pu.t[:], wup.t[:, kc, fs], yT2[i2].t[:, kc, :], start=(kc == k0), stop=(kc == k1 - 1)),
                                 r=[wup.b(), yT2[i2].b()], w=[pu.b()])
                        S.op("act", lambda e: e.activation(out=gs[b].t[:], in_=pg.t[:], func=AF.Sigmoid,
                                                           bias=cvec.t[:, C_BG + b * 8 + fo:C_BG + b * 8 + fo + 1]), r=[pg.b()] + cb, w=[gs[b].b()])
                        S.op("dve", lambda e: e.tensor_tensor(out=mm_[b].t[:], in0=pu.t[:], in1=gs[b].t[:], op=ALU.mult),
                             r=[pu.b(), gs[b].b()], w=[mm_[b].b()])
                    S.op("pool", lambda e: e.tensor_tensor(out=mm_[0].t[:], in0=mm_[0].t[:], in1=mm_[1].t[:], op=ALU.add),
                         r=[mm_[0].b(), mm_[1].b()], w=[mm_[0].b()])
                    S.op("pool", lambda e: e.tensor_tensor(out=mg.t[:, fo, :], in0=mm_[0].t[:], in1=mm_[2].t[:], op=ALU.add),
                         r=[mm_[0].b(), mm_[2].b()], w=[mg.b()])
                for sub in range(4):
                    i = (st * 4 + sub) % 2
                    r0 = t0 + sub * 128
                    S.dma("sp", dx2[i], xs2[i].t[:], x_d[r0:r0 + 128, :], w=[xs2[i].b()])
                    for half in range(2):
                        pbk = nb()
                        for kc in range(8):
                            S.op("pe", lambda e: e.matmul(pbk.t[:], mg.t[:, kc, sub * 128:(sub + 1) * 128], wo.t[:, kc, half * 512:(half + 1) * 512],
                                                          start=(kc == 0), stop=(kc == 7)), r=[mg.b(), wo.b()], w=[pbk.b()])
                        S.op("dve", lambda e: e.tensor_tensor(out=x1s[i].t[:, half * 512:(half + 1) * 512], in0=pbk.t[:],
                                                              in1=xs2[i].t[:, half * 512:(half + 1) * 512], op=ALU.add),
                             r=[pbk.b(), xs2[i].b()], w=[x1s[i].b()])
                    S.dma("sp", ds2[i], x1_d[r0:r0 + 128, :], x1s[i].t[:], r=[x1s[i].b()], w=[x1_db[st * 4 + sub]])
            S.emit()
            if stop == 3:
                S.barrier()
                S.emit()
                return nc

        with contextlib.ExitStack() as p3:
            S.barrier()
            dw3 = S.dsem()
            wfi = load_bf(p3, "wfi", 8, 2 * DFF, dw3)
            wfo = load_bf(p3, "wfo", NFC, D, dw3)
            gfin = sb(p3, "gfin", [128, D])
            dgf = S.dsem()
            S.dma("sp", dgf, gfin.t[:], gfin_d[:, :], w=[gfin.b()])
            S.seal(dw3, [wfi.b(), wfo.b()])
            x1k = sb(p3, "x1k", [128, 4, D])
            dk = [S.dsem() for _ in range(4)]
            h2T = sb(p3, "h2T", [128, 8, TT], BF16)
            ub = [sb(p3, "ub%d" % i, [128, 2 + TT]) for i in range(2)]
            ucar = sb(p3, "ucar", [128, NFC, 2])
            c1 = [sb(p3, "c1%d" % i, [128, TT]) for i in range(2)]
            actT = sb(p3, "actT", [128, NFC, TT], BF16)
            x2s = [sb(p3, "x2s%d" % i, [128, D]) for i in range(2)]
            do = [S.dsem() for _ in range(2)]
            fst = sb(p3, "fst", [128, 2])
            S.op("dve", lambda e: e.memset(ucar.t[:], 0.0), w=[ucar.b()])
            dxr = [S.dsem() for _ in range(2)]

            def ld3(st_):
                for sub_ in range(4):
                    q0 = st_ * TT + sub_ * 128
                    S.dma("sp", dk[sub_], x1k.t[:, sub_, :], x1_d[q0:q0 + 128, :], r=[x1_db[st_ * 4 + sub_]], w=[x1k.b(sub_)])

            ld3(0)
            for st in range(NST):
                t0 = st * TT
                for sub in range(4):
                    norm_T(x1k.t[:, sub, :], x1k.b(sub), C_G2, h2T.t, h2T.b(), sub * 128)
                if st + 1 < NST:
                    ld3(st + 1)
                for fc in range(NFC):
                    pu, pgv = nb(), nb()
                    u_, c_ = ub[fc % 2], c1[fc % 2]
                    for half, pbk in ((0, pu), (1, pgv)):
                        for kc in range(8):
                            S.op("pe", lambda e: e.matmul(pbk.t[:], wfi.t[:, kc, half * DFF + fc * 128:half * DFF + (fc + 1) * 128], h2T.t[:, kc, :],
                                                          start=(kc == 0), stop=(kc == 7)), r=[wfi.b(), h2T.b()], w=[pbk.b()])
                    cw = lambda jx: cvec.t[:, C_CW + jx * NFC + fc:C_CW + jx * NFC + fc + 1]
                    S.op("act", lambda e: e.activation(out=u_.t[:, 2:2 + TT], in_=pu.t[:], func=AF.Copy), r=[pu.b()], w=[u_.b()])
                    S.op("act", lambda e: e.activation(out=c_.t[:, 2:TT], in_=pu.t[:, 0:TT - 2], func=AF.Copy, scale=cw(0)), r=[pu.b()] + cb, w=[c_.b()])
                    S.op("pool", lambda e: e.tensor_copy(out=u_.t[:, 0:2], in_=ucar.t[:, fc, :]), r=[ucar.b()], w=[u_.b()])
                    S.op("pool", lambda e: e.tensor_scalar(out=c_.t[:, 0:2], in0=ucar.t[:, fc, :], scalar1=cw(0), scalar2=None, op0=ALU.mult),
                         r=[ucar.b()] + cb, w=[c_.b()])
                    S.op("pool", lambda e: e.tensor_copy(out=ucar.t[:, fc, :], in_=u_.t[:, TT:TT + 2]), r=[u_.b()], w=[ucar.b()])
                    S.op("dve", lambda e: e.scalar_tensor_tensor(out=c_.t[:], in0=u_.t[:, 1:1 + TT], scalar=cw(1), in1=c_.t[:], op0=ALU.mult, op1=ALU.add),
                         r=[u_.b(), c_.b()] + cb, w=[c_.b()])
                    S.op("dve", lambda e: e.scalar_tensor_tensor(out=c_.t[:], in0=u_.t[:, 2:2 + TT], scalar=cw(2), in1=c_.t[:], op0=ALU.mult, op1=ALU.add),
                         r=[u_.b(), c_.b()] + cb, w=[c_.b()])
                    S.op("act", lambda e: e.activation(out=c_.t[:], in_=c_.t[:], func=AF.Gelu, bias=cvec.t[:, C_CB + fc:C_CB + fc + 1]),
                         r=[c_.b()] + cb, w=[c_.b()])
                    S.op("dve", lambda e: e.tensor_tensor(out=actT.t[:, fc, :], in0=pgv.t[:], in1=c_.t[:], op=ALU.mult),
                         r=[pgv.b(), c_.b()], w=[actT.b()])
                for sub in range(4):
                    i = (st * 4 + sub) % 2
                    r0 = t0 + sub * 128
                    S.dma("sp", dxr[i], x2s[i].t[:], x1_d[r0:r0 + 128, :], r=[x1_db[st * 4 + sub]], w=[x2s[i].b()])
                    for half in range(2):
                        pbk = nb()
                        for fc in range(NFC):
                            S.op("pe", lambda e: e.matmul(pbk.t[:], actT.t[:, fc, sub * 128:(sub + 1) * 128], wfo.t[:, fc, half * 512:(half + 1) * 512],
                                                          start=(fc == 0), stop=(fc == NFC - 1)), r=[actT.b(), wfo.b()], w=[pbk.b()])
                        S.op("dve", lambda e: e.tensor_tensor(out=x2s[i].t[:, half * 512:(half + 1) * 512], in0=pbk.t[:],
                                                              in1=x2s[i].t[:, half * 512:(half + 1) * 512], op=ALU.add),
                             r=[pbk.b(), x2s[i].b()], w=[x2s[i].b()])
                    ss, ms = fst.t[:, 0:1], fst.t[:, 1:2]
                    S.op("act", lambda e: e.activation(out=junk.t[:], in_=x2s[i].t[:], func=AF.Square, accum_out=ss),
                         r=[x2s[i].b(), fst.b()], w=[junk.b(), fst.b()])
                    S.op("dve", lambda e: e.tensor_scalar(out=ms, in0=ss, scalar1=1.0 / D, scalar2=1e-6, op0=ALU.mult, op1=ALU.add), r=[fst.b()], w=[fst.b()])
                    S.op("act", lambda e: e.activation(out=ms, in_=ms, func=AF.Sqrt), r=[fst.b()], w=[fst.b()])
                    S.op("dve", lambda e: e.reciprocal(out=ms, in_=ms), r=[fst.b()], w=[fst.b()])
                    S.op("act", lambda e: e.activation(out=x2s[i].t[:], in_=x2s[i].t[:], func=AF.Copy, scale=ms), r=[x2s[i].b(), fst.b()], w=[x2s[i].b()])
                    S.op("dve", lambda e: e.tensor_tensor(out=x2s[i].t[:], in0=x2s[i].t[:], in1=gfin.t[:], op=ALU.mult),
                         r=[x2s[i].b(), gfin.b()], w=[x2s[i].b()])
                    S.dma("sp", do[i], out_d[r0:r0 + 128, :], x2s[i].t[:], r=[x2s[i].b()])
            S.wait_all("sp", [(d[0], d[1], None) for d in do])
            S.emit()
    return nc


def _cols(v, n):
    return np.ascontiguousarray(np.asarray(v, np.float32).reshape(n, 128).T)


def _host_consts():
    cm = np.zeros((128, NCM), np.float32)
    s = np.arange(128)[:, None] % 64
    t = np.arange(64)[None, :]
    cm[:, M_M2:M_M2 + 64] = (s < t)
    cm[:, M_M2 + 64:M_M2 + 128] = (s <= t)
    tt = np.arange(128)[:, None] % 64
    ss = np.arange(64)[None, :]
    cm[:, M_ML:M_ML + 64] = (tt > ss)
    cm[:, M_I64:M_I64 + 64] = (tt == ss)
    sc = np.ones(512, np.float32)
    sc[::64] = 0.0
    cm[:, M_SCAN:M_SCAN + 512] = sc[None, :]
    wins = [2, 4, 8, 16]
    for g in range(4):
        ci, pb = g // 2, 64 * (g % 2)
        pos = np.arange(1, 513)
        cm[pb:pb + 64, M_INVC0 + ci * 512:M_INVC0 + (ci + 1) * 512] = (1.0 / np.minimum(pos, wins[g]))[None, :]
        cm[pb:pb + 64, M_INVC + ci * 512:M_INVC + (ci + 1) * 512] = 1.0 / wins[g]
    cmb = np.zeros((128, 320), np.float32)
    cmb[:, 0:128] = np.eye(128)
    cmb[0:64, 128:192] = 1.0
    cmb[64:128, 192:256] = 1.0
    cmb[:, 256:320] = 1.0
    return cm, cmb


_NC_CACHE = {}


def _prep(inputs):
    g = lambda k: np.asarray(inputs[k], np.float32)
    cv = np.zeros((128, NCV), np.float32)
    cv[:, C_G1:C_G1 + 8] = _cols(g("norm_mix_g")[0], 8)
    cv[:, C_MU:C_MU + 14] = _cols(g("mu_shift")[0], 14)
    cv[:, C_W0:C_W0 + 4] = _cols(g("w0")[0], 4)
    cv[:, C_A0:C_A0 + 4] = _cols(g("a0")[0], 4)
    cv[:, C_KK:C_KK + 4] = _cols(g("k_k")[0], 4)
    cv[:, C_KA:C_KA + 4] = _cols(g("k_a")[0], 4)
    cv[:, C_RK:C_RK + 4] = _cols(g("r_k")[0].reshape(-1), 4)
    cv[:, C_LNW:C_LNW + 4] = _cols(g("ln_x_w")[0], 4)
    cv[:, C_LNB:C_LNB + 4] = _cols(g("ln_x_b")[0], 4)
    cv[:, C_PS:C_PS + 2] = _cols(g("pool_scale")[0], 2)
    cv[:, C_BG:C_BG + 24] = _cols(g("b_gate")[0], 24)
    cv[:, C_G2:C_G2 + 8] = _cols(g("norm_ffn_g")[0], 8)
    for j in range(3):
        cv[:, C_CW + j * NFC:C_CW + (j + 1) * NFC] = _cols(g("ffn_conv_w")[0, j], NFC)
    cv[:, C_CB:C_CB + NFC] = _cols(g("ffn_conv_b")[0], NFC)
    cv[:, C_GM:C_GM + 8] = _cols(g("norm_mem_g")[0], 8)
    cm, cmb = _host_consts()
    shared = {
        "w_in_mix": g("w_in_mix")[0], "w_lora_b": g("w_lora_b")[0], "a_lora_b": g("a_lora_b")[0], "g_lora_b": g("g_lora_b")[0],
        "pool_w": g("pool_w")[0], "w_mem_kv": g("w_mem_kv")[0], "w_up_rwkv": g("w_up_rwkv")[0], "w_up_pool": g("w_up_pool")[0],
        "w_up_mem": g("w_up_mem")[0], "w_gate": g("w_gate")[0], "w_o": g("w_o")[0], "w_ffn_in": g("w_ffn_in")[0],
        "w_ffn_out": g("w_ffn_out")[0], "cvec": cv, "cm32": cm, "cmb": cmb,
        "gfin": np.ascontiguousarray(np.broadcast_to(g("norm_final_g")[None, :], (128, D))),
    }
    shared = {k: np.ascontiguousarray(v, dtype=np.float32) for k, v in shared.items()}
    x = g("x")
    mem = g("mem")
    return [dict(shared, x=np.ascontiguousarray(x[b]), mem=np.ascontiguousarray(mem[b])) for b in range(8)]


def kernel(**inputs):
    in_maps = _prep(inputs)
    if "nc" not in _NC_CACHE:
        _NC_CACHE["nc"] = build(False)
    res = run_bass_kernel_spmd(_NC_CACHE["nc"], in_maps, core_ids=list(range(8)))
    return np.stack([np.asarray(r["out"], np.float32) for r in res.results], axis=0)
```

```python
import numpy as np
import contextlib
import concourse.bass as bass
import concourse.mybir as mybir
from concourse.bass_utils import run_bass_kernel_spmd

F32 = mybir.dt.float32
BF16 = mybir.dt.bfloat16
AF = mybir.ActivationFunctionType
ALU = mybir.AluOpType
AX = mybir.AxisListType

D = 1024
T = 4096
TT = 512
NST = T // TT
NCH = TT // 64
DFF = 2816
NFC = DFF // 128
MIX_IN = 2304
NEG_EH = -float(np.exp(-0.5))

C_G1, C_MU, C_W0, C_A0, C_KK, C_KA, C_RK, C_LNW, C_LNB, C_PS, C_BG, C_G2, C_CW, C_CB, C_GM = (
    0, 8, 22, 26, 30, 34, 38, 42, 46, 50, 52, 76, 84, 150, 172)
NCV = 180
M_M2, M_ML, M_I64, M_SCAN, M_INVC0, M_INVC = 0, 128, 192, 256, 768, 1792
NCM = 2816


class Buf:
    __slots__ = ("w", "r")

    def __init__(self):
        self.w = None
        self.r = {}


class Rec:
    def __getattr__(self, name):
        def f(*a, **k):
            self.call = (name, a, k)
            return self
        return f


class Eng:
    def __init__(self, name, sem, is_pe=False):
        self.name = name
        self.sem = sem
        self.count = 0
        self.seen = {}
        self.prog = []
        self.is_pe = is_pe


class Sched:
    def __init__(self, nc, es):
        self.nc = nc
        self.E = {}
        for n in ("pe", "act", "dve", "pool", "sp"):
            self.E[n] = Eng(n, es.enter_context(nc.semaphore("s_" + n)), is_pe=(n == "pe"))
        self.es = es
        self.ndsem = 0
        self.dsems = []

    def dsem(self):
        self.ndsem += 1
        ds = [self.es.enter_context(self.nc.semaphore("d%d" % self.ndsem)), 0]
        self.dsems.append(ds)
        return ds

    def barrier(self):
        for E in self.E.values():
            for X in self.E.values():
                if X is not E and X.count > 0 and E.seen.get(id(X.sem), 0) < X.count:
                    E.seen[id(X.sem)] = X.count
                    E.prog.append(("w", X.sem, X.count))
            for ds in self.dsems:
                if ds[1] > 0 and E.seen.get(id(ds[0]), 0) < ds[1]:
                    E.seen[id(ds[0])] = ds[1]
                    E.prog.append(("w", ds[0], ds[1]))

    def _deps(self, E, reads, writes):
        need = {}

        def add(tok, raw):
            sem, val, eng = tok
            if eng is E and E.is_pe:
                return
            k = id(sem)
            if k not in need or need[k][1] < val:
                need[k] = (sem, val)

        for b in reads:
            if b.w is not None:
                add(b.w, True)
        for b in writes:
            if b.w is not None:
                add(b.w, False)
            for t in b.r.values():
                add(t, False)
        for k, (sem, val) in need.items():
            if E.seen.get(k, 0) >= val:
                continue
            E.seen[k] = val
            E.prog.append(("w", sem, val))

    def _commit(self, tok, reads, writes):
        for b in writes:
            b.w = tok
            b.r = {}
        k = id(tok[0])
        for b in reads:
            b.r[k] = tok

    def op(self, en, fn, r=(), w=()):
        E = self.E[en]
        self._deps(E, r, w)
        E.count += 1
        rec = Rec()
        fn(rec)
        E.prog.append(("i", rec.call, E.sem, 1))
        self._commit((E.sem, E.count, E), r, w)

    def dma(self, en, ds, out, in_, r=(), w=()):
        E = self.E[en]
        self._deps(E, r, w)
        ds[1] += 16
        E.prog.append(("i", ("dma_start", (), dict(out=out, in_=in_)), ds[0], 16))
        self._commit((ds[0], ds[1], None), r, w)

    def seal(self, ds, bufs):
        for b in bufs:
            b.w = (ds[0], ds[1], None)

    def wait_all(self, en, toks):
        E = self.E[en]
        for sem, val, _ in toks:
            E.prog.append(("w", sem, val))

    def emit(self):
        nc = self.nc
        progs = {n: e.prog for n, e in self.E.items()}
        for e in self.E.values():
            e.prog = []

        def run(eng, prog):
            for it in prog:
                if it[0] == "w":
                    eng.wait_ge(it[1], it[2])
                else:
                    name, a, k = it[1]
                    getattr(eng, name)(*a, **k).then_inc(it[2], it[3])

        with nc.Block() as block:
            @block.tensor
            def _(e):
                run(e, progs["pe"])

            @block.scalar
            def _(e):
                run(e, progs["act"])

            @block.vector
            def _(e):
                run(e, progs["dve"])

            @block.gpsimd
            def _(e):
                run(e, progs["pool"])

            @block.sync
            def _(e):
                run(e, progs["sp"])


class Tl:
    def __init__(self, t):
        self.t = t
        self.bufs = {}

    def b(self, key=0):
        if key not in self.bufs:
            self.bufs[key] = Buf()
        return self.bufs[key]


def build(debug=False, stop=99):
    nc = bass.Bass("TRN2", target_bir_lowering=False)
    din = lambda n, s, dt=F32: nc.dram_tensor(n, s, dt, kind="ExternalInput").ap()
    x_d = din("x", [T, D])
    mem_d = din("mem", [256, D])
    win_d = din("w_in_mix", [D, MIX_IN])
    wl_d = din("w_lora_b", [64, 512])
    al_d = din("a_lora_b", [64, 512])
    gl_d = din("g_lora_b", [128, 512])
    pw_d = din("pool_w", [4, 64, 64])
    wkv_d = din("w_mem_kv", [D, 512])
    wupr_d = din("w_up_rwkv", [512, D])
    wupp_d = din("w_up_pool", [256, D])
    wupm_d = din("w_up_mem", [256, D])
    wg_d = din("w_gate", [D, 3 * D])
    wo_d = din("w_o", [D, D])
    wfi_d = din("w_ffn_in", [D, 2 * DFF])
    wfo_d = din("w_ffn_out", [DFF, D])
    cvec_d = din("cvec", [128, NCV])
    cm32_d = din("cm32", [128, NCM])
    cmb_d = din("cmb", [128, 320])
    gfin_d = din("gfin", [128, D])
    skind = "ExternalOutput" if debug else "Internal"
    hT_d = nc.dram_tensor("hT_d", [128, 8, T], BF16, kind=skind).ap()
    yT_d = nc.dram_tensor("yT_d", [128, 8, T], BF16, kind=skind).ap()
    x1_d = nc.dram_tensor("x1_d", [T, D], F32, kind=skind).ap()
    out_d = nc.dram_tensor("out", [T, D], F32, kind="ExternalOutput").ap()
    wsc = {n: nc.dram_tensor(n + "_bf", shp, BF16, kind="Internal").ap() for n, shp in
           (("wg", [D, 3 * D]), ("wup", [D, D]), ("wo", [D, D]), ("wfi", [D, 2 * DFF]), ("wfo", [DFF, D]))}
    wsc_b = {n: Buf() for n in wsc}
    hT_db = [Buf() for _ in range(NST)]
    yT_db = [Buf() for _ in range(NST)]
    x1_db = [Buf() for _ in range(NST * 4)]

    with contextlib.ExitStack() as top:
        S = Sched(nc, top)
        sb = lambda es, n, s, dt=F32: Tl(es.enter_context(nc.sbuf_tensor("sb_" + n, s, dt)))
        PB = [Tl(top.enter_context(nc.psum_tensor("pb%d" % i, [128, 512], F32))) for i in range(7)]
        PT = Tl(top.enter_context(nc.psum_tensor("pt", [128, 1024], BF16)))
        cvec = sb(top, "cvec", [128, NCV])
        cder = sb(top, "cder", [128, 18])
        ident = sb(top, "ident", [128, 320], BF16)
        junk = sb(top, "junk", [128, D], BF16)
        hb = sb(top, "hb", [128, D], BF16)
        st4 = sb(top, "st4", [128, 4])
        dconst = S.dsem()
        S.dma("sp", dconst, cvec.t[:], cvec_d[:, :], w=[cvec.b()])
        dconst2 = S.dsem()
        S.dma("pool", dconst2, ident.t[:], cmb_d[:, :], w=[ident.b()])
        S.op("dve", lambda e: e.tensor_scalar(out=cder.t[:, 0:14], in0=cvec.t[:, C_MU:C_MU + 14], scalar1=-1.0, scalar2=1.0,
                                              op0=ALU.mult, op1=ALU.add), r=[cvec.b()], w=[cder.b()])
        S.op("dve", lambda e: e.tensor_scalar(out=cder.t[:, 14:18], in0=cvec.t[:, C_KA:C_KA + 4], scalar1=-1.0, scalar2=1.0,
                                              op0=ALU.mult, op1=ALU.add), r=[cvec.b()], w=[cder.b()])
        idn = ident.t[:, 0:128]
        bdo = ident.t[:, 128:256]
        ones64 = ident.t[:, 256:320]
        cb = [cvec.b(), cder.b(), ident.b()]

        def norm_T(xs_ap, bx, gcol, hT, bh, col0, npart=128):
            ss, ms = st4.t[0:npart, 0:1], st4.t[0:npart, 1:2]
            S.op("act", lambda e: e.activation(out=junk.t[0:npart, :], in_=xs_ap, func=AF.Square, accum_out=ss),
                 r=[bx, st4.b()], w=[junk.b(), st4.b()])
            S.op("dve", lambda e: e.tensor_scalar(out=ms, in0=ss, scalar1=1.0 / D, scalar2=1e-6, op0=ALU.mult, op1=ALU.add),
                 r=[st4.b()], w=[st4.b()])
            S.op("act", lambda e: e.activation(out=ms, in_=ms, func=AF.Sqrt), r=[st4.b()], w=[st4.b()])
            S.op("dve", lambda e: e.reciprocal(out=ms, in_=ms), r=[st4.b()], w=[st4.b()])
            S.op("dve", lambda e: e.tensor_scalar(out=hb.t[0:npart, :], in0=xs_ap, scalar1=ms, scalar2=None, op0=ALU.mult),
                 r=[bx, st4.b()], w=[hb.b()])
            for c in range(8):
                S.op("pe", lambda e, c=c: e.transpose(PT.t[:, c * 128:c * 128 + npart], hb.t[0:npart, c * 128:(c + 1) * 128],
                                                      idn[0:npart, 0:npart]), r=[hb.b(), ident.b()], w=[PT.b("A"), PT.b("B")])
            pv = PT.t[:, :].rearrange("p (c t) -> p c t", c=8)[:, :, 0:npart]
            gv = cvec.t[:, gcol:gcol + 8].unsqueeze(2).broadcast_to([128, 8, npart])
            S.op("dve", lambda e: e.tensor_tensor(out=hT[:, :, col0:col0 + npart], in0=pv, in1=gv, op=ALU.mult),
                 r=[PT.b("A"), PT.b("B"), cvec.b()], w=[bh])

        def load_w(es, name, wd, kchunks, ncols, ds, row0=0, tile=None, kc0=0):
            if tile is None:
                tile = sb(es, name, [128, kchunks, ncols], BF16)
            step = 1024
            for kc in range(kchunks):
                for n0 in range(0, ncols, step):
                    n1 = min(ncols, n0 + step)
                    S.dma("pool", ds, tile.t[:, kc0 + kc, n0:n1], wd[row0 + kc * 128:row0 + (kc + 1) * 128, n0:n1], w=[tile.b()])
            return tile

        def stage_w(ds, name, wd, nrows, ncols, row0=0):
            for r0 in range(0, nrows, 128):
                for n0 in range(0, ncols, 2048):
                    n1 = min(ncols, n0 + 2048)
                    S.dma("pool", ds, wsc[name][row0 + r0:row0 + r0 + 128, n0:n1], wd[r0:r0 + 128, n0:n1], w=[wsc_b[name]])

        def load_bf(es, name, kchunks, ncols, ds):
            tile = sb(es, name, [128, kchunks, ncols], BF16)
            for kc in range(kchunks):
                S.dma("sp", ds, tile.t[:, kc, :], wsc[name][kc * 128:(kc + 1) * 128, :], r=[wsc_b[name]], w=[tile.b()])
            return tile

        with contextlib.ExitStack() as p1:
            dw1 = S.dsem()
            cm = sb(p1, "cm", [128, NCM])
            dcm = S.dsem()
            S.dma("sp", dcm, cm.t[:], cm32_d[:, :], w=[cm.b()])
            cb = cb + [cm.b()]
            win = load_w(p1, "win", win_d, 8, MIX_IN, dw1)
            lora = sb(p1, "lora", [128, 512], BF16)
            S.dma("pool", dw1, lora.t[0:64, :], wl_d[:, :], w=[lora.b()])
            S.dma("pool", dw1, lora.t[64:128, :], al_d[:, :], w=[lora.b()])
            gl = sb(p1, "gl", [128, 512], BF16)
            S.dma("pool", dw1, gl.t[:], gl_d[:, :], w=[gl.b()])
            pw = sb(p1, "pw", [128, 2, 64], BF16)
            for g in range(4):
                S.dma("pool", dw1, pw.t[64 * (g % 2):64 * (g % 2) + 64, g // 2, :], pw_d[g, :, :], w=[pw.b()])
            kT = sb(p1, "kT", [128, 2, 256], BF16)
            vtok = sb(p1, "vtok", [128, 2, 256], BF16)
            with contextlib.ExitStack() as p0:
                wkv = load_w(p0, "wkv", wkv_d, 8, 512, dw1)
                S.seal(dw1, [win.b(), lora.b(), gl.b(), pw.b(), wkv.b()])
                mems = sb(p0, "mems", [128, 2, D])
                memT = sb(p0, "memT", [128, 8, 256], BF16)
                dm = S.dsem()
                for mh in range(2):
                    S.dma("sp", dm, mems.t[:, mh, :], mem_d[mh * 128:(mh + 1) * 128, :], w=[mems.b(mh)])
                S.seal(dm, [mems.b(0), mems.b(1)])
                for mh in range(2):
                    norm_T(mems.t[:, mh, :], mems.b(mh), C_GM, memT.t, memT.b(), mh * 128)
                for fc in range(2):
                    for kc in range(8):
                        S.op("pe", lambda e, fc=fc, kc=kc: e.matmul(PB[0].t[:, 0:256], wkv.t[:, kc, fc * 128:(fc + 1) * 128],
                                                                     memT.t[:, kc, :], start=(kc == 0), stop=(kc == 7)),
                             r=[wkv.b(), memT.b()], w=[PB[0].b()])
                    S.op("act", lambda e, fc=fc: e.activation(out=kT.t[:, fc, :], in_=PB[0].t[:, 0:256], func=AF.Copy),
                         r=[PB[0].b()], w=[kT.b()])
                for mh in range(2):
                    for kc in range(8):
                        S.op("pe", lambda e, mh=mh, kc=kc: e.matmul(PB[1].t[:, 0:256], memT.t[:, kc, mh * 128:(mh + 1) * 128],
                                                                     wkv.t[:, kc, 256:512], start=(kc == 0), stop=(kc == 7)),
                             r=[wkv.b(), memT.b()], w=[PB[1].b()])
                    S.op("act", lambda e, mh=mh: e.activation(out=vtok.t[:, mh, :], in_=PB[1].t[:, 0:256], func=AF.Copy),
                         r=[PB[1].b()], w=[vtok.b()])
                S.emit()
            S.barrier()
            if stop == 0:
                S.emit()
                return nc
            dstg = S.dsem()
            stage_w(dstg, "wg", wg_d, D, 3 * D)
            stage_w(dstg, "wup", wupr_d, 512, D, row0=0)
            stage_w(dstg, "wup", wupp_d, 256, D, row0=512)
            stage_w(dstg, "wup", wupm_d, 256, D, row0=768)
            stage_w(dstg, "wo", wo_d, D, D)
            stage_w(dstg, "wfi", wfi_d, D, 2 * DFF)
            stage_w(dstg, "wfo", wfo_d, DFF, D)
            S.seal(dstg, list(wsc_b.values()))

            xs = [sb(p1, "xs%d" % i, [128, D]) for i in range(2)]
            dxs = [S.dsem() for _ in range(2)]
            hT = sb(p1, "hT", [128, 8, TT], BF16)
            dh = S.dsem()
            pm = sb(p1, "pm", [128, 14, TT], BF16)
            tmp = sb(p1, "tmp", [128, TT])
            cy = sb(p1, "cy", [128, 14])
            pp = sb(p1, "pp", [128, 2, 16 + TT])
            ppa = sb(p1, "ppa", [128, 16 + TT])
            ppb = sb(p1, "ppb", [128, 16 + TT])
            dT = sb(p1, "dT", [128, 2, TT], BF16)
            qT = sb(p1, "qT", [128, 2, TT], BF16)
            tw = sb(p1, "tw", [128, TT], BF16)
            sg = sb(p1, "sg", [128, TT], BF16)
            fn = ["ld", "cum", "cx", "E1", "E2", "E3", "a", "kk", "rs", "kkn", "x1", "kf"]
            f = {n: sb(p1, "f_" + n, [128, TT]) for n in fn}
            kk2 = sb(p1, "kk2", [128, TT], BF16)
            KR = sb(p1, "KR", [128, 4, NCH, 2, 64], BF16)
            BK = sb(p1, "BK", [128, 4, NCH, 2, 64], BF16)
            WC = sb(p1, "WC", [128, 4, NCH])
            bonT = sb(p1, "bonT", [128, 4, TT], BF16)
            gT = sb(p1, "gT", [128, 4, TT], BF16)
            Asb = [sb(p1, "Asb%d" % i, [128, 4, 2, 128], BF16) for i in range(2)]
            GF = [sb(p1, "GF%d" % i, [128, 4, 64], BF16) for i in range(2)]
            Lsb = sb(p1, "Lsb", [128, 4, 64], BF16)
            GT = [sb(p1, "GT%d" % i, [128, 4, 64], BF16) for i in range(2)]
            P2 = [sb(p1, "P2%d" % i, [128, 4, 64], BF16) for i in range(2)]
            P2T = [sb(p1, "P2T%d" % i, [128, 4, 64], BF16) for i in range(2)]
            Zsb = sb(p1, "Zsb", [128, 4, 64], BF16)
            Un = sb(p1, "Un", [128, 4, 64], BF16)
            VBK = [sb(p1, "VBK%d" % i, [128, 3, 4, 64], BF16) for i in range(2)]
            Ysb = [sb(p1, "Ysb%d" % i, [128, 4, 64]) for i in range(2)]
            Ysq = [sb(p1, "Ysq%d" % i, [128, 4, 64]) for i in range(2)]
            yh = [sb(p1, "yh%d" % i, [128, 4, 64], BF16) for i in range(2)]
            yst = [sb(p1, "yst%d" % i, [128, 4, 4]) for i in range(2)]
            eps_gn = sb(p1, "eps_gn", [128, 1])
            Sf = sb(p1, "Sf", [128, 4, 64])
            Sbf = sb(p1, "Sbf", [128, 4, 64], BF16)
            yhT = sb(p1, "yhT", [128, 4, TT], BF16)
            yT = sb(p1, "yT", [128, 8, TT], BF16)
            dy = S.dsem()
            eT = sb(p1, "eT", [128, 2, TT], BF16)
            rden = sb(p1, "rden", [128, TT])

            S.op("dve", lambda e: e.memset(cy.t[:], 0.0), w=[cy.b()])
            S.op("dve", lambda e: e.memset(pp.t[:], 0.0), w=[pp.b()])
            S.op("dve", lambda e: e.memset(ppa.t[:], 0.0), w=[ppa.b(0), ppa.b(64)])
            S.op("dve", lambda e: e.memset(ppb.t[:], 0.0), w=[ppb.b(0), ppb.b(64)])
            S.op("dve", lambda e: e.memset(Sf.t[:], 0.0), w=[Sf.b()])
            S.op("dve", lambda e: e.memset(Sbf.t[:], 0.0), w=[Sbf.b()])
            S.op("dve", lambda e: e.memset(eps_gn.t[:], 64e-5), w=[eps_gn.b()])
            M2v = cm.t[:, M_M2:M_M2 + 128].unsqueeze(1).broadcast_to([128, 4, 128])
            MLv = cm.t[:, M_ML:M_ML + 64].unsqueeze(1).broadcast_to([128, 4, 64])
            I64v = cm.t[:, M_I64:M_I64 + 64].unsqueeze(1).broadcast_to([128, 4, 64])
            scanm = cm.t[:, M_SCAN:M_SCAN + 512]
            mmb = 0

            for st in range(NST if stop >= 2 else 1):
                t0 = st * TT
                def ldx(idx):
                    ii = idx % 2
                    S.dma("sp", dxs[ii], xs[ii].t[:], x_d[idx * 128:(idx + 1) * 128, :], w=[xs[ii].b()])

                if st == 0:
                    ldx(0)
                for sub in range(4):
                    idx = st * 4 + sub
                    i = idx % 2
                    if idx + 1 < 4 * (NST if stop >= 2 else 1):
                        ldx(idx + 1)
                    norm_T(xs[i].t[:], xs[i].b(), C_G1, hT.t, hT.b(), sub * 128)
                S.dma("sp", dh, hT_d[:, :, t0:t0 + TT], hT.t[:], r=[hT.b()], w=[hT_db[st]])
                if stop == 1.1:
                    break
                for oc in range(18):
                    pbk = PB[mmb % 2]
                    mmb += 1
                    for kc in range(8):
                        S.op("pe", lambda e, oc=oc, kc=kc, pbk=pbk: e.matmul(pbk.t[:], win.t[:, kc, oc * 128:(oc + 1) * 128], hT.t[:, kc, :],
                                                                            start=(kc == 0), stop=(kc == 7)),
                             r=[win.b(), hT.b()], w=[pbk.b()])
                    ps = pbk.t
                    if oc < 14:
                        mu = cvec.t[:, C_MU + oc:C_MU + oc + 1]
                        om = cder.t[:, oc:oc + 1]
                        S.op("act", lambda e, ps=ps, mu=mu: e.activation(out=tmp.t[:, 1:TT], in_=ps[:, 0:TT - 1], func=AF.Copy, scale=mu),
                             r=[pbk.b()] + cb, w=[tmp.b()])
                        S.op("dve", lambda e, oc=oc, mu=mu: e.tensor_scalar(out=tmp.t[:, 0:1], in0=cy.t[:, oc:oc + 1], scalar1=mu, scalar2=None,
                                                                           op0=ALU.mult), r=[cy.b()] + cb, w=[tmp.b()])
                        S.op("dve", lambda e, oc=oc, ps=ps: e.tensor_copy(out=cy.t[:, oc:oc + 1], in_=ps[:, TT - 1:TT]), r=[pbk.b()], w=[cy.b()])
                        S.op("dve", lambda e, oc=oc, ps=ps, om=om: e.scalar_tensor_tensor(out=pm.t[:, oc, :], in0=ps[:, :], scalar=om, in1=tmp.t[:, :],
                                                                                          op0=ALU.mult, op1=ALU.add),
                             r=[pbk.b(), tmp.b()] + cb, w=[pm.b(oc)])
                    elif oc < 16:
                        S.op("act", lambda e, oc=oc, ps=ps: e.activation(out=pp.t[:, oc - 14, 16:16 + TT], in_=ps[:, :], func=AF.Copy),
                             r=[pbk.b()], w=[pp.b()])
                    else:
                        S.op("act", lambda e, oc=oc, ps=ps: e.activation(out=qT.t[:, oc - 16, :], in_=ps[:, :], func=AF.Copy),
                             r=[pbk.b()], w=[qT.b()])
                if stop == 1.2:
                    break
                invc = cm.t[:, (M_INVC0 if st == 0 else M_INVC):(M_INVC0 if st == 0 else M_INVC) + 1024].rearrange("p (c t) -> p c t", c=2)
                for g in range(4):
                    ci, pb = g // 2, 64 * (g % 2)
                    src, bsrc = pp.t[pb:pb + 64, ci, :], pp.b()
                    for lv in range(g + 1):
                        sh = 1 << lv
                        dst = ppa if lv % 2 == 0 else ppb
                        S.op("dve", lambda e, src=src, dst=dst, sh=sh, pb=pb: e.tensor_tensor(out=dst.t[pb:pb + 64, sh:16 + TT], in0=src[:, sh:16 + TT],
                                                                                            in1=src[:, 0:16 + TT - sh], op=ALU.add),
                             r=[bsrc], w=[dst.b(pb)])
                        if sh > 1:
                            pass
                        src, bsrc = dst.t[pb:pb + 64, :], dst.b(pb)
                    S.op("dve", lambda e, src=src, pb=pb, ci=ci: e.tensor_tensor(out=ppa.t[pb:pb + 64, 16:16 + TT] if False else tmp.t[pb:pb + 64, :],
                                                                                 in0=src[:, 16:16 + TT], in1=invc[pb:pb + 64, ci, :], op=ALU.mult),
                         r=[bsrc] + cb, w=[tmp.b()])
                    S.op("dve", lambda e, pb=pb, ci=ci: e.tensor_tensor(out=dT.t[pb:pb + 64, ci, :], in0=tmp.t[pb:pb + 64, :],
                                                                        in1=pp.t[pb:pb + 64, ci, 16:16 + TT], op=ALU.subtract),
                         r=[tmp.b(), pp.b()], w=[dT.b()])
                for ci in range(2):
                    pbk = PB[mmb % 2]
                    mmb += 1
                    for g2 in range(2):
                        pb = 64 * g2
                        S.op("pe", lambda e, ci=ci, pb=pb, pbk=pbk: e.matmul(pbk.t[pb:pb + 64, :], pw.t[pb:pb + 64, ci, :], dT.t[pb:pb + 64, ci, :],
                                                                            start=True, stop=True), r=[pw.b(), dT.b()], w=[pbk.b()])
                    S.op("act", lambda e, ci=ci, pbk=pbk: e.activation(out=yT.t[:, 4 + ci, :], in_=pbk.t[:, :], func=AF.Copy,
                                                                      scale=cvec.t[:, C_PS + ci:C_PS + ci + 1]), r=[pbk.b()] + cb, w=[yT.b(4 + ci)])
                S.op("dve", lambda e: e.tensor_copy(out=pp.t[:, :, 0:16], in_=pp.t[:, :, TT:TT + 16]), r=[pp.b()], w=[pp.b()])
                if stop == 1.3:
                    break
                for jm in range(2):
                    for hh in range(2):
                        pb = 64 * hh
                        hm = 2 * jm + hh
                        for mh in range(2):
                            S.op("pe", lambda e, jm=jm, pb=pb, mh=mh: e.matmul(PB[2 + mh].t[:, :], kT.t[pb:pb + 64, jm, mh * 128:(mh + 1) * 128],
                                                                               qT.t[pb:pb + 64, jm, :], start=True, stop=True),
                                 r=[kT.b(), qT.b()], w=[PB[2 + mh].b()])
                            S.op("act", lambda e, mh=mh: e.activation(out=eT.t[:, mh, :], in_=PB[2 + mh].t[:, :], func=AF.Exp, scale=0.125),
                                 r=[PB[2 + mh].b()], w=[eT.b(mh)])
                        for mh in range(2):
                            S.op("pe", lambda e, hm=hm, pb=pb, mh=mh: e.matmul(PB[4].t[pb:pb + 64, :], vtok.t[:, mh, hm * 64:(hm + 1) * 64], eT.t[:, mh, :],
                                                                               start=(mh == 0), stop=(mh == 1)),
                                 r=[vtok.b(), eT.b(mh)], w=[PB[4].b()])
                        for mh in range(2):
                            S.op("pe", lambda e, pb=pb, mh=mh: e.matmul(PB[5].t[pb:pb + 64, :], ones64, eT.t[:, mh, :],
                                                                        start=(mh == 0), stop=(mh == 1)),
                                 r=[ident.b(), eT.b(mh)], w=[PB[5].b()])
                    S.op("dve", lambda e: e.reciprocal(out=rden.t[:], in_=PB[5].t[:, :]), r=[PB[5].b()], w=[rden.b()])
                    S.op("dve", lambda e, jm=jm: e.tensor_tensor(out=yT.t[:, 6 + jm, :], in0=PB[4].t[:, :], in1=rden.t[:], op=ALU.mult),
                         r=[PB[4].b(), rden.b()], w=[yT.b(6 + jm)])
                if stop == 1.4:
                    break
                S.op("act", lambda e: e.activation(out=tw.t[0:64, :], in_=pm.t[0:64, 12, :], func=AF.Tanh), r=[pm.b(12)], w=[tw.b()])
                S.op("act", lambda e: e.activation(out=sg.t[:], in_=pm.t[:, 13, :], func=AF.Sigmoid), r=[pm.b(13)], w=[sg.b()])
                for j in range(4):
                    cs = slice(j * 128, (j + 1) * 128)
                    r_, k_, v_ = pm.t[:, j, :], pm.t[:, 4 + j, :], pm.t[:, 8 + j, :]
                    cv = lambda c0: cvec.t[:, c0 + j:c0 + j + 1]
                    pbk = PB[mmb % 2]
                    mmb += 1
                    S.op("pe", lambda e, pbk=pbk, cs=cs: e.matmul(pbk.t[:], lora.t[0:64, cs], tw.t[0:64, :], start=True, stop=True),
                         r=[lora.b(), tw.b()], w=[pbk.b()])
                    S.op("act", lambda e, pbk=pbk, cv=cv: e.activation(out=f["ld"].t[:], in_=pbk.t[:], func=AF.Sigmoid, bias=cv(C_W0)),
                         r=[pbk.b()] + cb, w=[f["ld"].b()])
                    S.op("dve", lambda e: e.tensor_scalar(out=f["ld"].t[:], in0=f["ld"].t[:], scalar1=NEG_EH, scalar2=None, op0=ALU.mult),
                         r=[f["ld"].b()], w=[f["ld"].b()])
                    S.op("dve", lambda e: e.tensor_tensor_scan(out=f["cum"].t[:], data0=scanm, data1=f["ld"].t[:], initial=0.0,
                                                               op0=ALU.mult, op1=ALU.add), r=[f["ld"].b()] + cb, w=[f["cum"].b()])
                    S.op("dve", lambda e: e.tensor_tensor(out=f["cx"].t[:], in0=f["cum"].t[:], in1=f["ld"].t[:], op=ALU.subtract),
                         r=[f["cum"].b(), f["ld"].b()], w=[f["cx"].b()])
                    S.op("act", lambda e: e.activation(out=f["E1"].t[:], in_=f["cum"].t[:], func=AF.Exp), r=[f["cum"].b()], w=[f["E1"].b()])
                    S.op("act", lambda e: e.activation(out=f["E2"].t[:], in_=f["cum"].t[:], func=AF.Exp, scale=-1.0), r=[f["cum"].b()], w=[f["E2"].b()])
                    S.op("act", lambda e: e.activation(out=f["E3"].t[:], in_=f["cx"].t[:], func=AF.Exp), r=[f["cx"].b()], w=[f["E3"].b()])
                    S.op("dve", lambda e, j=j: e.tensor_copy(out=WC.t[:, j, :], in_=f["E1"].t[:, :].rearrange("p (c t) -> p c t", t=64)[:, :, 63]),
                         r=[f["E1"].b()], w=[WC.b()])
                    pbk = PB[mmb % 2]
                    mmb += 1
                    S.op("pe", lambda e, pbk=pbk, cs=cs: e.matmul(pbk.t[:], lora.t[64:128, cs], pm.t[64:128, 12, :], start=True, stop=True),
                         r=[lora.b(), pm.b(12)], w=[pbk.b()])
                    S.op("act", lambda e, pbk=pbk, cv=cv: e.activation(out=f["a"].t[:], in_=pbk.t[:], func=AF.Sigmoid, bias=cv(C_A0)),
                         r=[pbk.b()] + cb, w=[f["a"].b()])
                    S.op("dve", lambda e, k_=k_, cv=cv: e.tensor_scalar(out=f["kk"].t[:], in0=k_, scalar1=cv(C_KK), scalar2=None, op0=ALU.mult),
                         r=[pm.b(4 + j)] + cb, w=[f["kk"].b()])
                    S.op("dve", lambda e: e.tensor_tensor(out=kk2.t[:], in0=f["kk"].t[:], in1=f["kk"].t[:], op=ALU.mult),
                         r=[f["kk"].b()], w=[kk2.b()])
                    pbk = PB[mmb % 2]
                    mmb += 1
                    S.op("pe", lambda e, pbk=pbk: e.matmul(pbk.t[:], bdo, kk2.t[:], start=True, stop=True), r=[ident.b(), kk2.b()], w=[pbk.b()])
                    S.op("act", lambda e, pbk=pbk: e.activation(out=f["rs"].t[:], in_=pbk.t[:], func=AF.Sqrt, bias=1e-12), r=[pbk.b()], w=[f["rs"].b()])
                    S.op("dve", lambda e: e.reciprocal(out=f["rs"].t[:], in_=f["rs"].t[:]), r=[f["rs"].b()], w=[f["rs"].b()])
                    S.op("dve", lambda e: e.tensor_tensor(out=f["kkn"].t[:], in0=f["kk"].t[:], in1=f["rs"].t[:], op=ALU.mult),
                         r=[f["kk"].b(), f["rs"].b()], w=[f["kkn"].b()])
                    c3 = lambda tl: tl.t[:, :].rearrange("p (c t) -> p c t", t=64)
                    S.op("dve", lambda e, j=j: e.tensor_tensor(out=KR.t[:, j, :, 0, :], in0=c3(f["kkn"]), in1=c3(f["E3"]), op=ALU.mult),
                         r=[f["kkn"].b(), f["E3"].b()], w=[KR.b(j)])
                    S.op("dve", lambda e, j=j, r_=r_: e.tensor_tensor(out=KR.t[:, j, :, 1, :], in0=r_.rearrange("p (c t) -> p c t", t=64), in1=c3(f["E1"]),
                                                                     op=ALU.mult), r=[pm.b(j), f["E1"].b()], w=[KR.b(j)])
                    S.op("dve", lambda e: e.tensor_tensor(out=f["x1"].t[:], in0=f["kkn"].t[:], in1=f["a"].t[:], op=ALU.mult),
                         r=[f["kkn"].b(), f["a"].b()], w=[f["x1"].b()])
                    S.op("dve", lambda e, j=j: e.tensor_tensor(out=BK.t[:, j, :, 0, :], in0=c3(f["x1"]), in1=c3(f["E2"]), op=ALU.mult),
                         r=[f["x1"].b(), f["E2"].b()], w=[BK.b(j)])
                    S.op("dve", lambda e, j=j, cv=cv: e.tensor_scalar(out=f["x1"].t[:], in0=f["a"].t[:], scalar1=cv(C_KA), scalar2=cder.t[:, 14 + j:15 + j],
                                                                     op0=ALU.mult, op1=ALU.add), r=[f["a"].b()] + cb, w=[f["x1"].b()])
                    S.op("dve", lambda e, k_=k_: e.tensor_tensor(out=f["kf"].t[:], in0=f["x1"].t[:], in1=k_, op=ALU.mult),
                         r=[f["x1"].b(), pm.b(4 + j)], w=[f["kf"].b()])
                    S.op("dve", lambda e, j=j: e.tensor_tensor(out=BK.t[:, j, :, 1, :], in0=c3(f["kf"]), in1=c3(f["E2"]), op=ALU.mult),
                         r=[f["kf"].b(), f["E2"].b()], w=[BK.b(j)])
                    S.op("dve", lambda e, r_=r_: e.tensor_tensor(out=f["x1"].t[:], in0=f["kf"].t[:], in1=r_, op=ALU.mult),
                         r=[f["kf"].b(), pm.b(j)], w=[f["x1"].b()])
                    S.op("dve", lambda e, cv=cv: e.tensor_scalar(out=kk2.t[:], in0=f["x1"].t[:], scalar1=cv(C_RK), scalar2=None, op0=ALU.mult),
                         r=[f["x1"].b()] + cb, w=[kk2.b()])
                    pbk = PB[mmb % 2]
                    mmb += 1
                    S.op("pe", lambda e, pbk=pbk: e.matmul(pbk.t[:], bdo, kk2.t[:], start=True, stop=True), r=[ident.b(), kk2.b()], w=[pbk.b()])
                    S.op("dve", lambda e, pbk=pbk, j=j, v_=v_: e.tensor_tensor(out=bonT.t[:, j, :], in0=pbk.t[:], in1=v_, op=ALU.mult),
                         r=[pbk.b(), pm.b(8 + j)], w=[bonT.b(j)])
                    pbk = PB[mmb % 2]
                    mmb += 1
                    S.op("pe", lambda e, pbk=pbk, cs=cs: e.matmul(pbk.t[:], gl.t[:, cs], sg.t[:], start=True, stop=True), r=[gl.b(), sg.b()], w=[pbk.b()])
                    S.op("act", lambda e, pbk=pbk, j=j: e.activation(out=gT.t[:, j, :], in_=pbk.t[:], func=AF.Copy), r=[pbk.b()], w=[gT.b(j)])
                if stop == 1.5:
                    break
                krb = [KR.b(j) for j in range(4)]
                bkb = [BK.b(j) for j in range(4)]
                HP = [(h // 2, 64 * (h % 2)) for h in range(8)]
                v3 = lambda ap_, w=64: ap_.rearrange("p (j c) -> p j c", j=4)
                lo = lambda bank, w=64: v3(bank.t[:, 0:4 * w], w)
                hi = lambda bank: v3(bank.t[:, 256:512])

                def stageA(c):
                    par = c % 2
                    cc = slice(c * 64, (c + 1) * 64)
                    vbk, asb, gf = VBK[par], Asb[par], GF[par]
                    for j, pb in HP:
                        ps_ = slice(pb, pb + 64)
                        S.op("pe", lambda e: e.transpose(PT.t[ps_, j * 64:(j + 1) * 64], pm.t[ps_, 8 + j, cc], idn[ps_, ps_]),
                             r=[pm.b(8 + j), ident.b()], w=[PT.b("A")])
                        S.op("pe", lambda e: e.transpose(PT.t[ps_, 256 + j * 64:256 + (j + 1) * 64], BK.t[ps_, j, c, 0, :], idn[ps_, ps_]),
                             r=[bkb[j], ident.b()], w=[PT.b("A")])
                        S.op("pe", lambda e: e.transpose(PT.t[ps_, 512 + j * 64:512 + (j + 1) * 64], BK.t[ps_, j, c, 1, :], idn[ps_, ps_]),
                             r=[bkb[j], ident.b()], w=[PT.b("A")])
                    for j, pb in HP:
                        ps_ = slice(pb, pb + 64)
                        for kind in range(2):
                            S.op("pe", lambda e: e.matmul(PB[2 + kind].t[ps_, j * 128:(j + 1) * 128], BK.t[ps_, j, c, kind, :],
                                                          KR.t[ps_, j, c, :, :].rearrange("p a b -> p (a b)"), start=True, stop=True),
                                 r=[bkb[j], krb[j]], w=[PB[2 + kind].b()])
                        S.op("pe", lambda e: e.matmul(PB[4].t[ps_, j * 64:(j + 1) * 64], KR.t[ps_, j, c, 0, :], BK.t[ps_, j, c, 0, :],
                                                      start=True, stop=True), r=[bkb[j], krb[j]], w=[PB[4].b()])
                    yield
                    S.op("act", lambda e: e.activation(out=vbk.t[:], in_=PT.t[:, 0:768].rearrange("p (a j c) -> p a j c", a=3, j=4), func=AF.Copy),
                         r=[PT.b("A")], w=[vbk.b()])
                    S.op("dve", lambda e: e.tensor_tensor(out=asb.t[:, :, 0, :], in0=lo(PB[2], 128), in1=M2v, op=ALU.mult),
                         r=[PB[2].b()] + cb, w=[asb.b()])
                    S.op("dve", lambda e: e.tensor_tensor(out=Lsb.t[:], in0=lo(PB[4]), in1=MLv, op=ALU.mult), r=[PB[4].b()] + cb, w=[Lsb.b()])
                    S.op("dve", lambda e: e.scalar_tensor_tensor(out=GT[0].t[:], in0=asb.t[:, :, 0, 0:64], scalar=-1.0, in1=I64v,
                                                                 op0=ALU.mult, op1=ALU.add), r=[asb.b()] + cb, w=[GT[0].b()])
                    S.op("dve", lambda e: e.tensor_tensor(out=asb.t[:, :, 1, :], in0=lo(PB[3], 128), in1=M2v, op=ALU.mult),
                         r=[PB[3].b()] + cb, w=[asb.b()])
                    yield
                    Pc, PTc, bP, bPT = Lsb.t, asb.t[:, :, 0, 0:64], Lsb.b(), asb.b()
                    gi = 0
                    pend = None
                    for lv in range(6):
                        if lv < 5:
                            for j, pb in HP:
                                ps_ = slice(pb, pb + 64)
                                S.op("pe", lambda e: e.matmul(PB[4].t[ps_, j * 64:(j + 1) * 64], PTc[ps_, j, :], Pc[ps_, j, :], start=True, stop=True),
                                     r=[bP, bPT], w=[PB[4].b()])
                            if lv < 4:
                                for j, pb in HP:
                                    ps_ = slice(pb, pb + 64)
                                    S.op("pe", lambda e: e.matmul(PB[5].t[ps_, j * 64:(j + 1) * 64], Pc[ps_, j, :], PTc[ps_, j, :], start=True, stop=True),
                                         r=[bP, bPT], w=[PB[5].b()])
                        if pend is not None:
                            pn2, plv = pend
                            gsrc = GT[gi]
                            gdst = gf if plv == 4 else GT[1 - gi]
                            for j, pb in HP:
                                ps_ = slice(pb, pb + 64)
                                S.op("pe", lambda e: e.matmul(PB[6].t[ps_, j * 64:(j + 1) * 64], pn2.t[ps_, j, :], gsrc.t[ps_, j, :], start=True, stop=True),
                                     r=[pn2.b(), gsrc.b()], w=[PB[6].b()])
                        yield
                        if lv < 5:
                            n2 = P2[lv % 2]
                            S.op("act", lambda e: e.activation(out=n2.t[:], in_=lo(PB[4]), func=AF.Copy), r=[PB[4].b()], w=[n2.b()])
                            if lv < 4:
                                n2t = P2T[lv % 2]
                                S.op("act", lambda e: e.activation(out=n2t.t[:], in_=lo(PB[5]), func=AF.Copy), r=[PB[5].b()], w=[n2t.b()])
                        if pend is not None:
                            S.op("dve", lambda e: e.tensor_tensor(out=gdst.t[:], in0=lo(PB[6]), in1=gsrc.t[:], op=ALU.add),
                                 r=[PB[6].b(), gsrc.b()], w=[gdst.b()])
                            gi = 1 - gi
                            pend = None
                        if lv < 5:
                            pend = (n2, lv)
                            if lv < 4:
                                Pc, PTc, bP, bPT = n2.t, n2t.t, n2.b(), n2t.b()
                            yield

                def stageB1(c):
                    par = c % 2
                    vbk, asb, G = VBK[par], Asb[par], GF[par]
                    ysb, ysq = Ysb[par], Ysq[par]
                    Vt, Bt, Kt = vbk.t[:, 0], vbk.t[:, 1], vbk.t[:, 2]
                    b0, b1 = PB[0].b(), PB[1].b()
                    for j, pb in HP:
                        ps_ = slice(pb, pb + 64)
                        S.op("pe", lambda e: e.matmul(PB[0].t[ps_, j * 64:(j + 1) * 64], KR.t[ps_, j, c, 0, :], Sbf.t[ps_, j, :], start=True, stop=False),
                             r=[krb[j], Sbf.b()], w=[b0])
                        S.op("pe", lambda e: e.matmul(PB[0].t[ps_, j * 64:(j + 1) * 64], asb.t[ps_, j, 1, 0:64], Vt[ps_, j, :], start=False, stop=True),
                             r=[asb.b(), vbk.b()], w=[b0])
                    yield
                    S.op("act", lambda e: e.activation(out=Zsb.t[:], in_=lo(PB[0]), func=AF.Copy), r=[b0], w=[Zsb.b()])
                    yield
                    for j, pb in HP:
                        ps_ = slice(pb, pb + 64)
                        S.op("pe", lambda e: e.matmul(PB[0].t[ps_, j * 64:(j + 1) * 64], G.t[ps_, j, :], Zsb.t[ps_, j, :], start=True, stop=True),
                             r=[G.b(), Zsb.b()], w=[b0])
                    yield
                    S.op("act", lambda e: e.activation(out=Un.t[:], in_=lo(PB[0]), func=AF.Copy, scale=-1.0), r=[b0], w=[Un.b()])
                    yield
                    for j, pb in HP:
                        ps_ = slice(pb, pb + 64)
                        S.op("pe", lambda e: e.matmul(PB[0].t[ps_, j * 64:(j + 1) * 64], Kt[ps_, j, :], Vt[ps_, j, :], start=True, stop=False),
                             r=[vbk.b()], w=[b0])
                        S.op("pe", lambda e: e.matmul(PB[0].t[ps_, j * 64:(j + 1) * 64], Bt[ps_, j, :], Un.t[ps_, j, :], start=False, stop=True),
                             r=[vbk.b(), Un.b()], w=[b0])
                    for j, pb in HP:
                        ps_ = slice(pb, pb + 64)
                        S.op("pe", lambda e: e.matmul(PB[1].t[ps_, j * 64:(j + 1) * 64], KR.t[ps_, j, c, 1, :], Sbf.t[ps_, j, :], start=True, stop=False),
                             r=[krb[j], Sbf.b()], w=[b1])
                        S.op("pe", lambda e: e.matmul(PB[1].t[ps_, j * 64:(j + 1) * 64], asb.t[ps_, j, 1, 64:128], Vt[ps_, j, :], start=False, stop=False),
                             r=[asb.b(), vbk.b()], w=[b1])
                        S.op("pe", lambda e: e.matmul(PB[1].t[ps_, j * 64:(j + 1) * 64], asb.t[ps_, j, 0, 64:128], Un.t[ps_, j, :], start=False, stop=True),
                             r=[asb.b(), Un.b()], w=[b1])
                    yield
                    S.op("dve", lambda e: e.tensor_tensor(out=Sf.t[:], in0=lo(PB[0]), in1=Sf.t[:], op=ALU.add), r=[b0, Sf.b()], w=[Sf.b()])
                    S.op("act", lambda e: e.activation(out=ysb.t[:], in_=lo(PB[1]), func=AF.Copy), r=[b1], w=[ysb.b()])
                    S.op("act", lambda e: e.activation(out=ysq.t[:], in_=lo(PB[1]), func=AF.Square), r=[b1], w=[ysq.b()])
                    yield
                    S.op("dve", lambda e: e.tensor_tensor(out=Sf.t[:], in0=Sf.t[:], in1=WC.t[:, :, c].unsqueeze(2).broadcast_to([128, 4, 64]), op=ALU.mult),
                         r=[Sf.b(), WC.b()], w=[Sf.b()])
                    yield
                    S.op("act", lambda e: e.activation(out=Sbf.t[:], in_=Sf.t[:], func=AF.Copy), r=[Sf.b()], w=[Sbf.b()])

                def stageB2(c):
                    par = c % 2
                    cc = slice(c * 64, (c + 1) * 64)
                    ysb, ysq, ys, yh_ = Ysb[par], Ysq[par], yst[par], yh[par]
                    S.op("dve", lambda e: e.tensor_reduce(out=ys.t[:, 0, :], in_=ysb.t[:], axis=AX.X, op=ALU.add), r=[ysb.b()], w=[ys.b()])
                    S.op("dve", lambda e: e.tensor_reduce(out=ys.t[:, 1, :], in_=ysq.t[:], axis=AX.X, op=ALU.add), r=[ysq.b()], w=[ys.b()])
                    yield
                    S.op("dve", lambda e: e.tensor_scalar(out=ys.t[:, 0, :], in0=ys.t[:, 0, :], scalar1=1.0 / 64, scalar2=None, op0=ALU.mult),
                         r=[ys.b()], w=[ys.b()])
                    yield
                    S.op("dve", lambda e: e.tensor_tensor(out=ys.t[:, 2, :], in0=ys.t[:, 0, :], in1=ys.t[:, 0, :], op=ALU.mult), r=[ys.b()], w=[ys.b()])
                    yield
                    S.op("dve", lambda e: e.scalar_tensor_tensor(out=ys.t[:, 3, :], in0=ys.t[:, 1, :], scalar=1.0 / 64, in1=ys.t[:, 2, :],
                                                                 op0=ALU.mult, op1=ALU.subtract), r=[ys.b()], w=[ys.b()])
                    yield
                    S.op("act", lambda e: e.activation(out=ys.t[:, 3, :], in_=ys.t[:, 3, :], func=AF.Sqrt, bias=eps_gn.t[:, 0:1]), r=[ys.b(), eps_gn.b()], w=[ys.b()])
                    yield
                    S.op("dve", lambda e: e.reciprocal(out=ys.t[:, 3, :], in_=ys.t[:, 3, :]), r=[ys.b()], w=[ys.b()])
                    S.op("dve", lambda e: e.tensor_tensor(out=ysb.t[:], in0=ysb.t[:], in1=ys.t[:, 0, :].unsqueeze(2).broadcast_to([128, 4, 64]), op=ALU.subtract),
                         r=[ysb.b(), ys.b()], w=[ysb.b()])
                    yield
                    S.op("dve", lambda e: e.tensor_tensor(out=yh_.t[:], in0=ysb.t[:], in1=ys.t[:, 3, :].unsqueeze(2).broadcast_to([128, 4, 64]), op=ALU.mult),
                         r=[ysb.b(), ys.b()], w=[yh_.b()])
                    yield
                    for j, pb in HP:
                        ps_ = slice(pb, pb + 64)
                        S.op("pe", lambda e: e.transpose(PT.t[ps_, 768 + j * 64:768 + (j + 1) * 64], yh_.t[ps_, j, :], idn[ps_, ps_]),
                             r=[yh_.b(), ident.b()], w=[PT.b("A")])
                    yield
                    S.op("act", lambda e: e.activation(out=yhT.t[:, :, cc], in_=PT.t[:, 768:1024].rearrange("p (j t) -> p j t", j=4), func=AF.Copy),
                         r=[PT.b("A")], w=[yhT.b()])

                for c in range(NCH + 2):
                    gens = []
                    if 1 <= c <= NCH:
                        gens.append(stageB1(c - 1))
                    if c < NCH:
                        gens.append(stageA(c))
                    if 2 <= c:
                        gens.append(stageB2(c - 2))
                    while gens:
                        for g_ in list(gens):
                            try:
                                next(g_)
                            except StopIteration:
                                gens.remove(g_)
                if stop == 1.6:
                    break
                for j in range(4):
                    S.op("dve", lambda e, j=j: e.tensor_scalar(out=f["x1"].t[:], in0=yhT.t[:, j, :], scalar1=cvec.t[:, C_LNW + j:C_LNW + j + 1],
                                                               scalar2=cvec.t[:, C_LNB + j:C_LNB + j + 1], op0=ALU.mult, op1=ALU.add),
                         r=[yhT.b()] + cb, w=[f["x1"].b()])
                    S.op("dve", lambda e, j=j: e.tensor_tensor(out=f["x1"].t[:], in0=f["x1"].t[:], in1=bonT.t[:, j, :], op=ALU.add),
                         r=[f["x1"].b(), bonT.b(j)], w=[f["x1"].b()])
                    S.op("dve", lambda e, j=j: e.tensor_tensor(out=yT.t[:, j, :], in0=f["x1"].t[:], in1=gT.t[:, j, :], op=ALU.mult),
                         r=[f["x1"].b(), gT.b(j)], w=[yT.b(j)])
                S.dma("sp", dy, yT_d[:, :, t0:t0 + TT], yT.t[:], r=[yT.b(jj) for jj in range(8)], w=[yT_db[st]])
            S.emit()
            if stop <= 2:
                S.barrier()
                S.emit()
                return nc

        rot = [0]

        def nb():
            rot[0] += 1
            return PB[rot[0] % 7]

        with contextlib.ExitStack() as p2:
            S.barrier()
            dw2 = S.dsem()
            wg = load_bf(p2, "wg", 8, 3 * D, dw2)
            wup = load_bf(p2, "wup", 8, D, dw2)
            wo = load_bf(p2, "wo", 8, D, dw2)
            S.seal(dw2, [wg.b(), wup.b(), wo.b()])
            hT2 = [sb(p2, "hT2%d" % i, [128, 8, TT], BF16) for i in range(2)]
            yT2 = [sb(p2, "yT2%d" % i, [128, 8, TT], BF16) for i in range(2)]
            dl2 = [S.dsem() for _ in range(2)]
            gs = [sb(p2, "gs%d" % i, [128, TT]) for i in range(3)]
            mm_ = [sb(p2, "mm%d" % i, [128, TT]) for i in range(3)]
            mg = sb(p2, "mg", [128, 8, TT], BF16)
            xs2 = [sb(p2, "xs2%d" % i, [128, D]) for i in range(2)]
            dx2 = [S.dsem() for _ in range(2)]
            x1s = [sb(p2, "x1s%d" % i, [128, D]) for i in range(2)]
            ds2 = [S.dsem() for _ in range(2)]
            kr = [(0, 4), (4, 6), (6, 8)]
            def ld2(st_):
                i_, c0 = st_ % 2, st_ * TT
                S.dma("sp", dl2[i_], hT2[i_].t[:], hT_d[:, :, c0:c0 + TT], r=[hT_db[st_]], w=[hT2[i_].b()])
                S.dma("sp", dl2[i_], yT2[i_].t[:], yT_d[:, :, c0:c0 + TT], r=[yT_db[st_]], w=[yT2[i_].b()])
                S.seal(dl2[i_], [hT2[i_].b(), yT2[i_].b()])

            ld2(0)
            for st in range(NST):
                t0 = st * TT
                i2 = st % 2
                if st + 1 < NST:
                    ld2(st + 1)
                for fo in range(8):
                    fs = slice(fo * 128, (fo + 1) * 128)
                    for b in range(3):
                        pg, pu = nb(), nb()
                        for kc in range(8):
                            S.op("pe", lambda e: e.matmul(pg.t[:], wg.t[:, kc, b * D + fo * 128:b * D + (fo + 1) * 128], hT2[i2].t[:, kc, :],
                                                          start=(kc == 0), stop=(kc == 7)), r=[wg.b(), hT2[i2].b()], w=[pg.b()])
                        k0, k1 = kr[b]
                        for kc in range(k0, k1):
                            S.op("pe", lambda e: e.matmul(pu.t[:], wup.t[:, kc, fs], yT2[i2].t[:, kc, :], start=(kc == k0), stop=(kc == k1 - 1)),
                                 r=[wup.b(), yT2[i2].b()], w=[pu.b()])
                        S.op("act", lambda e: e.activation(out=gs[b].t[:], in_=pg.t[:], func=AF.Sigmoid,
                                                           bias=cvec.t[:, C_BG + b * 8 + fo:C_BG + b * 8 + fo + 1]), r=[pg.b()] + cb, w=[gs[b].b()])
                        S.op("dve", lambda e: e.tensor_tensor(out=mm_[b].t[:], in0=pu.t[:], in1=gs[b].t[:], op=ALU.mult),
                             r=[pu.b(), gs[b].b()], w=[mm_[b].b()])
                    S.op("pool", lambda e: e.tensor_tensor(out=mm_[0].t[:], in0=mm_[0].t[:], in1=mm_[1].t[:], op=ALU.add),
                         r=[mm_[0].b(), mm_[1].b()], w=[mm_[0].b()])
                    S.op("pool", lambda e: e.tensor_tensor(out=mg.t[:, fo, :], in0=mm_[0].t[:], in1=mm_[2].t[:], op=ALU.add),
                         r=[mm_[0].b(), mm_[2].b()], w=[mg.b()])
                def ldx2(idx):
                    ii = idx % 2
                    S.dma("sp", dx2[ii], xs2[ii].t[:], x_d[idx * 128:(idx + 1) * 128, :], w=[xs2[ii].b()])

                if st == 0:
                    ldx2(0)
                for sub in range(4):
                    idx = st * 4 + sub
                    i = idx % 2
                    r0 = t0 + sub * 128
                    if idx + 1 < 4 * NST:
                        ldx2(idx + 1)
                    for half in range(2):
                        pbk = nb()
                        for kc in range(8):
                            S.op("pe", lambda e: e.matmul(pbk.t[:], mg.t[:, kc, sub * 128:(sub + 1) * 128], wo.t[:, kc, half * 512:(half + 1) * 512],
                                                          start=(kc == 0), stop=(kc == 7)), r=[mg.b(), wo.b()], w=[pbk.b()])
                        S.op("dve", lambda e: e.tensor_tensor(out=x1s[i].t[:, half * 512:(half + 1) * 512], in0=pbk.t[:],
                                                              in1=xs2[i].t[:, half * 512:(half + 1) * 512], op=ALU.add),
                             r=[pbk.b(), xs2[i].b()], w=[x1s[i].b()])
                    S.dma("sp", ds2[i], x1_d[r0:r0 + 128, :], x1s[i].t[:], r=[x1s[i].b()], w=[x1_db[st * 4 + sub]])
            S.emit()
            if stop == 3:
                S.barrier()
                S.emit()
                return nc

        with contextlib.ExitStack() as p3:
            S.barrier()
            dw3 = S.dsem()
            wfi = load_bf(p3, "wfi", 8, 2 * DFF, dw3)
            wfo = load_bf(p3, "wfo", NFC, D, dw3)
            gfin = sb(p3, "gfin", [128, D])
            dgf = S.dsem()
            S.dma("sp", dgf, gfin.t[:], gfin_d[:, :], w=[gfin.b()])
            S.seal(dw3, [wfi.b(), wfo.b()])
            x1k = sb(p3, "x1k", [128, 4, D])
            dk = [S.dsem() for _ in range(4)]
            h2T = sb(p3, "h2T", [128, 8, TT], BF16)
            ub = [sb(p3, "ub%d" % i, [128, 2 + TT]) for i in range(2)]
            ucar = sb(p3, "ucar", [128, NFC, 2])
            c1 = [sb(p3, "c1%d" % i, [128, TT]) for i in range(2)]
            actT = sb(p3, "actT", [128, NFC, TT], BF16)
            x2s = [sb(p3, "x2s%d" % i, [128, D]) for i in range(2)]
            do = [S.dsem() for _ in range(2)]
            fst = sb(p3, "fst", [128, 2])
            S.op("dve", lambda e: e.memset(ucar.t[:], 0.0), w=[ucar.b()])
            dxr = [S.dsem() for _ in range(2)]

            def ld3(st_):
                for sub_ in range(4):
                    q0 = st_ * TT + sub_ * 128
                    S.dma("sp", dk[sub_], x1k.t[:, sub_, :], x1_d[q0:q0 + 128, :], r=[x1_db[st_ * 4 + sub_]], w=[x1k.b(sub_)])

            ld3(0)
            for st in range(NST):
                t0 = st * TT
                for sub in range(4):
                    norm_T(x1k.t[:, sub, :], x1k.b(sub), C_G2, h2T.t, h2T.b(), sub * 128)
                if st + 1 < NST:
                    ld3(st + 1)
                for fc in range(NFC):
                    pu, pgv = nb(), nb()
                    u_, c_ = ub[fc % 2], c1[fc % 2]
                    for half, pbk in ((0, pu), (1, pgv)):
                        for kc in range(8):
                            S.op("pe", lambda e: e.matmul(pbk.t[:], wfi.t[:, kc, half * DFF + fc * 128:half * DFF + (fc + 1) * 128], h2T.t[:, kc, :],
                                                          start=(kc == 0), stop=(kc == 7)), r=[wfi.b(), h2T.b()], w=[pbk.b()])
                    cw = lambda jx: cvec.t[:, C_CW + jx * NFC + fc:C_CW + jx * NFC + fc + 1]
                    S.op("act", lambda e: e.activation(out=u_.t[:, 2:2 + TT], in_=pu.t[:], func=AF.Copy), r=[pu.b()], w=[u_.b()])
                    S.op("act", lambda e: e.activation(out=c_.t[:, 2:TT], in_=pu.t[:, 0:TT - 2], func=AF.Copy, scale=cw(0)), r=[pu.b()] + cb, w=[c_.b()])
                    S.op("pool", lambda e: e.tensor_copy(out=u_.t[:, 0:2], in_=ucar.t[:, fc, :]), r=[ucar.b()], w=[u_.b()])
                    S.op("pool", lambda e: e.tensor_scalar(out=c_.t[:, 0:2], in0=ucar.t[:, fc, :], scalar1=cw(0), scalar2=None, op0=ALU.mult),
                         r=[ucar.b()] + cb, w=[c_.b()])
                    S.op("pool", lambda e: e.tensor_copy(out=ucar.t[:, fc, :], in_=u_.t[:, TT:TT + 2]), r=[u_.b()], w=[ucar.b()])
                    S.op("dve", lambda e: e.scalar_tensor_tensor(out=c_.t[:], in0=u_.t[:, 1:1 + TT], scalar=cw(1), in1=c_.t[:], op0=ALU.mult, op1=ALU.add),
                         r=[u_.b(), c_.b()] + cb, w=[c_.b()])
                    S.op("dve", lambda e: e.scalar_tensor_tensor(out=c_.t[:], in0=u_.t[:, 2:2 + TT], scalar=cw(2), in1=c_.t[:], op0=ALU.mult, op1=ALU.add),
                         r=[u_.b(), c_.b()] + cb, w=[c_.b()])
                    S.op("act", lambda e: e.activation(out=c_.t[:], in_=c_.t[:], func=AF.Gelu, bias=cvec.t[:, C_CB + fc:C_CB + fc + 1]),
                         r=[c_.b()] + cb, w=[c_.b()])
                    S.op("dve", lambda e: e.tensor_tensor(out=actT.t[:, fc, :], in0=pgv.t[:], in1=c_.t[:], op=ALU.mult),
                         r=[pgv.b(), c_.b()], w=[actT.b()])
                for sub in range(4):
                    i = (st * 4 + sub) % 2
                    r0 = t0 + sub * 128
                    S.dma("sp", dxr[i], x2s[i].t[:], x1_d[r0:r0 + 128, :], r=[x1_db[st * 4 + sub]], w=[x2s[i].b()])
                    for half in range(2):
                        pbk = nb()
                        for fc in range(NFC):
                            S.op("pe", lambda e: e.matmul(pbk.t[:], actT.t[:, fc, sub * 128:(sub + 1) * 128], wfo.t[:, fc, half * 512:(half + 1) * 512],
                                                          start=(fc == 0), stop=(fc == NFC - 1)), r=[actT.b(), wfo.b()], w=[pbk.b()])
                        S.op("dve", lambda e: e.tensor_tensor(out=x2s[i].t[:, half * 512:(half + 1) * 512], in0=pbk.t[:],
                                                              in1=x2s[i].t[:, half * 512:(half + 1) * 512], op=ALU.add),
                             r=[pbk.b(), x2s[i].b()], w=[x2s[i].b()])
                    ss, ms = fst.t[:, 0:1], fst.t[:, 1:2]
                    S.op("act", lambda e: e.activation(out=junk.t[:], in_=x2s[i].t[:], func=AF.Square, accum_out=ss),
                         r=[x2s[i].b(), fst.b()], w=[junk.b(), fst.b()])
                    S.op("dve", lambda e: e.tensor_scalar(out=ms, in0=ss, scalar1=1.0 / D, scalar2=1e-6, op0=ALU.mult, op1=ALU.add), r=[fst.b()], w=[fst.b()])
                    S.op("act", lambda e: e.activation(out=ms, in_=ms, func=AF.Sqrt), r=[fst.b()], w=[fst.b()])
                    S.op("dve", lambda e: e.reciprocal(out=ms, in_=ms), r=[fst.b()], w=[fst.b()])
                    S.op("act", lambda e: e.activation(out=x2s[i].t[:], in_=x2s[i].t[:], func=AF.Copy, scale=ms), r=[x2s[i].b(), fst.b()], w=[x2s[i].b()])
                    S.op("dve", lambda e: e.tensor_tensor(out=x2s[i].t[:], in0=x2s[i].t[:], in1=gfin.t[:], op=ALU.mult),
                         r=[x2s[i].b(), gfin.b()], w=[x2s[i].b()])
                    S.dma("sp", do[i], out_d[r0:r0 + 128, :], x2s[i].t[:], r=[x2s[i].b()])
            S.wait_all("sp", [(d[0], d[1], None) for d in do])
            S.emit()
    return nc


def _cols(v, n):
    return np.ascontiguousarray(np.asarray(v, np.float32).reshape(n, 128).T)


def _host_consts():
    cm = np.zeros((128, NCM), np.float32)
    s = np.arange(128)[:, None] % 64
    t = np.arange(64)[None, :]
    cm[:, M_M2:M_M2 + 64] = (s < t)
    cm[:, M_M2 + 64:M_M2 + 128] = (s <= t)
    tt = np.arange(128)[:, None] % 64
    ss = np.arange(64)[None, :]
    cm[:, M_ML:M_ML + 64] = (tt > ss)
    cm[:, M_I64:M_I64 + 64] = (tt == ss)
    sc = np.ones(512, np.float32)
    sc[::64] = 0.0
    cm[:, M_SCAN:M_SCAN + 512] = sc[None, :]
    wins = [2, 4, 8, 16]
    for g in range(4):
        ci, pb = g // 2, 64 * (g % 2)
        pos = np.arange(1, 513)
        cm[pb:pb + 64, M_INVC0 + ci * 512:M_INVC0 + (ci + 1) * 512] = (1.0 / np.minimum(pos, wins[g]))[None, :]
        cm[pb:pb + 64, M_INVC + ci * 512:M_INVC + (ci + 1) * 512] = 1.0 / wins[g]
    cmb = np.zeros((128, 320), np.float32)
    cmb[:, 0:128] = np.eye(128)
    cmb[0:64, 128:192] = 1.0
    cmb[64:128, 192:256] = 1.0
    cmb[:, 256:320] = 1.0
    return cm, cmb


_NC_CACHE = {}


def _prep(inputs):
    g = lambda k: np.asarray(inputs[k], np.float32)
    cv = np.zeros((128, NCV), np.float32)
    cv[:, C_G1:C_G1 + 8] = _cols(g("norm_mix_g")[0], 8)
    cv[:, C_MU:C_MU + 14] = _cols(g("mu_shift")[0], 14)
    cv[:, C_W0:C_W0 + 4] = _cols(g("w0")[0], 4)
    cv[:, C_A0:C_A0 + 4] = _cols(g("a0")[0], 4)
    cv[:, C_KK:C_KK + 4] = _cols(g("k_k")[0], 4)
    cv[:, C_KA:C_KA + 4] = _cols(g("k_a")[0], 4)
    cv[:, C_RK:C_RK + 4] = _cols(g("r_k")[0].reshape(-1), 4)
    cv[:, C_LNW:C_LNW + 4] = _cols(g("ln_x_w")[0], 4)
    cv[:, C_LNB:C_LNB + 4] = _cols(g("ln_x_b")[0], 4)
    cv[:, C_PS:C_PS + 2] = _cols(g("pool_scale")[0], 2)
    cv[:, C_BG:C_BG + 24] = _cols(g("b_gate")[0], 24)
    cv[:, C_G2:C_G2 + 8] = _cols(g("norm_ffn_g")[0], 8)
    for j in range(3):
        cv[:, C_CW + j * NFC:C_CW + (j + 1) * NFC] = _cols(g("ffn_conv_w")[0, j], NFC)
    cv[:, C_CB:C_CB + NFC] = _cols(g("ffn_conv_b")[0], NFC)
    cv[:, C_GM:C_GM + 8] = _cols(g("norm_mem_g")[0], 8)
    cm, cmb = _host_consts()
    shared = {
        "w_in_mix": g("w_in_mix")[0], "w_lora_b": g("w_lora_b")[0], "a_lora_b": g("a_lora_b")[0], "g_lora_b": g("g_lora_b")[0],
        "pool_w": g("pool_w")[0], "w_mem_kv": g("w_mem_kv")[0], "w_up_rwkv": g("w_up_rwkv")[0], "w_up_pool": g("w_up_pool")[0],
        "w_up_mem": g("w_up_mem")[0], "w_gate": g("w_gate")[0], "w_o": g("w_o")[0], "w_ffn_in": g("w_ffn_in")[0],
        "w_ffn_out": g("w_ffn_out")[0], "cvec": cv, "cm32": cm, "cmb": cmb,
        "gfin": np.ascontiguousarray(np.broadcast_to(g("norm_final_g")[None, :], (128, D))),
    }
    shared = {k: np.ascontiguousarray(v, dtype=np.float32) for k, v in shared.items()}
    x = g("x")
    mem = g("mem")
    return [dict(shared, x=np.ascontiguousarray(x[b]), mem=np.ascontiguousarray(mem[b])) for b in range(8)]


def kernel(**inputs):
    in_maps = _prep(inputs)
    if "nc" not in _NC_CACHE:
        _NC_CACHE["nc"] = build(False)
    res = run_bass_kernel_spmd(_NC_CACHE["nc"], in_maps, core_ids=list(range(8)))
    return np.stack([np.asarray(r["out"], np.float32) for r in res.results], axis=0)
```

```python
import numpy as np
import contextlib
import concourse.bass as bass
import concourse.mybir as mybir
from concourse.bass_utils import run_bass_kernel_spmd

F32 = mybir.dt.float32
BF16 = mybir.dt.bfloat16
AF = mybir.ActivationFunctionType
ALU = mybir.AluOpType
AX = mybir.AxisListType

D = 1024
T = 4096
TT = 512
NST = T // TT
NCH = TT // 64
DFF = 2816
NFC = DFF // 128
MIX_IN = 2304
NEG_EH = -float(np.exp(-0.5))

C_G1, C_MU, C_W0, C_A0, C_KK, C_KA, C_RK, C_LNW, C_LNB, C_PS, C_BG, C_G2, C_CW, C_CB, C_GM = (
    0, 8, 22, 26, 30, 34, 38, 42, 46, 50, 52, 76, 84, 150, 172)
NCV = 180
M_M2, M_ML, M_I64, M_SCAN, M_INVC0, M_INVC = 0, 128, 192, 256, 768, 1792
NCM = 2816


class Buf:
    __slots__ = ("w", "r")

    def __init__(self):
        self.w = None
        self.r = {}


class Rec:
    def __getattr__(self, name):
        def f(*a, **k):
            self.call = (name, a, k)
            return self
        return f


class Eng:
    def __init__(self, name, sem, is_pe=False):
        self.name = name
        self.sem = sem
        self.count = 0
        self.seen = {}
        self.prog = []
        self.is_pe = is_pe


class Sched:
    def __init__(self, nc, es):
        self.nc = nc
        self.E = {}
        for n in ("pe", "act", "dve", "pool", "sp"):
            self.E[n] = Eng(n, es.enter_context(nc.semaphore("s_" + n)), is_pe=(n == "pe"))
        self.es = es
        self.ndsem = 0
        self.dsems = []

    def dsem(self):
        self.ndsem += 1
        ds = [self.es.enter_context(self.nc.semaphore("d%d" % self.ndsem)), 0]
        self.dsems.append(ds)
        return ds

    def barrier(self):
        for E in self.E.values():
            for X in self.E.values():
                if X is not E and X.count > 0 and E.seen.get(id(X.sem), 0) < X.count:
                    E.seen[id(X.sem)] = X.count
                    E.prog.append(("w", X.sem, X.count))
            for ds in self.dsems:
                if ds[1] > 0 and E.seen.get(id(ds[0]), 0) < ds[1]:
                    E.seen[id(ds[0])] = ds[1]
                    E.prog.append(("w", ds[0], ds[1]))

    def _deps(self, E, reads, writes):
        need = {}

        def add(tok, raw):
            sem, val, eng = tok
            if eng is E and E.is_pe:
                return
            k = id(sem)
            if k not in need or need[k][1] < val:
                need[k] = (sem, val)

        for b in reads:
            if b.w is not None:
                add(b.w, True)
        for b in writes:
            if b.w is not None:
                add(b.w, False)
            for t in b.r.values():
                add(t, False)
        for k, (sem, val) in need.items():
            if E.seen.get(k, 0) >= val:
                continue
            E.seen[k] = val
            E.prog.append(("w", sem, val))

    def _commit(self, tok, reads, writes):
        for b in writes:
            b.w = tok
            b.r = {}
        k = id(tok[0])
        for b in reads:
            b.r[k] = tok

    def op(self, en, fn, r=(), w=()):
        E = self.E[en]
        self._deps(E, r, w)
        E.count += 1
        rec = Rec()
        fn(rec)
        E.prog.append(("i", rec.call, E.sem, 1))
        self._commit((E.sem, E.count, E), r, w)

    def dma(self, en, ds, out, in_, r=(), w=()):
        E = self.E[en]
        self._deps(E, r, w)
        ds[1] += 16
        E.prog.append(("i", ("dma_start", (), dict(out=out, in_=in_)), ds[0], 16))
        self._commit((ds[0], ds[1], None), r, w)

    def seal(self, ds, bufs):
        for b in bufs:
            b.w = (ds[0], ds[1], None)

    def wait_all(self, en, toks):
        E = self.E[en]
        for sem, val, _ in toks:
            E.prog.append(("w", sem, val))

    def emit(self):
        nc = self.nc
        progs = {n: e.prog for n, e in self.E.items()}
        for e in self.E.values():
            e.prog = []

        def run(eng, prog):
            for it in prog:
                if it[0] == "w":
                    eng.wait_ge(it[1], it[2])
                else:
                    name, a, k = it[1]
                    getattr(eng, name)(*a, **k).then_inc(it[2], it[3])

        with nc.Block() as block:
            @block.tensor
            def _(e):
                run(e, progs["pe"])

            @block.scalar
            def _(e):
                run(e, progs["act"])

            @block.vector
            def _(e):
                run(e, progs["dve"])

            @block.gpsimd
            def _(e):
                run(e, progs["pool"])

            @block.sync
            def _(e):
                run(e, progs["sp"])


class Tl:
    def __init__(self, t):
        self.t = t
        self.bufs = {}

    def b(self, key=0):
        if key not in self.bufs:
            self.bufs[key] = Buf()
        return self.bufs[key]


def build(debug=False, stop=99):
    nc = bass.Bass("TRN2", target_bir_lowering=False)
    din = lambda n, s, dt=F32: nc.dram_tensor(n, s, dt, kind="ExternalInput").ap()
    x_d = din("x", [T, D])
    mem_d = din("mem", [256, D])
    win_d = din("w_in_mix", [D, MIX_IN])
    wl_d = din("w_lora_b", [64, 512])
    al_d = din("a_lora_b", [64, 512])
    gl_d = din("g_lora_b", [128, 512])
    pw_d = din("pool_w", [4, 64, 64])
    wkv_d = din("w_mem_kv", [D, 512])
    wupr_d = din("w_up_rwkv", [512, D])
    wupp_d = din("w_up_pool", [256, D])
    wupm_d = din("w_up_mem", [256, D])
    wg_d = din("w_gate", [D, 3 * D])
    wo_d = din("w_o", [D, D])
    wfi_d = din("w_ffn_in", [D, 2 * DFF])
    wfo_d = din("w_ffn_out", [DFF, D])
    cvec_d = din("cvec", [128, NCV])
    cm32_d = din("cm32", [128, NCM])
    cmb_d = din("cmb", [128, 320])
    gfin_d = din("gfin", [128, D])
    skind = "ExternalOutput" if debug else "Internal"
    hT_d = nc.dram_tensor("hT_d", [128, 8, T], BF16, kind=skind).ap()
    yT_d = nc.dram_tensor("yT_d", [128, 8, T], BF16, kind=skind).ap()
    x1_d = nc.dram_tensor("x1_d", [T, D], F32, kind=skind).ap()
    out_d = nc.dram_tensor("out", [T, D], F32, kind="ExternalOutput").ap()
    wsc = {n: nc.dram_tensor(n + "_bf", shp, BF16, kind="Internal").ap() for n, shp in
           (("wg", [D, 3 * D]), ("wup", [D, D]), ("wo", [D, D]), ("wfi", [D, 2 * DFF]), ("wfo", [DFF, D]))}
    wsc_b = {n: Buf() for n in wsc}
    hT_db = [Buf() for _ in range(NST)]
    yT_db = [Buf() for _ in range(NST)]
    x1_db = [Buf() for _ in range(NST * 4)]

    with contextlib.ExitStack() as top:
        S = Sched(nc, top)
        sb = lambda es, n, s, dt=F32: Tl(es.enter_context(nc.sbuf_tensor("sb_" + n, s, dt)))
        PB = [Tl(top.enter_context(nc.psum_tensor("pb%d" % i, [128, 512], F32))) for i in range(7)]
        PT = Tl(top.enter_context(nc.psum_tensor("pt", [128, 1024], BF16)))
        cvec = sb(top, "cvec", [128, NCV])
        cder = sb(top, "cder", [128, 18])
        ident = sb(top, "ident", [128, 320], BF16)
        junk = sb(top, "junk", [128, D], BF16)
        hb = sb(top, "hb", [128, D], BF16)
        st4 = sb(top, "st4", [128, 4])
        dconst = S.dsem()
        S.dma("sp", dconst, cvec.t[:], cvec_d[:, :], w=[cvec.b()])
        dconst2 = S.dsem()
        S.dma("pool", dconst2, ident.t[:], cmb_d[:, :], w=[ident.b()])
        S.op("dve", lambda e: e.tensor_scalar(out=cder.t[:, 0:14], in0=cvec.t[:, C_MU:C_MU + 14], scalar1=-1.0, scalar2=1.0,
                                              op0=ALU.mult, op1=ALU.add), r=[cvec.b()], w=[cder.b()])
        S.op("dve", lambda e: e.tensor_scalar(out=cder.t[:, 14:18], in0=cvec.t[:, C_KA:C_KA + 4], scalar1=-1.0, scalar2=1.0,
                                              op0=ALU.mult, op1=ALU.add), r=[cvec.b()], w=[cder.b()])
        idn = ident.t[:, 0:128]
        bdo = ident.t[:, 128:256]
        ones64 = ident.t[:, 256:320]
        cb = [cvec.b(), cder.b(), ident.b()]

        def norm_T(xs_ap, bx, gcol, hT, bh, col0, npart=128):
            ss, ms = st4.t[0:npart, 0:1], st4.t[0:npart, 1:2]
            S.op("act", lambda e: e.activation(out=junk.t[0:npart, :], in_=xs_ap, func=AF.Square, accum_out=ss),
                 r=[bx, st4.b()], w=[junk.b(), st4.b()])
            S.op("dve", lambda e: e.tensor_scalar(out=ms, in0=ss, scalar1=1.0 / D, scalar2=1e-6, op0=ALU.mult, op1=ALU.add),
                 r=[st4.b()], w=[st4.b()])
            S.op("act", lambda e: e.activation(out=ms, in_=ms, func=AF.Sqrt), r=[st4.b()], w=[st4.b()])
            S.op("dve", lambda e: e.reciprocal(out=ms, in_=ms), r=[st4.b()], w=[st4.b()])
            S.op("dve", lambda e: e.tensor_scalar(out=hb.t[0:npart, :], in0=xs_ap, scalar1=ms, scalar2=None, op0=ALU.mult),
                 r=[bx, st4.b()], w=[hb.b()])
            for c in range(8):
                S.op("pe", lambda e, c=c: e.transpose(PT.t[:, c * 128:c * 128 + npart], hb.t[0:npart, c * 128:(c + 1) * 128],
                                                      idn[0:npart, 0:npart]), r=[hb.b(), ident.b()], w=[PT.b("A"), PT.b("B")])
            pv = PT.t[:, :].rearrange("p (c t) -> p c t", c=8)[:, :, 0:npart]
            gv = cvec.t[:, gcol:gcol + 8].unsqueeze(2).broadcast_to([128, 8, npart])
            S.op("dve", lambda e: e.tensor_tensor(out=hT[:, :, col0:col0 + npart], in0=pv, in1=gv, op=ALU.mult),
                 r=[PT.b("A"), PT.b("B"), cvec.b()], w=[bh])

        def load_w(es, name, wd, kchunks, ncols, ds, row0=0, tile=None, kc0=0):
            if tile is None:
                tile = sb(es, name, [128, kchunks, ncols], BF16)
            step = 1024
            for kc in range(kchunks):
                for n0 in range(0, ncols, step):
                    n1 = min(ncols, n0 + step)
                    S.dma("pool", ds, tile.t[:, kc0 + kc, n0:n1], wd[row0 + kc * 128:row0 + (kc + 1) * 128, n0:n1], w=[tile.b()])
            return tile

        def stage_w(ds, name, wd, nrows, ncols, row0=0):
            for r0 in range(0, nrows, 128):
                for n0 in range(0, ncols, 2048):
                    n1 = min(ncols, n0 + 2048)
                    S.dma("pool", ds, wsc[name][row0 + r0:row0 + r0 + 128, n0:n1], wd[r0:r0 + 128, n0:n1], w=[wsc_b[name]])

        def load_bf(es, name, kchunks, ncols, ds):
            tile = sb(es, name, [128, kchunks, ncols], BF16)
            for kc in range(kchunks):
                S.dma("sp", ds, tile.t[:, kc, :], wsc[name][kc * 128:(kc + 1) * 128, :], r=[wsc_b[name]], w=[tile.b()])
            return tile

        with contextlib.ExitStack() as p1:
            dw1 = S.dsem()
            cm = sb(p1, "cm", [128, NCM])
            dcm = S.dsem()
            S.dma("sp", dcm, cm.t[:], cm32_d[:, :], w=[cm.b()])
            cb = cb + [cm.b()]
            win = load_w(p1, "win", win_d, 8, MIX_IN, dw1)
            lora = sb(p1, "lora", [128, 512], BF16)
            S.dma("pool", dw1, lora.t[0:64, :], wl_d[:, :], w=[lora.b()])
            S.dma("pool", dw1, lora.t[64:128, :], al_d[:, :], w=[lora.b()])
            gl = sb(p1, "gl", [128, 512], BF16)
            S.dma("pool", dw1, gl.t[:], gl_d[:, :], w=[gl.b()])
            pw = sb(p1, "pw", [128, 2, 64], BF16)
            for g in range(4):
                S.dma("pool", dw1, pw.t[64 * (g % 2):64 * (g % 2) + 64, g // 2, :], pw_d[g, :, :], w=[pw.b()])
            kT = sb(p1, "kT", [128, 2, 256], BF16)
            vtok = sb(p1, "vtok", [128, 2, 256], BF16)
            with contextlib.ExitStack() as p0:
                wkv = load_w(p0, "wkv", wkv_d, 8, 512, dw1)
                S.seal(dw1, [win.b(), lora.b(), gl.b(), pw.b(), wkv.b()])
                mems = sb(p0, "mems", [128, 2, D])
                memT = sb(p0, "memT", [128, 8, 256], BF16)
                dm = S.dsem()
                for mh in range(2):
                    S.dma("sp", dm, mems.t[:, mh, :], mem_d[mh * 128:(mh + 1) * 128, :], w=[mems.b(mh)])
                S.seal(dm, [mems.b(0), mems.b(1)])
                for mh in range(2):
                    norm_T(mems.t[:, mh, :], mems.b(mh), C_GM, memT.t, memT.b(), mh * 128)
                for fc in range(2):
                    for kc in range(8):
                        S.op("pe", lambda e, fc=fc, kc=kc: e.matmul(PB[0].t[:, 0:256], wkv.t[:, kc, fc * 128:(fc + 1) * 128],
                                                                     memT.t[:, kc, :], start=(kc == 0), stop=(kc == 7)),
                             r=[wkv.b(), memT.b()], w=[PB[0].b()])
                    S.op("act", lambda e, fc=fc: e.activation(out=kT.t[:, fc, :], in_=PB[0].t[:, 0:256], func=AF.Copy),
                         r=[PB[0].b()], w=[kT.b()])
                for mh in range(2):
                    for kc in range(8):
                        S.op("pe", lambda e, mh=mh, kc=kc: e.matmul(PB[1].t[:, 0:256], memT.t[:, kc, mh * 128:(mh + 1) * 128],
                                                                     wkv.t[:, kc, 256:512], start=(kc == 0), stop=(kc == 7)),
                             r=[wkv.b(), memT.b()], w=[PB[1].b()])
                    S.op("act", lambda e, mh=mh: e.activation(out=vtok.t[:, mh, :], in_=PB[1].t[:, 0:256], func=AF.Copy),
                         r=[PB[1].b()], w=[vtok.b()])
                S.emit()
            S.barrier()
            if stop == 0:
                S.emit()
                return nc
            dstg = S.dsem()
            stage_w(dstg, "wg", wg_d, D, 3 * D)
            stage_w(dstg, "wup", wupr_d, 512, D, row0=0)
            stage_w(dstg, "wup", wupp_d, 256, D, row0=512)
            stage_w(dstg, "wup", wupm_d, 256, D, row0=768)
            stage_w(dstg, "wo", wo_d, D, D)
            stage_w(dstg, "wfi", wfi_d, D, 2 * DFF)
            stage_w(dstg, "wfo", wfo_d, DFF, D)
            S.seal(dstg, list(wsc_b.values()))

            xs = [sb(p1, "xs%d" % i, [128, D]) for i in range(2)]
            dxs = [S.dsem() for _ in range(2)]
            hT = sb(p1, "hT", [128, 8, TT], BF16)
            dh = S.dsem()
            pm = sb(p1, "pm", [128, 14, TT], BF16)
            tmp = sb(p1, "tmp", [128, TT])
            cy = sb(p1, "cy", [128, 14])
            pp = sb(p1, "pp", [128, 2, 16 + TT])
            ppa = sb(p1, "ppa", [128, 16 + TT])
            ppb = sb(p1, "ppb", [128, 16 + TT])
            dT = sb(p1, "dT", [128, 2, TT], BF16)
            qT = sb(p1, "qT", [128, 2, TT], BF16)
            tw = sb(p1, "tw", [128, TT], BF16)
            sg = sb(p1, "sg", [128, TT], BF16)
            fn = ["ld", "cum", "cx", "E1", "E2", "E3", "a", "kk", "rs", "kkn", "x1", "kf"]
            f = {n: sb(p1, "f_" + n, [128, TT]) for n in fn}
            kk2 = sb(p1, "kk2", [128, TT], BF16)
            KR = sb(p1, "KR", [128, 4, NCH, 2, 64], BF16)
            BK = sb(p1, "BK", [128, 4, NCH, 2, 64], BF16)
            WC = sb(p1, "WC", [128, 4, NCH])
            bonT = sb(p1, "bonT", [128, 4, TT], BF16)
            gT = sb(p1, "gT", [128, 4, TT], BF16)
            Asb = [sb(p1, "Asb%d" % i, [128, 4, 2, 128], BF16) for i in range(2)]
            GF = [sb(p1, "GF%d" % i, [128, 4, 64], BF16) for i in range(2)]
            Lsb = sb(p1, "Lsb", [128, 4, 64], BF16)
            GT = [sb(p1, "GT%d" % i, [128, 4, 64], BF16) for i in range(2)]
            P2 = [sb(p1, "P2%d" % i, [128, 4, 64], BF16) for i in range(2)]
            P2T = [sb(p1, "P2T%d" % i, [128, 4, 64], BF16) for i in range(2)]
            Zsb = sb(p1, "Zsb", [128, 4, 64], BF16)
            Un = sb(p1, "Un", [128, 4, 64], BF16)
            VBK = [sb(p1, "VBK%d" % i, [128, 3, 4, 64], BF16) for i in range(2)]
            Ysb = [sb(p1, "Ysb%d" % i, [128, 4, 64]) for i in range(2)]
            Ysq = [sb(p1, "Ysq%d" % i, [128, 4, 64]) for i in range(2)]
            yh = [sb(p1, "yh%d" % i, [128, 4, 64], BF16) for i in range(2)]
            yst = [sb(p1, "yst%d" % i, [128, 4, 4]) for i in range(2)]
            eps_gn = sb(p1, "eps_gn", [128, 1])
            Sf = sb(p1, "Sf", [128, 4, 64])
            Sbf = sb(p1, "Sbf", [128, 4, 64], BF16)
            yhT = sb(p1, "yhT", [128, 4, TT], BF16)
            yT = sb(p1, "yT", [128, 8, TT], BF16)
            dy = S.dsem()
            eT = sb(p1, "eT", [128, 2, TT], BF16)
            rden = sb(p1, "rden", [128, TT])

            S.op("dve", lambda e: e.memset(cy.t[:], 0.0), w=[cy.b()])
            S.op("dve", lambda e: e.memset(pp.t[:], 0.0), w=[pp.b()])
            S.op("dve", lambda e: e.memset(ppa.t[:], 0.0), w=[ppa.b(0), ppa.b(64)])
            S.op("dve", lambda e: e.memset(ppb.t[:], 0.0), w=[ppb.b(0), ppb.b(64)])
            S.op("dve", lambda e: e.memset(Sf.t[:], 0.0), w=[Sf.b()])
            S.op("dve", lambda e: e.memset(Sbf.t[:], 0.0), w=[Sbf.b()])
            S.op("dve", lambda e: e.memset(eps_gn.t[:], 64e-5), w=[eps_gn.b()])
            M2v = cm.t[:, M_M2:M_M2 + 128].unsqueeze(1).broadcast_to([128, 4, 128])
            MLv = cm.t[:, M_ML:M_ML + 64].unsqueeze(1).broadcast_to([128, 4, 64])
            I64v = cm.t[:, M_I64:M_I64 + 64].unsqueeze(1).broadcast_to([128, 4, 64])
            scanm = cm.t[:, M_SCAN:M_SCAN + 512]
            mmb = 0

            for st in range(NST if stop >= 2 else 1):
                t0 = st * TT
                def ldx(idx):
                    ii = idx % 2
                    S.dma("sp", dxs[ii], xs[ii].t[:], x_d[idx * 128:(idx + 1) * 128, :], w=[xs[ii].b()])

                if st == 0:
                    ldx(0)
                for sub in range(4):
                    idx = st * 4 + sub
                    i = idx % 2
                    if idx + 1 < 4 * (NST if stop >= 2 else 1):
                        ldx(idx + 1)
                    norm_T(xs[i].t[:], xs[i].b(), C_G1, hT.t, hT.b(), sub * 128)
                S.dma("sp", dh, hT_d[:, :, t0:t0 + TT], hT.t[:], r=[hT.b()], w=[hT_db[st]])
                if stop == 1.1:
                    break
                for oc in range(18):
                    pbk = PB[mmb % 2]
                    mmb += 1
                    for kc in range(8):
                        S.op("pe", lambda e, oc=oc, kc=kc, pbk=pbk: e.matmul(pbk.t[:], win.t[:, kc, oc * 128:(oc + 1) * 128], hT.t[:, kc, :],
                                                                            start=(kc == 0), stop=(kc == 7)),
                             r=[win.b(), hT.b()], w=[pbk.b()])
                    ps = pbk.t
                    if oc < 14:
                        mu = cvec.t[:, C_MU + oc:C_MU + oc + 1]
                        om = cder.t[:, oc:oc + 1]
                        S.op("act", lambda e, ps=ps, mu=mu: e.activation(out=tmp.t[:, 1:TT], in_=ps[:, 0:TT - 1], func=AF.Copy, scale=mu),
                             r=[pbk.b()] + cb, w=[tmp.b()])
                        S.op("dve", lambda e, oc=oc, mu=mu: e.tensor_scalar(out=tmp.t[:, 0:1], in0=cy.t[:, oc:oc + 1], scalar1=mu, scalar2=None,
                                                                           op0=ALU.mult), r=[cy.b()] + cb, w=[tmp.b()])
                        S.op("dve", lambda e, oc=oc, ps=ps: e.tensor_copy(out=cy.t[:, oc:oc + 1], in_=ps[:, TT - 1:TT]), r=[pbk.b()], w=[cy.b()])
                        S.op("dve", lambda e, oc=oc, ps=ps, om=om: e.scalar_tensor_tensor(out=pm.t[:, oc, :], in0=ps[:, :], scalar=om, in1=tmp.t[:, :],
                                                                                          op0=ALU.mult, op1=ALU.add),
                             r=[pbk.b(), tmp.b()] + cb, w=[pm.b(oc)])
                    elif oc < 16:
                        S.op("act", lambda e, oc=oc, ps=ps: e.activation(out=pp.t[:, oc - 14, 16:16 + TT], in_=ps[:, :], func=AF.Copy),
                             r=[pbk.b()], w=[pp.b()])
                    else:
                        S.op("act", lambda e, oc=oc, ps=ps: e.activation(out=qT.t[:, oc - 16, :], in_=ps[:, :], func=AF.Copy),
                             r=[pbk.b()], w=[qT.b()])
                if stop == 1.2:
                    break
                invc = cm.t[:, (M_INVC0 if st == 0 else M_INVC):(M_INVC0 if st == 0 else M_INVC) + 1024].rearrange("p (c t) -> p c t", c=2)
                for g in range(4):
                    ci, pb = g // 2, 64 * (g % 2)
                    src, bsrc = pp.t[pb:pb + 64, ci, :], pp.b()
                    for lv in range(g + 1):
                        sh = 1 << lv
                        dst = ppa if lv % 2 == 0 else ppb
                        S.op("dve", lambda e, src=src, dst=dst, sh=sh, pb=pb: e.tensor_tensor(out=dst.t[pb:pb + 64, sh:16 + TT], in0=src[:, sh:16 + TT],
                                                                                            in1=src[:, 0:16 + TT - sh], op=ALU.add),
                             r=[bsrc], w=[dst.b(pb)])
                        if sh > 1:
                            pass
                        src, bsrc = dst.t[pb:pb + 64, :], dst.b(pb)
                    S.op("dve", lambda e, src=src, pb=pb, ci=ci: e.tensor_tensor(out=ppa.t[pb:pb + 64, 16:16 + TT] if False else tmp.t[pb:pb + 64, :],
                                                                                 in0=src[:, 16:16 + TT], in1=invc[pb:pb + 64, ci, :], op=ALU.mult),
                         r=[bsrc] + cb, w=[tmp.b()])
                    S.op("dve", lambda e, pb=pb, ci=ci: e.tensor_tensor(out=dT.t[pb:pb + 64, ci, :], in0=tmp.t[pb:pb + 64, :],
                                                                        in1=pp.t[pb:pb + 64, ci, 16:16 + TT], op=ALU.subtract),
                         r=[tmp.b(), pp.b()], w=[dT.b()])
                for ci in range(2):
                    pbk = PB[mmb % 2]
                    mmb += 1
                    for g2 in range(2):
                        pb = 64 * g2
                        S.op("pe", lambda e, ci=ci, pb=pb, pbk=pbk: e.matmul(pbk.t[pb:pb + 64, :], pw.t[pb:pb + 64, ci, :], dT.t[pb:pb + 64, ci, :],
                                                                            start=True, stop=True), r=[pw.b(), dT.b()], w=[pbk.b()])
                    S.op("act", lambda e, ci=ci, pbk=pbk: e.activation(out=yT.t[:, 4 + ci, :], in_=pbk.t[:, :], func=AF.Copy,
                                                                      scale=cvec.t[:, C_PS + ci:C_PS + ci + 1]), r=[pbk.b()] + cb, w=[yT.b(4 + ci)])
                S.op("dve", lambda e: e.tensor_copy(out=pp.t[:, :, 0:16], in_=pp.t[:, :, TT:TT + 16]), r=[pp.b()], w=[pp.b()])
                if stop == 1.3:
                    break
                for jm in range(2):
                    for hh in range(2):
                        pb = 64 * hh
                        hm = 2 * jm + hh
                        for mh in range(2):
                            S.op("pe", lambda e, jm=jm, pb=pb, mh=mh: e.matmul(PB[2 + mh].t[:, :], kT.t[pb:pb + 64, jm, mh * 128:(mh + 1) * 128],
                                                                               qT.t[pb:pb + 64, jm, :], start=True, stop=True),
                                 r=[kT.b(), qT.b()], w=[PB[2 + mh].b()])
                            S.op("act", lambda e, mh=mh: e.activation(out=eT.t[:, mh, :], in_=PB[2 + mh].t[:, :], func=AF.Exp, scale=0.125),
                                 r=[PB[2 + mh].b()], w=[eT.b(mh)])
                        for mh in range(2):
                            S.op("pe", lambda e, hm=hm, pb=pb, mh=mh: e.matmul(PB[4].t[pb:pb + 64, :], vtok.t[:, mh, hm * 64:(hm + 1) * 64], eT.t[:, mh, :],
                                                                               start=(mh == 0), stop=(mh == 1)),
                                 r=[vtok.b(), eT.b(mh)], w=[PB[4].b()])
                        for mh in range(2):
                            S.op("pe", lambda e, pb=pb, mh=mh: e.matmul(PB[5].t[pb:pb + 64, :], ones64, eT.t[:, mh, :],
                                                                        start=(mh == 0), stop=(mh == 1)),
                                 r=[ident.b(), eT.b(mh)], w=[PB[5].b()])
                    S.op("dve", lambda e: e.reciprocal(out=rden.t[:], in_=PB[5].t[:, :]), r=[PB[5].b()], w=[rden.b()])
                    S.op("dve", lambda e, jm=jm: e.tensor_tensor(out=yT.t[:, 6 + jm, :], in0=PB[4].t[:, :], in1=rden.t[:], op=ALU.mult),
                         r=[PB[4].b(), rden.b()], w=[yT.b(6 + jm)])
                if stop == 1.4:
                    break
                S.op("act", lambda e: e.activation(out=tw.t[0:64, :], in_=pm.t[0:64, 12, :], func=AF.Tanh), r=[pm.b(12)], w=[tw.b()])
                S.op("act", lambda e: e.activation(out=sg.t[:], in_=pm.t[:, 13, :], func=AF.Sigmoid), r=[pm.b(13)], w=[sg.b()])
                for j in range(4):
                    cs = slice(j * 128, (j + 1) * 128)
                    r_, k_, v_ = pm.t[:, j, :], pm.t[:, 4 + j, :], pm.t[:, 8 + j, :]
                    cv = lambda c0: cvec.t[:, c0 + j:c0 + j + 1]
                    pbk = PB[mmb % 2]
                    mmb += 1
                    S.op("pe", lambda e, pbk=pbk, cs=cs: e.matmul(pbk.t[:], lora.t[0:64, cs], tw.t[0:64, :], start=True, stop=True),
                         r=[lora.b(), tw.b()], w=[pbk.b()])
                    S.op("act", lambda e, pbk=pbk, cv=cv: e.activation(out=f["ld"].t[:], in_=pbk.t[:], func=AF.Sigmoid, bias=cv(C_W0)),
                         r=[pbk.b()] + cb, w=[f["ld"].b()])
                    S.op("dve", lambda e: e.tensor_scalar(out=f["ld"].t[:], in0=f["ld"].t[:], scalar1=NEG_EH, scalar2=None, op0=ALU.mult),
                         r=[f["ld"].b()], w=[f["ld"].b()])
                    S.op("dve", lambda e: e.tensor_tensor_scan(out=f["cum"].t[:], data0=scanm, data1=f["ld"].t[:], initial=0.0,
                                                               op0=ALU.mult, op1=ALU.add), r=[f["ld"].b()] + cb, w=[f["cum"].b()])
                    S.op("dve", lambda e: e.tensor_tensor(out=f["cx"].t[:], in0=f["cum"].t[:], in1=f["ld"].t[:], op=ALU.subtract),
                         r=[f["cum"].b(), f["ld"].b()], w=[f["cx"].b()])
                    S.op("act", lambda e: e.activation(out=f["E1"].t[:], in_=f["cum"].t[:], func=AF.Exp), r=[f["cum"].b()], w=[f["E1"].b()])
                    S.op("act", lambda e: e.activation(out=f["E2"].t[:], in_=f["cum"].t[:], func=AF.Exp, scale=-1.0), r=[f["cum"].b()], w=[f["E2"].b()])
                    S.op("act", lambda e: e.activation(out=f["E3"].t[:], in_=f["cx"].t[:], func=AF.Exp), r=[f["cx"].b()], w=[f["E3"].b()])
                    S.op("dve", lambda e, j=j: e.tensor_copy(out=WC.t[:, j, :], in_=f["E1"].t[:, :].rearrange("p (c t) -> p c t", t=64)[:, :, 63]),
                         r=[f["E1"].b()], w=[WC.b()])
                    pbk = PB[mmb % 2]
                    mmb += 1
                    S.op("pe", lambda e, pbk=pbk, cs=cs: e.matmul(pbk.t[:], lora.t[64:128, cs], pm.t[64:128, 12, :], start=True, stop=True),
                         r=[lora.b(), pm.b(12)], w=[pbk.b()])
                    S.op("act", lambda e, pbk=pbk, cv=cv: e.activation(out=f["a"].t[:], in_=pbk.t[:], func=AF.Sigmoid, bias=cv(C_A0)),
                         r=[pbk.b()] + cb, w=[f["a"].b()])
                    S.op("dve", lambda e, k_=k_, cv=cv: e.tensor_scalar(out=f["kk"].t[:], in0=k_, scalar1=cv(C_KK), scalar2=None, op0=ALU.mult),
                         r=[pm.b(4 + j)] + cb, w=[f["kk"].b()])
                    S.op("dve", lambda e: e.tensor_tensor(out=kk2.t[:], in0=f["kk"].t[:], in1=f["kk"].t[:], op=ALU.mult),
                         r=[f["kk"].b()], w=[kk2.b()])
                    pbk = PB[mmb % 2]
                    mmb += 1
                    S.op("pe", lambda e, pbk=pbk: e.matmul(pbk.t[:], bdo, kk2.t[:], start=True, stop=True), r=[ident.b(), kk2.b()], w=[pbk.b()])
                    S.op("act", lambda e, pbk=pbk: e.activation(out=f["rs"].t[:], in_=pbk.t[:], func=AF.Sqrt, bias=1e-12), r=[pbk.b()], w=[f["rs"].b()])
                    S.op("dve", lambda e: e.reciprocal(out=f["rs"].t[:], in_=f["rs"].t[:]), r=[f["rs"].b()], w=[f["rs"].b()])
                    S.op("dve", lambda e: e.tensor_tensor(out=f["kkn"].t[:], in0=f["kk"].t[:], in1=f["rs"].t[:], op=ALU.mult),
                         r=[f["kk"].b(), f["rs"].b()], w=[f["kkn"].b()])
                    c3 = lambda tl: tl.t[:, :].rearrange("p (c t) -> p c t", t=64)
                    S.op("dve", lambda e, j=j: e.tensor_tensor(out=KR.t[:, j, :, 0, :], in0=c3(f["kkn"]), in1=c3(f["E3"]), op=ALU.mult),
                         r=[f["kkn"].b(), f["E3"].b()], w=[KR.b(j)])
                    S.op("dve", lambda e, j=j, r_=r_: e.tensor_tensor(out=KR.t[:, j, :, 1, :], in0=r_.rearrange("p (c t) -> p c t", t=64), in1=c3(f["E1"]),
                                                                     op=ALU.mult), r=[pm.b(j), f["E1"].b()], w=[KR.b(j)])
                    S.op("dve", lambda e: e.tensor_tensor(out=f["x1"].t[:], in0=f["kkn"].t[:], in1=f["a"].t[:], op=ALU.mult),
                         r=[f["kkn"].b(), f["a"].b()], w=[f["x1"].b()])
                    S.op("dve", lambda e, j=j: e.tensor_tensor(out=BK.t[:, j, :, 0, :], in0=c3(f["x1"]), in1=c3(f["E2"]), op=ALU.mult),
                         r=[f["x1"].b(), f["E2"].b()], w=[BK.b(j)])
                    S.op("dve", lambda e, j=j, cv=cv: e.tensor_scalar(out=f["x1"].t[:], in0=f["a"].t[:], scalar1=cv(C_KA), scalar2=cder.t[:, 14 + j:15 + j],
                                                                     op0=ALU.mult, op1=ALU.add), r=[f["a"].b()] + cb, w=[f["x1"].b()])
                    S.op("dve", lambda e, k_=k_: e.tensor_tensor(out=f["kf"].t[:], in0=f["x1"].t[:], in1=k_, op=ALU.mult),
                         r=[f["x1"].b(), pm.b(4 + j)], w=[f["kf"].b()])
                    S.op("dve", lambda e, j=j: e.tensor_tensor(out=BK.t[:, j, :, 1, :], in0=c3(f["kf"]), in1=c3(f["E2"]), op=ALU.mult),
                         r=[f["kf"].b(), f["E2"].b()], w=[BK.b(j)])
                    S.op("dve", lambda e, r_=r_: e.tensor_tensor(out=f["x1"].t[:], in0=f["kf"].t[:], in1=r_, op=ALU.mult),
                         r=[f["kf"].b(), pm.b(j)], w=[f["x1"].b()])
                    S.op("dve", lambda e, cv=cv: e.tensor_scalar(out=kk2.t[:], in0=f["x1"].t[:], scalar1=cv(C_RK), scalar2=None, op0=ALU.mult),
                         r=[f["x1"].b()] + cb, w=[kk2.b()])
                    pbk = PB[mmb % 2]
                    mmb += 1
                    S.op("pe", lambda e, pbk=pbk: e.matmul(pbk.t[:], bdo, kk2.t[:], start=True, stop=True), r=[ident.b(), kk2.b()], w=[pbk.b()])
                    S.op("dve", lambda e, pbk=pbk, j=j, v_=v_: e.tensor_tensor(out=bonT.t[:, j, :], in0=pbk.t[:], in1=v_, op=ALU.mult),
                         r=[pbk.b(), pm.b(8 + j)], w=[bonT.b(j)])
                    pbk = PB[mmb % 2]
                    mmb += 1
                    S.op("pe", lambda e, pbk=pbk, cs=cs: e.matmul(pbk.t[:], gl.t[:, cs], sg.t[:], start=True, stop=True), r=[gl.b(), sg.b()], w=[pbk.b()])
                    S.op("act", lambda e, pbk=pbk, j=j: e.activation(out=gT.t[:, j, :], in_=pbk.t[:], func=AF.Copy), r=[pbk.b()], w=[gT.b(j)])
                if stop == 1.5:
                    break
                krb = [KR.b(j) for j in range(4)]
                bkb = [BK.b(j) for j in range(4)]
                HP = [(h // 2, 64 * (h % 2)) for h in range(8)]
                v3 = lambda ap_, w=64: ap_.rearrange("p (j c) -> p j c", j=4)
                lo = lambda bank, w=64: v3(bank.t[:, 0:4 * w], w)
                hi = lambda bank: v3(bank.t[:, 256:512])

                def stageA(c):
                    par = c % 2
                    cc = slice(c * 64, (c + 1) * 64)
                    vbk, asb, gf = VBK[par], Asb[par], GF[par]
                    for j, pb in HP:
                        ps_ = slice(pb, pb + 64)
                        S.op("pe", lambda e: e.transpose(PT.t[ps_, j * 64:(j + 1) * 64], pm.t[ps_, 8 + j, cc], idn[ps_, ps_]),
                             r=[pm.b(8 + j), ident.b()], w=[PT.b("A")])
                        S.op("pe", lambda e: e.transpose(PT.t[ps_, 256 + j * 64:256 + (j + 1) * 64], BK.t[ps_, j, c, 0, :], idn[ps_, ps_]),
                             r=[bkb[j], ident.b()], w=[PT.b("A")])
                        S.op("pe", lambda e: e.transpose(PT.t[ps_, 512 + j * 64:512 + (j + 1) * 64], BK.t[ps_, j, c, 1, :], idn[ps_, ps_]),
                             r=[bkb[j], ident.b()], w=[PT.b("A")])
                    for j, pb in HP:
                        ps_ = slice(pb, pb + 64)
                        for kind in range(2):
                            S.op("pe", lambda e: e.matmul(PB[2 + kind].t[ps_, j * 128:(j + 1) * 128], BK.t[ps_, j, c, kind, :],
                                                          KR.t[ps_, j, c, :, :].rearrange("p a b -> p (a b)"), start=True, stop=True),
                                 r=[bkb[j], krb[j]], w=[PB[2 + kind].b()])
                        S.op("pe", lambda e: e.matmul(PB[4].t[ps_, j * 64:(j + 1) * 64], KR.t[ps_, j, c, 0, :], BK.t[ps_, j, c, 0, :],
                                                      start=True, stop=True), r=[bkb[j], krb[j]], w=[PB[4].b()])
                    yield
                    S.op("act", lambda e: e.activation(out=vbk.t[:], in_=PT.t[:, 0:768].rearrange("p (a j c) -> p a j c", a=3, j=4), func=AF.Copy),
                         r=[PT.b("A")], w=[vbk.b()])
                    S.op("dve", lambda e: e.tensor_tensor(out=asb.t[:, :, 0, :], in0=lo(PB[2], 128), in1=M2v, op=ALU.mult),
                         r=[PB[2].b()] + cb, w=[asb.b()])
                    S.op("dve", lambda e: e.tensor_tensor(out=Lsb.t[:], in0=lo(PB[4]), in1=MLv, op=ALU.mult), r=[PB[4].b()] + cb, w=[Lsb.b()])
                    S.op("dve", lambda e: e.scalar_tensor_tensor(out=GT[0].t[:], in0=asb.t[:, :, 0, 0:64], scalar=-1.0, in1=I64v,
                                                                 op0=ALU.mult, op1=ALU.add), r=[asb.b()] + cb, w=[GT[0].b()])
                    S.op("dve", lambda e: e.tensor_tensor(out=asb.t[:, :, 1, :], in0=lo(PB[3], 128), in1=M2v, op=ALU.mult),
                         r=[PB[3].b()] + cb, w=[asb.b()])
                    yield
                    Pc, PTc, bP, bPT = Lsb.t, asb.t[:, :, 0, 0:64], Lsb.b(), asb.b()
                    gi = 0
                    pend = None
                    for lv in range(6):
                        if lv < 5:
                            for j, pb in HP:
                                ps_ = slice(pb, pb + 64)
                                S.op("pe", lambda e: e.matmul(PB[4].t[ps_, j * 64:(j + 1) * 64], PTc[ps_, j, :], Pc[ps_, j, :], start=True, stop=True),
                                     r=[bP, bPT], w=[PB[4].b()])
                            if lv < 4:
                                for j, pb in HP:
                                    ps_ = slice(pb, pb + 64)
                                    S.op("pe", lambda e: e.matmul(PB[5].t[ps_, j * 64:(j + 1) * 64], Pc[ps_, j, :], PTc[ps_, j, :], start=True, stop=True),
                                         r=[bP, bPT], w=[PB[5].b()])
                        if pend is not None:
                            pn2, plv = pend
                            gsrc = GT[gi]
                            gdst = gf if plv == 4 else GT[1 - gi]
                            for j, pb in HP:
                                ps_ = slice(pb, pb + 64)
                                S.op("pe", lambda e: e.matmul(PB[6].t[ps_, j * 64:(j + 1) * 64], pn2.t[ps_, j, :], gsrc.t[ps_, j, :], start=True, stop=True),
                                     r=[pn2.b(), gsrc.b()], w=[PB[6].b()])
                        yield
                        if lv < 5:
                            n2 = P2[lv % 2]
                            S.op("act", lambda e: e.activation(out=n2.t[:], in_=lo(PB[4]), func=AF.Copy), r=[PB[4].b()], w=[n2.b()])
                            if lv < 4:
                                n2t = P2T[lv % 2]
                                S.op("act", lambda e: e.activation(out=n2t.t[:], in_=lo(PB[5]), func=AF.Copy), r=[PB[5].b()], w=[n2t.b()])
                        if pend is not None:
                            S.op("dve", lambda e: e.tensor_tensor(out=gdst.t[:], in0=lo(PB[6]), in1=gsrc.t[:], op=ALU.add),
                                 r=[PB[6].b(), gsrc.b()], w=[gdst.b()])
                            gi = 1 - gi
                            pend = None
                        if lv < 5:
                            pend = (n2, lv)
                            if lv < 4:
                                Pc, PTc, bP, bPT = n2.t, n2t.t, n2.b(), n2t.b()
                            yield

                def stageB1(c):
                    par = c % 2
                    vbk, asb, G = VBK[par], Asb[par], GF[par]
                    ysb, ysq = Ysb[par], Ysq[par]
                    Vt, Bt, Kt = vbk.t[:, 0], vbk.t[:, 1], vbk.t[:, 2]
                    b0, b1 = PB[0].b(), PB[1].b()
                    for j, pb in HP:
                        ps_ = slice(pb, pb + 64)
                        S.op("pe", lambda e: e.matmul(PB[0].t[ps_, j * 64:(j + 1) * 64], KR.t[ps_, j, c, 0, :], Sbf.t[ps_, j, :], start=True, stop=False),
                             r=[krb[j], Sbf.b()], w=[b0])
                        S.op("pe", lambda e: e.matmul(PB[0].t[ps_, j * 64:(j + 1) * 64], asb.t[ps_, j, 1, 0:64], Vt[ps_, j, :], start=False, stop=True),
                             r=[asb.b(), vbk.b()], w=[b0])
                    yield
                    S.op("act", lambda e: e.activation(out=Zsb.t[:], in_=lo(PB[0]), func=AF.Copy), r=[b0], w=[Zsb.b()])
                    yield
                    for j, pb in HP:
                        ps_ = slice(pb, pb + 64)
                        S.op("pe", lambda e: e.matmul(PB[0].t[ps_, j * 64:(j + 1) * 64], G.t[ps_, j, :], Zsb.t[ps_, j, :], start=True, stop=True),
                             r=[G.b(), Zsb.b()], w=[b0])
                    yield
                    S.op("act", lambda e: e.activation(out=Un.t[:], in_=lo(PB[0]), func=AF.Copy, scale=-1.0), r=[b0], w=[Un.b()])
                    yield
                    for j, pb in HP:
                        ps_ = slice(pb, pb + 64)
                        S.op("pe", lambda e: e.matmul(PB[0].t[ps_, j * 64:(j + 1) * 64], Kt[ps_, j, :], Vt[ps_, j, :], start=True, stop=False),
                             r=[vbk.b()], w=[b0])
                        S.op("pe", lambda e: e.matmul(PB[0].t[ps_, j * 64:(j + 1) * 64], Bt[ps_, j, :], Un.t[ps_, j, :], start=False, stop=True),
                             r=[vbk.b(), Un.b()], w=[b0])
                    for j, pb in HP:
                        ps_ = slice(pb, pb + 64)
                        S.op("pe", lambda e: e.matmul(PB[1].t[ps_, j * 64:(j + 1) * 64], KR.t[ps_, j, c, 1, :], Sbf.t[ps_, j, :], start=True, stop=False),
                             r=[krb[j], Sbf.b()], w=[b1])
                        S.op("pe", lambda e: e.matmul(PB[1].t[ps_, j * 64:(j + 1) * 64], asb.t[ps_, j, 1, 64:128], Vt[ps_, j, :], start=False, stop=False),
                             r=[asb.b(), vbk.b()], w=[b1])
                        S.op("pe", lambda e: e.matmul(PB[1].t[ps_, j * 64:(j + 1) * 64], asb.t[ps_, j, 0, 64:128], Un.t[ps_, j, :], start=False, stop=True),
                             r=[asb.b(), Un.b()], w=[b1])
                    yield
                    S.op("dve", lambda e: e.tensor_tensor(out=Sf.t[:], in0=lo(PB[0]), in1=Sf.t[:], op=ALU.add), r=[b0, Sf.b()], w=[Sf.b()])
                    S.op("act", lambda e: e.activation(out=ysb.t[:], in_=lo(PB[1]), func=AF.Copy), r=[b1], w=[ysb.b()])
                    S.op("act", lambda e: e.activation(out=ysq.t[:], in_=lo(PB[1]), func=AF.Square), r=[b1], w=[ysq.b()])
                    yield
                    S.op("dve", lambda e: e.tensor_tensor(out=Sf.t[:], in0=Sf.t[:], in1=WC.t[:, :, c].unsqueeze(2).broadcast_to([128, 4, 64]), op=ALU.mult),
                         r=[Sf.b(), WC.b()], w=[Sf.b()])
                    yield
                    S.op("act", lambda e: e.activation(out=Sbf.t[:], in_=Sf.t[:], func=AF.Copy), r=[Sf.b()], w=[Sbf.b()])

                def stageB2(c):
                    par = c % 2
                    cc = slice(c * 64, (c + 1) * 64)
                    ysb, ysq, ys, yh_ = Ysb[par], Ysq[par], yst[par], yh[par]
                    S.op("dve", lambda e: e.tensor_reduce(out=ys.t[:, 0, :], in_=ysb.t[:], axis=AX.X, op=ALU.add), r=[ysb.b()], w=[ys.b()])
                    S.op("dve", lambda e: e.tensor_reduce(out=ys.t[:, 1, :], in_=ysq.t[:], axis=AX.X, op=ALU.add), r=[ysq.b()], w=[ys.b()])
                    yield
                    S.op("dve", lambda e: e.tensor_scalar(out=ys.t[:, 0, :], in0=ys.t[:, 0, :], scalar1=1.0 / 64, scalar2=None, op0=ALU.mult),
                         r=[ys.b()], w=[ys.b()])
                    yield
                    S.op("dve", lambda e: e.tensor_tensor(out=ys.t[:, 2, :], in0=ys.t[:, 0, :], in1=ys.t[:, 0, :], op=ALU.mult), r=[ys.b()], w=[ys.b()])
                    yield
                    S.op("dve", lambda e: e.scalar_tensor_tensor(out=ys.t[:, 3, :], in0=ys.t[:, 1, :], scalar=1.0 / 64, in1=ys.t[:, 2, :],
                                                                 op0=ALU.mult, op1=ALU.subtract), r=[ys.b()], w=[ys.b()])
                    yield
                    S.op("act", lambda e: e.activation(out=ys.t[:, 3, :], in_=ys.t[:, 3, :], func=AF.Sqrt, bias=eps_gn.t[:, 0:1]), r=[ys.b(), eps_gn.b()], w=[ys.b()])
                    yield
                    S.op("dve", lambda e: e.reciprocal(out=ys.t[:, 3, :], in_=ys.t[:, 3, :]), r=[ys.b()], w=[ys.b()])
                    S.op("dve", lambda e: e.tensor_tensor(out=ysb.t[:], in0=ysb.t[:], in1=ys.t[:, 0, :].unsqueeze(2).broadcast_to([128, 4, 64]), op=ALU.subtract),
                         r=[ysb.b(), ys.b()], w=[ysb.b()])
                    yield
                    S.op("dve", lambda e: e.tensor_tensor(out=yh_.t[:], in0=ysb.t[:], in1=ys.t[:, 3, :].unsqueeze(2).broadcast_to([128, 4, 64]), op=ALU.mult),
                         r=[ysb.b(), ys.b()], w=[yh_.b()])
                    yield
                    for j, pb in HP:
                        ps_ = slice(pb, pb + 64)
                        S.op("pe", lambda e: e.transpose(PT.t[ps_, 768 + j * 64:768 + (j + 1) * 64], yh_.t[ps_, j, :], idn[ps_, ps_]),
                             r=[yh_.b(), ident.b()], w=[PT.b("A")])
                    yield
                    S.op("act", lambda e: e.activation(out=yhT.t[:, :, cc], in_=PT.t[:, 768:1024].rearrange("p (j t) -> p j t", j=4), func=AF.Copy),
                         r=[PT.b("A")], w=[yhT.b()])

                for c in range(NCH + 2):
                    gens = []
                    if 1 <= c <= NCH:
                        gens.append(stageB1(c - 1))
                    if c < NCH:
                        gens.append(stageA(c))
                    if 2 <= c:
                        gens.append(stageB2(c - 2))
                    while gens:
                        for g_ in list(gens):
                            try:
                                next(g_)
                            except StopIteration:
                                gens.remove(g_)
                if stop == 1.6:
                    break
                for j in range(4):
                    S.op("dve", lambda e, j=j: e.tensor_scalar(out=f["x1"].t[:], in0=yhT.t[:, j, :], scalar1=cvec.t[:, C_LNW + j:C_LNW + j + 1],
                                                               scalar2=cvec.t[:, C_LNB + j:C_LNB + j + 1], op0=ALU.mult, op1=ALU.add),
                         r=[yhT.b()] + cb, w=[f["x1"].b()])
                    S.op("dve", lambda e, j=j: e.tensor_tensor(out=f["x1"].t[:], in0=f["x1"].t[:], in1=bonT.t[:, j, :], op=ALU.add),
                         r=[f["x1"].b(), bonT.b(j)], w=[f["x1"].b()])
                    S.op("dve", lambda e, j=j: e.tensor_tensor(out=yT.t[:, j, :], in0=f["x1"].t[:], in1=gT.t[:, j, :], op=ALU.mult),
                         r=[f["x1"].b(), gT.b(j)], w=[yT.b(j)])
                S.dma("sp", dy, yT_d[:, :, t0:t0 + TT], yT.t[:], r=[yT.b(jj) for jj in range(8)], w=[yT_db[st]])
            S.emit()
            if stop <= 2:
                S.barrier()
                S.emit()
                return nc

        rot = [0]

        def nb():
            rot[0] += 1
            return PB[rot[0] % 7]

        with contextlib.ExitStack() as p2:
            S.barrier()
            dw2 = S.dsem()
            wg = load_bf(p2, "wg", 8, 3 * D, dw2)
            wup = load_bf(p2, "wup", 8, D, dw2)
            wo = load_bf(p2, "wo", 8, D, dw2)
            S.seal(dw2, [wg.b(), wup.b(), wo.b()])
            hT2 = [sb(p2, "hT2%d" % i, [128, 8, TT], BF16) for i in range(2)]
            yT2 = [sb(p2, "yT2%d" % i, [128, 8, TT], BF16) for i in range(2)]
            dl2 = [S.dsem() for _ in range(2)]
            gs = [sb(p2, "gs%d" % i, [128, TT]) for i in range(3)]
            mm_ = [sb(p2, "mm%d" % i, [128, TT]) for i in range(3)]
            mg = sb(p2, "mg", [128, 8, TT], BF16)
            xs2 = [sb(p2, "xs2%d" % i, [128, D]) for i in range(2)]
            dx2 = [S.dsem() for _ in range(2)]
            x1s = [sb(p2, "x1s%d" % i, [128, D]) for i in range(2)]
            ds2 = [S.dsem() for _ in range(2)]
            kr = [(0, 4), (4, 6), (6, 8)]
            def ld2(st_):
                i_, c0 = st_ % 2, st_ * TT
                S.dma("sp", dl2[i_], hT2[i_].t[:], hT_d[:, :, c0:c0 + TT], r=[hT_db[st_]], w=[hT2[i_].b()])
                S.dma("sp", dl2[i_], yT2[i_].t[:], yT_d[:, :, c0:c0 + TT], r=[yT_db[st_]], w=[yT2[i_].b()])
                S.seal(dl2[i_], [hT2[i_].b(), yT2[i_].b()])

            ld2(0)
            for st in range(NST):
                t0 = st * TT
                i2 = st % 2
                if st + 1 < NST:
                    ld2(st + 1)
                for fo in range(8):
                    fs = slice(fo * 128, (fo + 1) * 128)
                    for b in range(3):
                        pg, pu = nb(), nb()
                        for kc in range(8):
                            S.op("pe", lambda e: e.matmul(pg.t[:], wg.t[:, kc, b * D + fo * 128:b * D + (fo + 1) * 128], hT2[i2].t[:, kc, :],
                                                          start=(kc == 0), stop=(kc == 7)), r=[wg.b(), hT2[i2].b()], w=[pg.b()])
                        k0, k1 = kr[b]
                        for kc in range(k0, k1):
                            S.op("pe", lambda e: e.matmul(pu.t[:], wup.t[:, kc, fs], yT2[i2].t[:, kc, :], start=(kc == k0), stop=(kc == k1 - 1)),
                                 r=[wup.b(), yT2[i2].b()], w=[pu.b()])
                        S.op("act", lambda e: e.activation(out=gs[b].t[:], in_=pg.t[:], func=AF.Sigmoid,
                                                           bias=cvec.t[:, C_BG + b * 8 + fo:C_BG + b * 8 + fo + 1]), r=[pg.b()] + cb, w=[gs[b].b()])
                        S.op("dve", lambda e: e.tensor_tensor(out=mm_[b].t[:], in0=pu.t[:], in1=gs[b].t[:], op=ALU.mult),
                             r=[pu.b(), gs[b].b()], w=[mm_[b].b()])
                    S.op("pool", lambda e: e.tensor_tensor(out=mm_[0].t[:], in0=mm_[0].t[:], in1=mm_[1].t[:], op=ALU.add),
                         r=[mm_[0].b(), mm_[1].b()], w=[mm_[0].b()])
                    S.op("pool", lambda e: e.tensor_tensor(out=mg.t[:, fo, :], in0=mm_[0].t[:], in1=mm_[2].t[:], op=ALU.add),
                         r=[mm_[0].b(), mm_[2].b()], w=[mg.b()])
                def ldx2(idx):
                    ii = idx % 2
                    S.dma("sp", dx2[ii], xs2[ii].t[:], x_d[idx * 128:(idx + 1) * 128, :], w=[xs2[ii].b()])

                if st == 0:
                    ldx2(0)
                for sub in range(4):
                    idx = st * 4 + sub
                    i = idx % 2
                    r0 = t0 + sub * 128
                    if idx + 1 < 4 * NST:
                        ldx2(idx + 1)
                    for half in range(2):
                        pbk = nb()
                        for kc in range(8):
                            S.op("pe", lambda e: e.matmul(pbk.t[:], mg.t[:, kc, sub * 128:(sub + 1) * 128], wo.t[:, kc, half * 512:(half + 1) * 512],
                                                          start=(kc == 0), stop=(kc == 7)), r=[mg.b(), wo.b()], w=[pbk.b()])
                        S.op("dve", lambda e: e.tensor_tensor(out=x1s[i].t[:, half * 512:(half + 1) * 512], in0=pbk.t[:],
                                                              in1=xs2[i].t[:, half * 512:(half + 1) * 512], op=ALU.add),
                             r=[pbk.b(), xs2[i].b()], w=[x1s[i].b()])
                    S.dma("sp", ds2[i], x1_d[r0:r0 + 128, :], x1s[i].t[:], r=[x1s[i].b()], w=[x1_db[st * 4 + sub]])
            S.emit()
            if stop == 3:
                S.barrier()
                S.emit()
                return nc

        with contextlib.ExitStack() as p3:
            S.barrier()
            dw3 = S.dsem()
            wfi = load_bf(p3, "wfi", 8, 2 * DFF, dw3)
            wfo = load_bf(p3, "wfo", NFC, D, dw3)
            gfin = sb(p3, "gfin", [128, D])
            dgf = S.dsem()
            S.dma("sp", dgf, gfin.t[:], gfin_d[:, :], w=[gfin.b()])
            S.seal(dw3, [wfi.b(), wfo.b()])
            x1k = sb(p3, "x1k", [128, 4, D])
            dk = [S.dsem() for _ in range(4)]
            h2T = sb(p3, "h2T", [128, 8, TT], BF16)
            ub = [sb(p3, "ub%d" % i, [128, 2 + TT]) for i in range(2)]
            ucar = sb(p3, "ucar", [128, NFC, 2])
            c1 = [sb(p3, "c1%d" % i, [128, TT]) for i in range(2)]
            actT = sb(p3, "actT", [128, NFC, TT], BF16)
            x2s = [sb(p3, "x2s%d" % i, [128, D]) for i in range(2)]
            do = [S.dsem() for _ in range(2)]
            fst = sb(p3, "fst", [128, 2])
            S.op("dve", lambda e: e.memset(ucar.t[:], 0.0), w=[ucar.b()])
            dxr = [S.dsem() for _ in range(2)]

            def ld3(st_):
                for sub_ in range(4):
                    q0 = st_ * TT + sub_ * 128
                    S.dma("sp", dk[sub_], x1k.t[:, sub_, :], x1_d[q0:q0 + 128, :], r=[x1_db[st_ * 4 + sub_]], w=[x1k.b(sub_)])

            ld3(0)
            for st in range(NST):
                t0 = st * TT
                for sub in range(4):
                    norm_T(x1k.t[:, sub, :], x1k.b(sub), C_G2, h2T.t, h2T.b(), sub * 128)
                if st + 1 < NST:
                    ld3(st + 1)
                for fc in range(NFC):
                    pu, pgv = nb(), nb()
                    u_, c_ = ub[fc % 2], c1[fc % 2]
                    for half, pbk in ((0, pu), (1, pgv)):
                        for kc in range(8):
                            S.op("pe", lambda e: e.matmul(pbk.t[:], wfi.t[:, kc, half * DFF + fc * 128:half * DFF + (fc + 1) * 128], h2T.t[:, kc, :],
                                                          start=(kc == 0), stop=(kc == 7)), r=[wfi.b(), h2T.b()], w=[pbk.b()])
                    cw = lambda jx: cvec.t[:, C_CW + jx * NFC + fc:C_CW + jx * NFC + fc + 1]
                    S.op("act", lambda e: e.activation(out=u_.t[:, 2:2 + TT], in_=pu.t[:], func=AF.Copy), r=[pu.b()], w=[u_.b()])
                    S.op("act", lambda e: e.activation(out=c_.t[:, 2:TT], in_=pu.t[:, 0:TT - 2], func=AF.Copy, scale=cw(0)), r=[pu.b()] + cb, w=[c_.b()])
                    S.op("pool", lambda e: e.tensor_copy(out=u_.t[:, 0:2], in_=ucar.t[:, fc, :]), r=[ucar.b()], w=[u_.b()])
                    S.op("pool", lambda e: e.tensor_scalar(out=c_.t[:, 0:2], in0=ucar.t[:, fc, :], scalar1=cw(0), scalar2=None, op0=ALU.mult),
                         r=[ucar.b()] + cb, w=[c_.b()])
                    S.op("pool", lambda e: e.tensor_copy(out=ucar.t[:, fc, :], in_=u_.t[:, TT:TT + 2]), r=[u_.b()], w=[ucar.b()])
                    S.op("dve", lambda e: e.scalar_tensor_tensor(out=c_.t[:], in0=u_.t[:, 1:1 + TT], scalar=cw(1), in1=c_.t[:], op0=ALU.mult, op1=ALU.add),
                         r=[u_.b(), c_.b()] + cb, w=[c_.b()])
                    S.op("dve", lambda e: e.scalar_tensor_tensor(out=c_.t[:], in0=u_.t[:, 2:2 + TT], scalar=cw(2), in1=c_.t[:], op0=ALU.mult, op1=ALU.add),
                         r=[u_.b(), c_.b()] + cb, w=[c_.b()])
                    S.op("act", lambda e: e.activation(out=c_.t[:], in_=c_.t[:], func=AF.Gelu, bias=cvec.t[:, C_CB + fc:C_CB + fc + 1]),
                         r=[c_.b()] + cb, w=[c_.b()])
                    S.op("dve", lambda e: e.tensor_tensor(out=actT.t[:, fc, :], in0=pgv.t[:], in1=c_.t[:], op=ALU.mult),
                         r=[pgv.b(), c_.b()], w=[actT.b()])
                def ldr(idx):
                    ii = idx % 2
                    S.dma("sp", dxr[ii], x2s[ii].t[:], x1_d[idx * 128:(idx + 1) * 128, :], r=[x1_db[idx]], w=[x2s[ii].b()])

                if st == 0:
                    ldr(0)
                for sub in range(4):
                    idx = st * 4 + sub
                    i = idx % 2
                    r0 = t0 + sub * 128
                    if idx + 1 < 4 * NST:
                        ldr(idx + 1)
                    for half in range(2):
                        pbk = nb()
                        for fc in range(NFC):
                            S.op("pe", lambda e: e.matmul(pbk.t[:], actT.t[:, fc, sub * 128:(sub + 1) * 128], wfo.t[:, fc, half * 512:(half + 1) * 512],
                                                          start=(fc == 0), stop=(fc == NFC - 1)), r=[actT.b(), wfo.b()], w=[pbk.b()])
                        S.op("dve", lambda e: e.tensor_tensor(out=x2s[i].t[:, half * 512:(half + 1) * 512], in0=pbk.t[:],
                                                              in1=x2s[i].t[:, half * 512:(half + 1) * 512], op=ALU.add),
                             r=[pbk.b(), x2s[i].b()], w=[x2s[i].b()])
                    ss, ms = fst.t[:, 0:1], fst.t[:, 1:2]
                    S.op("act", lambda e: e.activation(out=junk.t[:], in_=x2s[i].t[:], func=AF.Square, accum_out=ss),
                         r=[x2s[i].b(), fst.b()], w=[junk.b(), fst.b()])
                    S.op("dve", lambda e: e.tensor_scalar(out=ms, in0=ss, scalar1=1.0 / D, scalar2=1e-6, op0=ALU.mult, op1=ALU.add), r=[fst.b()], w=[fst.b()])
                    S.op("act", lambda e: e.activation(out=ms, in_=ms, func=AF.Sqrt), r=[fst.b()], w=[fst.b()])
                    S.op("dve", lambda e: e.reciprocal(out=ms, in_=ms), r=[fst.b()], w=[fst.b()])
                    S.op("act", lambda e: e.activation(out=x2s[i].t[:], in_=x2s[i].t[:], func=AF.Copy, scale=ms), r=[x2s[i].b(), fst.b()], w=[x2s[i].b()])
                    S.op("dve", lambda e: e.tensor_tensor(out=x2s[i].t[:], in0=x2s[i].t[:], in1=gfin.t[:], op=ALU.mult),
                         r=[x2s[i].b(), gfin.b()], w=[x2s[i].b()])
                    S.dma("sp", do[i], out_d[r0:r0 + 128, :], x2s[i].t[:], r=[x2s[i].b()])
            S.wait_all("sp", [(d[0], d[1], None) for d in do])
            S.emit()
    return nc


def _cols(v, n):
    return np.ascontiguousarray(np.asarray(v, np.float32).reshape(n, 128).T)


def _host_consts():
    cm = np.zeros((128, NCM), np.float32)
    s = np.arange(128)[:, None] % 64
    t = np.arange(64)[None, :]
    cm[:, M_M2:M_M2 + 64] = (s < t)
    cm[:, M_M2 + 64:M_M2 + 128] = (s <= t)
    tt = np.arange(128)[:, None] % 64
    ss = np.arange(64)[None, :]
    cm[:, M_ML:M_ML + 64] = (tt > ss)
    cm[:, M_I64:M_I64 + 64] = (tt == ss)
    sc = np.ones(512, np.float32)
    sc[::64] = 0.0
    cm[:, M_SCAN:M_SCAN + 512] = sc[None, :]
    wins = [2, 4, 8, 16]
    for g in range(4):
        ci, pb = g // 2, 64 * (g % 2)
        pos = np.arange(1, 513)
        cm[pb:pb + 64, M_INVC0 + ci * 512:M_INVC0 + (ci + 1) * 512] = (1.0 / np.minimum(pos, wins[g]))[None, :]
        cm[pb:pb + 64, M_INVC + ci * 512:M_INVC + (ci + 1) * 512] = 1.0 / wins[g]
    cmb = np.zeros((128, 320), np.float32)
    cmb[:, 0:128] = np.eye(128)
    cmb[0:64, 128:192] = 1.0
    cmb[64:128, 192:256] = 1.0
    cmb[:, 256:320] = 1.0
    return cm, cmb


_NC_CACHE = {}


def _prep(inputs):
    g = lambda k: np.asarray(inputs[k], np.float32)
    cv = np.zeros((128, NCV), np.float32)
    cv[:, C_G1:C_G1 + 8] = _cols(g("norm_mix_g")[0], 8)
    cv[:, C_MU:C_MU + 14] = _cols(g("mu_shift")[0], 14)
    cv[:, C_W0:C_W0 + 4] = _cols(g("w0")[0], 4)
    cv[:, C_A0:C_A0 + 4] = _cols(g("a0")[0], 4)
    cv[:, C_KK:C_KK + 4] = _cols(g("k_k")[0], 4)
    cv[:, C_KA:C_KA + 4] = _cols(g("k_a")[0], 4)
    cv[:, C_RK:C_RK + 4] = _cols(g("r_k")[0].reshape(-1), 4)
    cv[:, C_LNW:C_LNW + 4] = _cols(g("ln_x_w")[0], 4)
    cv[:, C_LNB:C_LNB + 4] = _cols(g("ln_x_b")[0], 4)
    cv[:, C_PS:C_PS + 2] = _cols(g("pool_scale")[0], 2)
    cv[:, C_BG:C_BG + 24] = _cols(g("b_gate")[0], 24)
    cv[:, C_G2:C_G2 + 8] = _cols(g("norm_ffn_g")[0], 8)
    for j in range(3):
        cv[:, C_CW + j * NFC:C_CW + (j + 1) * NFC] = _cols(g("ffn_conv_w")[0, j], NFC)
    cv[:, C_CB:C_CB + NFC] = _cols(g("ffn_conv_b")[0], NFC)
    cv[:, C_GM:C_GM + 8] = _cols(g("norm_mem_g")[0], 8)
    cm, cmb = _host_consts()
    shared = {
        "w_in_mix": g("w_in_mix")[0], "w_lora_b": g("w_lora_b")[0], "a_lora_b": g("a_lora_b")[0], "g_lora_b": g("g_lora_b")[0],
        "pool_w": g("pool_w")[0], "w_mem_kv": g("w_mem_kv")[0], "w_up_rwkv": g("w_up_rwkv")[0], "w_up_pool": g("w_up_pool")[0],
        "w_up_mem": g("w_up_mem")[0], "w_gate": g("w_gate")[0], "w_o": g("w_o")[0], "w_ffn_in": g("w_ffn_in")[0],
        "w_ffn_out": g("w_ffn_out")[0], "cvec": cv, "cm32": cm, "cmb": cmb,
        "gfin": np.ascontiguousarray(np.broadcast_to(g("norm_final_g")[None, :], (128, D))),
    }
    shared = {k: np.ascontiguousarray(v, dtype=np.float32) for k, v in shared.items()}
    x = g("x")
    mem = g("mem")
    return [dict(shared, x=np.ascontiguousarray(x[b]), mem=np.ascontiguousarray(mem[b])) for b in range(8)]


def kernel(**inputs):
    in_maps = _prep(inputs)
    if "nc" not in _NC_CACHE:
        _NC_CACHE["nc"] = build(False)
    res = run_bass_kernel_spmd(_NC_CACHE["nc"], in_maps, core_ids=list(range(8)))
    return np.stack([np.asarray(r["out"], np.float32) for r in res.results], axis=0)
```
